# Optimizing a Trainium2 kernel written in Bass

```python
import math
import jax, jax.numpy as jnp
from jax import lax
import numpy as np

D_MODEL = 1024
BATCH = 16
SEQ = 2048
DEPTH = 4

N_EVEN = (DEPTH + 1) // 2
N_ODD = DEPTH // 2

CONV_WIDTH = D_MODEL
LRU_WIDTH = D_MODEL
CONV_KERNEL = 31
LRU_CONV_KERNEL = 4
LRU_HEADS = 16
LRU_HEAD_DIM = LRU_WIDTH // LRU_HEADS
LRU_C = 8.0
EVEN_IN = 3 * CONV_WIDTH + 2 * LRU_WIDTH
EVEN_OUT = CONV_WIDTH + LRU_WIDTH

MLA_HEADS = 16
QK_NOPE = 64
QK_ROPE = 32
V_HEAD = 64
Q_LORA = 256
KV_LORA = 256
MLA_WIDTH = MLA_HEADS * V_HEAD
ODD_IN = Q_LORA + KV_LORA + QK_ROPE + MLA_WIDTH
ROPE_THETA = 10000.0
Q_BLOCK = 128
EPS = 1e-6

kernel_name = 'hybrid_conv_rglru_mla_adaln_trunk'

F32 = jnp.float32


def rms_norm(x, g):
    xf = x.astype(F32)
    y = xf * lax.rsqrt(jnp.mean(xf * xf, axis=-1, keepdims=True) + EPS)
    return (y * g.astype(F32)).astype(x.dtype)


def layer_norm(x, g, b):
    xf = x.astype(F32)
    mu = jnp.mean(xf, axis=-1, keepdims=True)
    xc = xf - mu
    y = xc * lax.rsqrt(jnp.mean(xc * xc, axis=-1, keepdims=True) + EPS)
    return (y * g.astype(F32) + b.astype(F32)).astype(x.dtype)


def causal_depthwise_conv(x, w, b):
    k = w.shape[0]
    y = lax.conv_general_dilated(
        x, w[:, None, :], window_strides=(1,), padding=[(k - 1, 0)],
        dimension_numbers=('NWC', 'WIO', 'NWC'), feature_group_count=x.shape[-1])
    return y + b


def _linear_recurrence_combine(left, right):
    a_l, b_l = left
    a_r, b_r = right
    return a_l * a_r, a_r * b_l + b_r


def rg_lru(x, wa, ba, wx, bx, lam):
    bsz, s, e = x.shape
    xh = x.reshape(bsz, s, LRU_HEADS, LRU_HEAD_DIM)
    r = jax.nn.sigmoid((jnp.einsum('bshi,hij->bshj', xh, wa).reshape(bsz, s, e) + ba).astype(F32))
    i = jax.nn.sigmoid((jnp.einsum('bshi,hij->bshj', xh, wx).reshape(bsz, s, e) + bx).astype(F32))
    log_a = -LRU_C * r * jax.nn.softplus(-lam.astype(F32))
    a = jnp.exp(log_a)
    mult = jnp.sqrt(-jnp.expm1(2.0 * log_a))
    b = mult * i * x.astype(F32)
    _, h = lax.associative_scan(_linear_recurrence_combine, (a, b), axis=1)
    return h.astype(x.dtype)


def conv_lru_mixer(h, w_in, conv_w, conv_b, ln_g, ln_b, lru_conv_w, lru_conv_b,
                   lru_wa, lru_ba, lru_wx, lru_bx, lru_lam, w_out):
    u = h @ w_in
    c0, c1, c2, c3 = CONV_WIDTH, 2 * CONV_WIDTH, 3 * CONV_WIDTH, 3 * CONV_WIDTH + LRU_WIDTH
    va, ga, za, xb, zb = jnp.split(u, [c0, c1, c2, c3], axis=-1)
    a = va * jax.nn.sigmoid(ga)
    a = causal_depthwise_conv(a, conv_w, conv_b)
    a = layer_norm(a, ln_g, ln_b)
    a = jax.nn.silu(a) * jax.nn.silu(za)
    xb = causal_depthwise_conv(xb, lru_conv_w, lru_conv_b)
    yb = rg_lru(xb, lru_wa, lru_ba, lru_wx, lru_bx, lru_lam) * jax.nn.silu(zb)
    return jnp.concatenate([a, yb], axis=-1) @ w_out


def rope_cos_sin(positions):
    inv = ROPE_THETA ** (-jnp.arange(0, QK_ROPE, 2, dtype=F32) / QK_ROPE)
    ang = positions.astype(F32)[..., None] * inv
    return jnp.cos(ang), jnp.sin(ang)


def apply_rope(x, cos, sin):
    xf = x.astype(F32)
    x1, x2 = jnp.split(xf, 2, axis=-1)
    return jnp.concatenate([x1 * cos - x2 * sin, x1 * sin + x2 * cos], axis=-1).astype(x.dtype)


def causal_mla_attention(q_nope, q_rope, k_nope, k_rope, v):
    s_len = q_nope.shape[1]
    scale = (QK_NOPE + QK_ROPE) ** -0.5
    outs = []
    for blk in range(s_len // Q_BLOCK):
        q0 = blk * Q_BLOCK
        kend = q0 + Q_BLOCK
        s = (jnp.einsum('bqhd,bkhd->bhqk', q_nope[:, q0:kend], k_nope[:, :kend])
             + jnp.einsum('bqhr,bkr->bhqk', q_rope[:, q0:kend], k_rope[:, :kend]))
        s = s.astype(F32) * scale
        mask = jnp.arange(kend)[None, :] <= (q0 + jnp.arange(Q_BLOCK))[:, None]
        s = jnp.where(mask, s, -jnp.inf)
        p = jax.nn.softmax(s, axis=-1).astype(v.dtype)
        outs.append(jnp.einsum('bhqk,bkhd->bqhd', p, v[:, :kend]))
    return jnp.concatenate(outs, axis=1)


def mla_mixer(h, cos, sin, w_in, q_norm, kv_norm, w_uq, w_ukv, w_out):
    bsz, s, _ = h.shape
    u = h @ w_in
    cq, ckv, k_rope, z = jnp.split(u, [Q_LORA, Q_LORA + KV_LORA, Q_LORA + KV_LORA + QK_ROPE], axis=-1)
    q = (rms_norm(cq, q_norm) @ w_uq).reshape(bsz, s, MLA_HEADS, QK_NOPE + QK_ROPE)
    kv = (rms_norm(ckv, kv_norm) @ w_ukv).reshape(bsz, s, MLA_HEADS, QK_NOPE + V_HEAD)
    q_nope, q_rope = jnp.split(q, [QK_NOPE], axis=-1)
    k_nope, v = jnp.split(kv, [QK_NOPE], axis=-1)
    q_rope = apply_rope(q_rope, cos[:, :, None, :], sin[:, :, None, :])
    k_rope = apply_rope(k_rope, cos, sin)
    o = causal_mla_attention(q_nope, q_rope, k_nope, k_rope, v).reshape(bsz, s, MLA_WIDTH)
    return (o * jax.nn.silu(z)) @ w_out


def setup_inputs(seed: int = 0) -> dict:
    key = jax.random.key(seed)
    ks = iter(jax.random.split(key, 40))

    def nrm(shape, scale):
        return jax.random.normal(next(ks), shape, F32) * scale

    def gain(shape):
        return 1.0 + 0.05 * jax.random.normal(next(ks), shape, F32)

    x = nrm((BATCH, SEQ, D_MODEL), 1.0)
    c = nrm((BATCH, D_MODEL), 1.0)
    offsets = jax.random.randint(next(ks), (BATCH, 1), 0, 4096, dtype=jnp.int32)
    positions = (offsets + jnp.arange(SEQ, dtype=jnp.int32)[None, :]).astype(jnp.int32)

    u = jax.random.uniform(next(ks), (N_EVEN, LRU_WIDTH), F32, minval=0.9, maxval=0.999)
    a_base = u ** (1.0 / LRU_C)
    lru_lam = jnp.log(a_base) - jnp.log1p(-a_base)

    return {
        'x': x,
        'c': c,
        'positions': positions,
        'ada_w': nrm((DEPTH, D_MODEL, 3 * D_MODEL), 0.5 * D_MODEL ** -0.5),
        'ada_b': nrm((DEPTH, 3 * D_MODEL), 0.02),
        'pre_g': gain((DEPTH, D_MODEL)),
        'post_g': gain((DEPTH, D_MODEL)),
        'ev_w_in': nrm((N_EVEN, D_MODEL, EVEN_IN), D_MODEL ** -0.5),
        'ev_conv_w': nrm((N_EVEN, CONV_KERNEL, CONV_WIDTH), CONV_KERNEL ** -0.5),
        'ev_conv_b': nrm((N_EVEN, CONV_WIDTH), 0.02),
        'ev_ln_g': gain((N_EVEN, CONV_WIDTH)),
        'ev_ln_b': nrm((N_EVEN, CONV_WIDTH), 0.02),
        'ev_lru_conv_w': nrm((N_EVEN, LRU_CONV_KERNEL, LRU_WIDTH), LRU_CONV_KERNEL ** -0.5),
        'ev_lru_conv_b': nrm((N_EVEN, LRU_WIDTH), 0.02),
        'ev_lru_wa': nrm((N_EVEN, LRU_HEADS, LRU_HEAD_DIM, LRU_HEAD_DIM), LRU_HEAD_DIM ** -0.5),
        'ev_lru_ba': nrm((N_EVEN, LRU_WIDTH), 0.02),
        'ev_lru_wx': nrm((N_EVEN, LRU_HEADS, LRU_HEAD_DIM, LRU_HEAD_DIM), LRU_HEAD_DIM ** -0.5),
        'ev_lru_bx': nrm((N_EVEN, LRU_WIDTH), 0.02),
        'ev_lru_lam': lru_lam,
        'ev_w_out': nrm((N_EVEN, EVEN_OUT, D_MODEL), EVEN_OUT ** -0.5),
        'od_w_in': nrm((N_ODD, D_MODEL, ODD_IN), D_MODEL ** -0.5),
        'od_q_norm': gain((N_ODD, Q_LORA)),
        'od_kv_norm': gain((N_ODD, KV_LORA)),
        'od_w_uq': nrm((N_ODD, Q_LORA, MLA_HEADS * (QK_NOPE + QK_ROPE)), Q_LORA ** -0.5),
        'od_w_ukv': nrm((N_ODD, KV_LORA, MLA_HEADS * (QK_NOPE + V_HEAD)), KV_LORA ** -0.5),
        'od_w_out': nrm((N_ODD, MLA_WIDTH, D_MODEL), MLA_WIDTH ** -0.5),
    }


def reference(x, c, positions, ada_w, ada_b, pre_g, post_g,
              ev_w_in, ev_conv_w, ev_conv_b, ev_ln_g, ev_ln_b, ev_lru_conv_w, ev_lru_conv_b,
              ev_lru_wa, ev_lru_ba, ev_lru_wx, ev_lru_bx, ev_lru_lam, ev_w_out,
              od_w_in, od_q_norm, od_kv_norm, od_w_uq, od_w_ukv, od_w_out):
    cos, sin = rope_cos_sin(positions)
    c_act = jax.nn.silu(c)
    for layer in range(DEPTH):
        mod = c_act @ ada_w[layer] + ada_b[layer]
        shift, scale, gate = jnp.split(mod, 3, axis=-1)
        h = rms_norm(x, pre_g[layer]) * (1.0 + scale[:, None, :]) + shift[:, None, :]
        j = layer // 2
        if layer % 2 == 0:
            y = conv_lru_mixer(h, ev_w_in[j], ev_conv_w[j], ev_conv_b[j], ev_ln_g[j], ev_ln_b[j],
                               ev_lru_conv_w[j], ev_lru_conv_b[j], ev_lru_wa[j], ev_lru_ba[j],
                               ev_lru_wx[j], ev_lru_bx[j], ev_lru_lam[j], ev_w_out[j])
        else:
            y = mla_mixer(h, cos, sin, od_w_in[j], od_q_norm[j], od_kv_norm[j],
                          od_w_uq[j], od_w_ukv[j], od_w_out[j])
        x = x + gate[:, None, :] * rms_norm(y, post_g[layer])
    return x
```

```python
import math
from contextlib import ExitStack
import numpy as np
import concourse.bass as bass
import concourse.mybir as mybir
from concourse.ap import AP
from concourse.bass_utils import run_bass_kernel_spmd

F32 = mybir.dt.float32
BF16 = mybir.dt.bfloat16
I32 = mybir.dt.int32
AF = mybir.ActivationFunctionType
ALU = mybir.AluOpType

D = 1024
KC = 8
NCORES = 8
EPS = 1e-6
SLOT = 1024
RING = 10
SAME_ENG_WINDOW = 3
EVEN_STAGGER = 9
ATT_SCALE = 96.0 ** -0.5
TWO_PI = 2.0 * math.pi
PI_HI = 6.28125
PI_LO = TWO_PI - 6.28125


def weight_plan():
    plan = {}
    off = 0

    def add(name, F):
        nonlocal off
        plan[name] = (off, F)
        off += 128 * F

    for l in range(4):
        for n in range(24):
            add(("ada", l, n), 1024)
    for j in range(2):
        for c in range(40):
            add(("ewin", j, c), 1024)
        for kc in range(8):
            add(("egate", j, kc), 256)
        for n in range(8):
            add(("ewout", j, n, 0), 1024)
            add(("ewout", j, n, 1), 1024)
    for j in range(2):
        for c in range(14):
            add(("owin", j, c), 1024)
        for h in range(16):
            add(("ouq", j, h), 384)
            add(("oukv", j, h), 256)
        for n in range(8):
            add(("owout", j, n), 1024)
    return plan, off


def pv_plan():
    cols = {}
    off = 0

    def add(name, n):
        nonlocal off
        cols[name] = off
        off += n

    for l in range(4):
        add(("pre_g", l), 8)
        add(("post_g", l), 8)
        add(("ada_b", l), 24)
    for j in range(2):
        add(("conv_w", j), 8 * 31)
        add(("conv_b", j), 8)
        add(("ln_g", j), 8)
        add(("ln_b", j), 8)
        add(("lconv_w", j), 8 * 4)
        add(("lconv_b", j), 8)
        add(("ba", j), 8)
        add(("bx", j), 8)
        add(("lam", j), 8)
        add(("q_norm", j), 2)
        add(("kv_norm", j), 2)
    add("inv", 1)
    add("sgn", 1)
    return cols, off


def chunkify(W):
    K, N = W.shape
    return np.ascontiguousarray(W.reshape(K // 128, 128, N // 128, 128).transpose(2, 1, 0, 3))


def vec_pc(v):
    return np.ascontiguousarray(v.reshape(-1, 128).T)


def pack_weights(inp):
    plan, total = weight_plan()
    flat = np.zeros(total, np.float32)

    def put(name, arr):
        off, F = plan[name]
        a = np.ascontiguousarray(arr, dtype=np.float32).reshape(128, F)
        flat[off:off + 128 * F] = a.reshape(-1)

    for l in range(4):
        ch = chunkify(inp["ada_w"][l])
        for n in range(24):
            put(("ada", l, n), ch[n])
    for j in range(2):
        ch = chunkify(inp["ev_w_in"][j])
        for c in range(40):
            put(("ewin", j, c), ch[c])
        wa, wx = inp["ev_lru_wa"][j], inp["ev_lru_wx"][j]
        for kc in range(8):
            g = np.zeros((128, 2, 128), np.float32)
            for hh in range(2):
                g[hh * 64:(hh + 1) * 64, 0, hh * 64:(hh + 1) * 64] = wa[2 * kc + hh]
                g[hh * 64:(hh + 1) * 64, 1, hh * 64:(hh + 1) * 64] = wx[2 * kc + hh]
            put(("egate", j, kc), g)
        wo = inp["ev_w_out"][j]
        c0 = chunkify(wo[:1024])
        c1 = chunkify(wo[1024:])
        for n in range(8):
            put(("ewout", j, n, 0), c0[n])
            put(("ewout", j, n, 1), c1[n])
    for j in range(2):
        w = inp["od_w_in"][j]
        z64 = np.zeros((1024, 64), np.float32)
        z32 = np.zeros((1024, 32), np.float32)
        kr = w[:, 512:544]
        kr1 = np.concatenate([z64, kr, z32], axis=1)
        kr2 = np.concatenate([z64, kr[:, 16:32], kr[:, 0:16], z32], axis=1)
        wcat = np.concatenate([w[:, 0:512], kr1, kr2, w[:, 544:1568]], axis=1)
        ch = chunkify(wcat)
        for c in range(14):
            put(("owin", j, c), ch[c])
        uq = inp["od_w_uq"][j].reshape(2, 128, 16, 96)
        ukv = inp["od_w_ukv"][j].reshape(2, 128, 16, 128)
        for h in range(16):
            a = uq[:, :, h, :]
            sw = np.concatenate([a[:, :, 0:64], a[:, :, 80:96], a[:, :, 64:80]], axis=2)
            both = np.concatenate([a, sw], axis=2)
            put(("ouq", j, h), both.transpose(1, 0, 2))
            put(("oukv", j, h), ukv[:, :, h, :].transpose(1, 0, 2))
        ch = chunkify(inp["od_w_out"][j])
        for n in range(8):
            put(("owout", j, n), ch[n])
    return flat


def pack_pv(inp):
    cols, n = pv_plan()
    pv = np.zeros((128, n), np.float32)

    def put(name, arr):
        a = np.asarray(arr, np.float32)
        pv[:, cols[name]:cols[name] + a.shape[1]] = a

    for l in range(4):
        put(("pre_g", l), vec_pc(inp["pre_g"][l]))
        put(("post_g", l), vec_pc(inp["post_g"][l]))
        put(("ada_b", l), vec_pc(inp["ada_b"][l]))
    for j in range(2):
        cw = inp["ev_conv_w"][j]
        put(("conv_w", j), cw.reshape(31, 8, 128).transpose(2, 1, 0).reshape(128, 8 * 31))
        put(("conv_b", j), vec_pc(inp["ev_conv_b"][j]))
        put(("ln_g", j), vec_pc(inp["ev_ln_g"][j]))
        put(("ln_b", j), vec_pc(inp["ev_ln_b"][j]))
        lw = inp["ev_lru_conv_w"][j]
        put(("lconv_w", j), lw.reshape(4, 8, 128).transpose(2, 1, 0).reshape(128, 8 * 4))
        put(("lconv_b", j), vec_pc(inp["ev_lru_conv_b"][j]))
        put(("ba", j), vec_pc(inp["ev_lru_ba"][j]))
        put(("bx", j), vec_pc(inp["ev_lru_bx"][j]))
        put(("lam", j), vec_pc(inp["ev_lru_lam"][j]))
        put(("q_norm", j), vec_pc(inp["od_q_norm"][j]))
        put(("kv_norm", j), vec_pc(inp["od_kv_norm"][j]))
    inv = (10000.0 ** (-np.arange(0, 32, 2, dtype=np.float32) / 32.0)).astype(np.float32)
    iv = np.zeros((128, 1), np.float32)
    sg = np.ones((128, 1), np.float32)
    for i in range(32):
        iv[64 + i, 0] = inv[i % 16]
        sg[64 + i, 0] = -1.0 if i < 16 else 1.0
    put("inv", iv)
    put("sgn", sg)
    return pv


_UID = [0]


class Buf:
    __slots__ = ("name", "w", "r", "dsem", "dcnt")

    def __init__(self, name):
        _UID[0] += 1
        self.name = f"{name}_{_UID[0]}"
        self.w = None
        self.r = {}
        self.dsem = None
        self.dcnt = 0


class Sync:
    ENG = ("pe", "act", "dve", "pool", "sp")

    def __init__(self, nc, es):
        self.nc = nc
        self.es = es
        self.eng = {"pe": nc.tensor, "act": nc.scalar, "dve": nc.vector, "pool": nc.gpsimd, "sp": nc.sync}
        self.sem = {k: es.enter_context(nc.semaphore("s_" + k)) for k in self.ENG}
        self.cnt = {k: 0 for k in self.ENG}
        self.known = {k: {} for k in self.ENG}
        self.nins = 0

    def _need(self, E, dep, strict):
        key, sem, val, src = dep
        if src == E and not strict and E != "pool":
            if E == "pe" or (self.cnt[E] - val) >= SAME_ENG_WINDOW:
                return
        if self.known[E].get(key, 0) >= val:
            return
        self.eng[E].wait_ge(sem, val)
        self.known[E][key] = val
        self.nins += 1

    def _deps(self, E, reads, writes, sreads, strict_all=False):
        for b in reads:
            if b.w is not None:
                self._need(E, b.w, strict_all)
        for b in sreads:
            if b.w is not None:
                self._need(E, b.w, True)
        for b in writes:
            if b.w is not None:
                self._need(E, b.w, strict_all)
            for d in b.r.values():
                self._need(E, d, strict_all)

    def op(self, E, fn, reads=(), writes=(), sreads=(), inc=True):
        self._deps(E, reads, writes, sreads)
        ins = fn(self.eng[E])
        self.nins += 1
        if inc:
            self.cnt[E] += 1
            ins.then_inc(self.sem[E], 1)
            me = (E, self.sem[E], self.cnt[E], E)
        else:
            me = (E, self.sem[E], self.cnt[E] + 1, E)
        for b in writes:
            b.w = me
            b.r = {}
        for b in reads:
            b.r[E] = me
        for b in sreads:
            b.r[E] = me
        return ins

    def dma(self, Q, out, in_, reads=(), writes=()):
        self._deps(Q, reads, writes, (), strict_all=True)
        tgt = writes[0] if writes else reads[0]
        if tgt.dsem is None:
            tgt.dsem = self.es.enter_context(self.nc.semaphore("d_" + tgt.name))
        tgt.dcnt += 16
        self.eng[Q].dma_start(out=out, in_=in_).then_inc(tgt.dsem, 16)
        self.nins += 1
        me = ("d_" + tgt.name, tgt.dsem, tgt.dcnt, None)
        for b in writes:
            b.w = me
            b.r = {}
        for b in reads:
            b.r["dma_" + tgt.name] = me
        return me

    def fence(self):
        for E in ("pe", "act", "dve", "pool", "sp"):
            for Fg in ("pe", "act", "dve", "pool"):
                if Fg != E and self.cnt[Fg] > 0:
                    self._need(E, (Fg, self.sem[Fg], self.cnt[Fg], Fg), True)


def build(S=2048, NSEQ=2, LAYERS=(0, 1, 2, 3), DEBUG=False):
    order = _build(S, NSEQ, LAYERS, DEBUG, None)
    return _build(S, NSEQ, LAYERS, DEBUG, order)


def _build(S, NSEQ, LAYERS, DEBUG, ORDER):
    RECORD = ORDER is None
    nc = bass.Bass("TRN2", target_bir_lowering=False)
    plan, wtotal = weight_plan()
    pcols, npv = pv_plan()
    NB = S // 512

    x_d = nc.dram_tensor("x", [NSEQ, 128, KC, S], F32, kind="ExternalInput")
    c_d = nc.dram_tensor("c", [128, KC, NSEQ], F32, kind="ExternalInput")
    pos_d = nc.dram_tensor("pos", [NSEQ, S], I32, kind="ExternalInput")
    w_d = nc.dram_tensor("wts", [wtotal], F32, kind="ExternalInput")
    pv_d = nc.dram_tensor("pv", [128, npv], F32, kind="ExternalInput")
    out_d = nc.dram_tensor("out", [NSEQ, 128, KC, S], F32, kind="ExternalOutput")

    dbg_d = nc.dram_tensor("dbg", [128, 16384], F32, kind="ExternalOutput") if DEBUG else None
    dbg_cols = {}
    build.dbg_cols = dbg_cols
    dbg_state = {"c": 0}

    with ExitStack() as es:
        sy = Sync(nc, es)
        dbgB = Buf("dbg")

        def dump(name, ap, bufs):
            if not DEBUG or name in dbg_cols:
                return
            n = ap.shape[-1]
            p0 = 0
            c0 = dbg_state["c"]
            dbg_cols[name] = (c0, n)
            dbg_state["c"] += n
            sy.dma("pool", dbg_d.ap()[0:ap.shape[0], c0:c0 + n], ap, reads=bufs, writes=[dbgB])

        def sb(name, shape, dt):
            return es.enter_context(nc.sbuf_tensor(name, shape, dt))

        xs = sb("xs", [128, KC, S], F32)
        xsB = [[Buf(f"xs{c}_{b}") for b in range(NB)] for c in range(KC)]
        pvt = sb("pvt", [128, npv], F32)
        pvB = Buf("pv")
        ring = sb("ring", [128, RING, SLOT], BF16)
        ringB = [Buf(f"ring{i}") for i in range(RING)]
        ident_f = sb("ident_f", [128, 128], F32)
        ident_b = sb("ident_b", [128, 128], BF16)
        od1024 = sb("od1024", [128, 128], BF16)
        od256 = sb("od256", [128, 128], BF16)
        maskT = sb("maskT", [128, 128], BF16)
        constB = Buf("const")
        cin = sb("cin", [128, KC, NSEQ], F32)
        cact = sb("cact", [128, KC, NSEQ], BF16)
        cB = Buf("c")
        modt = sb("modt", [128, 96, NSEQ], F32)
        modB = Buf("mod")
        drvA = sb("drvA", [128, 4, NSEQ, 8], F32)
        drvG = sb("drvG", [128, 4, NSEQ, 8], F32)
        drvB = Buf("drv")
        nsp = sb("nsp", [128, 2, 8], F32)
        nspB = Buf("nsp")
        carry = sb("carry", [128, 8], F32)
        carryB = [Buf(f"carry{j}") for j in range(8)]
        xbh = sb("xbh", [128, 8, 4], BF16)
        xbhB = [Buf(f"xbh{j}") for j in range(8)]

        psum = [es.enter_context(nc.psum_tensor(f"ps{i}", [128, 512], F32)) for i in range(8)]
        psB = [Buf(f"ps{i}") for i in range(8)]
        ps_state = {"i": 0}

        def nextps():
            i = 2 + ps_state["i"] % 6
            ps_state["i"] += 1
            return psum[i], psB[i]

        def pv(name, a, b=None):
            c0 = pcols[name]
            if b is None:
                return pvt[:, c0 + a:c0 + a + 1]
            return pvt[:, c0 + a:c0 + b]

        ws = {"order": [] if RECORD else ORDER, "issued": 0, "acq": 0, "rel": 0, "done": set()}

        def w_ap(name):
            off, Fw = plan[name]
            return AP(w_d, off, [[Fw, 128], [1, Fw]]), Fw

        def ws_issue():
            if RECORD:
                return
            while ws["issued"] < len(ws["order"]) and ws["issued"] < ws["rel"] + RING:
                i = ws["issued"]
                src, Fw = w_ap(ws["order"][i])
                slot = i % RING
                sy.dma("pool", ring[:, slot, 0:Fw], src, writes=[ringB[slot]])
                ws["issued"] += 1

        def acquire(name):
            i = ws["acq"]
            ws["acq"] += 1
            if RECORD:
                ws["order"].append(name)
                return ring[:, 0, :], ringB[0], i
            assert ws["order"][i] == name, (ws["order"][i], name)
            assert ws["issued"] > i, "weight ring deadlock (acquire beyond issued)"
            slot = i % RING
            return ring[:, slot, :], ringB[slot], i

        def release(i):
            ws["done"].add(i)
            while ws["rel"] in ws["done"]:
                ws["done"].remove(ws["rel"])
                ws["rel"] += 1
            ws_issue()

        sy.dma("sp", pvt[:], pv_d.ap()[:, :], writes=[pvB])
        sy.dma("sp", cin[:], c_d.ap()[:, :, :], writes=[cB])
        ws_issue()
        sy.op("pool", lambda e: e.memset(ident_f[:], 1.0), writes=[constB])
        sy.op("pool", lambda e: e.affine_select(out=ident_f[:], in_=ident_f[:], pattern=[[-1, 128]], base=0,
                                                channel_multiplier=1, compare_op=ALU.is_equal, fill=0.0),
              writes=[constB])
        sy.op("pool", lambda e: e.tensor_copy(out=ident_b[:], in_=ident_f[:]), writes=[constB])
        sy.op("pool", lambda e: e.memset(od1024[:], 1.0 / 1024), writes=[constB])
        sy.op("pool", lambda e: e.memset(od256[:], 1.0 / 256), writes=[constB])
        sy.op("pool", lambda e: e.memset(maskT[:], 0.0), writes=[constB])
        sy.op("pool", lambda e: e.affine_select(out=maskT[:], in_=maskT[:], pattern=[[1, 128]], base=0,
                                                channel_multiplier=-1, compare_op=ALU.is_ge, fill=-30000.0),
              writes=[constB])
        sy.op("act", lambda e: e.activation(out=cact[:], in_=cin[:], func=AF.Silu), reads=[cB], writes=[cB])

        for l in range(4):
            for n in range(24):
                wt, wB, wi = acquire(("ada", l, n))
                pt, pB = nextps()
                for kc in range(KC):
                    sy.op("pe", lambda e, kc=kc: e.matmul(pt[:, 0:NSEQ], lhsT=wt[:, kc * 128:(kc + 1) * 128],
                                                          rhs=cact[:, kc, :], start=(kc == 0), stop=(kc == KC - 1)),
                          reads=[wB, cB], writes=[pB], inc=(kc == KC - 1))
                release(wi)
                sy.op("act", lambda e: e.activation(out=modt[:, l * 24 + n, :], in_=pt[:, 0:NSEQ], func=AF.Identity,
                                                    bias=pv(("ada_b", l), n), scale=1.0),
                      reads=[pB], sreads=[pvB], writes=[modB])
        for l in range(4):
            for s in range(NSEQ):
                sy.op("dve", lambda e: e.tensor_scalar(out=drvA[:, l, s, :], in0=modt[:, l * 24 + 8:l * 24 + 16, s],
                                                       scalar1=1.0, scalar2=None, op0=ALU.add),
                      reads=[modB], writes=[drvB])
                sy.op("dve", lambda e: e.tensor_tensor(out=drvA[:, l, s, :], in0=drvA[:, l, s, :],
                                                       in1=pv(("pre_g", l), 0, 8), op=ALU.mult),
                      reads=[pvB], writes=[drvB])
                sy.op("dve", lambda e: e.tensor_tensor(out=drvG[:, l, s, :], in0=modt[:, l * 24 + 16:l * 24 + 24, s],
                                                       in1=pv(("post_g", l), 0, 8), op=ALU.mult),
                      reads=[pvB, modB], writes=[drvB])
        spt = [sb(f"spt{i}", [128, 8], F32) for i in range(4)]
        for j in range(2):
            lam_ap = pv(("lam", j), 0, 8)
            al, ee, ww, w2 = spt
            sy.op("act", lambda e: e.activation(out=al[:], in_=lam_ap, func=AF.Abs),
                  reads=[pvB], writes=[nspB])
            sy.op("act", lambda e: e.activation(out=ee[:], in_=al[:], func=AF.Exp, scale=-1.0),
                  reads=[nspB], writes=[nspB])
            sy.op("dve", lambda e: e.tensor_scalar(out=ww[:], in0=ee[:], scalar1=2.0, scalar2=None, op0=ALU.add),
                  reads=[nspB], writes=[nspB])
            sy.op("dve", lambda e: e.reciprocal(out=ww[:], in_=ww[:]), writes=[nspB])
            sy.op("dve", lambda e: e.tensor_tensor(out=ww[:], in0=ww[:], in1=ee[:], op=ALU.mult), writes=[nspB])
            sy.op("dve", lambda e: e.tensor_tensor(out=w2[:], in0=ww[:], in1=ww[:], op=ALU.mult), writes=[nspB])
            sy.op("dve", lambda e: e.tensor_scalar(out=al[:], in0=w2[:], scalar1=1.0 / 11, scalar2=1.0 / 9, op0=ALU.mult,
                                                   op1=ALU.add), writes=[nspB])
            for cf in (1.0 / 7, 1.0 / 5, 1.0 / 3, 1.0):
                sy.op("dve", lambda e: e.tensor_tensor(out=al[:], in0=al[:], in1=w2[:], op=ALU.mult), writes=[nspB])
                sy.op("dve", lambda e, cf=cf: e.tensor_scalar(out=al[:], in0=al[:], scalar1=cf, scalar2=None, op0=ALU.add),
                      writes=[nspB])
            sy.op("dve", lambda e: e.tensor_tensor(out=al[:], in0=al[:], in1=ww[:], op=ALU.mult), writes=[nspB])
            sy.op("dve", lambda e: e.tensor_scalar(out=ee[:], in0=lam_ap, scalar1=-1.0, scalar2=0.0, op0=ALU.mult,
                                                   op1=ALU.max), reads=[pvB], writes=[nspB])
            sy.op("dve", lambda e: e.scalar_tensor_tensor(out=al[:], in0=al[:], scalar=2.0, in1=ee[:], op0=ALU.mult,
                                                          op1=ALU.add), writes=[nspB])
            sy.op("dve", lambda e: e.tensor_scalar(out=nsp[:, j, :], in0=al[:], scalar1=-8.0, scalar2=None,
                                                   op0=ALU.mult), writes=[nspB])
        dump("modt", modt[:].rearrange("p a b -> p (a b)"), [modB])
        dump("drvA", drvA[:].rearrange("p a b c -> p (a b c)"), [drvB])
        dump("drvG", drvG[:].rearrange("p a b c -> p (a b c)"), [drvB])
        dump("nsp", nsp[:].rearrange("p a b -> p (a b)"), [nspB])
        tp_stats = [None]
        def mm8(wt, wB, rhs_fn, rB, M=128, wcol0=0, prow=None):
            pt, pB = nextps()
            for kc in range(KC):
                sy.op("pe", lambda e, kc=kc: e.matmul(pt[0:M, :], lhsT=wt[:, kc * 128 + wcol0:kc * 128 + wcol0 + M],
                                                      rhs=rhs_fn(kc), start=(kc == 0), stop=(kc == KC - 1)),
                      reads=[wB] + rB, writes=[pB], inc=(kc == KC - 1))
            return pt, pB

        def rms_stats(src_fn, srcB, nch, onesm, sqt, sqB, tp):
            tp = tp_stats[0] or tp
            sy.op("act", lambda e: e.activation(out=sqt[:, 0:nch, :], in_=src_fn(), func=AF.Square),
                  reads=srcB, writes=[sqB])
            pt, pB = nextps()
            for kc in range(nch):
                sy.op("pe", lambda e, kc=kc: e.matmul(pt[:, :], lhsT=onesm[:], rhs=sqt[:, kc, :], start=(kc == 0),
                                                      stop=(kc == nch - 1)),
                      reads=[constB, sqB], writes=[pB], inc=(kc == nch - 1))
            sd, sdB = tp()
            sy.op("act", lambda e: e.activation(out=sd[:], in_=pt[:], func=AF.Sqrt, bias=EPS, scale=1.0),
                  reads=[pB], writes=[sdB])
            rs, rsB = tp()
            sy.op("dve", lambda e: e.reciprocal(out=rs[:], in_=sd[:]), reads=[sdB], writes=[rsB])
            return rs, rsB

        def prenorm(l, s, t0, hn_fn, hnB, sqt, sqB, tp, tpt=None):
            tpt = tpt or tp
            b = t0 // 512
            rs, rsB = rms_stats(lambda: xs[:, :, t0:t0 + 512], [xsB[c][b] for c in range(KC)], KC, od1024, sqt, sqB, tp)
            for kc in range(KC):
                tt, ttB = tpt()
                sy.op("dve", lambda e: e.tensor_tensor(out=tt[:], in0=xs[:, kc, t0:t0 + 512], in1=rs[:], op=ALU.mult),
                      reads=[xsB[kc][b], rsB], writes=[ttB])
                sy.op("act", lambda e: e.activation(out=hn_fn(kc), in_=tt[:], func=AF.Identity,
                                                    scale=drvA[:, l, s, kc:kc + 1], bias=modt[:, l * 24 + kc, s:s + 1]),
                      reads=[ttB], sreads=[drvB, modB], writes=[hnB[kc]])

        def postnorm(l, s, t0, y, yB, sqt, sqB, tp, tpt=None):
            tpt = tpt or tp
            b = t0 // 512
            rs, rsB = rms_stats(lambda: y[:, :, :], yB, KC, od1024, sqt, sqB, tp)
            for n in range(KC):
                tt, ttB = tpt()
                sy.op("dve", lambda e: e.tensor_tensor(out=tt[:], in0=y[:, n, :], in1=rs[:], op=ALU.mult),
                      reads=[yB[n], rsB], writes=[ttB])
                sy.op("dve", lambda e: e.scalar_tensor_tensor(out=xs[:, n, t0:t0 + 512], in0=tt[:],
                                                              scalar=drvG[:, l, s, n:n + 1], in1=xs[:, n, t0:t0 + 512],
                                                              op0=ALU.mult, op1=ALU.add),
                      reads=[ttB], sreads=[drvB], writes=[xsB[n][b]])

        def mk_tmp_pool(ess, name, n, dt=F32, w=512):
            _UID[0] += 1
            tiles = [ess.enter_context(nc.sbuf_tensor(f"{name}{i}_{_UID[0]}", [128, w], dt)) for i in range(n)]
            bufs = [Buf(f"{name}{i}") for i in range(n)]
            st = {"i": 0}

            def get():
                i = st["i"] % n
                st["i"] += 1
                return tiles[i], bufs[i]
            return get

        def even_layer(l, s):
            j = l // 2
            with ExitStack() as el:
                def sbl(name, shape, dt):
                    return el.enter_context(nc.sbuf_tensor(f"{name}_{l}_{s}", shape, dt))
                hn = sbl("e_hn", [128, KC, 512], BF16)
                hnB = [Buf(f"e_hn{c}") for c in range(KC)]
                G = sbl("e_G", [128, KC, 544], BF16)
                GB = [Buf(f"e_G{c}") for c in range(KC)]
                Z = sbl("e_Z", [128, KC, 512], BF16)
                ZB = [Buf(f"e_Z{c}") for c in range(KC)]
                big = sbl("e_big", [128, KC, 512], F32)
                bigB = [Buf(f"e_big{c}") for c in range(KC)]
                Lo = sbl("e_Lo", [128, KC, 512], BF16)
                LoB = [Buf(f"e_Lo{c}") for c in range(KC)]
                sqt = sbl("e_sq", [128, KC, 512], BF16)
                sqB = Buf("e_sq")
                dg = sbl("e_dg", [128, 31, 128], BF16)
                dgB = Buf("e_dg")
                d4 = [sbl(f"e_d4{i}", [128, 4, 128], BF16) for i in range(2)]
                d4B = [Buf(f"e_d4{i}") for i in range(2)]
                XB = [sbl(f"e_XB{i}", [128, 516], BF16) for i in range(2)]
                XBB = [Buf(f"e_XB{i}") for i in range(2)]
                tp = mk_tmp_pool(el, "e_tf", 5, F32)
                tpt = mk_tmp_pool(el, "e_tt", 3, F32)
                tpb = mk_tmp_pool(el, "e_tb", 2, BF16)
                lt = [[sbl(f"e_lt{p}{i}", [128, 512], F32) for i in range(5)] for p in range(2)]
                ltB = [[Buf(f"e_lt{p}{i}") for i in range(5)] for p in range(2)]
                sgt = [sbl(f"e_sgt{p}", [128, 512], BF16) for p in range(2)]
                sgtB = [Buf(f"e_sgt{p}") for p in range(2)]
                zbt = [sbl(f"e_zbt{p}", [128, 512], BF16) for p in range(2)]
                zbtB = [Buf(f"e_zbt{p}") for p in range(2)]
                xcbt = [sbl(f"e_xcbt{p}", [128, 512], BF16) for p in range(2)]
                xcbtB = [Buf(f"e_xcbt{p}") for p in range(2)]

                sy.op("dve", lambda e: e.memset(G[:, :, 0:32], 0.0), writes=GB)
                sy.op("dve", lambda e: e.memset(carry[:], 0.0), writes=carryB)
                sy.op("dve", lambda e: e.memset(xbh[:], 0.0), writes=xbhB)
                hrhs = lambda kc: hn[:, kc, :]

                def w_mm8(name):
                    wt, wB, wi = acquire(name)
                    pt, pB = mm8(wt, wB, hrhs, hnB)
                    release(wi)
                    return pt, pB

                def conv_stage(jj):
                    p = jj % 2
                    pt, pB = w_mm8(("ewin", j, 8 + jj))
                    yield
                    sy.op("act", lambda e: e.activation(out=sgt[p][:], in_=pt[:], func=AF.Sigmoid),
                          reads=[pB], writes=[sgtB[p]])
                    pt, pB = w_mm8(("ewin", j, jj))
                    yield
                    sy.op("dve", lambda e: e.tensor_tensor(out=G[:, jj, 32:544], in0=pt[:], in1=sgt[p][:], op=ALU.mult),
                          reads=[pB, sgtB[p]], writes=[GB[jj]])
                    pt, pB = w_mm8(("ewin", j, 16 + jj))
                    yield
                    sy.op("act", lambda e: e.activation(out=Z[:, jj, :], in_=pt[:], func=AF.Silu),
                          reads=[pB], writes=[ZB[jj]])
                    cw0 = pcols[("conv_w", j)] + jj * 31
                    sy.op("pool", lambda e: e.tensor_tensor(
                        out=dg[:], in0=ident_b[:].unsqueeze(1).to_broadcast([128, 31, 128]),
                        in1=pvt[:, cw0:cw0 + 31].unsqueeze(2).to_broadcast([128, 31, 128]), op=ALU.mult),
                        reads=[constB, pvB], writes=[dgB])
                    yield
                    yield
                    pt, pB = nextps()
                    for k in range(31):
                        sy.op("pe", lambda e, k=k: e.matmul(pt[:, :], lhsT=dg[:, k, :], rhs=G[:, jj, k + 2:k + 514],
                                                            start=(k == 0), stop=(k == 30)),
                              reads=[dgB, GB[jj]], writes=[pB], inc=(k == 30))
                        if k % 8 == 7:
                            yield
                    sy.op("act", lambda e: e.activation(out=big[:, jj, :], in_=pt[:], func=AF.Identity,
                                                        bias=pv(("conv_b", j), jj), scale=1.0),
                          reads=[pB], sreads=[pvB], writes=[bigB[jj]])
                    if jj == 0:
                        dump("G0", G[:, 0, 32:544], [GB[0]])
                        dump("aconv0", big[:, 0, :], [bigB[0]])
                        dump("Zraw0", Z[:, 0, :], [ZB[0]])
                    yield
                    sy.op("dve", lambda e: e.tensor_copy(out=G[:, jj, 0:32], in_=G[:, jj, 512:544]),
                          reads=[GB[jj]], writes=[GB[jj]])

                def lru_stage(jj):
                    p = jj % 2
                    xbt, xbB = XB[p], XBB[p]
                    pt, pB = w_mm8(("ewin", j, 24 + jj))
                    yield
                    sy.op("act", lambda e: e.activation(out=xbt[:, 4:516], in_=pt[:], func=AF.Identity),
                          reads=[pB], writes=[xbB])
                    sy.op("dve", lambda e: e.tensor_copy(out=xbt[:, 0:4], in_=xbh[:, jj, :]),
                          reads=[xbhB[jj]], writes=[xbB])
                    pt, pB = w_mm8(("ewin", j, 32 + jj))
                    yield
                    sy.op("act", lambda e: e.activation(out=zbt[p][:], in_=pt[:], func=AF.Silu), reads=[pB],
                          writes=[zbtB[p]])
                    lw0 = pcols[("lconv_w", j)] + jj * 4
                    sy.op("dve", lambda e: e.tensor_tensor(
                        out=d4[p][:], in0=ident_b[:].unsqueeze(1).to_broadcast([128, 4, 128]),
                        in1=pvt[:, lw0:lw0 + 4].unsqueeze(2).to_broadcast([128, 4, 128]), op=ALU.mult),
                        reads=[constB, pvB], writes=[d4B[p]])
                    yield
                    pt, pB = nextps()
                    for k in range(4):
                        sy.op("pe", lambda e, k=k: e.matmul(pt[:, :], lhsT=d4[p][:, k, :], rhs=xbt[:, k + 1:k + 513],
                                                            start=(k == 0), stop=(k == 3)),
                              reads=[d4B[p], xbB], writes=[pB], inc=(k == 3))
                    yield
                    xc, xcB = lt[p][0], ltB[p][0]
                    rg, rgB = lt[p][1], ltB[p][1]
                    ig, igB = lt[p][2], ltB[p][2]
                    at, atB = lt[p][3], ltB[p][3]
                    hh_, hhB = lt[p][4], ltB[p][4]
                    sy.op("act", lambda e: e.activation(out=xc[:], in_=pt[:], func=AF.Identity,
                                                        bias=pv(("lconv_b", j), jj), scale=1.0),
                          reads=[pB], sreads=[pvB], writes=[xcB])
                    yield
                    sy.op("dve", lambda e: e.tensor_copy(out=xcbt[p][:], in_=xc[:]), reads=[xcB], writes=[xcbtB[p]])
                    sy.op("dve", lambda e: e.tensor_copy(out=xbh[:, jj, :], in_=xbt[:, 512:516]),
                          reads=[xbB], writes=[xbhB[jj]])
                    yield
                    wt, wB, wi = acquire(("egate", j, jj))
                    pr, prB = nextps()
                    sy.op("pe", lambda e: e.matmul(pr[:, :], lhsT=wt[:, 0:128], rhs=xcbt[p][:], start=True, stop=True),
                          reads=[wB, xcbtB[p]], writes=[prB])
                    pi_, piB = nextps()
                    sy.op("pe", lambda e: e.matmul(pi_[:, :], lhsT=wt[:, 128:256], rhs=xcbt[p][:], start=True, stop=True),
                          reads=[wB, xcbtB[p]], writes=[piB])
                    release(wi)
                    yield
                    sy.op("act", lambda e: e.activation(out=rg[:], in_=pr[:], func=AF.Sigmoid,
                                                        bias=pv(("ba", j), jj), scale=1.0),
                          reads=[prB], sreads=[pvB], writes=[rgB])
                    yield
                    sy.op("act", lambda e: e.activation(out=ig[:], in_=pi_[:], func=AF.Sigmoid,
                                                        bias=pv(("bx", j), jj), scale=1.0),
                          reads=[piB], sreads=[pvB], writes=[igB])
                    yield
                    sy.op("act", lambda e: e.activation(out=at[:], in_=rg[:], func=AF.Exp,
                                                        scale=nsp[:, j, jj:jj + 1]),
                          reads=[rgB], sreads=[nspB], writes=[atB])
                    sy.op("dve", lambda e: e.tensor_tensor(out=ig[:], in0=ig[:], in1=xc[:], op=ALU.mult),
                          reads=[xcB], writes=[igB])
                    yield
                    sy.op("dve", lambda e: e.tensor_tensor(out=rg[:], in0=at[:], in1=at[:], op=ALU.mult),
                          reads=[atB], writes=[rgB])
                    yield
                    sy.op("act", lambda e: e.activation(out=rg[:], in_=rg[:], func=AF.Sqrt, bias=1.0, scale=-1.0),
                          writes=[rgB])
                    yield
                    sy.op("dve", lambda e: e.tensor_tensor(out=ig[:], in0=ig[:], in1=rg[:], op=ALU.mult),
                          reads=[rgB], writes=[igB])
                    yield
                    sy.op("dve", lambda e: e.tensor_tensor_scan(out=hh_[:], data0=at[:], data1=ig[:],
                                                                initial=carry[:, jj:jj + 1], op0=ALU.mult,
                                                                op1=ALU.add),
                          reads=[atB, igB], sreads=[carryB[jj]], writes=[hhB])
                    yield
                    sy.op("act", lambda e: e.activation(out=carry[:, jj:jj + 1], in_=hh_[:, 511:512],
                                                        func=AF.Identity),
                          reads=[hhB], writes=[carryB[jj]])
                    sy.op("dve", lambda e: e.tensor_tensor(out=Lo[:, jj, :], in0=hh_[:], in1=zbt[p][:], op=ALU.mult),
                          reads=[hhB, zbtB[p]], writes=[LoB[jj]])

                for t in range(NB):
                    t0 = t * 512
                    prenorm(l, s, t0, lambda kc: hn[:, kc, :], hnB, sqt, sqB, tp, tpt)
                    dump("hn0", hn[:, 0, :], [hnB[0]])
                    active = []
                    for jj in range(8):
                        active += [conv_stage(jj), lru_stage(jj)]
                        steps = 0
                        while active and (jj == 7 or steps < EVEN_STAGGER):
                            for g_ in list(active):
                                try:
                                    next(g_)
                                except StopIteration:
                                    active.remove(g_)
                            steps += 1
                    sy.op("dve", lambda e: e.tensor_copy(out=sqt[:], in_=big[:]), reads=bigB, writes=[sqB])
                    pm, pmB = nextps()
                    for kc in range(KC):
                        sy.op("pe", lambda e, kc=kc: e.matmul(pm[:, :], lhsT=od1024[:], rhs=sqt[:, kc, :],
                                                              start=(kc == 0), stop=(kc == KC - 1)),
                              reads=[constB, sqB], writes=[pmB], inc=(kc == KC - 1))
                    sy.op("act", lambda e: e.activation(out=sqt[:], in_=big[:], func=AF.Square),
                          reads=bigB, writes=[sqB])
                    p2, p2B = nextps()
                    for kc in range(KC):
                        sy.op("pe", lambda e, kc=kc: e.matmul(p2[:, :], lhsT=od1024[:], rhs=sqt[:, kc, :],
                                                              start=(kc == 0), stop=(kc == KC - 1)),
                              reads=[constB, sqB], writes=[p2B], inc=(kc == KC - 1))
                    mean, meanB = tp()
                    sy.op("act", lambda e: e.activation(out=mean[:], in_=pm[:], func=AF.Identity), reads=[pmB],
                          writes=[meanB])
                    var, varB = tp()
                    sy.op("dve", lambda e: e.tensor_tensor(out=var[:], in0=mean[:], in1=mean[:], op=ALU.mult),
                          reads=[meanB], writes=[varB])
                    sy.op("dve", lambda e: e.tensor_tensor(out=var[:], in0=p2[:], in1=var[:], op=ALU.subtract),
                          reads=[p2B], writes=[varB])
                    sy.op("dve", lambda e: e.tensor_scalar(out=var[:], in0=var[:], scalar1=0.0, scalar2=None,
                                                           op0=ALU.max), writes=[varB])
                    sd, sdB = tp()
                    sy.op("act", lambda e: e.activation(out=sd[:], in_=var[:], func=AF.Sqrt, bias=EPS, scale=1.0),
                          reads=[varB], writes=[sdB])
                    rs, rsB = tp()
                    sy.op("dve", lambda e: e.reciprocal(out=rs[:], in_=sd[:]), reads=[sdB], writes=[rsB])
                    mr, mrB = tp()
                    sy.op("dve", lambda e: e.tensor_tensor(out=mr[:], in0=mean[:], in1=rs[:], op=ALU.mult),
                          reads=[meanB, rsB], writes=[mrB])
                    for jj in range(8):
                        t1, t1B = tpt()
                        sy.op("dve", lambda e: e.tensor_tensor(out=t1[:], in0=big[:, jj, :], in1=rs[:], op=ALU.mult),
                              reads=[bigB[jj], rsB], writes=[t1B])
                        sy.op("dve", lambda e: e.tensor_tensor(out=t1[:], in0=t1[:], in1=mr[:], op=ALU.subtract),
                              reads=[mrB], writes=[t1B])
                        s1, s1B = tpb()
                        sy.op("act", lambda e: e.activation(out=s1[:], in_=t1[:], func=AF.Silu,
                                                            scale=pv(("ln_g", j), jj), bias=pv(("ln_b", j), jj)),
                              reads=[t1B], sreads=[pvB], writes=[s1B])
                        sy.op("dve", lambda e: e.tensor_tensor(out=Z[:, jj, :], in0=s1[:], in1=Z[:, jj, :], op=ALU.mult),
                              reads=[s1B], writes=[ZB[jj]])
                    dump("Aout0", Z[:, 0, :], [ZB[0]])
                    dump("Lo0", Lo[:, 0, :], [LoB[0]])
                    for n in range(8):
                        w0, w0B, wi0 = acquire(("ewout", j, n, 0))
                        w1, w1B, wi1 = acquire(("ewout", j, n, 1))
                        pt, pB = nextps()
                        for kc in range(16):
                            wsrc, wsB = (w0, w0B) if kc < 8 else (w1, w1B)
                            src, srcB = (Z, ZB) if kc < 8 else (Lo, LoB)
                            k8 = kc % 8
                            sy.op("pe", lambda e, kc=kc, k8=k8, wsrc=wsrc, src=src: e.matmul(
                                pt[:, :], lhsT=wsrc[:, k8 * 128:(k8 + 1) * 128], rhs=src[:, k8, :],
                                start=(kc == 0), stop=(kc == 15)),
                                reads=[wsB, srcB[k8]], writes=[pB], inc=(kc == 15))
                        release(wi0)
                        release(wi1)
                        sy.op("act", lambda e: e.activation(out=big[:, n, :], in_=pt[:], func=AF.Identity),
                              reads=[pB], writes=[bigB[n]])
                    dump("y0", big[:, 0, :], [bigB[0]])
                    postnorm(l, s, t0, big, bigB, sqt, sqB, tp, tpt)
                sy.fence()

        def odd_layer(l, s):
            j = l // 2
            with ExitStack() as ol:
                def sbl(name, shape, dt):
                    return ol.enter_context(nc.sbuf_tensor(f"{name}_{l}_{s}", shape, dt))
                Zs = sbl("o_Zs", [128, KC, S], BF16)
                ZsB = [[Buf(f"o_Zs{c}_{b}") for b in range(NB)] for c in range(KC)]
                cqn = sbl("o_cqn", [128, 2, S], BF16)
                cqnB = [Buf(f"o_cqn{b}") for b in range(NB)]
                ckvn = sbl("o_ckvn", [128, 2, S], BF16)
                ckvnB = [Buf(f"o_ckvn{b}") for b in range(NB)]
                kr = sbl("o_kr", [128, S], BF16)
                krB = Buf("o_kr")
                COS = sbl("o_cos", [128, S], BF16)
                SIN = sbl("o_sin", [128, S], BF16)
                csB = Buf("o_cs")
                tp = mk_tmp_pool(ol, "o_tf", 4, F32)
                tp_stats[0] = mk_tmp_pool(ol, "o_ts", 3, F32)

                with ExitStack() as rl:
                    ang = rl.enter_context(nc.sbuf_tensor(f"o_ang_{l}_{s}", [128, S], F32))
                    wk = rl.enter_context(nc.sbuf_tensor(f"o_wk_{l}_{s}", [128, S], F32))
                    wk2 = rl.enter_context(nc.sbuf_tensor(f"o_wk2_{l}_{s}", [128, S], F32))
                    ki = rl.enter_context(nc.sbuf_tensor(f"o_ki_{l}_{s}", [128, S], I32))
                    posi = ki
                    rB = Buf("o_rope")
                    R = slice(64, 96)
                    src = AP(pos_d, s * S, [[0, 32], [1, S]])
                    sy.dma("sp", posi[R, :], src, writes=[rB])
                    sy.op("dve", lambda e: e.tensor_copy(out=ang[R, :], in_=posi[R, :]), reads=[rB], writes=[rB])
                    sy.op("dve", lambda e: e.tensor_scalar(out=ang[R, :], in0=ang[R, :], scalar1=pvt[R, pcols["inv"]:pcols["inv"] + 1],
                                                           scalar2=None, op0=ALU.mult), sreads=[pvB], writes=[rB])
                    for which in range(2):
                        if which == 0:
                            sy.op("dve", lambda e: e.tensor_scalar(out=wk2[R, :], in0=ang[R, :], scalar1=math.pi / 2,
                                                                   scalar2=None, op0=ALU.add), writes=[rB])
                            a_in = wk2
                        else:
                            a_in = ang
                        sy.op("dve", lambda e: e.tensor_scalar(out=wk[R, :], in0=a_in[R, :], scalar1=1.0 / TWO_PI,
                                                               scalar2=None, op0=ALU.mult), writes=[rB])
                        sy.op("dve", lambda e: e.tensor_copy(out=ki[R, :], in_=wk[R, :]), writes=[rB])
                        sy.op("dve", lambda e: e.tensor_copy(out=wk[R, :], in_=ki[R, :]), writes=[rB])
                        sy.op("dve", lambda e: e.scalar_tensor_tensor(out=wk2[R, :], in0=wk[R, :], scalar=-PI_HI,
                                                                      in1=a_in[R, :], op0=ALU.mult, op1=ALU.add),
                              writes=[rB])
                        sy.op("dve", lambda e: e.scalar_tensor_tensor(out=wk2[R, :], in0=wk[R, :], scalar=-PI_LO,
                                                                      in1=wk2[R, :], op0=ALU.mult, op1=ALU.add),
                              writes=[rB])
                        sy.op("dve", lambda e: e.tensor_scalar(out=wk[R, :], in0=wk2[R, :], scalar1=math.pi,
                                                               scalar2=-TWO_PI, op0=ALU.is_gt, op1=ALU.mult), writes=[rB])
                        sy.op("dve", lambda e: e.tensor_tensor(out=wk2[R, :], in0=wk2[R, :], in1=wk[R, :], op=ALU.add),
                              writes=[rB])
                        sy.op("dve", lambda e: e.tensor_scalar(out=wk[R, :], in0=wk2[R, :], scalar1=-math.pi,
                                                               scalar2=TWO_PI, op0=ALU.is_lt, op1=ALU.mult), writes=[rB])
                        sy.op("dve", lambda e: e.tensor_tensor(out=wk2[R, :], in0=wk2[R, :], in1=wk[R, :], op=ALU.add),
                              writes=[rB])
                        sy.op("dve", lambda e: e.tensor_scalar(out=wk2[R, :], in0=wk2[R, :], scalar1=3.1415925,
                                                               scalar2=-3.1415925, op0=ALU.min, op1=ALU.max), writes=[rB])
                        if which == 0:
                            sy.op("act", lambda e: e.activation(out=COS[R, :], in_=wk2[R, :], func=AF.Sin),
                                  reads=[rB], writes=[csB])
                        else:
                            sy.op("dve", lambda e: e.tensor_scalar(out=wk2[R, :], in0=wk2[R, :],
                                                                   scalar1=pvt[R, pcols["sgn"]:pcols["sgn"] + 1],
                                                                   scalar2=None, op0=ALU.mult), sreads=[pvB], writes=[rB])
                            sy.op("act", lambda e: e.activation(out=SIN[R, :], in_=wk2[R, :], func=AF.Sin),
                                  reads=[rB], writes=[csB])
                    sy.fence()

                with ExitStack() as p1:
                    hn = p1.enter_context(nc.sbuf_tensor(f"o_hn_{l}_{s}", [128, KC, 1024], BF16))
                    hnB2 = [[Buf(f"o_hn{c}_{b}") for c in range(KC)] for b in range(2)]
                    raw = p1.enter_context(nc.sbuf_tensor(f"o_raw_{l}_{s}", [128, 2, 1024], F32))
                    rawB = [Buf(f"o_raw{b}") for b in range(2)]
                    krA = p1.enter_context(nc.sbuf_tensor(f"o_krA_{l}_{s}", [128, 1024], F32))
                    krAB = [Buf(f"o_krA{b}") for b in range(2)]
                    sqt = p1.enter_context(nc.sbuf_tensor(f"o_sq_{l}_{s}", [128, KC, 512], BF16))
                    sqB = Buf("o_sq")
                    R = slice(64, 96)
                    for t in range(S // 1024):
                        for b in range(2):
                            t0 = t * 1024 + b * 512
                            prenorm(l, s, t0, lambda kc, b=b: hn[:, kc, b * 512:(b + 1) * 512], hnB2[b], sqt, sqB, tp)
                        for grp, (dst, dstB, nrm) in enumerate(((cqn, cqnB, "q_norm"), (ckvn, ckvnB, "kv_norm"))):
                            for c in range(2):
                                wt, wB, wi = acquire(("owin", j, grp * 2 + c))
                                for b in range(2):
                                    pt, pB = mm8(wt, wB, lambda kc, b=b: hn[:, kc, b * 512:(b + 1) * 512], hnB2[b])
                                    sy.op("act", lambda e: e.activation(out=raw[:, c, b * 512:(b + 1) * 512], in_=pt[:],
                                                                        func=AF.Identity),
                                          reads=[pB], writes=[rawB[b]])
                                release(wi)
                            for b in range(2):
                                gb = t * 2 + b
                                rs, rsB = rms_stats(lambda b=b: raw[:, :, b * 512:(b + 1) * 512], [rawB[b]], 2, od256,
                                                    sqt, sqB, tp)
                                for c in range(2):
                                    sy.op("dve", lambda e, c=c: e.scalar_tensor_tensor(
                                        out=dst[:, c, gb * 512:(gb + 1) * 512], in0=raw[:, c, b * 512:(b + 1) * 512],
                                        scalar=pv((nrm, j), c), in1=rs[:], op0=ALU.mult, op1=ALU.mult),
                                        reads=[rawB[b], rsB], sreads=[pvB], writes=[dstB[gb]])
                        wt, wB, wi = acquire(("owin", j, 4))
                        for b in range(2):
                            pt, pB = mm8(wt, wB, lambda kc, b=b: hn[:, kc, b * 512:(b + 1) * 512], hnB2[b])
                            sy.op("act", lambda e: e.activation(out=krA[R, b * 512:(b + 1) * 512], in_=pt[R, :],
                                                                func=AF.Identity), reads=[pB], writes=[krAB[b]])
                        release(wi)
                        wt, wB, wi = acquire(("owin", j, 5))
                        for b in range(2):
                            gb = t * 2 + b
                            tk = slice(gb * 512, (gb + 1) * 512)
                            pt, pB = mm8(wt, wB, lambda kc, b=b: hn[:, kc, b * 512:(b + 1) * 512], hnB2[b])
                            t1, t1B = tp()
                            sy.op("dve", lambda e: e.tensor_tensor(out=t1[R, :], in0=krA[R, b * 512:(b + 1) * 512],
                                                                   in1=COS[R, tk], op=ALU.mult),
                                  reads=[krAB[b], csB], writes=[t1B])
                            t2, t2B = tp()
                            sy.op("dve", lambda e: e.tensor_tensor(out=t2[R, :], in0=pt[R, :], in1=SIN[R, tk], op=ALU.mult),
                                  reads=[pB, csB], writes=[t2B])
                            sy.op("dve", lambda e: e.tensor_tensor(out=kr[R, tk], in0=t1[R, :], in1=t2[R, :], op=ALU.add),
                                  reads=[t1B, t2B], writes=[krB])
                        release(wi)
                        for c in range(8):
                            wt, wB, wi = acquire(("owin", j, 6 + c))
                            for b in range(2):
                                gb = t * 2 + b
                                pt, pB = mm8(wt, wB, lambda kc, b=b: hn[:, kc, b * 512:(b + 1) * 512], hnB2[b])
                                sy.op("act", lambda e: e.activation(out=Zs[:, c, gb * 512:(gb + 1) * 512], in_=pt[:],
                                                                    func=AF.Silu), reads=[pB], writes=[ZsB[c][gb]])
                            release(wi)
                    dump("cqn", cqn[:, 0, 0:512], [cqnB[0]])
                    dump("ckvn", ckvn[:, 0, 0:512], [ckvnB[0]])
                    dump("kr", kr[:, 0:512], [krB])
                    dump("cos", COS[:, 0:512], [csB])
                    dump("sin", SIN[:, 0:512], [csB])
                    dump("zs", Zs[:, 0, 0:512], [ZsB[0][0]])
                    sy.fence()

                with ExitStack() as p2:
                    QT = [p2.enter_context(nc.sbuf_tensor(f"o_QT{i}_{l}_{s}", [128, S], BF16)) for i in range(2)]
                    KT = [p2.enter_context(nc.sbuf_tensor(f"o_KT{i}_{l}_{s}", [128, S], BF16)) for i in range(2)]
                    V = [p2.enter_context(nc.sbuf_tensor(f"o_V{i}_{l}_{s}", [128, S // 128, 128], BF16)) for i in range(2)]
                    QTB = [Buf(f"o_QT{i}") for i in range(2)]
                    KTB = [Buf(f"o_KT{i}") for i in range(2)]
                    VB = [Buf(f"o_V{i}") for i in range(2)]
                    NPT = 6
                    PT = [p2.enter_context(nc.sbuf_tensor(f"o_PT{i}_{l}_{s}", [128, 512], BF16)) for i in range(NPT)]
                    PTB = [Buf(f"o_PT{i}") for i in range(NPT)]
                    rec = p2.enter_context(nc.sbuf_tensor(f"o_rec_{l}_{s}", [128, 512], F32))
                    recB = Buf("o_rec")
                    tpg = mk_tmp_pool(p2, "o_tg", 2, F32)
                    R = slice(64, 96)
                    sy.op("dve", lambda e: e.memset(V[0][:, :, 64:128], 1.0), writes=[VB[0]])
                    sy.op("dve", lambda e: e.memset(V[1][:, :, 0:64], 1.0), writes=[VB[1]])
                    pt_i = {"i": 0}

                    def gen_head(h):
                        par = h % 2
                        qt, qB = QT[par], QTB[par]
                        kt, kB = KT[par], KTB[par]
                        vt, vB = V[par], VB[par]
                        wt, wB, wi = acquire(("ouq", j, h))
                        for b in range(NB):
                            tk = slice(b * 512, (b + 1) * 512)
                            pa, paB = nextps()
                            pb, pbB = nextps()
                            for kc in range(2):
                                sy.op("pe", lambda e, kc=kc: e.matmul(pa[0:96, :], lhsT=wt[:, kc * 192:kc * 192 + 96],
                                                                      rhs=cqn[:, kc, tk], start=(kc == 0), stop=(kc == 1)),
                                      reads=[wB, cqnB[b]], writes=[paB], inc=(kc == 1))
                            for kc in range(2):
                                sy.op("pe", lambda e, kc=kc: e.matmul(pb[0:96, :], lhsT=wt[:, kc * 192 + 96:kc * 192 + 192],
                                                                      rhs=cqn[:, kc, tk], start=(kc == 0), stop=(kc == 1)),
                                      reads=[wB, cqnB[b]], writes=[pbB], inc=(kc == 1))
                            yield
                            sy.op("act", lambda e: e.activation(out=qt[0:64, tk], in_=pa[0:64, :], func=AF.Identity),
                                  reads=[paB], writes=[qB])
                            t1, t1B = tpg()
                            sy.op("dve", lambda e: e.tensor_tensor(out=t1[R, :], in0=pa[R, :], in1=COS[R, tk], op=ALU.mult),
                                  reads=[paB, csB], writes=[t1B])
                            t2, t2B = tpg()
                            sy.op("dve", lambda e: e.tensor_tensor(out=t2[R, :], in0=pb[R, :], in1=SIN[R, tk], op=ALU.mult),
                                  reads=[pbB, csB], writes=[t2B])
                            yield
                            sy.op("dve", lambda e: e.tensor_tensor(out=qt[R, tk], in0=t1[R, :], in1=t2[R, :], op=ALU.add),
                                  reads=[t1B, t2B], writes=[qB])
                            yield
                        release(wi)
                        wt, wB, wi = acquire(("oukv", j, h))
                        for b in range(NB):
                            tk = slice(b * 512, (b + 1) * 512)
                            pa, paB = nextps()
                            for kc in range(2):
                                sy.op("pe", lambda e, kc=kc: e.matmul(pa[0:64, :], lhsT=wt[:, kc * 128:kc * 128 + 64],
                                                                      rhs=ckvn[:, kc, tk], start=(kc == 0), stop=(kc == 1)),
                                      reads=[wB, ckvnB[b]], writes=[paB], inc=(kc == 1))
                            yield
                            sy.op("act", lambda e: e.activation(out=kt[0:64, tk], in_=pa[0:64, :], func=AF.Identity),
                                  reads=[paB], writes=[kB])
                            yield
                        sy.op("pool", lambda e: e.tensor_copy(out=kt[R, :], in_=kr[R, :]), reads=[krB], writes=[kB])
                        vo = 0 if par == 0 else 64
                        for g8 in range(S // 1024):
                            pa, paB = nextps()
                            for i8 in range(8):
                                kb = g8 * 8 + i8
                                for kc in range(2):
                                    sy.op("pe", lambda e, kc=kc, kb=kb, i8=i8: e.matmul(
                                        pa[:, i8 * 64:(i8 + 1) * 64], lhsT=ckvn[:, kc, kb * 128:(kb + 1) * 128],
                                        rhs=wt[:, kc * 128 + 64:kc * 128 + 128], start=(kc == 0), stop=(kc == 1)),
                                        reads=[wB, ckvnB[kb // 4]], writes=[paB], inc=(kc == 1 and i8 == 7))
                            yield
                            sy.op("act", lambda e: e.activation(
                                out=vt[:, g8 * 8:(g8 + 1) * 8, vo:vo + 64],
                                in_=pa[:, :].rearrange("p (a b) -> p a b", b=64), func=AF.Identity),
                                reads=[paB], writes=[vB])
                            yield
                        release(wi)

                    def attn_head(h):
                        par = h % 2
                        hp = h // 2
                        qt, qB = QT[par], QTB[par]
                        kt, kB = KT[par], KTB[par]
                        vt, vB = V[par], VB[par]
                        if h == 0:
                            dump("qt", qt[:, 0:512], [qB])
                            dump("kt", kt[:, 0:512], [kB])
                            dump("v0", vt[:, 0, :], [vB])
                        items = []
                        for g in range(NB):
                            for kb in range(4 * g + 4):
                                items.append((g, kb))
                        LA = 3
                        inflight = {}
                        for i in range(len(items) + LA):
                            if i < len(items):
                                g, kb = items[i]
                                d = kb - 4 * g
                                c0 = max(0, d) * 128
                                ncols = 512 - c0
                                sp_, spB = nextps()
                                sy.op("pe", lambda e, kb=kb, g=g, c0=c0, ncols=ncols, sp_=sp_: e.matmul(
                                    sp_[:, 0:ncols], lhsT=kt[0:96, kb * 128:(kb + 1) * 128],
                                    rhs=qt[0:96, g * 512 + c0:(g + 1) * 512], start=True, stop=(d < 0)),
                                    reads=[kB, qB], writes=[spB], inc=(d < 0))
                                if d >= 0:
                                    sy.op("pe", lambda e, sp_=sp_: e.matmul(sp_[:, 0:128], lhsT=ident_b[:], rhs=maskT[:],
                                                                            start=False, stop=True),
                                          reads=[constB], writes=[spB])
                                inflight[i] = (sp_, spB, c0, ncols)
                            if i >= LA:
                                ii = i - LA
                                g, kb = items[ii]
                                sp_, spB, c0, ncols = inflight.pop(ii)
                                pi = pt_i["i"] % NPT
                                pt_i["i"] += 1
                                ptile, ptB = PT[pi], PTB[pi]
                                sy.op("act", lambda e, sp_=sp_, ncols=ncols, ptile=ptile: e.activation(
                                    out=ptile[:, 0:ncols], in_=sp_[:, 0:ncols], func=AF.Exp, scale=ATT_SCALE),
                                    reads=[spB], writes=[ptB])
                                op_, opB = psum[g % 2], psB[g % 2]
                                nkb = 4 * g + 4
                                sy.op("pe", lambda e, kb=kb, c0=c0, ncols=ncols, ptile=ptile, op_=op_, nkb=nkb: e.matmul(
                                    op_[:, c0:512], lhsT=vt[:, kb, :], rhs=ptile[:, 0:ncols],
                                    start=(kb == 0), stop=(kb == nkb - 1)),
                                    reads=[vB, ptB], writes=[opB], inc=True)
                                if kb == nkb - 1:
                                    if par == 0:
                                        num, den = slice(0, 64), slice(64, 128)
                                    else:
                                        num, den = slice(64, 128), slice(0, 64)
                                    tk = slice(g * 512, (g + 1) * 512)
                                    sy.op("dve", lambda e, op_=op_: e.reciprocal(out=rec[num, :], in_=op_[den, :]),
                                          reads=[opB], writes=[recB])
                                    o1, o1B = tp()
                                    sy.op("dve", lambda e, op_=op_, o1=o1: e.tensor_tensor(out=o1[num, :], in0=op_[num, :],
                                                                                         in1=rec[num, :], op=ALU.mult),
                                          reads=[opB, recB], writes=[o1B])
                                    sy.op("dve", lambda e, o1=o1: e.tensor_tensor(out=Zs[num, hp, tk], in0=o1[num, :],
                                                                                in1=Zs[num, hp, tk], op=ALU.mult),
                                          reads=[o1B], writes=[ZsB[hp][g]])
                            yield

                    for _ in gen_head(0):
                        pass
                    for h in range(16):
                        gens = [attn_head(h)]
                        if h + 1 < 16:
                            gens.append(gen_head(h + 1))
                        while gens:
                            for g_ in list(gens):
                                try:
                                    next(g_)
                                except StopIteration:
                                    gens.remove(g_)
                    sy.fence()

                dump("og", Zs[:, 0, 0:512], [ZsB[0][0]])
                with ExitStack() as p3:
                    big = p3.enter_context(nc.sbuf_tensor(f"o_big_{l}_{s}", [128, KC, 512], F32))
                    bigB = [Buf(f"o_big{c}") for c in range(KC)]
                    sqt = p3.enter_context(nc.sbuf_tensor(f"o_sq3_{l}_{s}", [128, KC, 512], BF16))
                    sqB = Buf("o_sq3")
                    wl = [acquire(("owout", j, n)) for n in range(8)]
                    for b in range(NB):
                        tk = slice(b * 512, (b + 1) * 512)
                        for n in range(8):
                            wt, wB, _ = wl[n]
                            pt, pB = mm8(wt, wB, lambda kc: Zs[:, kc, tk], [ZsB[c][b] for c in range(KC)])
                            sy.op("act", lambda e: e.activation(out=big[:, n, :], in_=pt[:], func=AF.Identity),
                                  reads=[pB], writes=[bigB[n]])
                        postnorm(l, s, b * 512, big, bigB, sqt, sqB, tp)
                    for _w in wl:
                        release(_w[2])
                    sy.fence()
                tp_stats[0] = None

        allxs = [xsB[c][b] for c in range(KC) for b in range(NB)]
        outB = Buf("outst")
        for s in range(NSEQ):
            for c in range(KC):
                sy.dma("sp", xs[:, c, :], x_d.ap()[s, :, c, :], writes=xsB[c])
            for l in LAYERS:
                if l % 2 == 0:
                    even_layer(l, s)
                else:
                    odd_layer(l, s)
            for c in range(KC):
                sy.dma("sp", out_d.ap()[s, :, c, :], xs[:, c, :], reads=xsB[c], writes=[outB])
        nc.sync.wait_ge(outB.dsem, outB.dcnt)
        if DEBUG and dbgB.dsem is not None:
            nc.sync.wait_ge(dbgB.dsem, dbgB.dcnt)
        if RECORD:
            return ws["order"]
        assert ws["acq"] == len(ws["order"]) == ws["issued"], (ws["acq"], len(ws["order"]), ws["issued"])
        build.nins = sy.nins
    return nc


_CACHE = {}


def kernel(**inp):
    inp = {k: np.asarray(v) for k, v in inp.items()}
    x = inp["x"].astype(np.float32, copy=False)
    B, S, Dm = x.shape
    nseq = B // NCORES
    wts = pack_weights(inp)
    pvv = pack_pv(inp)
    key = (S, nseq)
    if key not in _CACHE:
        _CACHE[key] = build(S=S, NSEQ=nseq)
    nc = _CACHE[key]
    in_maps = []
    for cid in range(NCORES):
        bs = slice(cid * nseq, (cid + 1) * nseq)
        xf = np.ascontiguousarray(x[bs].reshape(nseq, S, KC, 128).transpose(0, 3, 2, 1))
        cf = np.ascontiguousarray(inp["c"][bs].astype(np.float32).reshape(nseq, KC, 128).transpose(2, 1, 0))
        pos = np.ascontiguousarray(inp["positions"][bs].astype(np.int32))
        in_maps.append({"x": xf, "c": cf, "pos": pos, "wts": wts, "pv": pvv})
    res = run_bass_kernel_spmd(nc, in_maps, core_ids=list(range(NCORES)))
    outs = []
    for cid in range(NCORES):
        o = np.asarray(res.results[cid]["out"]).reshape(nseq, 128, KC, S)
        outs.append(o.transpose(0, 3, 2, 1).reshape(nseq, S, Dm))
    return np.ascontiguousarray(np.concatenate(outs, axis=0).astype(np.float32))
```

```python
import math
from contextlib import ExitStack
import numpy as np
import concourse.bass as bass
import concourse.mybir as mybir
from concourse.ap import AP
from concourse.bass_utils import run_bass_kernel_spmd

F32 = mybir.dt.float32
BF16 = mybir.dt.bfloat16
I32 = mybir.dt.int32
AF = mybir.ActivationFunctionType
ALU = mybir.AluOpType

D = 1024
KC = 8
NCORES = 8
EPS = 1e-6
SLOT = 1024
RING = 10
SAME_ENG_WINDOW = 3
EVEN_STAGGER = 9
ATT_SCALE = 96.0 ** -0.5
TWO_PI = 2.0 * math.pi
PI_HI = 6.28125
PI_LO = TWO_PI - 6.28125


def weight_plan():
    plan = {}
    off = 0

    def add(name, F):
        nonlocal off
        plan[name] = (off, F)
        off += 128 * F

    for l in range(4):
        for n in range(24):
            add(("ada", l, n), 1024)
    for j in range(2):
        for c in range(40):
            add(("ewin", j, c), 1024)
        for kc in range(8):
            add(("egate", j, kc), 256)
        for n in range(8):
            add(("ewout", j, n, 0), 1024)
            add(("ewout", j, n, 1), 1024)
    for j in range(2):
        for c in range(14):
            add(("owin", j, c), 1024)
        for h in range(16):
            add(("ouq", j, h), 384)
            add(("oukv", j, h), 256)
        for n in range(8):
            add(("owout", j, n), 1024)
    return plan, off


def pv_plan():
    cols = {}
    off = 0

    def add(name, n):
        nonlocal off
        cols[name] = off
        off += n

    for l in range(4):
        add(("pre_g", l), 8)
        add(("post_g", l), 8)
        add(("ada_b", l), 24)
    for j in range(2):
        add(("conv_w", j), 8 * 31)
        add(("conv_b", j), 8)
        add(("ln_g", j), 8)
        add(("ln_b", j), 8)
        add(("lconv_w", j), 8 * 4)
        add(("lconv_b", j), 8)
        add(("ba", j), 8)
        add(("bx", j), 8)
        add(("lam", j), 8)
        add(("q_norm", j), 2)
        add(("kv_norm", j), 2)
    add("inv", 1)
    add("sgn", 1)
    return cols, off


def chunkify(W):
    K, N = W.shape
    return np.ascontiguousarray(W.reshape(K // 128, 128, N // 128, 128).transpose(2, 1, 0, 3))


def vec_pc(v):
    return np.ascontiguousarray(v.reshape(-1, 128).T)


def pack_weights(inp):
    plan, total = weight_plan()
    flat = np.zeros(total, np.float32)

    def put(name, arr):
        off, F = plan[name]
        a = np.ascontiguousarray(arr, dtype=np.float32).reshape(128, F)
        flat[off:off + 128 * F] = a.reshape(-1)

    for l in range(4):
        ch = chunkify(inp["ada_w"][l])
        for n in range(24):
            put(("ada", l, n), ch[n])
    for j in range(2):
        ch = chunkify(inp["ev_w_in"][j])
        for c in range(40):
            put(("ewin", j, c), ch[c])
        wa, wx = inp["ev_lru_wa"][j], inp["ev_lru_wx"][j]
        for kc in range(8):
            g = np.zeros((128, 2, 128), np.float32)
            for hh in range(2):
                g[hh * 64:(hh + 1) * 64, 0, hh * 64:(hh + 1) * 64] = wa[2 * kc + hh]
                g[hh * 64:(hh + 1) * 64, 1, hh * 64:(hh + 1) * 64] = wx[2 * kc + hh]
            put(("egate", j, kc), g)
        wo = inp["ev_w_out"][j]
        c0 = chunkify(wo[:1024])
        c1 = chunkify(wo[1024:])
        for n in range(8):
            put(("ewout", j, n, 0), c0[n])
            put(("ewout", j, n, 1), c1[n])
    for j in range(2):
        w = inp["od_w_in"][j]
        z64 = np.zeros((1024, 64), np.float32)
        z32 = np.zeros((1024, 32), np.float32)
        kr = w[:, 512:544]
        kr1 = np.concatenate([z64, kr, z32], axis=1)
        kr2 = np.concatenate([z64, kr[:, 16:32], kr[:, 0:16], z32], axis=1)
        wcat = np.concatenate([w[:, 0:512], kr1, kr2, w[:, 544:1568]], axis=1)
        ch = chunkify(wcat)
        for c in range(14):
            put(("owin", j, c), ch[c])
        uq = inp["od_w_uq"][j].reshape(2, 128, 16, 96)
        ukv = inp["od_w_ukv"][j].reshape(2, 128, 16, 128)
        for h in range(16):
            a = uq[:, :, h, :]
            sw = np.concatenate([a[:, :, 0:64], a[:, :, 80:96], a[:, :, 64:80]], axis=2)
            both = np.concatenate([a, sw], axis=2)
            put(("ouq", j, h), both.transpose(1, 0, 2))
            put(("oukv", j, h), ukv[:, :, h, :].transpose(1, 0, 2))
        ch = chunkify(inp["od_w_out"][j])
        for n in range(8):
            put(("owout", j, n), ch[n])
    return flat


def pack_pv(inp):
    cols, n = pv_plan()
    pv = np.zeros((128, n), np.float32)

    def put(name, arr):
        a = np.asarray(arr, np.float32)
        pv[:, cols[name]:cols[name] + a.shape[1]] = a

    for l in range(4):
        put(("pre_g", l), vec_pc(inp["pre_g"][l]))
        put(("post_g", l), vec_pc(inp["post_g"][l]))
        put(("ada_b", l), vec_pc(inp["ada_b"][l]))
    for j in range(2):
        cw = inp["ev_conv_w"][j]
        put(("conv_w", j), cw.reshape(31, 8, 128).transpose(2, 1, 0).reshape(128, 8 * 31))
        put(("conv_b", j), vec_pc(inp["ev_conv_b"][j]))
        put(("ln_g", j), vec_pc(inp["ev_ln_g"][j]))
        put(("ln_b", j), vec_pc(inp["ev_ln_b"][j]))
        lw = inp["ev_lru_conv_w"][j]
        put(("lconv_w", j), lw.reshape(4, 8, 128).transpose(2, 1, 0).reshape(128, 8 * 4))
        put(("lconv_b", j), vec_pc(inp["ev_lru_conv_b"][j]))
        put(("ba", j), vec_pc(inp["ev_lru_ba"][j]))
        put(("bx", j), vec_pc(inp["ev_lru_bx"][j]))
        put(("lam", j), vec_pc(inp["ev_lru_lam"][j]))
        put(("q_norm", j), vec_pc(inp["od_q_norm"][j]))
        put(("kv_norm", j), vec_pc(inp["od_kv_norm"][j]))
    inv = (10000.0 ** (-np.arange(0, 32, 2, dtype=np.float32) / 32.0)).astype(np.float32)
    iv = np.zeros((128, 1), np.float32)
    sg = np.ones((128, 1), np.float32)
    for i in range(32):
        iv[64 + i, 0] = inv[i % 16]
        sg[64 + i, 0] = -1.0 if i < 16 else 1.0
    put("inv", iv)
    put("sgn", sg)
    return pv


_UID = [0]


class Buf:
    __slots__ = ("name", "w", "r", "dsem", "dcnt")

    def __init__(self, name):
        _UID[0] += 1
        self.name = f"{name}_{_UID[0]}"
        self.w = None
        self.r = {}
        self.dsem = None
        self.dcnt = 0


class Sync:
    ENG = ("pe", "act", "dve", "pool", "sp")

    def __init__(self, nc, es):
        self.nc = nc
        self.es = es
        self.eng = {"pe": nc.tensor, "act": nc.scalar, "dve": nc.vector, "pool": nc.gpsimd, "sp": nc.sync}
        self.sem = {k: es.enter_context(nc.semaphore("s_" + k)) for k in self.ENG}
        self.cnt = {k: 0 for k in self.ENG}
        self.known = {k: {} for k in self.ENG}
        self.nins = 0

    def _need(self, E, dep, strict):
        key, sem, val, src = dep
        if src == E and not strict and E != "pool":
            if E == "pe" or (self.cnt[E] - val) >= SAME_ENG_WINDOW:
                return
        if self.known[E].get(key, 0) >= val:
            return
        self.eng[E].wait_ge(sem, val)
        self.known[E][key] = val
        self.nins += 1

    def _deps(self, E, reads, writes, sreads, strict_all=False):
        for b in reads:
            if b.w is not None:
                self._need(E, b.w, strict_all)
        for b in sreads:
            if b.w is not None:
                self._need(E, b.w, True)
        for b in writes:
            if b.w is not None:
                self._need(E, b.w, strict_all)
            for d in b.r.values():
                self._need(E, d, strict_all)

    def op(self, E, fn, reads=(), writes=(), sreads=(), inc=True):
        self._deps(E, reads, writes, sreads)
        ins = fn(self.eng[E])
        self.nins += 1
        if inc:
            self.cnt[E] += 1
            ins.then_inc(self.sem[E], 1)
            me = (E, self.sem[E], self.cnt[E], E)
        else:
            me = (E, self.sem[E], self.cnt[E] + 1, E)
        for b in writes:
            b.w = me
            b.r = {}
        for b in reads:
            b.r[E] = me
        for b in sreads:
            b.r[E] = me
        return ins

    def dma(self, Q, out, in_, reads=(), writes=()):
        self._deps(Q, reads, writes, (), strict_all=True)
        tgt = writes[0] if writes else reads[0]
        if tgt.dsem is None:
            tgt.dsem = self.es.enter_context(self.nc.semaphore("d_" + tgt.name))
        tgt.dcnt += 16
        self.eng[Q].dma_start(out=out, in_=in_).then_inc(tgt.dsem, 16)
        self.nins += 1
        me = ("d_" + tgt.name, tgt.dsem, tgt.dcnt, None)
        for b in writes:
            b.w = me
            b.r = {}
        for b in reads:
            b.r["dma_" + tgt.name] = me
        return me

    def fence(self):
        for E in ("pe", "act", "dve", "pool", "sp"):
            for Fg in ("pe", "act", "dve", "pool"):
                if Fg != E and self.cnt[Fg] > 0:
                    self._need(E, (Fg, self.sem[Fg], self.cnt[Fg], Fg), True)


def build(S=2048, NSEQ=2, LAYERS=(0, 1, 2, 3), DEBUG=False):
    order = _build(S, NSEQ, LAYERS, DEBUG, None)
    return _build(S, NSEQ, LAYERS, DEBUG, order)


def _build(S, NSEQ, LAYERS, DEBUG, ORDER):
    RECORD = ORDER is None
    nc = bass.Bass("TRN2", target_bir_lowering=False)
    plan, wtotal = weight_plan()
    pcols, npv = pv_plan()
    NB = S // 512

    x_d = nc.dram_tensor("x", [NSEQ, 128, KC, S], F32, kind="ExternalInput")
    c_d = nc.dram_tensor("c", [128, KC, NSEQ], F32, kind="ExternalInput")
    pos_d = nc.dram_tensor("pos", [NSEQ, S], I32, kind="ExternalInput")
    w_d = nc.dram_tensor("wts", [wtotal], F32, kind="ExternalInput")
    pv_d = nc.dram_tensor("pv", [128, npv], F32, kind="ExternalInput")
    out_d = nc.dram_tensor("out", [NSEQ, 128, KC, S], F32, kind="ExternalOutput")
    dgd = nc.dram_tensor("dgd", [2, 8, 128, 31 * 128], BF16, kind="Internal")

    dbg_d = nc.dram_tensor("dbg", [128, 16384], F32, kind="ExternalOutput") if DEBUG else None
    dbg_cols = {}
    build.dbg_cols = dbg_cols
    dbg_state = {"c": 0}

    with ExitStack() as es:
        sy = Sync(nc, es)
        dbgB = Buf("dbg")

        def dump(name, ap, bufs):
            if not DEBUG or name in dbg_cols:
                return
            n = ap.shape[-1]
            p0 = 0
            c0 = dbg_state["c"]
            dbg_cols[name] = (c0, n)
            dbg_state["c"] += n
            sy.dma("pool", dbg_d.ap()[0:ap.shape[0], c0:c0 + n], ap, reads=bufs, writes=[dbgB])

        def sb(name, shape, dt):
            return es.enter_context(nc.sbuf_tensor(name, shape, dt))

        xs = sb("xs", [128, KC, S], F32)
        xsB = [[Buf(f"xs{c}_{b}") for b in range(NB)] for c in range(KC)]
        pvt = sb("pvt", [128, npv], F32)
        pvB = Buf("pv")
        ring = sb("ring", [128, RING, SLOT], BF16)
        ringB = [Buf(f"ring{i}") for i in range(RING)]
        ident_f = sb("ident_f", [128, 128], F32)
        ident_b = sb("ident_b", [128, 128], BF16)
        od1024 = sb("od1024", [128, 128], BF16)
        od256 = sb("od256", [128, 128], BF16)
        maskT = sb("maskT", [128, 128], BF16)
        constB = Buf("const")
        cin = sb("cin", [128, KC, NSEQ], F32)
        cact = sb("cact", [128, KC, NSEQ], BF16)
        cB = Buf("c")
        modt = sb("modt", [128, 96, NSEQ], F32)
        modB = Buf("mod")
        drvA = sb("drvA", [128, 4, NSEQ, 8], F32)
        drvG = sb("drvG", [128, 4, NSEQ, 8], F32)
        drvB = Buf("drv")
        nsp = sb("nsp", [128, 2, 8], F32)
        nspB = Buf("nsp")
        carry = sb("carry", [128, 8], F32)
        carryB = [Buf(f"carry{j}") for j in range(8)]
        xbh = sb("xbh", [128, 8, 4], BF16)
        xbhB = [Buf(f"xbh{j}") for j in range(8)]

        psum = [es.enter_context(nc.psum_tensor(f"ps{i}", [128, 512], F32)) for i in range(8)]
        psB = [Buf(f"ps{i}") for i in range(8)]
        ps_state = {"i": 0}

        def nextps():
            i = 2 + ps_state["i"] % 6
            ps_state["i"] += 1
            return psum[i], psB[i]

        def pv(name, a, b=None):
            c0 = pcols[name]
            if b is None:
                return pvt[:, c0 + a:c0 + a + 1]
            return pvt[:, c0 + a:c0 + b]

        ws = {"order": [] if RECORD else ORDER, "issued": 0, "acq": 0, "rel": 0, "done": set()}

        def w_ap(name):
            off, Fw = plan[name]
            return AP(w_d, off, [[Fw, 128], [1, Fw]]), Fw

        def ws_issue():
            if RECORD:
                return
            while ws["issued"] < len(ws["order"]) and ws["issued"] < ws["rel"] + RING:
                i = ws["issued"]
                src, Fw = w_ap(ws["order"][i])
                slot = i % RING
                sy.dma("pool", ring[:, slot, 0:Fw], src, writes=[ringB[slot]])
                ws["issued"] += 1

        def acquire(name):
            i = ws["acq"]
            ws["acq"] += 1
            if RECORD:
                ws["order"].append(name)
                return ring[:, 0, :], ringB[0], i
            assert ws["order"][i] == name, (ws["order"][i], name)
            assert ws["issued"] > i, "weight ring deadlock (acquire beyond issued)"
            slot = i % RING
            return ring[:, slot, :], ringB[slot], i

        def release(i):
            ws["done"].add(i)
            while ws["rel"] in ws["done"]:
                ws["done"].remove(ws["rel"])
                ws["rel"] += 1
            ws_issue()

        sy.dma("sp", pvt[:], pv_d.ap()[:, :], writes=[pvB])
        sy.dma("sp", cin[:], c_d.ap()[:, :, :], writes=[cB])
        ws_issue()
        sy.op("pool", lambda e: e.memset(ident_f[:], 1.0), writes=[constB])
        sy.op("pool", lambda e: e.affine_select(out=ident_f[:], in_=ident_f[:], pattern=[[-1, 128]], base=0,
                                                channel_multiplier=1, compare_op=ALU.is_equal, fill=0.0),
              writes=[constB])
        sy.op("pool", lambda e: e.tensor_copy(out=ident_b[:], in_=ident_f[:]), writes=[constB])
        sy.op("pool", lambda e: e.memset(od1024[:], 1.0 / 1024), writes=[constB])
        sy.op("pool", lambda e: e.memset(od256[:], 1.0 / 256), writes=[constB])
        sy.op("pool", lambda e: e.memset(maskT[:], 0.0), writes=[constB])
        sy.op("pool", lambda e: e.affine_select(out=maskT[:], in_=maskT[:], pattern=[[1, 128]], base=0,
                                                channel_multiplier=-1, compare_op=ALU.is_ge, fill=-30000.0),
              writes=[constB])
        sy.op("act", lambda e: e.activation(out=cact[:], in_=cin[:], func=AF.Silu), reads=[cB], writes=[cB])

        dgdB = [[Buf(f"dgd{j}_{jj}") for jj in range(8)] for j in range(2)]
        with ExitStack() as st0:
            dgs = [st0.enter_context(nc.sbuf_tensor(f"dgs{i}", [128, 31, 128], BF16)) for i in range(2)]
            dgsB = [Buf(f"dgs{i}") for i in range(2)]
            for j in range(2):
                if (2 * j) not in LAYERS:
                    continue
                for jj in range(8):
                    i = jj % 2
                    cw0 = pcols[("conv_w", j)] + jj * 31
                    sy.op("pool", lambda e: e.tensor_tensor(
                        out=dgs[i][:], in0=ident_b[:].unsqueeze(1).to_broadcast([128, 31, 128]),
                        in1=pvt[:, cw0:cw0 + 31].unsqueeze(2).to_broadcast([128, 31, 128]), op=ALU.mult),
                        reads=[constB, pvB], writes=[dgsB[i]])
                    sy.dma("sp", dgd.ap()[j, jj, :, :], dgs[i][:].rearrange("p a b -> p (a b)"),
                           reads=[dgsB[i]], writes=[dgdB[j][jj]])
            sy.fence()
            for j in range(2):
                for jj in range(8):
                    if dgdB[j][jj].w is not None:
                        for E_ in ("pe", "act", "dve", "pool", "sp"):
                            sy._need(E_, dgdB[j][jj].w, True)

        for l in range(4):
            for n in range(24):
                wt, wB, wi = acquire(("ada", l, n))
                pt, pB = nextps()
                for kc in range(KC):
                    sy.op("pe", lambda e, kc=kc: e.matmul(pt[:, 0:NSEQ], lhsT=wt[:, kc * 128:(kc + 1) * 128],
                                                          rhs=cact[:, kc, :], start=(kc == 0), stop=(kc == KC - 1)),
                          reads=[wB, cB], writes=[pB], inc=(kc == KC - 1))
                release(wi)
                sy.op("act", lambda e: e.activation(out=modt[:, l * 24 + n, :], in_=pt[:, 0:NSEQ], func=AF.Identity,
                                                    bias=pv(("ada_b", l), n), scale=1.0),
                      reads=[pB], sreads=[pvB], writes=[modB])
        for l in range(4):
            for s in range(NSEQ):
                sy.op("dve", lambda e: e.tensor_scalar(out=drvA[:, l, s, :], in0=modt[:, l * 24 + 8:l * 24 + 16, s],
                                                       scalar1=1.0, scalar2=None, op0=ALU.add),
                      reads=[modB], writes=[drvB])
                sy.op("dve", lambda e: e.tensor_tensor(out=drvA[:, l, s, :], in0=drvA[:, l, s, :],
                                                       in1=pv(("pre_g", l), 0, 8), op=ALU.mult),
                      reads=[pvB], writes=[drvB])
                sy.op("dve", lambda e: e.tensor_tensor(out=drvG[:, l, s, :], in0=modt[:, l * 24 + 16:l * 24 + 24, s],
                                                       in1=pv(("post_g", l), 0, 8), op=ALU.mult),
                      reads=[pvB, modB], writes=[drvB])
        spt = [sb(f"spt{i}", [128, 8], F32) for i in range(4)]
        for j in range(2):
            lam_ap = pv(("lam", j), 0, 8)
            al, ee, ww, w2 = spt
            sy.op("act", lambda e: e.activation(out=al[:], in_=lam_ap, func=AF.Abs),
                  reads=[pvB], writes=[nspB])
            sy.op("act", lambda e: e.activation(out=ee[:], in_=al[:], func=AF.Exp, scale=-1.0),
                  reads=[nspB], writes=[nspB])
            sy.op("dve", lambda e: e.tensor_scalar(out=ww[:], in0=ee[:], scalar1=2.0, scalar2=None, op0=ALU.add),
                  reads=[nspB], writes=[nspB])
            sy.op("dve", lambda e: e.reciprocal(out=ww[:], in_=ww[:]), writes=[nspB])
            sy.op("dve", lambda e: e.tensor_tensor(out=ww[:], in0=ww[:], in1=ee[:], op=ALU.mult), writes=[nspB])
            sy.op("dve", lambda e: e.tensor_tensor(out=w2[:], in0=ww[:], in1=ww[:], op=ALU.mult), writes=[nspB])
            sy.op("dve", lambda e: e.tensor_scalar(out=al[:], in0=w2[:], scalar1=1.0 / 11, scalar2=1.0 / 9, op0=ALU.mult,
                                                   op1=ALU.add), writes=[nspB])
            for cf in (1.0 / 7, 1.0 / 5, 1.0 / 3, 1.0):
                sy.op("dve", lambda e: e.tensor_tensor(out=al[:], in0=al[:], in1=w2[:], op=ALU.mult), writes=[nspB])
                sy.op("dve", lambda e, cf=cf: e.tensor_scalar(out=al[:], in0=al[:], scalar1=cf, scalar2=None, op0=ALU.add),
                      writes=[nspB])
            sy.op("dve", lambda e: e.tensor_tensor(out=al[:], in0=al[:], in1=ww[:], op=ALU.mult), writes=[nspB])
            sy.op("dve", lambda e: e.tensor_scalar(out=ee[:], in0=lam_ap, scalar1=-1.0, scalar2=0.0, op0=ALU.mult,
                                                   op1=ALU.max), reads=[pvB], writes=[nspB])
            sy.op("dve", lambda e: e.scalar_tensor_tensor(out=al[:], in0=al[:], scalar=2.0, in1=ee[:], op0=ALU.mult,
                                                          op1=ALU.add), writes=[nspB])
            sy.op("dve", lambda e: e.tensor_scalar(out=nsp[:, j, :], in0=al[:], scalar1=-8.0, scalar2=None,
                                                   op0=ALU.mult), writes=[nspB])
        dump("modt", modt[:].rearrange("p a b -> p (a b)"), [modB])
        dump("drvA", drvA[:].rearrange("p a b c -> p (a b c)"), [drvB])
        dump("drvG", drvG[:].rearrange("p a b c -> p (a b c)"), [drvB])
        dump("nsp", nsp[:].rearrange("p a b -> p (a b)"), [nspB])
        tp_stats = [None]
        def mm8(wt, wB, rhs_fn, rB, M=128, wcol0=0, prow=None):
            pt, pB = nextps()
            for kc in range(KC):
                sy.op("pe", lambda e, kc=kc: e.matmul(pt[0:M, :], lhsT=wt[:, kc * 128 + wcol0:kc * 128 + wcol0 + M],
                                                      rhs=rhs_fn(kc), start=(kc == 0), stop=(kc == KC - 1)),
                      reads=[wB] + rB, writes=[pB], inc=(kc == KC - 1))
            return pt, pB

        def rms_stats(src_fn, srcB, nch, onesm, sqt, sqB, tp):
            tp = tp_stats[0] or tp
            sy.op("act", lambda e: e.activation(out=sqt[:, 0:nch, :], in_=src_fn(), func=AF.Square),
                  reads=srcB, writes=[sqB])
            pt, pB = nextps()
            for kc in range(nch):
                sy.op("pe", lambda e, kc=kc: e.matmul(pt[:, :], lhsT=onesm[:], rhs=sqt[:, kc, :], start=(kc == 0),
                                                      stop=(kc == nch - 1)),
                      reads=[constB, sqB], writes=[pB], inc=(kc == nch - 1))
            sd, sdB = tp()
            sy.op("act", lambda e: e.activation(out=sd[:], in_=pt[:], func=AF.Sqrt, bias=EPS, scale=1.0),
                  reads=[pB], writes=[sdB])
            rs, rsB = tp()
            sy.op("dve", lambda e: e.reciprocal(out=rs[:], in_=sd[:]), reads=[sdB], writes=[rsB])
            return rs, rsB

        def prenorm(l, s, t0, hn_fn, hnB, sqt, sqB, tp, tpt=None):
            tpt = tpt or tp
            b = t0 // 512
            rs, rsB = rms_stats(lambda: xs[:, :, t0:t0 + 512], [xsB[c][b] for c in range(KC)], KC, od1024, sqt, sqB, tp)
            for kc in range(KC):
                tt, ttB = tpt()
                sy.op("dve", lambda e: e.tensor_tensor(out=tt[:], in0=xs[:, kc, t0:t0 + 512], in1=rs[:], op=ALU.mult),
                      reads=[xsB[kc][b], rsB], writes=[ttB])
                sy.op("act", lambda e: e.activation(out=hn_fn(kc), in_=tt[:], func=AF.Identity,
                                                    scale=drvA[:, l, s, kc:kc + 1], bias=modt[:, l * 24 + kc, s:s + 1]),
                      reads=[ttB], sreads=[drvB, modB], writes=[hnB[kc]])

        def postnorm(l, s, t0, y, yB, sqt, sqB, tp, tpt=None):
            tpt = tpt or tp
            b = t0 // 512
            rs, rsB = rms_stats(lambda: y[:, :, :], yB, KC, od1024, sqt, sqB, tp)
            for n in range(KC):
                tt, ttB = tpt()
                sy.op("dve", lambda e: e.tensor_tensor(out=tt[:], in0=y[:, n, :], in1=rs[:], op=ALU.mult),
                      reads=[yB[n], rsB], writes=[ttB])
                sy.op("dve", lambda e: e.scalar_tensor_tensor(out=xs[:, n, t0:t0 + 512], in0=tt[:],
                                                              scalar=drvG[:, l, s, n:n + 1], in1=xs[:, n, t0:t0 + 512],
                                                              op0=ALU.mult, op1=ALU.add),
                      reads=[ttB], sreads=[drvB], writes=[xsB[n][b]])

        def mk_tmp_pool(ess, name, n, dt=F32, w=512):
            _UID[0] += 1
            tiles = [ess.enter_context(nc.sbuf_tensor(f"{name}{i}_{_UID[0]}", [128, w], dt)) for i in range(n)]
            bufs = [Buf(f"{name}{i}") for i in range(n)]
            st = {"i": 0}

            def get():
                i = st["i"] % n
                st["i"] += 1
                return tiles[i], bufs[i]
            return get

        def even_layer(l, s):
            j = l // 2
            with ExitStack() as el:
                def sbl(name, shape, dt):
                    return el.enter_context(nc.sbuf_tensor(f"{name}_{l}_{s}", shape, dt))
                hn = sbl("e_hn", [128, KC, 512], BF16)
                hnB = [Buf(f"e_hn{c}") for c in range(KC)]
                G = sbl("e_G", [128, KC, 544], BF16)
                GB = [Buf(f"e_G{c}") for c in range(KC)]
                Z = sbl("e_Z", [128, KC, 512], BF16)
                ZB = [Buf(f"e_Z{c}") for c in range(KC)]
                big = sbl("e_big", [128, KC, 512], F32)
                bigB = [Buf(f"e_big{c}") for c in range(KC)]
                Lo = sbl("e_Lo", [128, KC, 512], BF16)
                LoB = [Buf(f"e_Lo{c}") for c in range(KC)]
                sqt = sbl("e_sq", [128, KC, 512], BF16)
                sqB = Buf("e_sq")
                dg = sbl("e_dg", [128, 31, 128], BF16)
                dgB = Buf("e_dg")
                d4 = [sbl(f"e_d4{i}", [128, 4, 128], BF16) for i in range(2)]
                d4B = [Buf(f"e_d4{i}") for i in range(2)]
                XB = [sbl(f"e_XB{i}", [128, 516], BF16) for i in range(2)]
                XBB = [Buf(f"e_XB{i}") for i in range(2)]
                tp = mk_tmp_pool(el, "e_tf", 5, F32)
                tpt = mk_tmp_pool(el, "e_tt", 3, F32)
                tpb = mk_tmp_pool(el, "e_tb", 2, BF16)
                lt = [[sbl(f"e_lt{p}{i}", [128, 512], F32) for i in range(5)] for p in range(2)]
                ltB = [[Buf(f"e_lt{p}{i}") for i in range(5)] for p in range(2)]
                sgt = [sbl(f"e_sgt{p}", [128, 512], BF16) for p in range(2)]
                sgtB = [Buf(f"e_sgt{p}") for p in range(2)]
                zbt = [sbl(f"e_zbt{p}", [128, 512], BF16) for p in range(2)]
                zbtB = [Buf(f"e_zbt{p}") for p in range(2)]
                xcbt = [sbl(f"e_xcbt{p}", [128, 512], BF16) for p in range(2)]
                xcbtB = [Buf(f"e_xcbt{p}") for p in range(2)]

                sy.op("dve", lambda e: e.memset(G[:, :, 0:32], 0.0), writes=GB)
                sy.op("dve", lambda e: e.memset(carry[:], 0.0), writes=carryB)
                sy.op("dve", lambda e: e.memset(xbh[:], 0.0), writes=xbhB)
                hrhs = lambda kc: hn[:, kc, :]

                def w_mm8(name):
                    wt, wB, wi = acquire(name)
                    pt, pB = mm8(wt, wB, hrhs, hnB)
                    release(wi)
                    return pt, pB

                def conv_stage(jj):
                    p = jj % 2
                    pt, pB = w_mm8(("ewin", j, 8 + jj))
                    yield
                    sy.op("act", lambda e: e.activation(out=sgt[p][:], in_=pt[:], func=AF.Sigmoid),
                          reads=[pB], writes=[sgtB[p]])
                    pt, pB = w_mm8(("ewin", j, jj))
                    yield
                    sy.op("dve", lambda e: e.tensor_tensor(out=G[:, jj, 32:544], in0=pt[:], in1=sgt[p][:], op=ALU.mult),
                          reads=[pB, sgtB[p]], writes=[GB[jj]])
                    pt, pB = w_mm8(("ewin", j, 16 + jj))
                    yield
                    sy.op("act", lambda e: e.activation(out=Z[:, jj, :], in_=pt[:], func=AF.Silu),
                          reads=[pB], writes=[ZB[jj]])
                    sy.dma("sp", dg[:].rearrange("p a b -> p (a b)"), dgd.ap()[j, jj, :, :],
                           reads=[dgdB[j][jj]], writes=[dgB])
                    yield
                    yield
                    pt, pB = nextps()
                    for k in range(31):
                        sy.op("pe", lambda e, k=k: e.matmul(pt[:, :], lhsT=dg[:, k, :], rhs=G[:, jj, k + 2:k + 514],
                                                            start=(k == 0), stop=(k == 30)),
                              reads=[dgB, GB[jj]], writes=[pB], inc=(k == 30))
                        if k % 8 == 7:
                            yield
                    sy.op("act", lambda e: e.activation(out=big[:, jj, :], in_=pt[:], func=AF.Identity,
                                                        bias=pv(("conv_b", j), jj), scale=1.0),
                          reads=[pB], sreads=[pvB], writes=[bigB[jj]])
                    if jj == 0:
                        dump("G0", G[:, 0, 32:544], [GB[0]])
                        dump("aconv0", big[:, 0, :], [bigB[0]])
                        dump("Zraw0", Z[:, 0, :], [ZB[0]])
                    yield
                    sy.op("dve", lambda e: e.tensor_copy(out=G[:, jj, 0:32], in_=G[:, jj, 512:544]),
                          reads=[GB[jj]], writes=[GB[jj]])

                def lru_stage(jj):
                    p = jj % 2
                    xbt, xbB = XB[p], XBB[p]
                    pt, pB = w_mm8(("ewin", j, 24 + jj))
                    yield
                    sy.op("act", lambda e: e.activation(out=xbt[:, 4:516], in_=pt[:], func=AF.Identity),
                          reads=[pB], writes=[xbB])
                    sy.op("dve", lambda e: e.tensor_copy(out=xbt[:, 0:4], in_=xbh[:, jj, :]),
                          reads=[xbhB[jj]], writes=[xbB])
                    pt, pB = w_mm8(("ewin", j, 32 + jj))
                    yield
                    sy.op("act", lambda e: e.activation(out=zbt[p][:], in_=pt[:], func=AF.Silu), reads=[pB],
                          writes=[zbtB[p]])
                    lw0 = pcols[("lconv_w", j)] + jj * 4
                    sy.op("dve", lambda e: e.tensor_tensor(
                        out=d4[p][:], in0=ident_b[:].unsqueeze(1).to_broadcast([128, 4, 128]),
                        in1=pvt[:, lw0:lw0 + 4].unsqueeze(2).to_broadcast([128, 4, 128]), op=ALU.mult),
                        reads=[constB, pvB], writes=[d4B[p]])
                    yield
                    pt, pB = nextps()
                    for k in range(4):
                        sy.op("pe", lambda e, k=k: e.matmul(pt[:, :], lhsT=d4[p][:, k, :], rhs=xbt[:, k + 1:k + 513],
                                                            start=(k == 0), stop=(k == 3)),
                              reads=[d4B[p], xbB], writes=[pB], inc=(k == 3))
                    yield
                    xc, xcB = lt[p][0], ltB[p][0]
                    rg, rgB = lt[p][1], ltB[p][1]
                    ig, igB = lt[p][2], ltB[p][2]
                    at, atB = lt[p][3], ltB[p][3]
                    hh_, hhB = lt[p][4], ltB[p][4]
                    sy.op("act", lambda e: e.activation(out=xc[:], in_=pt[:], func=AF.Identity,
                                                        bias=pv(("lconv_b", j), jj), scale=1.0),
                          reads=[pB], sreads=[pvB], writes=[xcB])
                    yield
                    sy.op("dve", lambda e: e.tensor_copy(out=xcbt[p][:], in_=xc[:]), reads=[xcB], writes=[xcbtB[p]])
                    sy.op("dve", lambda e: e.tensor_copy(out=xbh[:, jj, :], in_=xbt[:, 512:516]),
                          reads=[xbB], writes=[xbhB[jj]])
                    yield
                    wt, wB, wi = acquire(("egate", j, jj))
                    pr, prB = nextps()
                    sy.op("pe", lambda e: e.matmul(pr[:, :], lhsT=wt[:, 0:128], rhs=xcbt[p][:], start=True, stop=True),
                          reads=[wB, xcbtB[p]], writes=[prB])
                    pi_, piB = nextps()
                    sy.op("pe", lambda e: e.matmul(pi_[:, :], lhsT=wt[:, 128:256], rhs=xcbt[p][:], start=True, stop=True),
                          reads=[wB, xcbtB[p]], writes=[piB])
                    release(wi)
                    yield
                    sy.op("act", lambda e: e.activation(out=rg[:], in_=pr[:], func=AF.Sigmoid,
                                                        bias=pv(("ba", j), jj), scale=1.0),
                          reads=[prB], sreads=[pvB], writes=[rgB])
                    yield
                    sy.op("act", lambda e: e.activation(out=ig[:], in_=pi_[:], func=AF.Sigmoid,
                                                        bias=pv(("bx", j), jj), scale=1.0),
                          reads=[piB], sreads=[pvB], writes=[igB])
                    yield
                    sy.op("act", lambda e: e.activation(out=at[:], in_=rg[:], func=AF.Exp,
                                                        scale=nsp[:, j, jj:jj + 1]),
                          reads=[rgB], sreads=[nspB], writes=[atB])
                    sy.op("dve", lambda e: e.tensor_tensor(out=ig[:], in0=ig[:], in1=xc[:], op=ALU.mult),
                          reads=[xcB], writes=[igB])
                    yield
                    sy.op("dve", lambda e: e.tensor_tensor(out=rg[:], in0=at[:], in1=at[:], op=ALU.mult),
                          reads=[atB], writes=[rgB])
                    yield
                    sy.op("act", lambda e: e.activation(out=rg[:], in_=rg[:], func=AF.Sqrt, bias=1.0, scale=-1.0),
                          writes=[rgB])
                    yield
                    sy.op("dve", lambda e: e.tensor_tensor(out=ig[:], in0=ig[:], in1=rg[:], op=ALU.mult),
                          reads=[rgB], writes=[igB])
                    yield
                    sy.op("dve", lambda e: e.tensor_tensor_scan(out=hh_[:], data0=at[:], data1=ig[:],
                                                                initial=carry[:, jj:jj + 1], op0=ALU.mult,
                                                                op1=ALU.add),
                          reads=[atB, igB], sreads=[carryB[jj]], writes=[hhB])
                    yield
                    sy.op("act", lambda e: e.activation(out=carry[:, jj:jj + 1], in_=hh_[:, 511:512],
                                                        func=AF.Identity),
                          reads=[hhB], writes=[carryB[jj]])
                    sy.op("dve", lambda e: e.tensor_tensor(out=Lo[:, jj, :], in0=hh_[:], in1=zbt[p][:], op=ALU.mult),
                          reads=[hhB, zbtB[p]], writes=[LoB[jj]])

                for t in range(NB):
                    t0 = t * 512
                    prenorm(l, s, t0, lambda kc: hn[:, kc, :], hnB, sqt, sqB, tp, tpt)
                    dump("hn0", hn[:, 0, :], [hnB[0]])
                    active = []
                    for jj in range(8):
                        active += [conv_stage(jj), lru_stage(jj)]
                        steps = 0
                        while active and (jj == 7 or steps < EVEN_STAGGER):
                            for g_ in list(active):
                                try:
                                    next(g_)
                                except StopIteration:
                                    active.remove(g_)
                            steps += 1
                    sy.op("dve", lambda e: e.tensor_copy(out=sqt[:], in_=big[:]), reads=bigB, writes=[sqB])
                    pm, pmB = nextps()
                    for kc in range(KC):
                        sy.op("pe", lambda e, kc=kc: e.matmul(pm[:, :], lhsT=od1024[:], rhs=sqt[:, kc, :],
                                                              start=(kc == 0), stop=(kc == KC - 1)),
                              reads=[constB, sqB], writes=[pmB], inc=(kc == KC - 1))
                    sy.op("act", lambda e: e.activation(out=sqt[:], in_=big[:], func=AF.Square),
                          reads=bigB, writes=[sqB])
                    p2, p2B = nextps()
                    for kc in range(KC):
                        sy.op("pe", lambda e, kc=kc: e.matmul(p2[:, :], lhsT=od1024[:], rhs=sqt[:, kc, :],
                                                              start=(kc == 0), stop=(kc == KC - 1)),
                              reads=[constB, sqB], writes=[p2B], inc=(kc == KC - 1))
                    mean, meanB = tp()
                    sy.op("act", lambda e: e.activation(out=mean[:], in_=pm[:], func=AF.Identity), reads=[pmB],
                          writes=[meanB])
                    var, varB = tp()
                    sy.op("dve", lambda e: e.tensor_tensor(out=var[:], in0=mean[:], in1=mean[:], op=ALU.mult),
                          reads=[meanB], writes=[varB])
                    sy.op("dve", lambda e: e.tensor_tensor(out=var[:], in0=p2[:], in1=var[:], op=ALU.subtract),
                          reads=[p2B], writes=[varB])
                    sy.op("dve", lambda e: e.tensor_scalar(out=var[:], in0=var[:], scalar1=0.0, scalar2=None,
                                                           op0=ALU.max), writes=[varB])
                    sd, sdB = tp()
                    sy.op("act", lambda e: e.activation(out=sd[:], in_=var[:], func=AF.Sqrt, bias=EPS, scale=1.0),
                          reads=[varB], writes=[sdB])
                    rs, rsB = tp()
                    sy.op("dve", lambda e: e.reciprocal(out=rs[:], in_=sd[:]), reads=[sdB], writes=[rsB])
                    mr, mrB = tp()
                    sy.op("dve", lambda e: e.tensor_tensor(out=mr[:], in0=mean[:], in1=rs[:], op=ALU.mult),
                          reads=[meanB, rsB], writes=[mrB])
                    for jj in range(8):
                        t1, t1B = tpt()
                        sy.op("dve", lambda e: e.tensor_tensor(out=t1[:], in0=big[:, jj, :], in1=rs[:], op=ALU.mult),
                              reads=[bigB[jj], rsB], writes=[t1B])
                        sy.op("dve", lambda e: e.tensor_tensor(out=t1[:], in0=t1[:], in1=mr[:], op=ALU.subtract),
                              reads=[mrB], writes=[t1B])
                        s1, s1B = tpb()
                        sy.op("act", lambda e: e.activation(out=s1[:], in_=t1[:], func=AF.Silu,
                                                            scale=pv(("ln_g", j), jj), bias=pv(("ln_b", j), jj)),
                              reads=[t1B], sreads=[pvB], writes=[s1B])
                        sy.op("dve", lambda e: e.tensor_tensor(out=Z[:, jj, :], in0=s1[:], in1=Z[:, jj, :], op=ALU.mult),
                              reads=[s1B], writes=[ZB[jj]])
                    dump("Aout0", Z[:, 0, :], [ZB[0]])
                    dump("Lo0", Lo[:, 0, :], [LoB[0]])
                    for n in range(8):
                        w0, w0B, wi0 = acquire(("ewout", j, n, 0))
                        w1, w1B, wi1 = acquire(("ewout", j, n, 1))
                        pt, pB = nextps()
                        for kc in range(16):
                            wsrc, wsB = (w0, w0B) if kc < 8 else (w1, w1B)
                            src, srcB = (Z, ZB) if kc < 8 else (Lo, LoB)
                            k8 = kc % 8
                            sy.op("pe", lambda e, kc=kc, k8=k8, wsrc=wsrc, src=src: e.matmul(
                                pt[:, :], lhsT=wsrc[:, k8 * 128:(k8 + 1) * 128], rhs=src[:, k8, :],
                                start=(kc == 0), stop=(kc == 15)),
                                reads=[wsB, srcB[k8]], writes=[pB], inc=(kc == 15))
                        release(wi0)
                        release(wi1)
                        sy.op("act", lambda e: e.activation(out=big[:, n, :], in_=pt[:], func=AF.Identity),
                              reads=[pB], writes=[bigB[n]])
                    dump("y0", big[:, 0, :], [bigB[0]])
                    postnorm(l, s, t0, big, bigB, sqt, sqB, tp, tpt)
                sy.fence()

        def odd_layer(l, s):
            j = l // 2
            with ExitStack() as ol:
                def sbl(name, shape, dt):
                    return ol.enter_context(nc.sbuf_tensor(f"{name}_{l}_{s}", shape, dt))
                Zs = sbl("o_Zs", [128, KC, S], BF16)
                ZsB = [[Buf(f"o_Zs{c}_{b}") for b in range(NB)] for c in range(KC)]
                cqn = sbl("o_cqn", [128, 2, S], BF16)
                cqnB = [Buf(f"o_cqn{b}") for b in range(NB)]
                ckvn = sbl("o_ckvn", [128, 2, S], BF16)
                ckvnB = [Buf(f"o_ckvn{b}") for b in range(NB)]
                kr = sbl("o_kr", [128, S], BF16)
                krB = Buf("o_kr")
                COS = sbl("o_cos", [128, S], BF16)
                SIN = sbl("o_sin", [128, S], BF16)
                csB = Buf("o_cs")
                tp = mk_tmp_pool(ol, "o_tf", 4, F32)
                tp_stats[0] = mk_tmp_pool(ol, "o_ts", 3, F32)

                with ExitStack() as rl:
                    ang = rl.enter_context(nc.sbuf_tensor(f"o_ang_{l}_{s}", [128, S], F32))
                    wk = rl.enter_context(nc.sbuf_tensor(f"o_wk_{l}_{s}", [128, S], F32))
                    wk2 = rl.enter_context(nc.sbuf_tensor(f"o_wk2_{l}_{s}", [128, S], F32))
                    ki = rl.enter_context(nc.sbuf_tensor(f"o_ki_{l}_{s}", [128, S], I32))
                    posi = ki
                    rB = Buf("o_rope")
                    R = slice(64, 96)
                    src = AP(pos_d, s * S, [[0, 32], [1, S]])
                    sy.dma("sp", posi[R, :], src, writes=[rB])
                    sy.op("dve", lambda e: e.tensor_copy(out=ang[R, :], in_=posi[R, :]), reads=[rB], writes=[rB])
                    sy.op("dve", lambda e: e.tensor_scalar(out=ang[R, :], in0=ang[R, :], scalar1=pvt[R, pcols["inv"]:pcols["inv"] + 1],
                                                           scalar2=None, op0=ALU.mult), sreads=[pvB], writes=[rB])
                    for which in range(2):
                        if which == 0:
                            sy.op("dve", lambda e: e.tensor_scalar(out=wk2[R, :], in0=ang[R, :], scalar1=math.pi / 2,
                                                                   scalar2=None, op0=ALU.add), writes=[rB])
                            a_in = wk2
                        else:
                            a_in = ang
                        sy.op("dve", lambda e: e.tensor_scalar(out=wk[R, :], in0=a_in[R, :], scalar1=1.0 / TWO_PI,
                                                               scalar2=None, op0=ALU.mult), writes=[rB])
                        sy.op("dve", lambda e: e.tensor_copy(out=ki[R, :], in_=wk[R, :]), writes=[rB])
                        sy.op("dve", lambda e: e.tensor_copy(out=wk[R, :], in_=ki[R, :]), writes=[rB])
                        sy.op("dve", lambda e: e.scalar_tensor_tensor(out=wk2[R, :], in0=wk[R, :], scalar=-PI_HI,
                                                                      in1=a_in[R, :], op0=ALU.mult, op1=ALU.add),
                              writes=[rB])
                        sy.op("dve", lambda e: e.scalar_tensor_tensor(out=wk2[R, :], in0=wk[R, :], scalar=-PI_LO,
                                                                      in1=wk2[R, :], op0=ALU.mult, op1=ALU.add),
                              writes=[rB])
                        sy.op("dve", lambda e: e.tensor_scalar(out=wk[R, :], in0=wk2[R, :], scalar1=math.pi,
                                                               scalar2=-TWO_PI, op0=ALU.is_gt, op1=ALU.mult), writes=[rB])
                        sy.op("dve", lambda e: e.tensor_tensor(out=wk2[R, :], in0=wk2[R, :], in1=wk[R, :], op=ALU.add),
                              writes=[rB])
                        sy.op("dve", lambda e: e.tensor_scalar(out=wk[R, :], in0=wk2[R, :], scalar1=-math.pi,
                                                               scalar2=TWO_PI, op0=ALU.is_lt, op1=ALU.mult), writes=[rB])
                        sy.op("dve", lambda e: e.tensor_tensor(out=wk2[R, :], in0=wk2[R, :], in1=wk[R, :], op=ALU.add),
                              writes=[rB])
                        sy.op("dve", lambda e: e.tensor_scalar(out=wk2[R, :], in0=wk2[R, :], scalar1=3.1415925,
                                                               scalar2=-3.1415925, op0=ALU.min, op1=ALU.max), writes=[rB])
                        if which == 0:
                            sy.op("act", lambda e: e.activation(out=COS[R, :], in_=wk2[R, :], func=AF.Sin),
                                  reads=[rB], writes=[csB])
                        else:
                            sy.op("dve", lambda e: e.tensor_scalar(out=wk2[R, :], in0=wk2[R, :],
                                                                   scalar1=pvt[R, pcols["sgn"]:pcols["sgn"] + 1],
                                                                   scalar2=None, op0=ALU.mult), sreads=[pvB], writes=[rB])
                            sy.op("act", lambda e: e.activation(out=SIN[R, :], in_=wk2[R, :], func=AF.Sin),
                                  reads=[rB], writes=[csB])
                    sy.fence()

                with ExitStack() as p1:
                    hn = p1.enter_context(nc.sbuf_tensor(f"o_hn_{l}_{s}", [128, KC, 1024], BF16))
                    hnB2 = [[Buf(f"o_hn{c}_{b}") for c in range(KC)] for b in range(2)]
                    raw = p1.enter_context(nc.sbuf_tensor(f"o_raw_{l}_{s}", [128, 2, 1024], F32))
                    rawB = [Buf(f"o_raw{b}") for b in range(2)]
                    krA = p1.enter_context(nc.sbuf_tensor(f"o_krA_{l}_{s}", [128, 1024], F32))
                    krAB = [Buf(f"o_krA{b}") for b in range(2)]
                    sqt = p1.enter_context(nc.sbuf_tensor(f"o_sq_{l}_{s}", [128, KC, 512], BF16))
                    sqB = Buf("o_sq")
                    R = slice(64, 96)
                    for t in range(S // 1024):
                        for b in range(2):
                            t0 = t * 1024 + b * 512
                            prenorm(l, s, t0, lambda kc, b=b: hn[:, kc, b * 512:(b + 1) * 512], hnB2[b], sqt, sqB, tp)
                        for grp, (dst, dstB, nrm) in enumerate(((cqn, cqnB, "q_norm"), (ckvn, ckvnB, "kv_norm"))):
                            for c in range(2):
                                wt, wB, wi = acquire(("owin", j, grp * 2 + c))
                                for b in range(2):
                                    pt, pB = mm8(wt, wB, lambda kc, b=b: hn[:, kc, b * 512:(b + 1) * 512], hnB2[b])
                                    sy.op("act", lambda e: e.activation(out=raw[:, c, b * 512:(b + 1) * 512], in_=pt[:],
                                                                        func=AF.Identity),
                                          reads=[pB], writes=[rawB[b]])
                                release(wi)
                            for b in range(2):
                                gb = t * 2 + b
                                rs, rsB = rms_stats(lambda b=b: raw[:, :, b * 512:(b + 1) * 512], [rawB[b]], 2, od256,
                                                    sqt, sqB, tp)
                                for c in range(2):
                                    sy.op("dve", lambda e, c=c: e.scalar_tensor_tensor(
                                        out=dst[:, c, gb * 512:(gb + 1) * 512], in0=raw[:, c, b * 512:(b + 1) * 512],
                                        scalar=pv((nrm, j), c), in1=rs[:], op0=ALU.mult, op1=ALU.mult),
                                        reads=[rawB[b], rsB], sreads=[pvB], writes=[dstB[gb]])
                        wt, wB, wi = acquire(("owin", j, 4))
                        for b in range(2):
                            pt, pB = mm8(wt, wB, lambda kc, b=b: hn[:, kc, b * 512:(b + 1) * 512], hnB2[b])
                            sy.op("act", lambda e: e.activation(out=krA[R, b * 512:(b + 1) * 512], in_=pt[R, :],
                                                                func=AF.Identity), reads=[pB], writes=[krAB[b]])
                        release(wi)
                        wt, wB, wi = acquire(("owin", j, 5))
                        for b in range(2):
                            gb = t * 2 + b
                            tk = slice(gb * 512, (gb + 1) * 512)
                            pt, pB = mm8(wt, wB, lambda kc, b=b: hn[:, kc, b * 512:(b + 1) * 512], hnB2[b])
                            t1, t1B = tp()
                            sy.op("dve", lambda e: e.tensor_tensor(out=t1[R, :], in0=krA[R, b * 512:(b + 1) * 512],
                                                                   in1=COS[R, tk], op=ALU.mult),
                                  reads=[krAB[b], csB], writes=[t1B])
                            t2, t2B = tp()
                            sy.op("dve", lambda e: e.tensor_tensor(out=t2[R, :], in0=pt[R, :], in1=SIN[R, tk], op=ALU.mult),
                                  reads=[pB, csB], writes=[t2B])
                            sy.op("dve", lambda e: e.tensor_tensor(out=kr[R, tk], in0=t1[R, :], in1=t2[R, :], op=ALU.add),
                                  reads=[t1B, t2B], writes=[krB])
                        release(wi)
                        for c in range(8):
                            wt, wB, wi = acquire(("owin", j, 6 + c))
                            for b in range(2):
                                gb = t * 2 + b
                                pt, pB = mm8(wt, wB, lambda kc, b=b: hn[:, kc, b * 512:(b + 1) * 512], hnB2[b])
                                sy.op("act", lambda e: e.activation(out=Zs[:, c, gb * 512:(gb + 1) * 512], in_=pt[:],
                                                                    func=AF.Silu), reads=[pB], writes=[ZsB[c][gb]])
                            release(wi)
                    dump("cqn", cqn[:, 0, 0:512], [cqnB[0]])
                    dump("ckvn", ckvn[:, 0, 0:512], [ckvnB[0]])
                    dump("kr", kr[:, 0:512], [krB])
                    dump("cos", COS[:, 0:512], [csB])
                    dump("sin", SIN[:, 0:512], [csB])
                    dump("zs", Zs[:, 0, 0:512], [ZsB[0][0]])
                    sy.fence()

                with ExitStack() as p2:
                    QT = [p2.enter_context(nc.sbuf_tensor(f"o_QT{i}_{l}_{s}", [128, S], BF16)) for i in range(2)]
                    KT = [p2.enter_context(nc.sbuf_tensor(f"o_KT{i}_{l}_{s}", [128, S], BF16)) for i in range(2)]
                    V = [p2.enter_context(nc.sbuf_tensor(f"o_V{i}_{l}_{s}", [128, S // 128, 128], BF16)) for i in range(2)]
                    QTB = [Buf(f"o_QT{i}") for i in range(2)]
                    KTB = [Buf(f"o_KT{i}") for i in range(2)]
                    VB = [Buf(f"o_V{i}") for i in range(2)]
                    NPT = 6
                    PT = [p2.enter_context(nc.sbuf_tensor(f"o_PT{i}_{l}_{s}", [128, 512], BF16)) for i in range(NPT)]
                    PTB = [Buf(f"o_PT{i}") for i in range(NPT)]
                    rec = p2.enter_context(nc.sbuf_tensor(f"o_rec_{l}_{s}", [128, 512], F32))
                    recB = Buf("o_rec")
                    tpg = mk_tmp_pool(p2, "o_tg", 2, F32)
                    R = slice(64, 96)
                    sy.op("dve", lambda e: e.memset(V[0][:, :, 64:128], 1.0), writes=[VB[0]])
                    sy.op("dve", lambda e: e.memset(V[1][:, :, 0:64], 1.0), writes=[VB[1]])
                    pt_i = {"i": 0}

                    def gen_head(h):
                        par = h % 2
                        qt, qB = QT[par], QTB[par]
                        kt, kB = KT[par], KTB[par]
                        vt, vB = V[par], VB[par]
                        wt, wB, wi = acquire(("ouq", j, h))
                        for b in range(NB):
                            tk = slice(b * 512, (b + 1) * 512)
                            pa, paB = nextps()
                            pb, pbB = nextps()
                            for kc in range(2):
                                sy.op("pe", lambda e, kc=kc: e.matmul(pa[0:96, :], lhsT=wt[:, kc * 192:kc * 192 + 96],
                                                                      rhs=cqn[:, kc, tk], start=(kc == 0), stop=(kc == 1)),
                                      reads=[wB, cqnB[b]], writes=[paB], inc=(kc == 1))
                            for kc in range(2):
                                sy.op("pe", lambda e, kc=kc: e.matmul(pb[0:96, :], lhsT=wt[:, kc * 192 + 96:kc * 192 + 192],
                                                                      rhs=cqn[:, kc, tk], start=(kc == 0), stop=(kc == 1)),
                                      reads=[wB, cqnB[b]], writes=[pbB], inc=(kc == 1))
                            yield
                            sy.op("act", lambda e: e.activation(out=qt[0:64, tk], in_=pa[0:64, :], func=AF.Identity),
                                  reads=[paB], writes=[qB])
                            t1, t1B = tpg()
                            sy.op("dve", lambda e: e.tensor_tensor(out=t1[R, :], in0=pa[R, :], in1=COS[R, tk], op=ALU.mult),
                                  reads=[paB, csB], writes=[t1B])
                            t2, t2B = tpg()
                            sy.op("dve", lambda e: e.tensor_tensor(out=t2[R, :], in0=pb[R, :], in1=SIN[R, tk], op=ALU.mult),
                                  reads=[pbB, csB], writes=[t2B])
                            yield
                            sy.op("dve", lambda e: e.tensor_tensor(out=qt[R, tk], in0=t1[R, :], in1=t2[R, :], op=ALU.add),
                                  reads=[t1B, t2B], writes=[qB])
                            yield
                        release(wi)
                        wt, wB, wi = acquire(("oukv", j, h))
                        for b in range(NB):
                            tk = slice(b * 512, (b + 1) * 512)
                            pa, paB = nextps()
                            for kc in range(2):
                                sy.op("pe", lambda e, kc=kc: e.matmul(pa[0:64, :], lhsT=wt[:, kc * 128:kc * 128 + 64],
                                                                      rhs=ckvn[:, kc, tk], start=(kc == 0), stop=(kc == 1)),
                                      reads=[wB, ckvnB[b]], writes=[paB], inc=(kc == 1))
                            yield
                            sy.op("act", lambda e: e.activation(out=kt[0:64, tk], in_=pa[0:64, :], func=AF.Identity),
                                  reads=[paB], writes=[kB])
                            yield
                        sy.op("pool", lambda e: e.tensor_copy(out=kt[R, :], in_=kr[R, :]), reads=[krB], writes=[kB])
                        vo = 0 if par == 0 else 64
                        for g8 in range(S // 1024):
                            pa, paB = nextps()
                            for i8 in range(8):
                                kb = g8 * 8 + i8
                                for kc in range(2):
                                    sy.op("pe", lambda e, kc=kc, kb=kb, i8=i8: e.matmul(
                                        pa[:, i8 * 64:(i8 + 1) * 64], lhsT=ckvn[:, kc, kb * 128:(kb + 1) * 128],
                                        rhs=wt[:, kc * 128 + 64:kc * 128 + 128], start=(kc == 0), stop=(kc == 1)),
                                        reads=[wB, ckvnB[kb // 4]], writes=[paB], inc=(kc == 1 and i8 == 7))
                            yield
                            sy.op("act", lambda e: e.activation(
                                out=vt[:, g8 * 8:(g8 + 1) * 8, vo:vo + 64],
                                in_=pa[:, :].rearrange("p (a b) -> p a b", b=64), func=AF.Identity),
                                reads=[paB], writes=[vB])
                            yield
                        release(wi)

                    def attn_head(h):
                        par = h % 2
                        hp = h // 2
                        qt, qB = QT[par], QTB[par]
                        kt, kB = KT[par], KTB[par]
                        vt, vB = V[par], VB[par]
                        if h == 0:
                            dump("qt", qt[:, 0:512], [qB])
                            dump("kt", kt[:, 0:512], [kB])
                            dump("v0", vt[:, 0, :], [vB])
                        items = []
                        for g in range(NB):
                            for kb in range(4 * g + 4):
                                items.append((g, kb))
                        LA = 3
                        inflight = {}
                        for i in range(len(items) + LA):
                            if i < len(items):
                                g, kb = items[i]
                                d = kb - 4 * g
                                c0 = max(0, d) * 128
                                ncols = 512 - c0
                                sp_, spB = nextps()
                                sy.op("pe", lambda e, kb=kb, g=g, c0=c0, ncols=ncols, sp_=sp_: e.matmul(
                                    sp_[:, 0:ncols], lhsT=kt[0:96, kb * 128:(kb + 1) * 128],
                                    rhs=qt[0:96, g * 512 + c0:(g + 1) * 512], start=True, stop=(d < 0)),
                                    reads=[kB, qB], writes=[spB], inc=(d < 0))
                                if d >= 0:
                                    sy.op("pe", lambda e, sp_=sp_: e.matmul(sp_[:, 0:128], lhsT=ident_b[:], rhs=maskT[:],
                                                                            start=False, stop=True),
                                          reads=[constB], writes=[spB])
                                inflight[i] = (sp_, spB, c0, ncols)
                            if i >= LA:
                                ii = i - LA
                                g, kb = items[ii]
                                sp_, spB, c0, ncols = inflight.pop(ii)
                                pi = pt_i["i"] % NPT
                                pt_i["i"] += 1
                                ptile, ptB = PT[pi], PTB[pi]
                                sy.op("act", lambda e, sp_=sp_, ncols=ncols, ptile=ptile: e.activation(
                                    out=ptile[:, 0:ncols], in_=sp_[:, 0:ncols], func=AF.Exp, scale=ATT_SCALE),
                                    reads=[spB], writes=[ptB])
                                op_, opB = psum[g % 2], psB[g % 2]
                                nkb = 4 * g + 4
                                sy.op("pe", lambda e, kb=kb, c0=c0, ncols=ncols, ptile=ptile, op_=op_, nkb=nkb: e.matmul(
                                    op_[:, c0:512], lhsT=vt[:, kb, :], rhs=ptile[:, 0:ncols],
                                    start=(kb == 0), stop=(kb == nkb - 1)),
                                    reads=[vB, ptB], writes=[opB], inc=True)
                                if kb == nkb - 1:
                                    if par == 0:
                                        num, den = slice(0, 64), slice(64, 128)
                                    else:
                                        num, den = slice(64, 128), slice(0, 64)
                                    tk = slice(g * 512, (g + 1) * 512)
                                    sy.op("dve", lambda e, op_=op_: e.reciprocal(out=rec[num, :], in_=op_[den, :]),
                                          reads=[opB], writes=[recB])
                                    o1, o1B = tp()
                                    sy.op("dve", lambda e, op_=op_, o1=o1: e.tensor_tensor(out=o1[num, :], in0=op_[num, :],
                                                                                         in1=rec[num, :], op=ALU.mult),
                                          reads=[opB, recB], writes=[o1B])
                                    sy.op("dve", lambda e, o1=o1: e.tensor_tensor(out=Zs[num, hp, tk], in0=o1[num, :],
                                                                                in1=Zs[num, hp, tk], op=ALU.mult),
                                          reads=[o1B], writes=[ZsB[hp][g]])
                            yield

                    for _ in gen_head(0):
                        pass
                    for h in range(16):
                        gens = [attn_head(h)]
                        if h + 1 < 16:
                            gens.append(gen_head(h + 1))
                        while gens:
                            for g_ in list(gens):
                                try:
                                    next(g_)
                                except StopIteration:
                                    gens.remove(g_)
                    sy.fence()

                dump("og", Zs[:, 0, 0:512], [ZsB[0][0]])
                with ExitStack() as p3:
                    big = p3.enter_context(nc.sbuf_tensor(f"o_big_{l}_{s}", [128, KC, 512], F32))
                    bigB = [Buf(f"o_big{c}") for c in range(KC)]
                    sqt = p3.enter_context(nc.sbuf_tensor(f"o_sq3_{l}_{s}", [128, KC, 512], BF16))
                    sqB = Buf("o_sq3")
                    wl = [acquire(("owout", j, n)) for n in range(8)]
                    for b in range(NB):
                        tk = slice(b * 512, (b + 1) * 512)
                        for n in range(8):
                            wt, wB, _ = wl[n]
                            pt, pB = mm8(wt, wB, lambda kc: Zs[:, kc, tk], [ZsB[c][b] for c in range(KC)])
                            sy.op("act", lambda e: e.activation(out=big[:, n, :], in_=pt[:], func=AF.Identity),
                                  reads=[pB], writes=[bigB[n]])
                        postnorm(l, s, b * 512, big, bigB, sqt, sqB, tp)
                    for _w in wl:
                        release(_w[2])
                    sy.fence()
                tp_stats[0] = None

        allxs = [xsB[c][b] for c in range(KC) for b in range(NB)]
        outB = Buf("outst")
        for s in range(NSEQ):
            for c in range(KC):
                sy.dma("sp", xs[:, c, :], x_d.ap()[s, :, c, :], writes=xsB[c])
            for l in LAYERS:
                if l % 2 == 0:
                    even_layer(l, s)
                else:
                    odd_layer(l, s)
            for c in range(KC):
                sy.dma("sp", out_d.ap()[s, :, c, :], xs[:, c, :], reads=xsB[c], writes=[outB])
        nc.sync.wait_ge(outB.dsem, outB.dcnt)
        if DEBUG and dbgB.dsem is not None:
            nc.sync.wait_ge(dbgB.dsem, dbgB.dcnt)
        if RECORD:
            return ws["order"]
        assert ws["acq"] == len(ws["order"]) == ws["issued"], (ws["acq"], len(ws["order"]), ws["issued"])
        build.nins = sy.nins
    return nc


_CACHE = {}


def kernel(**inp):
    inp = {k: np.asarray(v) for k, v in inp.items()}
    x = inp["x"].astype(np.float32, copy=False)
    B, S, Dm = x.shape
    nseq = B // NCORES
    wts = pack_weights(inp)
    pvv = pack_pv(inp)
    key = (S, nseq)
    if key not in _CACHE:
        _CACHE[key] = build(S=S, NSEQ=nseq)
    nc = _CACHE[key]
    in_maps = []
    for cid in range(NCORES):
        bs = slice(cid * nseq, (cid + 1) * nseq)
        xf = np.ascontiguousarray(x[bs].reshape(nseq, S, KC, 128).transpose(0, 3, 2, 1))
        cf = np.ascontiguousarray(inp["c"][bs].astype(np.float32).reshape(nseq, KC, 128).transpose(2, 1, 0))
        pos = np.ascontiguousarray(inp["positions"][bs].astype(np.int32))
        in_maps.append({"x": xf, "c": cf, "pos": pos, "wts": wts, "pv": pvv})
    res = run_bass_kernel_spmd(nc, in_maps, core_ids=list(range(NCORES)))
    outs = []
    for cid in range(NCORES):
        o = np.asarray(res.results[cid]["out"]).reshape(nseq, 128, KC, S)
        outs.append(o.transpose(0, 3, 2, 1).reshape(nseq, S, Dm))
    return np.ascontiguousarray(np.concatenate(outs, axis=0).astype(np.float32))
```

```python
import math
from contextlib import ExitStack
import numpy as np
import concourse.bass as bass
import concourse.mybir as mybir
from concourse.ap import AP
from concourse.bass_utils import run_bass_kernel_spmd

F32 = mybir.dt.float32
BF16 = mybir.dt.bfloat16
I32 = mybir.dt.int32
AF = mybir.ActivationFunctionType
ALU = mybir.AluOpType

D = 1024
KC = 8
NCORES = 8
EPS = 1e-6
SLOT = 1024
RING = 10
SAME_ENG_WINDOW = 3
EVEN_STAGGER = 9
ATT_SCALE = 96.0 ** -0.5
TWO_PI = 2.0 * math.pi
PI_HI = 6.28125
PI_LO = TWO_PI - 6.28125


def weight_plan():
    plan = {}
    off = 0

    def add(name, F):
        nonlocal off
        plan[name] = (off, F)
        off += 128 * F

    for l in range(4):
        for n in range(24):
            add(("ada", l, n), 1024)
    for j in range(2):
        for c in range(40):
            add(("ewin", j, c), 1024)
        for kc in range(8):
            add(("egate", j, kc), 256)
        for n in range(8):
            add(("ewout", j, n, 0), 1024)
            add(("ewout", j, n, 1), 1024)
    for j in range(2):
        for c in range(14):
            add(("owin", j, c), 1024)
        for h in range(16):
            add(("ouq", j, h), 384)
            add(("oukv", j, h), 256)
        for n in range(8):
            add(("owout", j, n), 1024)
    return plan, off


def pv_plan():
    cols = {}
    off = 0

    def add(name, n):
        nonlocal off
        cols[name] = off
        off += n

    for l in range(4):
        add(("pre_g", l), 8)
        add(("post_g", l), 8)
        add(("ada_b", l), 24)
    for j in range(2):
        add(("conv_w", j), 8 * 31)
        add(("conv_b", j), 8)
        add(("ln_g", j), 8)
        add(("ln_b", j), 8)
        add(("lconv_w", j), 8 * 4)
        add(("lconv_b", j), 8)
        add(("ba", j), 8)
        add(("bx", j), 8)
        add(("lam", j), 8)
        add(("q_norm", j), 2)
        add(("kv_norm", j), 2)
    add("inv", 1)
    add("sgn", 1)
    return cols, off


def chunkify(W):
    K, N = W.shape
    return np.ascontiguousarray(W.reshape(K // 128, 128, N // 128, 128).transpose(2, 1, 0, 3))


def vec_pc(v):
    return np.ascontiguousarray(v.reshape(-1, 128).T)


def pack_weights(inp):
    plan, total = weight_plan()
    flat = np.zeros(total, np.float32)

    def put(name, arr):
        off, F = plan[name]
        a = np.ascontiguousarray(arr, dtype=np.float32).reshape(128, F)
        flat[off:off + 128 * F] = a.reshape(-1)

    for l in range(4):
        ch = chunkify(inp["ada_w"][l])
        for n in range(24):
            put(("ada", l, n), ch[n])
    for j in range(2):
        ch = chunkify(inp["ev_w_in"][j])
        for c in range(40):
            put(("ewin", j, c), ch[c])
        wa, wx = inp["ev_lru_wa"][j], inp["ev_lru_wx"][j]
        for kc in range(8):
            g = np.zeros((128, 2, 128), np.float32)
            for hh in range(2):
                g[hh * 64:(hh + 1) * 64, 0, hh * 64:(hh + 1) * 64] = wa[2 * kc + hh]
                g[hh * 64:(hh + 1) * 64, 1, hh * 64:(hh + 1) * 64] = wx[2 * kc + hh]
            put(("egate", j, kc), g)
        wo = inp["ev_w_out"][j]
        c0 = chunkify(wo[:1024])
        c1 = chunkify(wo[1024:])
        for n in range(8):
            put(("ewout", j, n, 0), c0[n])
            put(("ewout", j, n, 1), c1[n])
    for j in range(2):
        w = inp["od_w_in"][j]
        z64 = np.zeros((1024, 64), np.float32)
        z32 = np.zeros((1024, 32), np.float32)
        kr = w[:, 512:544]
        kr1 = np.concatenate([z64, kr, z32], axis=1)
        kr2 = np.concatenate([z64, kr[:, 16:32], kr[:, 0:16], z32], axis=1)
        wcat = np.concatenate([w[:, 0:512], kr1, kr2, w[:, 544:1568]], axis=1)
        ch = chunkify(wcat)
        for c in range(14):
            put(("owin", j, c), ch[c])
        uq = inp["od_w_uq"][j].reshape(2, 128, 16, 96)
        ukv = inp["od_w_ukv"][j].reshape(2, 128, 16, 128)
        for h in range(16):
            a = uq[:, :, h, :]
            sw = np.concatenate([a[:, :, 0:64], a[:, :, 80:96], a[:, :, 64:80]], axis=2)
            both = np.concatenate([a, sw], axis=2)
            put(("ouq", j, h), both.transpose(1, 0, 2))
            put(("oukv", j, h), ukv[:, :, h, :].transpose(1, 0, 2))
        ch = chunkify(inp["od_w_out"][j])
        for n in range(8):
            put(("owout", j, n), ch[n])
    return flat


def pack_pv(inp):
    cols, n = pv_plan()
    pv = np.zeros((128, n), np.float32)

    def put(name, arr):
        a = np.asarray(arr, np.float32)
        pv[:, cols[name]:cols[name] + a.shape[1]] = a

    for l in range(4):
        put(("pre_g", l), vec_pc(inp["pre_g"][l]))
        put(("post_g", l), vec_pc(inp["post_g"][l]))
        put(("ada_b", l), vec_pc(inp["ada_b"][l]))
    for j in range(2):
        cw = inp["ev_conv_w"][j]
        put(("conv_w", j), cw.reshape(31, 8, 128).transpose(2, 1, 0).reshape(128, 8 * 31))
        put(("conv_b", j), vec_pc(inp["ev_conv_b"][j]))
        put(("ln_g", j), vec_pc(inp["ev_ln_g"][j]))
        put(("ln_b", j), vec_pc(inp["ev_ln_b"][j]))
        lw = inp["ev_lru_conv_w"][j]
        put(("lconv_w", j), lw.reshape(4, 8, 128).transpose(2, 1, 0).reshape(128, 8 * 4))
        put(("lconv_b", j), vec_pc(inp["ev_lru_conv_b"][j]))
        put(("ba", j), vec_pc(inp["ev_lru_ba"][j]))
        put(("bx", j), vec_pc(inp["ev_lru_bx"][j]))
        put(("lam", j), vec_pc(inp["ev_lru_lam"][j]))
        put(("q_norm", j), vec_pc(inp["od_q_norm"][j]))
        put(("kv_norm", j), vec_pc(inp["od_kv_norm"][j]))
    inv = (10000.0 ** (-np.arange(0, 32, 2, dtype=np.float32) / 32.0)).astype(np.float32)
    iv = np.zeros((128, 1), np.float32)
    sg = np.ones((128, 1), np.float32)
    for i in range(32):
        iv[64 + i, 0] = inv[i % 16]
        sg[64 + i, 0] = -1.0 if i < 16 else 1.0
    put("inv", iv)
    put("sgn", sg)
    return pv


_UID = [0]


class Buf:
    __slots__ = ("name", "w", "r", "dsem", "dcnt")

    def __init__(self, name):
        _UID[0] += 1
        self.name = f"{name}_{_UID[0]}"
        self.w = None
        self.r = {}
        self.dsem = None
        self.dcnt = 0


class Sync:
    ENG = ("pe", "act", "dve", "pool", "sp")

    def __init__(self, nc, es):
        self.nc = nc
        self.es = es
        self.eng = {"pe": nc.tensor, "act": nc.scalar, "dve": nc.vector, "pool": nc.gpsimd, "sp": nc.sync}
        self.sem = {k: es.enter_context(nc.semaphore("s_" + k)) for k in self.ENG}
        self.cnt = {k: 0 for k in self.ENG}
        self.known = {k: {} for k in self.ENG}
        self.nins = 0

    def _need(self, E, dep, strict):
        key, sem, val, src = dep
        if src == E and not strict and E != "pool":
            if E == "pe" or (self.cnt[E] - val) >= SAME_ENG_WINDOW:
                return
        if self.known[E].get(key, 0) >= val:
            return
        self.eng[E].wait_ge(sem, val)
        self.known[E][key] = val
        self.nins += 1

    def _deps(self, E, reads, writes, sreads, strict_all=False):
        for b in reads:
            if b.w is not None:
                self._need(E, b.w, strict_all)
        for b in sreads:
            if b.w is not None:
                self._need(E, b.w, True)
        for b in writes:
            if b.w is not None:
                self._need(E, b.w, strict_all)
            for d in b.r.values():
                self._need(E, d, strict_all)

    def op(self, E, fn, reads=(), writes=(), sreads=(), inc=True):
        self._deps(E, reads, writes, sreads)
        ins = fn(self.eng[E])
        self.nins += 1
        if inc:
            self.cnt[E] += 1
            ins.then_inc(self.sem[E], 1)
            me = (E, self.sem[E], self.cnt[E], E)
        else:
            me = (E, self.sem[E], self.cnt[E] + 1, E)
        for b in writes:
            b.w = me
            b.r = {}
        for b in reads:
            b.r[E] = me
        for b in sreads:
            b.r[E] = me
        return ins

    def dma(self, Q, out, in_, reads=(), writes=()):
        self._deps(Q, reads, writes, (), strict_all=True)
        tgt = writes[0] if writes else reads[0]
        if tgt.dsem is None:
            tgt.dsem = self.es.enter_context(self.nc.semaphore("d_" + tgt.name))
        tgt.dcnt += 16
        self.eng[Q].dma_start(out=out, in_=in_).then_inc(tgt.dsem, 16)
        self.nins += 1
        me = ("d_" + tgt.name, tgt.dsem, tgt.dcnt, None)
        for b in writes:
            b.w = me
            b.r = {}
        for b in reads:
            b.r["dma_" + tgt.name] = me
        return me

    def fence(self):
        for E in ("pe", "act", "dve", "pool", "sp"):
            for Fg in ("pe", "act", "dve", "pool"):
                if Fg != E and self.cnt[Fg] > 0:
                    self._need(E, (Fg, self.sem[Fg], self.cnt[Fg], Fg), True)


def build(S=2048, NSEQ=2, LAYERS=(0, 1, 2, 3), DEBUG=False):
    order = _build(S, NSEQ, LAYERS, DEBUG, None)
    return _build(S, NSEQ, LAYERS, DEBUG, order)


def _build(S, NSEQ, LAYERS, DEBUG, ORDER):
    RECORD = ORDER is None
    nc = bass.Bass("TRN2", target_bir_lowering=False)
    plan, wtotal = weight_plan()
    pcols, npv = pv_plan()
    NB = S // 512

    x_d = nc.dram_tensor("x", [NSEQ, 128, KC, S], F32, kind="ExternalInput")
    c_d = nc.dram_tensor("c", [128, KC, NSEQ], F32, kind="ExternalInput")
    pos_d = nc.dram_tensor("pos", [NSEQ, S], I32, kind="ExternalInput")
    w_d = nc.dram_tensor("wts", [wtotal], F32, kind="ExternalInput")
    pv_d = nc.dram_tensor("pv", [128, npv], F32, kind="ExternalInput")
    out_d = nc.dram_tensor("out", [NSEQ, 128, KC, S], F32, kind="ExternalOutput")
    dgd = nc.dram_tensor("dgd", [2, 8, 128, 31 * 128], BF16, kind="Internal")

    dbg_d = nc.dram_tensor("dbg", [128, 16384], F32, kind="ExternalOutput") if DEBUG else None
    dbg_cols = {}
    build.dbg_cols = dbg_cols
    dbg_state = {"c": 0}

    with ExitStack() as es:
        sy = Sync(nc, es)
        dbgB = Buf("dbg")

        def dump(name, ap, bufs):
            if not DEBUG or name in dbg_cols:
                return
            n = ap.shape[-1]
            p0 = 0
            c0 = dbg_state["c"]
            dbg_cols[name] = (c0, n)
            dbg_state["c"] += n
            sy.dma("pool", dbg_d.ap()[0:ap.shape[0], c0:c0 + n], ap, reads=bufs, writes=[dbgB])

        def sb(name, shape, dt):
            return es.enter_context(nc.sbuf_tensor(name, shape, dt))

        xs = sb("xs", [128, KC, S], F32)
        xsB = [[Buf(f"xs{c}_{b}") for b in range(NB)] for c in range(KC)]
        pvt = sb("pvt", [128, npv], F32)
        pvB = Buf("pv")
        ring = sb("ring", [128, RING, SLOT], BF16)
        ringB = [Buf(f"ring{i}") for i in range(RING)]
        ident_f = sb("ident_f", [128, 128], F32)
        ident_b = sb("ident_b", [128, 128], BF16)
        od1024 = sb("od1024", [128, 128], BF16)
        od256 = sb("od256", [128, 128], BF16)
        maskT = sb("maskT", [128, 128], BF16)
        constB = Buf("const")
        cin = sb("cin", [128, KC, NSEQ], F32)
        cact = sb("cact", [128, KC, NSEQ], BF16)
        cB = Buf("c")
        modt = sb("modt", [128, 96, NSEQ], F32)
        modB = Buf("mod")
        drvA = sb("drvA", [128, 4, NSEQ, 8], F32)
        drvG = sb("drvG", [128, 4, NSEQ, 8], F32)
        drvB = Buf("drv")
        nsp = sb("nsp", [128, 2, 8], F32)
        nspB = Buf("nsp")
        carry = sb("carry", [128, 8], F32)
        carryB = [Buf(f"carry{j}") for j in range(8)]
        xbh = sb("xbh", [128, 8, 4], BF16)
        xbhB = [Buf(f"xbh{j}") for j in range(8)]

        psum = [es.enter_context(nc.psum_tensor(f"ps{i}", [128, 512], F32)) for i in range(8)]
        psB = [Buf(f"ps{i}") for i in range(8)]
        ps_state = {"i": 0}

        def nextps():
            i = 2 + ps_state["i"] % 6
            ps_state["i"] += 1
            return psum[i], psB[i]

        def pv(name, a, b=None):
            c0 = pcols[name]
            if b is None:
                return pvt[:, c0 + a:c0 + a + 1]
            return pvt[:, c0 + a:c0 + b]

        ws = {"order": [] if RECORD else ORDER, "issued": 0, "acq": 0, "rel": 0, "done": set()}

        def w_ap(name):
            off, Fw = plan[name]
            return AP(w_d, off, [[Fw, 128], [1, Fw]]), Fw

        def ws_issue():
            if RECORD:
                return
            while ws["issued"] < len(ws["order"]) and ws["issued"] < ws["rel"] + RING:
                i = ws["issued"]
                src, Fw = w_ap(ws["order"][i])
                slot = i % RING
                sy.dma("pool", ring[:, slot, 0:Fw], src, writes=[ringB[slot]])
                ws["issued"] += 1

        def acquire(name):
            i = ws["acq"]
            ws["acq"] += 1
            if RECORD:
                ws["order"].append(name)
                return ring[:, 0, :], ringB[0], i
            assert ws["order"][i] == name, (ws["order"][i], name)
            assert ws["issued"] > i, "weight ring deadlock (acquire beyond issued)"
            slot = i % RING
            return ring[:, slot, :], ringB[slot], i

        def release(i):
            ws["done"].add(i)
            while ws["rel"] in ws["done"]:
                ws["done"].remove(ws["rel"])
                ws["rel"] += 1
            ws_issue()

        sy.dma("sp", pvt[:], pv_d.ap()[:, :], writes=[pvB])
        sy.dma("sp", cin[:], c_d.ap()[:, :, :], writes=[cB])
        ws_issue()
        sy.op("pool", lambda e: e.memset(ident_f[:], 1.0), writes=[constB])
        sy.op("pool", lambda e: e.affine_select(out=ident_f[:], in_=ident_f[:], pattern=[[-1, 128]], base=0,
                                                channel_multiplier=1, compare_op=ALU.is_equal, fill=0.0),
              writes=[constB])
        sy.op("pool", lambda e: e.tensor_copy(out=ident_b[:], in_=ident_f[:]), writes=[constB])
        sy.op("pool", lambda e: e.memset(od1024[:], 1.0 / 1024), writes=[constB])
        sy.op("pool", lambda e: e.memset(od256[:], 1.0 / 256), writes=[constB])
        sy.op("pool", lambda e: e.memset(maskT[:], 0.0), writes=[constB])
        sy.op("pool", lambda e: e.affine_select(out=maskT[:], in_=maskT[:], pattern=[[1, 128]], base=0,
                                                channel_multiplier=-1, compare_op=ALU.is_ge, fill=-30000.0),
              writes=[constB])
        sy.op("act", lambda e: e.activation(out=cact[:], in_=cin[:], func=AF.Silu), reads=[cB], writes=[cB])

        dgdB = [[Buf(f"dgd{j}_{jj}") for jj in range(8)] for j in range(2)]
        with ExitStack() as st0:
            dgs = [st0.enter_context(nc.sbuf_tensor(f"dgs{i}", [128, 31, 128], BF16)) for i in range(2)]
            dgsB = [Buf(f"dgs{i}") for i in range(2)]
            for j in range(2):
                if (2 * j) not in LAYERS:
                    continue
                for jj in range(8):
                    i = jj % 2
                    cw0 = pcols[("conv_w", j)] + jj * 31
                    sy.op("pool", lambda e: e.tensor_tensor(
                        out=dgs[i][:], in0=ident_b[:].unsqueeze(1).to_broadcast([128, 31, 128]),
                        in1=pvt[:, cw0:cw0 + 31].unsqueeze(2).to_broadcast([128, 31, 128]), op=ALU.mult),
                        reads=[constB, pvB], writes=[dgsB[i]])
                    sy.dma("sp", dgd.ap()[j, jj, :, :], dgs[i][:].rearrange("p a b -> p (a b)"),
                           reads=[dgsB[i]], writes=[dgdB[j][jj]])
            sy.fence()
            for j in range(2):
                for jj in range(8):
                    if dgdB[j][jj].w is not None:
                        for E_ in ("pe", "act", "dve", "pool", "sp"):
                            sy._need(E_, dgdB[j][jj].w, True)

        for l in range(4):
            for n in range(24):
                wt, wB, wi = acquire(("ada", l, n))
                pt, pB = nextps()
                for kc in range(KC):
                    sy.op("pe", lambda e, kc=kc: e.matmul(pt[:, 0:NSEQ], lhsT=wt[:, kc * 128:(kc + 1) * 128],
                                                          rhs=cact[:, kc, :], start=(kc == 0), stop=(kc == KC - 1)),
                          reads=[wB, cB], writes=[pB], inc=(kc == KC - 1))
                release(wi)
                sy.op("act", lambda e: e.activation(out=modt[:, l * 24 + n, :], in_=pt[:, 0:NSEQ], func=AF.Identity,
                                                    bias=pv(("ada_b", l), n), scale=1.0),
                      reads=[pB], sreads=[pvB], writes=[modB])
        for l in range(4):
            for s in range(NSEQ):
                sy.op("dve", lambda e: e.tensor_scalar(out=drvA[:, l, s, :], in0=modt[:, l * 24 + 8:l * 24 + 16, s],
                                                       scalar1=1.0, scalar2=None, op0=ALU.add),
                      reads=[modB], writes=[drvB])
                sy.op("dve", lambda e: e.tensor_tensor(out=drvA[:, l, s, :], in0=drvA[:, l, s, :],
                                                       in1=pv(("pre_g", l), 0, 8), op=ALU.mult),
                      reads=[pvB], writes=[drvB])
                sy.op("dve", lambda e: e.tensor_tensor(out=drvG[:, l, s, :], in0=modt[:, l * 24 + 16:l * 24 + 24, s],
                                                       in1=pv(("post_g", l), 0, 8), op=ALU.mult),
                      reads=[pvB, modB], writes=[drvB])
        spt = [sb(f"spt{i}", [128, 8], F32) for i in range(4)]
        for j in range(2):
            lam_ap = pv(("lam", j), 0, 8)
            al, ee, ww, w2 = spt
            sy.op("act", lambda e: e.activation(out=al[:], in_=lam_ap, func=AF.Abs),
                  reads=[pvB], writes=[nspB])
            sy.op("act", lambda e: e.activation(out=ee[:], in_=al[:], func=AF.Exp, scale=-1.0),
                  reads=[nspB], writes=[nspB])
            sy.op("dve", lambda e: e.tensor_scalar(out=ww[:], in0=ee[:], scalar1=2.0, scalar2=None, op0=ALU.add),
                  reads=[nspB], writes=[nspB])
            sy.op("dve", lambda e: e.reciprocal(out=ww[:], in_=ww[:]), writes=[nspB])
            sy.op("dve", lambda e: e.tensor_tensor(out=ww[:], in0=ww[:], in1=ee[:], op=ALU.mult), writes=[nspB])
            sy.op("dve", lambda e: e.tensor_tensor(out=w2[:], in0=ww[:], in1=ww[:], op=ALU.mult), writes=[nspB])
            sy.op("dve", lambda e: e.tensor_scalar(out=al[:], in0=w2[:], scalar1=1.0 / 11, scalar2=1.0 / 9, op0=ALU.mult,
                                                   op1=ALU.add), writes=[nspB])
            for cf in (1.0 / 7, 1.0 / 5, 1.0 / 3, 1.0):
                sy.op("dve", lambda e: e.tensor_tensor(out=al[:], in0=al[:], in1=w2[:], op=ALU.mult), writes=[nspB])
                sy.op("dve", lambda e, cf=cf: e.tensor_scalar(out=al[:], in0=al[:], scalar1=cf, scalar2=None, op0=ALU.add),
                      writes=[nspB])
            sy.op("dve", lambda e: e.tensor_tensor(out=al[:], in0=al[:], in1=ww[:], op=ALU.mult), writes=[nspB])
            sy.op("dve", lambda e: e.tensor_scalar(out=ee[:], in0=lam_ap, scalar1=-1.0, scalar2=0.0, op0=ALU.mult,
                                                   op1=ALU.max), reads=[pvB], writes=[nspB])
            sy.op("dve", lambda e: e.scalar_tensor_tensor(out=al[:], in0=al[:], scalar=2.0, in1=ee[:], op0=ALU.mult,
                                                          op1=ALU.add), writes=[nspB])
            sy.op("dve", lambda e: e.tensor_scalar(out=nsp[:, j, :], in0=al[:], scalar1=-8.0, scalar2=None,
                                                   op0=ALU.mult), writes=[nspB])
        dump("modt", modt[:].rearrange("p a b -> p (a b)"), [modB])
        dump("drvA", drvA[:].rearrange("p a b c -> p (a b c)"), [drvB])
        dump("drvG", drvG[:].rearrange("p a b c -> p (a b c)"), [drvB])
        dump("nsp", nsp[:].rearrange("p a b -> p (a b)"), [nspB])
        tp_stats = [None]
        def mm8(wt, wB, rhs_fn, rB, M=128, wcol0=0, prow=None):
            pt, pB = nextps()
            for kc in range(KC):
                sy.op("pe", lambda e, kc=kc: e.matmul(pt[0:M, :], lhsT=wt[:, kc * 128 + wcol0:kc * 128 + wcol0 + M],
                                                      rhs=rhs_fn(kc), start=(kc == 0), stop=(kc == KC - 1)),
                      reads=[wB] + rB, writes=[pB], inc=(kc == KC - 1))
            return pt, pB

        def rms_stats(src_fn, srcB, nch, onesm, sqt, sqB, tp):
            tp = tp_stats[0] or tp
            sy.op("act", lambda e: e.activation(out=sqt[:, 0:nch, :], in_=src_fn(), func=AF.Square),
                  reads=srcB, writes=[sqB])
            pt, pB = nextps()
            for kc in range(nch):
                sy.op("pe", lambda e, kc=kc: e.matmul(pt[:, :], lhsT=onesm[:], rhs=sqt[:, kc, :], start=(kc == 0),
                                                      stop=(kc == nch - 1)),
                      reads=[constB, sqB], writes=[pB], inc=(kc == nch - 1))
            sd, sdB = tp()
            sy.op("act", lambda e: e.activation(out=sd[:], in_=pt[:], func=AF.Ln, bias=EPS, scale=1.0),
                  reads=[pB], writes=[sdB])
            rs, rsB = tp()
            sy.op("act", lambda e: e.activation(out=rs[:], in_=sd[:], func=AF.Exp, scale=-0.5), reads=[sdB], writes=[rsB])
            return rs, rsB

        def prenorm(l, s, t0, hn_fn, hnB, sqt, sqB, tp, tpt=None):
            tpt = tpt or tp
            b = t0 // 512
            rs, rsB = rms_stats(lambda: xs[:, :, t0:t0 + 512], [xsB[c][b] for c in range(KC)], KC, od1024, sqt, sqB, tp)
            for kc in range(KC):
                tt, ttB = tpt()
                sy.op("dve", lambda e: e.tensor_tensor(out=tt[:], in0=xs[:, kc, t0:t0 + 512], in1=rs[:], op=ALU.mult),
                      reads=[xsB[kc][b], rsB], writes=[ttB])
                sy.op("act", lambda e: e.activation(out=hn_fn(kc), in_=tt[:], func=AF.Identity,
                                                    scale=drvA[:, l, s, kc:kc + 1], bias=modt[:, l * 24 + kc, s:s + 1]),
                      reads=[ttB], sreads=[drvB, modB], writes=[hnB[kc]])

        def postnorm(l, s, t0, y, yB, sqt, sqB, tp, tpt=None):
            tpt = tpt or tp
            b = t0 // 512
            rs, rsB = rms_stats(lambda: y[:, :, :], yB, KC, od1024, sqt, sqB, tp)
            for n in range(KC):
                tt, ttB = tpt()
                sy.op("dve", lambda e: e.tensor_tensor(out=tt[:], in0=y[:, n, :], in1=rs[:], op=ALU.mult),
                      reads=[yB[n], rsB], writes=[ttB])
                sy.op("dve", lambda e: e.scalar_tensor_tensor(out=xs[:, n, t0:t0 + 512], in0=tt[:],
                                                              scalar=drvG[:, l, s, n:n + 1], in1=xs[:, n, t0:t0 + 512],
                                                              op0=ALU.mult, op1=ALU.add),
                      reads=[ttB], sreads=[drvB], writes=[xsB[n][b]])

        def mk_tmp_pool(ess, name, n, dt=F32, w=512):
            _UID[0] += 1
            tiles = [ess.enter_context(nc.sbuf_tensor(f"{name}{i}_{_UID[0]}", [128, w], dt)) for i in range(n)]
            bufs = [Buf(f"{name}{i}") for i in range(n)]
            st = {"i": 0}

            def get():
                i = st["i"] % n
                st["i"] += 1
                return tiles[i], bufs[i]
            return get

        def even_layer(l, s):
            j = l // 2
            with ExitStack() as el:
                def sbl(name, shape, dt):
                    return el.enter_context(nc.sbuf_tensor(f"{name}_{l}_{s}", shape, dt))
                hn = sbl("e_hn", [128, KC, 512], BF16)
                hnB = [Buf(f"e_hn{c}") for c in range(KC)]
                G = sbl("e_G", [128, KC, 544], BF16)
                GB = [Buf(f"e_G{c}") for c in range(KC)]
                Z = sbl("e_Z", [128, KC, 512], BF16)
                ZB = [Buf(f"e_Z{c}") for c in range(KC)]
                big = sbl("e_big", [128, KC, 512], F32)
                bigB = [Buf(f"e_big{c}") for c in range(KC)]
                Lo = sbl("e_Lo", [128, KC, 512], BF16)
                LoB = [Buf(f"e_Lo{c}") for c in range(KC)]
                sqt = sbl("e_sq", [128, KC, 512], BF16)
                sqB = Buf("e_sq")
                dg = sbl("e_dg", [128, 31, 128], BF16)
                dgB = Buf("e_dg")
                d4 = [sbl(f"e_d4{i}", [128, 4, 128], BF16) for i in range(2)]
                d4B = [Buf(f"e_d4{i}") for i in range(2)]
                XB = [sbl(f"e_XB{i}", [128, 516], BF16) for i in range(2)]
                XBB = [Buf(f"e_XB{i}") for i in range(2)]
                tp = mk_tmp_pool(el, "e_tf", 5, F32)
                tpt = mk_tmp_pool(el, "e_tt", 3, F32)
                tpb = mk_tmp_pool(el, "e_tb", 2, BF16)
                lt = [[sbl(f"e_lt{p}{i}", [128, 512], F32) for i in range(5)] for p in range(2)]
                ltB = [[Buf(f"e_lt{p}{i}") for i in range(5)] for p in range(2)]
                sgt = [sbl(f"e_sgt{p}", [128, 512], BF16) for p in range(2)]
                sgtB = [Buf(f"e_sgt{p}") for p in range(2)]
                zbt = [sbl(f"e_zbt{p}", [128, 512], BF16) for p in range(2)]
                zbtB = [Buf(f"e_zbt{p}") for p in range(2)]
                xcbt = [sbl(f"e_xcbt{p}", [128, 512], BF16) for p in range(2)]
                xcbtB = [Buf(f"e_xcbt{p}") for p in range(2)]

                sy.op("dve", lambda e: e.memset(G[:, :, 0:32], 0.0), writes=GB)
                sy.op("dve", lambda e: e.memset(carry[:], 0.0), writes=carryB)
                sy.op("dve", lambda e: e.memset(xbh[:], 0.0), writes=xbhB)
                hrhs = lambda kc: hn[:, kc, :]

                def w_mm8(name):
                    wt, wB, wi = acquire(name)
                    pt, pB = mm8(wt, wB, hrhs, hnB)
                    release(wi)
                    return pt, pB

                def conv_stage(jj):
                    p = jj % 2
                    pt, pB = w_mm8(("ewin", j, 8 + jj))
                    yield
                    sy.op("act", lambda e: e.activation(out=sgt[p][:], in_=pt[:], func=AF.Sigmoid),
                          reads=[pB], writes=[sgtB[p]])
                    pt, pB = w_mm8(("ewin", j, jj))
                    yield
                    sy.op("dve", lambda e: e.tensor_tensor(out=G[:, jj, 32:544], in0=pt[:], in1=sgt[p][:], op=ALU.mult),
                          reads=[pB, sgtB[p]], writes=[GB[jj]])
                    pt, pB = w_mm8(("ewin", j, 16 + jj))
                    yield
                    sy.op("act", lambda e: e.activation(out=Z[:, jj, :], in_=pt[:], func=AF.Silu),
                          reads=[pB], writes=[ZB[jj]])
                    sy.dma("sp", dg[:].rearrange("p a b -> p (a b)"), dgd.ap()[j, jj, :, :],
                           reads=[dgdB[j][jj]], writes=[dgB])
                    yield
                    yield
                    pt, pB = nextps()
                    for k in range(31):
                        sy.op("pe", lambda e, k=k: e.matmul(pt[:, :], lhsT=dg[:, k, :], rhs=G[:, jj, k + 2:k + 514],
                                                            start=(k == 0), stop=(k == 30)),
                              reads=[dgB, GB[jj]], writes=[pB], inc=(k == 30))
                        if k % 8 == 7:
                            yield
                    sy.op("act", lambda e: e.activation(out=big[:, jj, :], in_=pt[:], func=AF.Identity,
                                                        bias=pv(("conv_b", j), jj), scale=1.0),
                          reads=[pB], sreads=[pvB], writes=[bigB[jj]])
                    if jj == 0:
                        dump("G0", G[:, 0, 32:544], [GB[0]])
                        dump("aconv0", big[:, 0, :], [bigB[0]])
                        dump("Zraw0", Z[:, 0, :], [ZB[0]])
                    yield
                    sy.op("dve", lambda e: e.tensor_copy(out=G[:, jj, 0:32], in_=G[:, jj, 512:544]),
                          reads=[GB[jj]], writes=[GB[jj]])

                def lru_stage(jj):
                    p = jj % 2
                    xbt, xbB = XB[p], XBB[p]
                    pt, pB = w_mm8(("ewin", j, 24 + jj))
                    yield
                    sy.op("act", lambda e: e.activation(out=xbt[:, 4:516], in_=pt[:], func=AF.Identity),
                          reads=[pB], writes=[xbB])
                    sy.op("dve", lambda e: e.tensor_copy(out=xbt[:, 0:4], in_=xbh[:, jj, :]),
                          reads=[xbhB[jj]], writes=[xbB])
                    pt, pB = w_mm8(("ewin", j, 32 + jj))
                    yield
                    sy.op("act", lambda e: e.activation(out=zbt[p][:], in_=pt[:], func=AF.Silu), reads=[pB],
                          writes=[zbtB[p]])
                    lw0 = pcols[("lconv_w", j)] + jj * 4
                    sy.op("dve", lambda e: e.tensor_tensor(
                        out=d4[p][:], in0=ident_b[:].unsqueeze(1).to_broadcast([128, 4, 128]),
                        in1=pvt[:, lw0:lw0 + 4].unsqueeze(2).to_broadcast([128, 4, 128]), op=ALU.mult),
                        reads=[constB, pvB], writes=[d4B[p]])
                    yield
                    pt, pB = nextps()
                    for k in range(4):
                        sy.op("pe", lambda e, k=k: e.matmul(pt[:, :], lhsT=d4[p][:, k, :], rhs=xbt[:, k + 1:k + 513],
                                                            start=(k == 0), stop=(k == 3)),
                              reads=[d4B[p], xbB], writes=[pB], inc=(k == 3))
                    yield
                    xc, xcB = lt[p][0], ltB[p][0]
                    rg, rgB = lt[p][1], ltB[p][1]
                    ig, igB = lt[p][2], ltB[p][2]
                    at, atB = lt[p][3], ltB[p][3]
                    hh_, hhB = lt[p][4], ltB[p][4]
                    sy.op("act", lambda e: e.activation(out=xc[:], in_=pt[:], func=AF.Identity,
                                                        bias=pv(("lconv_b", j), jj), scale=1.0),
                          reads=[pB], sreads=[pvB], writes=[xcB])
                    yield
                    sy.op("dve", lambda e: e.tensor_copy(out=xcbt[p][:], in_=xc[:]), reads=[xcB], writes=[xcbtB[p]])
                    sy.op("dve", lambda e: e.tensor_copy(out=xbh[:, jj, :], in_=xbt[:, 512:516]),
                          reads=[xbB], writes=[xbhB[jj]])
                    yield
                    wt, wB, wi = acquire(("egate", j, jj))
                    pr, prB = nextps()
                    sy.op("pe", lambda e: e.matmul(pr[:, :], lhsT=wt[:, 0:128], rhs=xcbt[p][:], start=True, stop=True),
                          reads=[wB, xcbtB[p]], writes=[prB])
                    pi_, piB = nextps()
                    sy.op("pe", lambda e: e.matmul(pi_[:, :], lhsT=wt[:, 128:256], rhs=xcbt[p][:], start=True, stop=True),
                          reads=[wB, xcbtB[p]], writes=[piB])
                    release(wi)
                    yield
                    sy.op("act", lambda e: e.activation(out=rg[:], in_=pr[:], func=AF.Sigmoid,
                                                        bias=pv(("ba", j), jj), scale=1.0),
                          reads=[prB], sreads=[pvB], writes=[rgB])
                    yield
                    sy.op("act", lambda e: e.activation(out=ig[:], in_=pi_[:], func=AF.Sigmoid,
                                                        bias=pv(("bx", j), jj), scale=1.0),
                          reads=[piB], sreads=[pvB], writes=[igB])
                    yield
                    sy.op("act", lambda e: e.activation(out=at[:], in_=rg[:], func=AF.Exp,
                                                        scale=nsp[:, j, jj:jj + 1]),
                          reads=[rgB], sreads=[nspB], writes=[atB])
                    sy.op("dve", lambda e: e.tensor_tensor(out=ig[:], in0=ig[:], in1=xc[:], op=ALU.mult),
                          reads=[xcB], writes=[igB])
                    yield
                    sy.op("dve", lambda e: e.tensor_tensor(out=rg[:], in0=at[:], in1=at[:], op=ALU.mult),
                          reads=[atB], writes=[rgB])
                    yield
                    sy.op("act", lambda e: e.activation(out=rg[:], in_=rg[:], func=AF.Sqrt, bias=1.0, scale=-1.0),
                          writes=[rgB])
                    yield
                    sy.op("dve", lambda e: e.tensor_tensor(out=ig[:], in0=ig[:], in1=rg[:], op=ALU.mult),
                          reads=[rgB], writes=[igB])
                    yield
                    sy.op("dve", lambda e: e.tensor_tensor_scan(out=hh_[:], data0=at[:], data1=ig[:],
                                                                initial=carry[:, jj:jj + 1], op0=ALU.mult,
                                                                op1=ALU.add),
                          reads=[atB, igB], sreads=[carryB[jj]], writes=[hhB])
                    yield
                    sy.op("act", lambda e: e.activation(out=carry[:, jj:jj + 1], in_=hh_[:, 511:512],
                                                        func=AF.Identity),
                          reads=[hhB], writes=[carryB[jj]])
                    sy.op("dve", lambda e: e.tensor_tensor(out=Lo[:, jj, :], in0=hh_[:], in1=zbt[p][:], op=ALU.mult),
                          reads=[hhB, zbtB[p]], writes=[LoB[jj]])

                for t in range(NB):
                    t0 = t * 512
                    prenorm(l, s, t0, lambda kc: hn[:, kc, :], hnB, sqt, sqB, tp, tpt)
                    dump("hn0", hn[:, 0, :], [hnB[0]])
                    active = []
                    for jj in range(8):
                        active += [conv_stage(jj), lru_stage(jj)]
                        steps = 0
                        while active and (jj == 7 or steps < EVEN_STAGGER):
                            for g_ in list(active):
                                try:
                                    next(g_)
                                except StopIteration:
                                    active.remove(g_)
                            steps += 1
                    sy.op("dve", lambda e: e.tensor_copy(out=sqt[:], in_=big[:]), reads=bigB, writes=[sqB])
                    pm, pmB = nextps()
                    for kc in range(KC):
                        sy.op("pe", lambda e, kc=kc: e.matmul(pm[:, :], lhsT=od1024[:], rhs=sqt[:, kc, :],
                                                              start=(kc == 0), stop=(kc == KC - 1)),
                              reads=[constB, sqB], writes=[pmB], inc=(kc == KC - 1))
                    sy.op("act", lambda e: e.activation(out=sqt[:], in_=big[:], func=AF.Square),
                          reads=bigB, writes=[sqB])
                    p2, p2B = nextps()
                    for kc in range(KC):
                        sy.op("pe", lambda e, kc=kc: e.matmul(p2[:, :], lhsT=od1024[:], rhs=sqt[:, kc, :],
                                                              start=(kc == 0), stop=(kc == KC - 1)),
                              reads=[constB, sqB], writes=[p2B], inc=(kc == KC - 1))
                    mean, meanB = tp()
                    sy.op("act", lambda e: e.activation(out=mean[:], in_=pm[:], func=AF.Identity), reads=[pmB],
                          writes=[meanB])
                    var, varB = tp()
                    sy.op("dve", lambda e: e.tensor_tensor(out=var[:], in0=mean[:], in1=mean[:], op=ALU.mult),
                          reads=[meanB], writes=[varB])
                    sy.op("dve", lambda e: e.tensor_tensor(out=var[:], in0=p2[:], in1=var[:], op=ALU.subtract),
                          reads=[p2B], writes=[varB])
                    sy.op("dve", lambda e: e.tensor_scalar(out=var[:], in0=var[:], scalar1=0.0, scalar2=None,
                                                           op0=ALU.max), writes=[varB])
                    sd, sdB = tp()
                    sy.op("act", lambda e: e.activation(out=sd[:], in_=var[:], func=AF.Ln, bias=EPS, scale=1.0),
                          reads=[varB], writes=[sdB])
                    rs, rsB = tp()
                    sy.op("act", lambda e: e.activation(out=rs[:], in_=sd[:], func=AF.Exp, scale=-0.5), reads=[sdB], writes=[rsB])
                    mr, mrB = tp()
                    sy.op("dve", lambda e: e.tensor_tensor(out=mr[:], in0=mean[:], in1=rs[:], op=ALU.mult),
                          reads=[meanB, rsB], writes=[mrB])
                    for jj in range(8):
                        t1, t1B = tpt()
                        sy.op("dve", lambda e: e.tensor_tensor(out=t1[:], in0=big[:, jj, :], in1=rs[:], op=ALU.mult),
                              reads=[bigB[jj], rsB], writes=[t1B])
                        sy.op("dve", lambda e: e.tensor_tensor(out=t1[:], in0=t1[:], in1=mr[:], op=ALU.subtract),
                              reads=[mrB], writes=[t1B])
                        s1, s1B = tpb()
                        sy.op("act", lambda e: e.activation(out=s1[:], in_=t1[:], func=AF.Silu,
                                                            scale=pv(("ln_g", j), jj), bias=pv(("ln_b", j), jj)),
                              reads=[t1B], sreads=[pvB], writes=[s1B])
                        sy.op("dve", lambda e: e.tensor_tensor(out=Z[:, jj, :], in0=s1[:], in1=Z[:, jj, :], op=ALU.mult),
                              reads=[s1B], writes=[ZB[jj]])
                    dump("Aout0", Z[:, 0, :], [ZB[0]])
                    dump("Lo0", Lo[:, 0, :], [LoB[0]])
                    for n in range(8):
                        w0, w0B, wi0 = acquire(("ewout", j, n, 0))
                        w1, w1B, wi1 = acquire(("ewout", j, n, 1))
                        pt, pB = nextps()
                        for kc in range(16):
                            wsrc, wsB = (w0, w0B) if kc < 8 else (w1, w1B)
                            src, srcB = (Z, ZB) if kc < 8 else (Lo, LoB)
                            k8 = kc % 8
                            sy.op("pe", lambda e, kc=kc, k8=k8, wsrc=wsrc, src=src: e.matmul(
                                pt[:, :], lhsT=wsrc[:, k8 * 128:(k8 + 1) * 128], rhs=src[:, k8, :],
                                start=(kc == 0), stop=(kc == 15)),
                                reads=[wsB, srcB[k8]], writes=[pB], inc=(kc == 15))
                        release(wi0)
                        release(wi1)
                        sy.op("act", lambda e: e.activation(out=big[:, n, :], in_=pt[:], func=AF.Identity),
                              reads=[pB], writes=[bigB[n]])
                    dump("y0", big[:, 0, :], [bigB[0]])
                    postnorm(l, s, t0, big, bigB, sqt, sqB, tp, tpt)
                sy.fence()

        def odd_layer(l, s):
            j = l // 2
            with ExitStack() as ol:
                def sbl(name, shape, dt):
                    return ol.enter_context(nc.sbuf_tensor(f"{name}_{l}_{s}", shape, dt))
                Zs = sbl("o_Zs", [128, KC, S], BF16)
                ZsB = [[Buf(f"o_Zs{c}_{b}") for b in range(NB)] for c in range(KC)]
                cqn = sbl("o_cqn", [128, 2, S], BF16)
                cqnB = [Buf(f"o_cqn{b}") for b in range(NB)]
                ckvn = sbl("o_ckvn", [128, 2, S], BF16)
                ckvnB = [Buf(f"o_ckvn{b}") for b in range(NB)]
                kr = sbl("o_kr", [128, S], BF16)
                krB = Buf("o_kr")
                COS = sbl("o_cos", [128, S], BF16)
                SIN = sbl("o_sin", [128, S], BF16)
                csB = Buf("o_cs")
                tp = mk_tmp_pool(ol, "o_tf", 4, F32)
                tp_stats[0] = mk_tmp_pool(ol, "o_ts", 3, F32)

                with ExitStack() as rl:
                    ang = rl.enter_context(nc.sbuf_tensor(f"o_ang_{l}_{s}", [128, S], F32))
                    wk = rl.enter_context(nc.sbuf_tensor(f"o_wk_{l}_{s}", [128, S], F32))
                    wk2 = rl.enter_context(nc.sbuf_tensor(f"o_wk2_{l}_{s}", [128, S], F32))
                    ki = rl.enter_context(nc.sbuf_tensor(f"o_ki_{l}_{s}", [128, S], I32))
                    posi = ki
                    rB = Buf("o_rope")
                    R = slice(64, 96)
                    src = AP(pos_d, s * S, [[0, 32], [1, S]])
                    sy.dma("sp", posi[R, :], src, writes=[rB])
                    sy.op("dve", lambda e: e.tensor_copy(out=ang[R, :], in_=posi[R, :]), reads=[rB], writes=[rB])
                    sy.op("dve", lambda e: e.tensor_scalar(out=ang[R, :], in0=ang[R, :], scalar1=pvt[R, pcols["inv"]:pcols["inv"] + 1],
                                                           scalar2=None, op0=ALU.mult), sreads=[pvB], writes=[rB])
                    for which in range(2):
                        if which == 0:
                            sy.op("dve", lambda e: e.tensor_scalar(out=wk2[R, :], in0=ang[R, :], scalar1=math.pi / 2,
                                                                   scalar2=None, op0=ALU.add), writes=[rB])
                            a_in = wk2
                        else:
                            a_in = ang
                        sy.op("dve", lambda e: e.tensor_scalar(out=wk[R, :], in0=a_in[R, :], scalar1=1.0 / TWO_PI,
                                                               scalar2=None, op0=ALU.mult), writes=[rB])
                        sy.op("dve", lambda e: e.tensor_copy(out=ki[R, :], in_=wk[R, :]), writes=[rB])
                        sy.op("dve", lambda e: e.tensor_copy(out=wk[R, :], in_=ki[R, :]), writes=[rB])
                        sy.op("dve", lambda e: e.scalar_tensor_tensor(out=wk2[R, :], in0=wk[R, :], scalar=-PI_HI,
                                                                      in1=a_in[R, :], op0=ALU.mult, op1=ALU.add),
                              writes=[rB])
                        sy.op("dve", lambda e: e.scalar_tensor_tensor(out=wk2[R, :], in0=wk[R, :], scalar=-PI_LO,
                                                                      in1=wk2[R, :], op0=ALU.mult, op1=ALU.add),
                              writes=[rB])
                        sy.op("dve", lambda e: e.tensor_scalar(out=wk[R, :], in0=wk2[R, :], scalar1=math.pi,
                                                               scalar2=-TWO_PI, op0=ALU.is_gt, op1=ALU.mult), writes=[rB])
                        sy.op("dve", lambda e: e.tensor_tensor(out=wk2[R, :], in0=wk2[R, :], in1=wk[R, :], op=ALU.add),
                              writes=[rB])
                        sy.op("dve", lambda e: e.tensor_scalar(out=wk[R, :], in0=wk2[R, :], scalar1=-math.pi,
                                                               scalar2=TWO_PI, op0=ALU.is_lt, op1=ALU.mult), writes=[rB])
                        sy.op("dve", lambda e: e.tensor_tensor(out=wk2[R, :], in0=wk2[R, :], in1=wk[R, :], op=ALU.add),
                              writes=[rB])
                        sy.op("dve", lambda e: e.tensor_scalar(out=wk2[R, :], in0=wk2[R, :], scalar1=3.1415925,
                                                               scalar2=-3.1415925, op0=ALU.min, op1=ALU.max), writes=[rB])
                        if which == 0:
                            sy.op("act", lambda e: e.activation(out=COS[R, :], in_=wk2[R, :], func=AF.Sin),
                                  reads=[rB], writes=[csB])
                        else:
                            sy.op("dve", lambda e: e.tensor_scalar(out=wk2[R, :], in0=wk2[R, :],
                                                                   scalar1=pvt[R, pcols["sgn"]:pcols["sgn"] + 1],
                                                                   scalar2=None, op0=ALU.mult), sreads=[pvB], writes=[rB])
                            sy.op("act", lambda e: e.activation(out=SIN[R, :], in_=wk2[R, :], func=AF.Sin),
                                  reads=[rB], writes=[csB])
                    sy.fence()

                with ExitStack() as p1:
                    hn = p1.enter_context(nc.sbuf_tensor(f"o_hn_{l}_{s}", [128, KC, 1024], BF16))
                    hnB2 = [[Buf(f"o_hn{c}_{b}") for c in range(KC)] for b in range(2)]
                    raw = p1.enter_context(nc.sbuf_tensor(f"o_raw_{l}_{s}", [128, 2, 1024], F32))
                    rawB = [Buf(f"o_raw{b}") for b in range(2)]
                    krA = p1.enter_context(nc.sbuf_tensor(f"o_krA_{l}_{s}", [128, 1024], F32))
                    krAB = [Buf(f"o_krA{b}") for b in range(2)]
                    sqt = p1.enter_context(nc.sbuf_tensor(f"o_sq_{l}_{s}", [128, KC, 512], BF16))
                    sqB = Buf("o_sq")
                    R = slice(64, 96)
                    for t in range(S // 1024):
                        for b in range(2):
                            t0 = t * 1024 + b * 512
                            prenorm(l, s, t0, lambda kc, b=b: hn[:, kc, b * 512:(b + 1) * 512], hnB2[b], sqt, sqB, tp)
                        for grp, (dst, dstB, nrm) in enumerate(((cqn, cqnB, "q_norm"), (ckvn, ckvnB, "kv_norm"))):
                            for c in range(2):
                                wt, wB, wi = acquire(("owin", j, grp * 2 + c))
                                for b in range(2):
                                    pt, pB = mm8(wt, wB, lambda kc, b=b: hn[:, kc, b * 512:(b + 1) * 512], hnB2[b])
                                    sy.op("act", lambda e: e.activation(out=raw[:, c, b * 512:(b + 1) * 512], in_=pt[:],
                                                                        func=AF.Identity),
                                          reads=[pB], writes=[rawB[b]])
                                release(wi)
                            for b in range(2):
                                gb = t * 2 + b
                                rs, rsB = rms_stats(lambda b=b: raw[:, :, b * 512:(b + 1) * 512], [rawB[b]], 2, od256,
                                                    sqt, sqB, tp)
                                for c in range(2):
                                    sy.op("dve", lambda e, c=c: e.scalar_tensor_tensor(
                                        out=dst[:, c, gb * 512:(gb + 1) * 512], in0=raw[:, c, b * 512:(b + 1) * 512],
                                        scalar=pv((nrm, j), c), in1=rs[:], op0=ALU.mult, op1=ALU.mult),
                                        reads=[rawB[b], rsB], sreads=[pvB], writes=[dstB[gb]])
                        wt, wB, wi = acquire(("owin", j, 4))
                        for b in range(2):
                            pt, pB = mm8(wt, wB, lambda kc, b=b: hn[:, kc, b * 512:(b + 1) * 512], hnB2[b])
                            sy.op("act", lambda e: e.activation(out=krA[R, b * 512:(b + 1) * 512], in_=pt[R, :],
                                                                func=AF.Identity), reads=[pB], writes=[krAB[b]])
                        release(wi)
                        wt, wB, wi = acquire(("owin", j, 5))
                        for b in range(2):
                            gb = t * 2 + b
                            tk = slice(gb * 512, (gb + 1) * 512)
                            pt, pB = mm8(wt, wB, lambda kc, b=b: hn[:, kc, b * 512:(b + 1) * 512], hnB2[b])
                            t1, t1B = tp()
                            sy.op("dve", lambda e: e.tensor_tensor(out=t1[R, :], in0=krA[R, b * 512:(b + 1) * 512],
                                                                   in1=COS[R, tk], op=ALU.mult),
                                  reads=[krAB[b], csB], writes=[t1B])
                            t2, t2B = tp()
                            sy.op("dve", lambda e: e.tensor_tensor(out=t2[R, :], in0=pt[R, :], in1=SIN[R, tk], op=ALU.mult),
                                  reads=[pB, csB], writes=[t2B])
                            sy.op("dve", lambda e: e.tensor_tensor(out=kr[R, tk], in0=t1[R, :], in1=t2[R, :], op=ALU.add),
                                  reads=[t1B, t2B], writes=[krB])
                        release(wi)
                        for c in range(8):
                            wt, wB, wi = acquire(("owin", j, 6 + c))
                            for b in range(2):
                                gb = t * 2 + b
                                pt, pB = mm8(wt, wB, lambda kc, b=b: hn[:, kc, b * 512:(b + 1) * 512], hnB2[b])
                                sy.op("act", lambda e: e.activation(out=Zs[:, c, gb * 512:(gb + 1) * 512], in_=pt[:],
                                                                    func=AF.Silu), reads=[pB], writes=[ZsB[c][gb]])
                            release(wi)
                    dump("cqn", cqn[:, 0, 0:512], [cqnB[0]])
                    dump("ckvn", ckvn[:, 0, 0:512], [ckvnB[0]])
                    dump("kr", kr[:, 0:512], [krB])
                    dump("cos", COS[:, 0:512], [csB])
                    dump("sin", SIN[:, 0:512], [csB])
                    dump("zs", Zs[:, 0, 0:512], [ZsB[0][0]])
                    sy.fence()

                with ExitStack() as p2:
                    QT = [p2.enter_context(nc.sbuf_tensor(f"o_QT{i}_{l}_{s}", [128, S], BF16)) for i in range(2)]
                    KT = [p2.enter_context(nc.sbuf_tensor(f"o_KT{i}_{l}_{s}", [128, S], BF16)) for i in range(2)]
                    V = [p2.enter_context(nc.sbuf_tensor(f"o_V{i}_{l}_{s}", [128, S // 128, 128], BF16)) for i in range(2)]
                    QTB = [Buf(f"o_QT{i}") for i in range(2)]
                    KTB = [Buf(f"o_KT{i}") for i in range(2)]
                    VB = [Buf(f"o_V{i}") for i in range(2)]
                    NPT = 6
                    PT = [p2.enter_context(nc.sbuf_tensor(f"o_PT{i}_{l}_{s}", [128, 512], BF16)) for i in range(NPT)]
                    PTB = [Buf(f"o_PT{i}") for i in range(NPT)]
                    rec = p2.enter_context(nc.sbuf_tensor(f"o_rec_{l}_{s}", [128, 512], F32))
                    recB = Buf("o_rec")
                    tpg = mk_tmp_pool(p2, "o_tg", 2, F32)
                    R = slice(64, 96)
                    sy.op("dve", lambda e: e.memset(V[0][:, :, 64:128], 1.0), writes=[VB[0]])
                    sy.op("dve", lambda e: e.memset(V[1][:, :, 0:64], 1.0), writes=[VB[1]])
                    pt_i = {"i": 0}

                    def gen_head(h):
                        par = h % 2
                        qt, qB = QT[par], QTB[par]
                        kt, kB = KT[par], KTB[par]
                        vt, vB = V[par], VB[par]
                        wt, wB, wi = acquire(("ouq", j, h))
                        for b in range(NB):
                            tk = slice(b * 512, (b + 1) * 512)
                            pa, paB = nextps()
                            pb, pbB = nextps()
                            for kc in range(2):
                                sy.op("pe", lambda e, kc=kc: e.matmul(pa[0:96, :], lhsT=wt[:, kc * 192:kc * 192 + 96],
                                                                      rhs=cqn[:, kc, tk], start=(kc == 0), stop=(kc == 1)),
                                      reads=[wB, cqnB[b]], writes=[paB], inc=(kc == 1))
                            for kc in range(2):
                                sy.op("pe", lambda e, kc=kc: e.matmul(pb[0:96, :], lhsT=wt[:, kc * 192 + 96:kc * 192 + 192],
                                                                      rhs=cqn[:, kc, tk], start=(kc == 0), stop=(kc == 1)),
                                      reads=[wB, cqnB[b]], writes=[pbB], inc=(kc == 1))
                            yield
                            sy.op("act", lambda e: e.activation(out=qt[0:64, tk], in_=pa[0:64, :], func=AF.Identity),
                                  reads=[paB], writes=[qB])
                            t1, t1B = tpg()
                            sy.op("dve", lambda e: e.tensor_tensor(out=t1[R, :], in0=pa[R, :], in1=COS[R, tk], op=ALU.mult),
                                  reads=[paB, csB], writes=[t1B])
                            t2, t2B = tpg()
                            sy.op("dve", lambda e: e.tensor_tensor(out=t2[R, :], in0=pb[R, :], in1=SIN[R, tk], op=ALU.mult),
                                  reads=[pbB, csB], writes=[t2B])
                            yield
                            sy.op("dve", lambda e: e.tensor_tensor(out=qt[R, tk], in0=t1[R, :], in1=t2[R, :], op=ALU.add),
                                  reads=[t1B, t2B], writes=[qB])
                            yield
                        release(wi)
                        wt, wB, wi = acquire(("oukv", j, h))
                        for b in range(NB):
                            tk = slice(b * 512, (b + 1) * 512)
                            pa, paB = nextps()
                            for kc in range(2):
                                sy.op("pe", lambda e, kc=kc: e.matmul(pa[0:64, :], lhsT=wt[:, kc * 128:kc * 128 + 64],
                                                                      rhs=ckvn[:, kc, tk], start=(kc == 0), stop=(kc == 1)),
                                      reads=[wB, ckvnB[b]], writes=[paB], inc=(kc == 1))
                            yield
                            sy.op("act", lambda e: e.activation(out=kt[0:64, tk], in_=pa[0:64, :], func=AF.Identity),
                                  reads=[paB], writes=[kB])
                            yield
                        sy.op("pool", lambda e: e.tensor_copy(out=kt[R, :], in_=kr[R, :]), reads=[krB], writes=[kB])
                        vo = 0 if par == 0 else 64
                        for g8 in range(S // 1024):
                            pa, paB = nextps()
                            for i8 in range(8):
                                kb = g8 * 8 + i8
                                for kc in range(2):
                                    sy.op("pe", lambda e, kc=kc, kb=kb, i8=i8: e.matmul(
                                        pa[:, i8 * 64:(i8 + 1) * 64], lhsT=ckvn[:, kc, kb * 128:(kb + 1) * 128],
                                        rhs=wt[:, kc * 128 + 64:kc * 128 + 128], start=(kc == 0), stop=(kc == 1)),
                                        reads=[wB, ckvnB[kb // 4]], writes=[paB], inc=(kc == 1 and i8 == 7))
                            yield
                            sy.op("act", lambda e: e.activation(
                                out=vt[:, g8 * 8:(g8 + 1) * 8, vo:vo + 64],
                                in_=pa[:, :].rearrange("p (a b) -> p a b", b=64), func=AF.Identity),
                                reads=[paB], writes=[vB])
                            yield
                        release(wi)

                    def attn_head(h):
                        par = h % 2
                        hp = h // 2
                        qt, qB = QT[par], QTB[par]
                        kt, kB = KT[par], KTB[par]
                        vt, vB = V[par], VB[par]
                        if h == 0:
                            dump("qt", qt[:, 0:512], [qB])
                            dump("kt", kt[:, 0:512], [kB])
                            dump("v0", vt[:, 0, :], [vB])
                        items = []
                        for g in range(NB):
                            for kb in range(4 * g + 4):
                                items.append((g, kb))
                        LA = 3
                        inflight = {}
                        for i in range(len(items) + LA):
                            if i < len(items):
                                g, kb = items[i]
                                d = kb - 4 * g
                                c0 = max(0, d) * 128
                                ncols = 512 - c0
                                sp_, spB = nextps()
                                sy.op("pe", lambda e, kb=kb, g=g, c0=c0, ncols=ncols, sp_=sp_: e.matmul(
                                    sp_[:, 0:ncols], lhsT=kt[0:96, kb * 128:(kb + 1) * 128],
                                    rhs=qt[0:96, g * 512 + c0:(g + 1) * 512], start=True, stop=(d < 0)),
                                    reads=[kB, qB], writes=[spB], inc=(d < 0))
                                if d >= 0:
                                    sy.op("pe", lambda e, sp_=sp_: e.matmul(sp_[:, 0:128], lhsT=ident_b[:], rhs=maskT[:],
                                                                            start=False, stop=True),
                                          reads=[constB], writes=[spB])
                                inflight[i] = (sp_, spB, c0, ncols)
                            if i >= LA:
                                ii = i - LA
                                g, kb = items[ii]
                                sp_, spB, c0, ncols = inflight.pop(ii)
                                pi = pt_i["i"] % NPT
                                pt_i["i"] += 1
                                ptile, ptB = PT[pi], PTB[pi]
                                sy.op("act", lambda e, sp_=sp_, ncols=ncols, ptile=ptile: e.activation(
                                    out=ptile[:, 0:ncols], in_=sp_[:, 0:ncols], func=AF.Exp, scale=ATT_SCALE),
                                    reads=[spB], writes=[ptB])
                                op_, opB = psum[g % 2], psB[g % 2]
                                nkb = 4 * g + 4
                                sy.op("pe", lambda e, kb=kb, c0=c0, ncols=ncols, ptile=ptile, op_=op_, nkb=nkb: e.matmul(
                                    op_[:, c0:512], lhsT=vt[:, kb, :], rhs=ptile[:, 0:ncols],
                                    start=(kb == 0), stop=(kb == nkb - 1)),
                                    reads=[vB, ptB], writes=[opB], inc=True)
                                if kb == nkb - 1:
                                    if par == 0:
                                        num, den = slice(0, 64), slice(64, 128)
                                    else:
                                        num, den = slice(64, 128), slice(0, 64)
                                    tk = slice(g * 512, (g + 1) * 512)
                                    sy.op("dve", lambda e, op_=op_: e.reciprocal(out=rec[num, :], in_=op_[den, :]),
                                          reads=[opB], writes=[recB])
                                    o1, o1B = tp()
                                    sy.op("dve", lambda e, op_=op_, o1=o1: e.tensor_tensor(out=o1[num, :], in0=op_[num, :],
                                                                                         in1=rec[num, :], op=ALU.mult),
                                          reads=[opB, recB], writes=[o1B])
                                    sy.op("dve", lambda e, o1=o1: e.tensor_tensor(out=Zs[num, hp, tk], in0=o1[num, :],
                                                                                in1=Zs[num, hp, tk], op=ALU.mult),
                                          reads=[o1B], writes=[ZsB[hp][g]])
                            yield

                    for _ in gen_head(0):
                        pass
                    for h in range(16):
                        gens = [attn_head(h)]
                        if h + 1 < 16:
                            gens.append(gen_head(h + 1))
                        while gens:
                            for g_ in list(gens):
                                try:
                                    next(g_)
                                except StopIteration:
                                    gens.remove(g_)
                    sy.fence()

                dump("og", Zs[:, 0, 0:512], [ZsB[0][0]])
                with ExitStack() as p3:
                    big = p3.enter_context(nc.sbuf_tensor(f"o_big_{l}_{s}", [128, KC, 512], F32))
                    bigB = [Buf(f"o_big{c}") for c in range(KC)]
                    sqt = p3.enter_context(nc.sbuf_tensor(f"o_sq3_{l}_{s}", [128, KC, 512], BF16))
                    sqB = Buf("o_sq3")
                    wl = [acquire(("owout", j, n)) for n in range(8)]
                    for b in range(NB):
                        tk = slice(b * 512, (b + 1) * 512)
                        for n in range(8):
                            wt, wB, _ = wl[n]
                            pt, pB = mm8(wt, wB, lambda kc: Zs[:, kc, tk], [ZsB[c][b] for c in range(KC)])
                            sy.op("act", lambda e: e.activation(out=big[:, n, :], in_=pt[:], func=AF.Identity),
                                  reads=[pB], writes=[bigB[n]])
                        postnorm(l, s, b * 512, big, bigB, sqt, sqB, tp)
                    for _w in wl:
                        release(_w[2])
                    sy.fence()
                tp_stats[0] = None

        allxs = [xsB[c][b] for c in range(KC) for b in range(NB)]
        outB = Buf("outst")
        for s in range(NSEQ):
            for c in range(KC):
                sy.dma("sp", xs[:, c, :], x_d.ap()[s, :, c, :], writes=xsB[c])
            for l in LAYERS:
                if l % 2 == 0:
                    even_layer(l, s)
                else:
                    odd_layer(l, s)
            for c in range(KC):
                sy.dma("sp", out_d.ap()[s, :, c, :], xs[:, c, :], reads=xsB[c], writes=[outB])
        nc.sync.wait_ge(outB.dsem, outB.dcnt)
        if DEBUG and dbgB.dsem is not None:
            nc.sync.wait_ge(dbgB.dsem, dbgB.dcnt)
        if RECORD:
            return ws["order"]
        assert ws["acq"] == len(ws["order"]) == ws["issued"], (ws["acq"], len(ws["order"]), ws["issued"])
        build.nins = sy.nins
    return nc


_CACHE = {}


def kernel(**inp):
    inp = {k: np.asarray(v) for k, v in inp.items()}
    x = inp["x"].astype(np.float32, copy=False)
    B, S, Dm = x.shape
    nseq = B // NCORES
    wts = pack_weights(inp)
    pvv = pack_pv(inp)
    key = (S, nseq)
    if key not in _CACHE:
        _CACHE[key] = build(S=S, NSEQ=nseq)
    nc = _CACHE[key]
    in_maps = []
    for cid in range(NCORES):
        bs = slice(cid * nseq, (cid + 1) * nseq)
        xf = np.ascontiguousarray(x[bs].reshape(nseq, S, KC, 128).transpose(0, 3, 2, 1))
        cf = np.ascontiguousarray(inp["c"][bs].astype(np.float32).reshape(nseq, KC, 128).transpose(2, 1, 0))
        pos = np.ascontiguousarray(inp["positions"][bs].astype(np.int32))
        in_maps.append({"x": xf, "c": cf, "pos": pos, "wts": wts, "pv": pvv})
    res = run_bass_kernel_spmd(nc, in_maps, core_ids=list(range(NCORES)))
    outs = []
    for cid in range(NCORES):
        o = np.asarray(res.results[cid]["out"]).reshape(nseq, 128, KC, S)
        outs.append(o.transpose(0, 3, 2, 1).reshape(nseq, S, Dm))
    return np.ascontiguousarray(np.concatenate(outs, axis=0).astype(np.float32))
```

```python
import math
from contextlib import ExitStack
import numpy as np
import concourse.bass as bass
import concourse.mybir as mybir
from concourse.ap import AP
from concourse.bass_utils import run_bass_kernel_spmd

F32 = mybir.dt.float32
BF16 = mybir.dt.bfloat16
I32 = mybir.dt.int32
AF = mybir.ActivationFunctionType
ALU = mybir.AluOpType

D = 1024
KC = 8
NCORES = 8
EPS = 1e-6
SLOT = 1024
RING = 10
SAME_ENG_WINDOW = 3
EVEN_STAGGER = 9
ATT_SCALE = 96.0 ** -0.5
TWO_PI = 2.0 * math.pi
PI_HI = 6.28125
PI_LO = TWO_PI - 6.28125


def weight_plan():
    plan = {}
    off = 0

    def add(name, F):
        nonlocal off
        plan[name] = (off, F)
        off += 128 * F

    for l in range(4):
        for n in range(24):
            add(("ada", l, n), 1024)
    for j in range(2):
        for c in range(40):
            add(("ewin", j, c), 1024)
        for kc in range(8):
            add(("egate", j, kc), 256)
        for n in range(8):
            add(("ewout", j, n, 0), 1024)
            add(("ewout", j, n, 1), 1024)
    for j in range(2):
        for c in range(14):
            add(("owin", j, c), 1024)
        for h in range(16):
            add(("ouq", j, h), 384)
            add(("oukv", j, h), 256)
        for n in range(8):
            add(("owout", j, n), 1024)
    return plan, off


def pv_plan():
    cols = {}
    off = 0

    def add(name, n):
        nonlocal off
        cols[name] = off
        off += n

    for l in range(4):
        add(("pre_g", l), 8)
        add(("post_g", l), 8)
        add(("ada_b", l), 24)
    for j in range(2):
        add(("conv_w", j), 8 * 31)
        add(("conv_b", j), 8)
        add(("ln_g", j), 8)
        add(("ln_b", j), 8)
        add(("lconv_w", j), 8 * 4)
        add(("lconv_b", j), 8)
        add(("ba", j), 8)
        add(("bx", j), 8)
        add(("lam", j), 8)
        add(("q_norm", j), 2)
        add(("kv_norm", j), 2)
    add("inv", 1)
    add("sgn", 1)
    return cols, off


def chunkify(W):
    K, N = W.shape
    return np.ascontiguousarray(W.reshape(K // 128, 128, N // 128, 128).transpose(2, 1, 0, 3))


def vec_pc(v):
    return np.ascontiguousarray(v.reshape(-1, 128).T)


def pack_weights(inp):
    plan, total = weight_plan()
    flat = np.zeros(total, np.float32)

    def put(name, arr):
        off, F = plan[name]
        a = np.ascontiguousarray(arr, dtype=np.float32).reshape(128, F)
        flat[off:off + 128 * F] = a.reshape(-1)

    for l in range(4):
        ch = chunkify(inp["ada_w"][l])
        for n in range(24):
            put(("ada", l, n), ch[n])
    for j in range(2):
        ch = chunkify(inp["ev_w_in"][j])
        for c in range(40):
            put(("ewin", j, c), ch[c])
        wa, wx = inp["ev_lru_wa"][j], inp["ev_lru_wx"][j]
        for kc in range(8):
            g = np.zeros((128, 2, 128), np.float32)
            for hh in range(2):
                g[hh * 64:(hh + 1) * 64, 0, hh * 64:(hh + 1) * 64] = wa[2 * kc + hh]
                g[hh * 64:(hh + 1) * 64, 1, hh * 64:(hh + 1) * 64] = wx[2 * kc + hh]
            put(("egate", j, kc), g)
        wo = inp["ev_w_out"][j]
        c0 = chunkify(wo[:1024])
        c1 = chunkify(wo[1024:])
        for n in range(8):
            put(("ewout", j, n, 0), c0[n])
            put(("ewout", j, n, 1), c1[n])
    for j in range(2):
        w = inp["od_w_in"][j]
        z64 = np.zeros((1024, 64), np.float32)
        z32 = np.zeros((1024, 32), np.float32)
        kr = w[:, 512:544]
        kr1 = np.concatenate([z64, kr, z32], axis=1)
        kr2 = np.concatenate([z64, kr[:, 16:32], kr[:, 0:16], z32], axis=1)
        wcat = np.concatenate([w[:, 0:512], kr1, kr2, w[:, 544:1568]], axis=1)
        ch = chunkify(wcat)
        for c in range(14):
            put(("owin", j, c), ch[c])
        uq = inp["od_w_uq"][j].reshape(2, 128, 16, 96)
        ukv = inp["od_w_ukv"][j].reshape(2, 128, 16, 128)
        for h in range(16):
            a = uq[:, :, h, :]
            sw = np.concatenate([a[:, :, 0:64], a[:, :, 80:96], a[:, :, 64:80]], axis=2)
            both = np.concatenate([a, sw], axis=2)
            put(("ouq", j, h), both.transpose(1, 0, 2))
            put(("oukv", j, h), ukv[:, :, h, :].transpose(1, 0, 2))
        ch = chunkify(inp["od_w_out"][j])
        for n in range(8):
            put(("owout", j, n), ch[n])
    return flat


def pack_pv(inp):
    cols, n = pv_plan()
    pv = np.zeros((128, n), np.float32)

    def put(name, arr):
        a = np.asarray(arr, np.float32)
        pv[:, cols[name]:cols[name] + a.shape[1]] = a

    for l in range(4):
        put(("pre_g", l), vec_pc(inp["pre_g"][l]))
        put(("post_g", l), vec_pc(inp["post_g"][l]))
        put(("ada_b", l), vec_pc(inp["ada_b"][l]))
    for j in range(2):
        cw = inp["ev_conv_w"][j]
        put(("conv_w", j), cw.reshape(31, 8, 128).transpose(2, 1, 0).reshape(128, 8 * 31))
        put(("conv_b", j), vec_pc(inp["ev_conv_b"][j]))
        put(("ln_g", j), vec_pc(inp["ev_ln_g"][j]))
        put(("ln_b", j), vec_pc(inp["ev_ln_b"][j]))
        lw = inp["ev_lru_conv_w"][j]
        put(("lconv_w", j), lw.reshape(4, 8, 128).transpose(2, 1, 0).reshape(128, 8 * 4))
        put(("lconv_b", j), vec_pc(inp["ev_lru_conv_b"][j]))
        put(("ba", j), vec_pc(inp["ev_lru_ba"][j]))
        put(("bx", j), vec_pc(inp["ev_lru_bx"][j]))
        put(("lam", j), vec_pc(inp["ev_lru_lam"][j]))
        put(("q_norm", j), vec_pc(inp["od_q_norm"][j]))
        put(("kv_norm", j), vec_pc(inp["od_kv_norm"][j]))
    inv = (10000.0 ** (-np.arange(0, 32, 2, dtype=np.float32) / 32.0)).astype(np.float32)
    iv = np.zeros((128, 1), np.float32)
    sg = np.ones((128, 1), np.float32)
    for i in range(32):
        iv[64 + i, 0] = inv[i % 16]
        sg[64 + i, 0] = -1.0 if i < 16 else 1.0
    put("inv", iv)
    put("sgn", sg)
    return pv


_UID = [0]


class Buf:
    __slots__ = ("name", "w", "r", "dsem", "dcnt")

    def __init__(self, name):
        _UID[0] += 1
        self.name = f"{name}_{_UID[0]}"
        self.w = None
        self.r = {}
        self.dsem = None
        self.dcnt = 0


class Sync:
    ENG = ("pe", "act", "dve", "pool", "sp")

    def __init__(self, nc, es):
        self.nc = nc
        self.es = es
        self.eng = {"pe": nc.tensor, "act": nc.scalar, "dve": nc.vector, "pool": nc.gpsimd, "sp": nc.sync}
        self.sem = {k: es.enter_context(nc.semaphore("s_" + k)) for k in self.ENG}
        self.cnt = {k: 0 for k in self.ENG}
        self.known = {k: {} for k in self.ENG}
        self.nins = 0

    def _need(self, E, dep, strict):
        key, sem, val, src = dep
        if src == E and not strict and E != "pool":
            if E == "pe" or (self.cnt[E] - val) >= SAME_ENG_WINDOW:
                return
        if self.known[E].get(key, 0) >= val:
            return
        self.eng[E].wait_ge(sem, val)
        self.known[E][key] = val
        self.nins += 1

    def _deps(self, E, reads, writes, sreads, strict_all=False):
        for b in reads:
            if b.w is not None:
                self._need(E, b.w, strict_all)
        for b in sreads:
            if b.w is not None:
                self._need(E, b.w, True)
        for b in writes:
            if b.w is not None:
                self._need(E, b.w, strict_all)
            for d in b.r.values():
                self._need(E, d, strict_all)

    def op(self, E, fn, reads=(), writes=(), sreads=(), inc=True):
        self._deps(E, reads, writes, sreads)
        ins = fn(self.eng[E])
        self.nins += 1
        if inc:
            self.cnt[E] += 1
            ins.then_inc(self.sem[E], 1)
            me = (E, self.sem[E], self.cnt[E], E)
        else:
            me = (E, self.sem[E], self.cnt[E] + 1, E)
        for b in writes:
            b.w = me
            b.r = {}
        for b in reads:
            b.r[E] = me
        for b in sreads:
            b.r[E] = me
        return ins

    def dma(self, Q, out, in_, reads=(), writes=()):
        self._deps(Q, reads, writes, (), strict_all=True)
        tgt = writes[0] if writes else reads[0]
        if tgt.dsem is None:
            tgt.dsem = self.es.enter_context(self.nc.semaphore("d_" + tgt.name))
        tgt.dcnt += 16
        self.eng[Q].dma_start(out=out, in_=in_).then_inc(tgt.dsem, 16)
        self.nins += 1
        me = ("d_" + tgt.name, tgt.dsem, tgt.dcnt, None)
        for b in writes:
            b.w = me
            b.r = {}
        for b in reads:
            b.r["dma_" + tgt.name] = me
        return me

    def fence(self):
        for E in ("pe", "act", "dve", "pool", "sp"):
            for Fg in ("pe", "act", "dve", "pool"):
                if Fg != E and self.cnt[Fg] > 0:
                    self._need(E, (Fg, self.sem[Fg], self.cnt[Fg], Fg), True)


def build(S=2048, NSEQ=2, LAYERS=(0, 1, 2, 3), DEBUG=False):
    order = _build(S, NSEQ, LAYERS, DEBUG, None)
    return _build(S, NSEQ, LAYERS, DEBUG, order)


def _build(S, NSEQ, LAYERS, DEBUG, ORDER):
    RECORD = ORDER is None
    nc = bass.Bass("TRN2", target_bir_lowering=False)
    plan, wtotal = weight_plan()
    pcols, npv = pv_plan()
    NB = S // 512

    x_d = nc.dram_tensor("x", [NSEQ, 128, KC, S], F32, kind="ExternalInput")
    c_d = nc.dram_tensor("c", [128, KC, NSEQ], F32, kind="ExternalInput")
    pos_d = nc.dram_tensor("pos", [NSEQ, S], I32, kind="ExternalInput")
    w_d = nc.dram_tensor("wts", [wtotal], F32, kind="ExternalInput")
    pv_d = nc.dram_tensor("pv", [128, npv], F32, kind="ExternalInput")
    out_d = nc.dram_tensor("out", [NSEQ, 128, KC, S], F32, kind="ExternalOutput")
    dgd = nc.dram_tensor("dgd", [2, 8, 128, 31 * 128], BF16, kind="Internal")

    dbg_d = nc.dram_tensor("dbg", [128, 16384], F32, kind="ExternalOutput") if DEBUG else None
    dbg_cols = {}
    build.dbg_cols = dbg_cols
    dbg_state = {"c": 0}

    with ExitStack() as es:
        sy = Sync(nc, es)
        dbgB = Buf("dbg")

        def dump(name, ap, bufs):
            if not DEBUG or name in dbg_cols:
                return
            n = ap.shape[-1]
            p0 = 0
            c0 = dbg_state["c"]
            dbg_cols[name] = (c0, n)
            dbg_state["c"] += n
            sy.dma("pool", dbg_d.ap()[0:ap.shape[0], c0:c0 + n], ap, reads=bufs, writes=[dbgB])

        def sb(name, shape, dt):
            return es.enter_context(nc.sbuf_tensor(name, shape, dt))

        xs = sb("xs", [128, KC, S], F32)
        xsB = [[Buf(f"xs{c}_{b}") for b in range(NB)] for c in range(KC)]
        pvt = sb("pvt", [128, npv], F32)
        pvB = Buf("pv")
        ring = sb("ring", [128, RING, SLOT], BF16)
        ringB = [Buf(f"ring{i}") for i in range(RING)]
        ident_f = sb("ident_f", [128, 128], F32)
        ident_b = sb("ident_b", [128, 128], BF16)
        od1024 = sb("od1024", [128, 128], BF16)
        od256 = sb("od256", [128, 128], BF16)
        maskT = sb("maskT", [128, 128], BF16)
        constB = Buf("const")
        cin = sb("cin", [128, KC, NSEQ], F32)
        cact = sb("cact", [128, KC, NSEQ], BF16)
        cB = Buf("c")
        modt = sb("modt", [128, 96, NSEQ], F32)
        modB = Buf("mod")
        drvA = sb("drvA", [128, 4, NSEQ, 8], F32)
        drvG = sb("drvG", [128, 4, NSEQ, 8], F32)
        drvB = Buf("drv")
        nsp = sb("nsp", [128, 2, 8], F32)
        nspB = Buf("nsp")
        carry = sb("carry", [128, 8], F32)
        carryB = [Buf(f"carry{j}") for j in range(8)]
        xbh = sb("xbh", [128, 8, 4], BF16)
        xbhB = [Buf(f"xbh{j}") for j in range(8)]

        psum = [es.enter_context(nc.psum_tensor(f"ps{i}", [128, 512], F32)) for i in range(8)]
        psB = [Buf(f"ps{i}") for i in range(8)]
        ps_state = {"i": 0}

        def nextps():
            i = 2 + ps_state["i"] % 6
            ps_state["i"] += 1
            return psum[i], psB[i]

        def pv(name, a, b=None):
            c0 = pcols[name]
            if b is None:
                return pvt[:, c0 + a:c0 + a + 1]
            return pvt[:, c0 + a:c0 + b]

        ws = {"order": [] if RECORD else ORDER, "issued": 0, "acq": 0, "rel": 0, "done": set()}

        def w_ap(name):
            off, Fw = plan[name]
            return AP(w_d, off, [[Fw, 128], [1, Fw]]), Fw

        def ws_issue():
            if RECORD:
                return
            while ws["issued"] < len(ws["order"]) and ws["issued"] < ws["rel"] + RING:
                i = ws["issued"]
                src, Fw = w_ap(ws["order"][i])
                slot = i % RING
                sy.dma("pool", ring[:, slot, 0:Fw], src, writes=[ringB[slot]])
                ws["issued"] += 1

        def acquire(name):
            i = ws["acq"]
            ws["acq"] += 1
            if RECORD:
                ws["order"].append(name)
                return ring[:, 0, :], ringB[0], i
            assert ws["order"][i] == name, (ws["order"][i], name)
            assert ws["issued"] > i, "weight ring deadlock (acquire beyond issued)"
            slot = i % RING
            return ring[:, slot, :], ringB[slot], i

        def release(i):
            ws["done"].add(i)
            while ws["rel"] in ws["done"]:
                ws["done"].remove(ws["rel"])
                ws["rel"] += 1
            ws_issue()

        sy.dma("sp", pvt[:], pv_d.ap()[:, :], writes=[pvB])
        sy.dma("sp", cin[:], c_d.ap()[:, :, :], writes=[cB])
        ws_issue()
        sy.op("pool", lambda e: e.memset(ident_f[:], 1.0), writes=[constB])
        sy.op("pool", lambda e: e.affine_select(out=ident_f[:], in_=ident_f[:], pattern=[[-1, 128]], base=0,
                                                channel_multiplier=1, compare_op=ALU.is_equal, fill=0.0),
              writes=[constB])
        sy.op("pool", lambda e: e.tensor_copy(out=ident_b[:], in_=ident_f[:]), writes=[constB])
        sy.op("pool", lambda e: e.memset(od1024[:], 1.0 / 1024), writes=[constB])
        sy.op("pool", lambda e: e.memset(od256[:], 1.0 / 256), writes=[constB])
        sy.op("pool", lambda e: e.memset(maskT[:], 0.0), writes=[constB])
        sy.op("pool", lambda e: e.affine_select(out=maskT[:], in_=maskT[:], pattern=[[1, 128]], base=0,
                                                channel_multiplier=-1, compare_op=ALU.is_ge, fill=-30000.0),
              writes=[constB])
        sy.op("act", lambda e: e.activation(out=cact[:], in_=cin[:], func=AF.Silu), reads=[cB], writes=[cB])

        dgdB = [[Buf(f"dgd{j}_{jj}") for jj in range(8)] for j in range(2)]
        with ExitStack() as st0:
            dgs = [st0.enter_context(nc.sbuf_tensor(f"dgs{i}", [128, 31, 128], BF16)) for i in range(2)]
            dgsB = [Buf(f"dgs{i}") for i in range(2)]
            for j in range(2):
                if (2 * j) not in LAYERS:
                    continue
                for jj in range(8):
                    i = jj % 2
                    cw0 = pcols[("conv_w", j)] + jj * 31
                    sy.op("pool", lambda e: e.tensor_tensor(
                        out=dgs[i][:], in0=ident_b[:].unsqueeze(1).to_broadcast([128, 31, 128]),
                        in1=pvt[:, cw0:cw0 + 31].unsqueeze(2).to_broadcast([128, 31, 128]), op=ALU.mult),
                        reads=[constB, pvB], writes=[dgsB[i]])
                    sy.dma("sp", dgd.ap()[j, jj, :, :], dgs[i][:].rearrange("p a b -> p (a b)"),
                           reads=[dgsB[i]], writes=[dgdB[j][jj]])
            sy.fence()
            for j in range(2):
                for jj in range(8):
                    if dgdB[j][jj].w is not None:
                        for E_ in ("pe", "act", "dve", "pool", "sp"):
                            sy._need(E_, dgdB[j][jj].w, True)

        for l in range(4):
            for n in range(24):
                wt, wB, wi = acquire(("ada", l, n))
                pt, pB = nextps()
                for kc in range(KC):
                    sy.op("pe", lambda e, kc=kc: e.matmul(pt[:, 0:NSEQ], lhsT=wt[:, kc * 128:(kc + 1) * 128],
                                                          rhs=cact[:, kc, :], start=(kc == 0), stop=(kc == KC - 1)),
                          reads=[wB, cB], writes=[pB], inc=(kc == KC - 1))
                release(wi)
                sy.op("act", lambda e: e.activation(out=modt[:, l * 24 + n, :], in_=pt[:, 0:NSEQ], func=AF.Identity,
                                                    bias=pv(("ada_b", l), n), scale=1.0),
                      reads=[pB], sreads=[pvB], writes=[modB])
        for l in range(4):
            for s in range(NSEQ):
                sy.op("dve", lambda e: e.tensor_scalar(out=drvA[:, l, s, :], in0=modt[:, l * 24 + 8:l * 24 + 16, s],
                                                       scalar1=1.0, scalar2=None, op0=ALU.add),
                      reads=[modB], writes=[drvB])
                sy.op("dve", lambda e: e.tensor_tensor(out=drvA[:, l, s, :], in0=drvA[:, l, s, :],
                                                       in1=pv(("pre_g", l), 0, 8), op=ALU.mult),
                      reads=[pvB], writes=[drvB])
                sy.op("dve", lambda e: e.tensor_tensor(out=drvG[:, l, s, :], in0=modt[:, l * 24 + 16:l * 24 + 24, s],
                                                       in1=pv(("post_g", l), 0, 8), op=ALU.mult),
                      reads=[pvB, modB], writes=[drvB])
        spt = [sb(f"spt{i}", [128, 8], F32) for i in range(4)]
        for j in range(2):
            lam_ap = pv(("lam", j), 0, 8)
            al, ee, ww, w2 = spt
            sy.op("act", lambda e: e.activation(out=al[:], in_=lam_ap, func=AF.Abs),
                  reads=[pvB], writes=[nspB])
            sy.op("act", lambda e: e.activation(out=ee[:], in_=al[:], func=AF.Exp, scale=-1.0),
                  reads=[nspB], writes=[nspB])
            sy.op("dve", lambda e: e.tensor_scalar(out=ww[:], in0=ee[:], scalar1=2.0, scalar2=None, op0=ALU.add),
                  reads=[nspB], writes=[nspB])
            sy.op("dve", lambda e: e.reciprocal(out=ww[:], in_=ww[:]), writes=[nspB])
            sy.op("dve", lambda e: e.tensor_tensor(out=ww[:], in0=ww[:], in1=ee[:], op=ALU.mult), writes=[nspB])
            sy.op("dve", lambda e: e.tensor_tensor(out=w2[:], in0=ww[:], in1=ww[:], op=ALU.mult), writes=[nspB])
            sy.op("dve", lambda e: e.tensor_scalar(out=al[:], in0=w2[:], scalar1=1.0 / 11, scalar2=1.0 / 9, op0=ALU.mult,
                                                   op1=ALU.add), writes=[nspB])
            for cf in (1.0 / 7, 1.0 / 5, 1.0 / 3, 1.0):
                sy.op("dve", lambda e: e.tensor_tensor(out=al[:], in0=al[:], in1=w2[:], op=ALU.mult), writes=[nspB])
                sy.op("dve", lambda e, cf=cf: e.tensor_scalar(out=al[:], in0=al[:], scalar1=cf, scalar2=None, op0=ALU.add),
                      writes=[nspB])
            sy.op("dve", lambda e: e.tensor_tensor(out=al[:], in0=al[:], in1=ww[:], op=ALU.mult), writes=[nspB])
            sy.op("dve", lambda e: e.tensor_scalar(out=ee[:], in0=lam_ap, scalar1=-1.0, scalar2=0.0, op0=ALU.mult,
                                                   op1=ALU.max), reads=[pvB], writes=[nspB])
            sy.op("dve", lambda e: e.scalar_tensor_tensor(out=al[:], in0=al[:], scalar=2.0, in1=ee[:], op0=ALU.mult,
                                                          op1=ALU.add), writes=[nspB])
            sy.op("dve", lambda e: e.tensor_scalar(out=nsp[:, j, :], in0=al[:], scalar1=-8.0, scalar2=None,
                                                   op0=ALU.mult), writes=[nspB])
        dump("modt", modt[:].rearrange("p a b -> p (a b)"), [modB])
        dump("drvA", drvA[:].rearrange("p a b c -> p (a b c)"), [drvB])
        dump("drvG", drvG[:].rearrange("p a b c -> p (a b c)"), [drvB])
        dump("nsp", nsp[:].rearrange("p a b -> p (a b)"), [nspB])
        tp_stats = [None]
        def mm8(wt, wB, rhs_fn, rB, M=128, wcol0=0, prow=None):
            pt, pB = nextps()
            for kc in range(KC):
                sy.op("pe", lambda e, kc=kc: e.matmul(pt[0:M, :], lhsT=wt[:, kc * 128 + wcol0:kc * 128 + wcol0 + M],
                                                      rhs=rhs_fn(kc), start=(kc == 0), stop=(kc == KC - 1)),
                      reads=[wB] + rB, writes=[pB], inc=(kc == KC - 1))
            return pt, pB

        def rms_stats(src_fn, srcB, nch, onesm, sqt, sqB, tp):
            tp = tp_stats[0] or tp
            sy.op("act", lambda e: e.activation(out=sqt[:, 0:nch, :], in_=src_fn(), func=AF.Square),
                  reads=srcB, writes=[sqB])
            pt, pB = nextps()
            for kc in range(nch):
                sy.op("pe", lambda e, kc=kc: e.matmul(pt[:, :], lhsT=onesm[:], rhs=sqt[:, kc, :], start=(kc == 0),
                                                      stop=(kc == nch - 1)),
                      reads=[constB, sqB], writes=[pB], inc=(kc == nch - 1))
            sd, sdB = tp()
            sy.op("act", lambda e: e.activation(out=sd[:], in_=pt[:], func=AF.Ln, bias=EPS, scale=1.0),
                  reads=[pB], writes=[sdB])
            rs, rsB = tp()
            sy.op("act", lambda e: e.activation(out=rs[:], in_=sd[:], func=AF.Exp, scale=-0.5), reads=[sdB], writes=[rsB])
            return rs, rsB

        def prenorm(l, s, t0, hn_fn, hnB, sqt, sqB, tp, tpt=None):
            tpt = tpt or tp
            b = t0 // 512
            rs, rsB = rms_stats(lambda: xs[:, :, t0:t0 + 512], [xsB[c][b] for c in range(KC)], KC, od1024, sqt, sqB, tp)
            for kc in range(KC):
                tt, ttB = tpt()
                sy.op("dve", lambda e: e.tensor_tensor(out=tt[:], in0=xs[:, kc, t0:t0 + 512], in1=rs[:], op=ALU.mult),
                      reads=[xsB[kc][b], rsB], writes=[ttB])
                sy.op("act", lambda e: e.activation(out=hn_fn(kc), in_=tt[:], func=AF.Identity,
                                                    scale=drvA[:, l, s, kc:kc + 1], bias=modt[:, l * 24 + kc, s:s + 1]),
                      reads=[ttB], sreads=[drvB, modB], writes=[hnB[kc]])

        def postnorm(l, s, t0, y, yB, sqt, sqB, tp, tpt=None):
            tpt = tpt or tp
            b = t0 // 512
            rs, rsB = rms_stats(lambda: y[:, :, :], yB, KC, od1024, sqt, sqB, tp)
            for n in range(KC):
                tt, ttB = tpt()
                sy.op("dve", lambda e: e.tensor_tensor(out=tt[:], in0=y[:, n, :], in1=rs[:], op=ALU.mult),
                      reads=[yB[n], rsB], writes=[ttB])
                sy.op("dve", lambda e: e.scalar_tensor_tensor(out=xs[:, n, t0:t0 + 512], in0=tt[:],
                                                              scalar=drvG[:, l, s, n:n + 1], in1=xs[:, n, t0:t0 + 512],
                                                              op0=ALU.mult, op1=ALU.add),
                      reads=[ttB], sreads=[drvB], writes=[xsB[n][b]])

        def mk_tmp_pool(ess, name, n, dt=F32, w=512):
            _UID[0] += 1
            tiles = [ess.enter_context(nc.sbuf_tensor(f"{name}{i}_{_UID[0]}", [128, w], dt)) for i in range(n)]
            bufs = [Buf(f"{name}{i}") for i in range(n)]
            st = {"i": 0}

            def get():
                i = st["i"] % n
                st["i"] += 1
                return tiles[i], bufs[i]
            return get

        def even_layer(l, s):
            j = l // 2
            with ExitStack() as el:
                def sbl(name, shape, dt):
                    return el.enter_context(nc.sbuf_tensor(f"{name}_{l}_{s}", shape, dt))
                hn = sbl("e_hn", [128, KC, 512], BF16)
                hnB = [Buf(f"e_hn{c}") for c in range(KC)]
                G = sbl("e_G", [128, KC, 544], BF16)
                GB = [Buf(f"e_G{c}") for c in range(KC)]
                Z = sbl("e_Z", [128, KC, 512], BF16)
                ZB = [Buf(f"e_Z{c}") for c in range(KC)]
                big = sbl("e_big", [128, KC, 512], F32)
                bigB = [Buf(f"e_big{c}") for c in range(KC)]
                Lo = sbl("e_Lo", [128, KC, 512], BF16)
                LoB = [Buf(f"e_Lo{c}") for c in range(KC)]
                sqt = sbl("e_sq", [128, KC, 512], BF16)
                sqB = Buf("e_sq")
                dg = sbl("e_dg", [128, 31, 128], BF16)
                dgB = Buf("e_dg")
                d4 = [sbl(f"e_d4{i}", [128, 4, 128], BF16) for i in range(2)]
                d4B = [Buf(f"e_d4{i}") for i in range(2)]
                XB = [sbl(f"e_XB{i}", [128, 516], BF16) for i in range(2)]
                XBB = [Buf(f"e_XB{i}") for i in range(2)]
                tp = mk_tmp_pool(el, "e_tf", 5, F32)
                tpt = mk_tmp_pool(el, "e_tt", 3, F32)
                tpb = mk_tmp_pool(el, "e_tb", 2, BF16)
                lt = [[sbl(f"e_lt{p}{i}", [128, 512], F32) for i in range(5)] for p in range(2)]
                ltB = [[Buf(f"e_lt{p}{i}") for i in range(5)] for p in range(2)]
                sgt = [sbl(f"e_sgt{p}", [128, 512], BF16) for p in range(2)]
                sgtB = [Buf(f"e_sgt{p}") for p in range(2)]
                zbt = [sbl(f"e_zbt{p}", [128, 512], BF16) for p in range(2)]
                zbtB = [Buf(f"e_zbt{p}") for p in range(2)]
                xcbt = [sbl(f"e_xcbt{p}", [128, 512], BF16) for p in range(2)]
                xcbtB = [Buf(f"e_xcbt{p}") for p in range(2)]

                sy.op("dve", lambda e: e.memset(G[:, :, 0:32], 0.0), writes=GB)
                sy.op("dve", lambda e: e.memset(carry[:], 0.0), writes=carryB)
                sy.op("dve", lambda e: e.memset(xbh[:], 0.0), writes=xbhB)
                hrhs = lambda kc: hn[:, kc, :]

                def w_mm8(name):
                    wt, wB, wi = acquire(name)
                    pt, pB = mm8(wt, wB, hrhs, hnB)
                    release(wi)
                    return pt, pB

                def conv_stage(jj):
                    p = jj % 2
                    pt, pB = w_mm8(("ewin", j, 8 + jj))
                    yield
                    sy.op("act", lambda e: e.activation(out=sgt[p][:], in_=pt[:], func=AF.Sigmoid),
                          reads=[pB], writes=[sgtB[p]])
                    pt, pB = w_mm8(("ewin", j, jj))
                    yield
                    sy.op("dve", lambda e: e.tensor_tensor(out=G[:, jj, 32:544], in0=pt[:], in1=sgt[p][:], op=ALU.mult),
                          reads=[pB, sgtB[p]], writes=[GB[jj]])
                    pt, pB = w_mm8(("ewin", j, 16 + jj))
                    yield
                    sy.op("act", lambda e: e.activation(out=Z[:, jj, :], in_=pt[:], func=AF.Silu),
                          reads=[pB], writes=[ZB[jj]])
                    sy.dma("sp", dg[:].rearrange("p a b -> p (a b)"), dgd.ap()[j, jj, :, :],
                           reads=[dgdB[j][jj]], writes=[dgB])
                    yield
                    yield
                    pt, pB = nextps()
                    for k in range(31):
                        sy.op("pe", lambda e, k=k: e.matmul(pt[:, :], lhsT=dg[:, k, :], rhs=G[:, jj, k + 2:k + 514],
                                                            start=(k == 0), stop=(k == 30)),
                              reads=[dgB, GB[jj]], writes=[pB], inc=(k == 30))
                        if k % 8 == 7:
                            yield
                    sy.op("act", lambda e: e.activation(out=big[:, jj, :], in_=pt[:], func=AF.Identity,
                                                        bias=pv(("conv_b", j), jj), scale=1.0),
                          reads=[pB], sreads=[pvB], writes=[bigB[jj]])
                    if jj == 0:
                        dump("G0", G[:, 0, 32:544], [GB[0]])
                        dump("aconv0", big[:, 0, :], [bigB[0]])
                        dump("Zraw0", Z[:, 0, :], [ZB[0]])
                    yield
                    sy.op("dve", lambda e: e.tensor_copy(out=G[:, jj, 0:32], in_=G[:, jj, 512:544]),
                          reads=[GB[jj]], writes=[GB[jj]])

                def lru_stage(jj):
                    p = jj % 2
                    xbt, xbB = XB[p], XBB[p]
                    pt, pB = w_mm8(("ewin", j, 24 + jj))
                    yield
                    sy.op("act", lambda e: e.activation(out=xbt[:, 4:516], in_=pt[:], func=AF.Identity),
                          reads=[pB], writes=[xbB])
                    sy.op("dve", lambda e: e.tensor_copy(out=xbt[:, 0:4], in_=xbh[:, jj, :]),
                          reads=[xbhB[jj]], writes=[xbB])
                    pt, pB = w_mm8(("ewin", j, 32 + jj))
                    yield
                    sy.op("act", lambda e: e.activation(out=zbt[p][:], in_=pt[:], func=AF.Silu), reads=[pB],
                          writes=[zbtB[p]])
                    lw0 = pcols[("lconv_w", j)] + jj * 4
                    sy.op("dve", lambda e: e.tensor_tensor(
                        out=d4[p][:], in0=ident_b[:].unsqueeze(1).to_broadcast([128, 4, 128]),
                        in1=pvt[:, lw0:lw0 + 4].unsqueeze(2).to_broadcast([128, 4, 128]), op=ALU.mult),
                        reads=[constB, pvB], writes=[d4B[p]])
                    yield
                    pt, pB = nextps()
                    for k in range(4):
                        sy.op("pe", lambda e, k=k: e.matmul(pt[:, :], lhsT=d4[p][:, k, :], rhs=xbt[:, k + 1:k + 513],
                                                            start=(k == 0), stop=(k == 3)),
                              reads=[d4B[p], xbB], writes=[pB], inc=(k == 3))
                    yield
                    xc, xcB = lt[p][0], ltB[p][0]
                    rg, rgB = lt[p][1], ltB[p][1]
                    ig, igB = lt[p][2], ltB[p][2]
                    at, atB = lt[p][3], ltB[p][3]
                    hh_, hhB = lt[p][4], ltB[p][4]
                    sy.op("act", lambda e: e.activation(out=xc[:], in_=pt[:], func=AF.Identity,
                                                        bias=pv(("lconv_b", j), jj), scale=1.0),
                          reads=[pB], sreads=[pvB], writes=[xcB])
                    yield
                    sy.op("dve", lambda e: e.tensor_copy(out=xcbt[p][:], in_=xc[:]), reads=[xcB], writes=[xcbtB[p]])
                    sy.op("dve", lambda e: e.tensor_copy(out=xbh[:, jj, :], in_=xbt[:, 512:516]),
                          reads=[xbB], writes=[xbhB[jj]])
                    yield
                    wt, wB, wi = acquire(("egate", j, jj))
                    pr, prB = nextps()
                    sy.op("pe", lambda e: e.matmul(pr[:, :], lhsT=wt[:, 0:128], rhs=xcbt[p][:], start=True, stop=True),
                          reads=[wB, xcbtB[p]], writes=[prB])
                    pi_, piB = nextps()
                    sy.op("pe", lambda e: e.matmul(pi_[:, :], lhsT=wt[:, 128:256], rhs=xcbt[p][:], start=True, stop=True),
                          reads=[wB, xcbtB[p]], writes=[piB])
                    release(wi)
                    yield
                    sy.op("act", lambda e: e.activation(out=rg[:], in_=pr[:], func=AF.Sigmoid,
                                                        bias=pv(("ba", j), jj), scale=1.0),
                          reads=[prB], sreads=[pvB], writes=[rgB])
                    yield
                    sy.op("act", lambda e: e.activation(out=ig[:], in_=pi_[:], func=AF.Sigmoid,
                                                        bias=pv(("bx", j), jj), scale=1.0),
                          reads=[piB], sreads=[pvB], writes=[igB])
                    yield
                    sy.op("act", lambda e: e.activation(out=at[:], in_=rg[:], func=AF.Exp,
                                                        scale=nsp[:, j, jj:jj + 1]),
                          reads=[rgB], sreads=[nspB], writes=[atB])
                    sy.op("dve", lambda e: e.tensor_tensor(out=ig[:], in0=ig[:], in1=xc[:], op=ALU.mult),
                          reads=[xcB], writes=[igB])
                    yield
                    sy.op("dve", lambda e: e.tensor_tensor(out=rg[:], in0=at[:], in1=at[:], op=ALU.mult),
                          reads=[atB], writes=[rgB])
                    yield
                    sy.op("act", lambda e: e.activation(out=rg[:], in_=rg[:], func=AF.Sqrt, bias=1.0, scale=-1.0),
                          writes=[rgB])
                    yield
                    sy.op("dve", lambda e: e.tensor_tensor(out=ig[:], in0=ig[:], in1=rg[:], op=ALU.mult),
                          reads=[rgB], writes=[igB])
                    yield
                    sy.op("dve", lambda e: e.tensor_tensor_scan(out=hh_[:], data0=at[:], data1=ig[:],
                                                                initial=carry[:, jj:jj + 1], op0=ALU.mult,
                                                                op1=ALU.add),
                          reads=[atB, igB], sreads=[carryB[jj]], writes=[hhB])
                    yield
                    sy.op("act", lambda e: e.activation(out=carry[:, jj:jj + 1], in_=hh_[:, 511:512],
                                                        func=AF.Identity),
                          reads=[hhB], writes=[carryB[jj]])
                    sy.op("dve", lambda e: e.tensor_tensor(out=Lo[:, jj, :], in0=hh_[:], in1=zbt[p][:], op=ALU.mult),
                          reads=[hhB, zbtB[p]], writes=[LoB[jj]])

                for t in range(NB):
                    t0 = t * 512
                    prenorm(l, s, t0, lambda kc: hn[:, kc, :], hnB, sqt, sqB, tp, tpt)
                    dump("hn0", hn[:, 0, :], [hnB[0]])
                    active = []
                    for jj in range(8):
                        active += [conv_stage(jj), lru_stage(jj)]
                        steps = 0
                        while active and (jj == 7 or steps < EVEN_STAGGER):
                            for g_ in list(active):
                                try:
                                    next(g_)
                                except StopIteration:
                                    active.remove(g_)
                            steps += 1
                    acc = []
                    for n in range(6):
                        w1, w1B, wi1 = acquire(("ewout", j, n, 1))
                        pt, pB = nextps()
                        for k8 in range(8):
                            sy.op("pe", lambda e, k8=k8: e.matmul(pt[:, :], lhsT=w1[:, k8 * 128:(k8 + 1) * 128],
                                                                  rhs=Lo[:, k8, :], start=(k8 == 0), stop=False),
                                  reads=[w1B, LoB[k8]], writes=[pB], inc=(k8 == 7))
                        release(wi1)
                        acc.append((pt, pB))
                    sy.op("dve", lambda e: e.tensor_copy(out=sqt[:], in_=big[:]), reads=bigB, writes=[sqB])
                    pm, pmB = psum[0], psB[0]
                    for kc in range(KC):
                        sy.op("pe", lambda e, kc=kc: e.matmul(pm[:, :], lhsT=od1024[:], rhs=sqt[:, kc, :],
                                                              start=(kc == 0), stop=(kc == KC - 1)),
                              reads=[constB, sqB], writes=[pmB], inc=(kc == KC - 1))
                    sy.op("act", lambda e: e.activation(out=sqt[:], in_=big[:], func=AF.Square),
                          reads=bigB, writes=[sqB])
                    p2, p2B = psum[1], psB[1]
                    for kc in range(KC):
                        sy.op("pe", lambda e, kc=kc: e.matmul(p2[:, :], lhsT=od1024[:], rhs=sqt[:, kc, :],
                                                              start=(kc == 0), stop=(kc == KC - 1)),
                              reads=[constB, sqB], writes=[p2B], inc=(kc == KC - 1))
                    mean, meanB = tp()
                    sy.op("act", lambda e: e.activation(out=mean[:], in_=pm[:], func=AF.Identity), reads=[pmB],
                          writes=[meanB])
                    var, varB = tp()
                    sy.op("dve", lambda e: e.tensor_tensor(out=var[:], in0=mean[:], in1=mean[:], op=ALU.mult),
                          reads=[meanB], writes=[varB])
                    sy.op("dve", lambda e: e.tensor_tensor(out=var[:], in0=p2[:], in1=var[:], op=ALU.subtract),
                          reads=[p2B], writes=[varB])
                    sy.op("dve", lambda e: e.tensor_scalar(out=var[:], in0=var[:], scalar1=0.0, scalar2=None,
                                                           op0=ALU.max), writes=[varB])
                    sd, sdB = tp()
                    sy.op("act", lambda e: e.activation(out=sd[:], in_=var[:], func=AF.Ln, bias=EPS, scale=1.0),
                          reads=[varB], writes=[sdB])
                    rs, rsB = tp()
                    sy.op("act", lambda e: e.activation(out=rs[:], in_=sd[:], func=AF.Exp, scale=-0.5), reads=[sdB], writes=[rsB])
                    mr, mrB = tp()
                    sy.op("dve", lambda e: e.tensor_tensor(out=mr[:], in0=mean[:], in1=rs[:], op=ALU.mult),
                          reads=[meanB, rsB], writes=[mrB])
                    for jj in range(8):
                        t1, t1B = tpt()
                        sy.op("dve", lambda e: e.tensor_tensor(out=t1[:], in0=big[:, jj, :], in1=rs[:], op=ALU.mult),
                              reads=[bigB[jj], rsB], writes=[t1B])
                        sy.op("dve", lambda e: e.tensor_tensor(out=t1[:], in0=t1[:], in1=mr[:], op=ALU.subtract),
                              reads=[mrB], writes=[t1B])
                        s1, s1B = tpb()
                        sy.op("act", lambda e: e.activation(out=s1[:], in_=t1[:], func=AF.Silu,
                                                            scale=pv(("ln_g", j), jj), bias=pv(("ln_b", j), jj)),
                              reads=[t1B], sreads=[pvB], writes=[s1B])
                        sy.op("dve", lambda e: e.tensor_tensor(out=Z[:, jj, :], in0=s1[:], in1=Z[:, jj, :], op=ALU.mult),
                              reads=[s1B], writes=[ZB[jj]])
                    dump("Aout0", Z[:, 0, :], [ZB[0]])
                    dump("Lo0", Lo[:, 0, :], [LoB[0]])
                    for n in range(8):
                        if n < 6:
                            pt, pB = acc[n]
                            w0, w0B, wi0 = acquire(("ewout", j, n, 0))
                            for k8 in range(8):
                                sy.op("pe", lambda e, k8=k8: e.matmul(pt[:, :], lhsT=w0[:, k8 * 128:(k8 + 1) * 128],
                                                                      rhs=Z[:, k8, :], start=False, stop=(k8 == 7)),
                                      reads=[w0B, ZB[k8]], writes=[pB], inc=(k8 == 7))
                            release(wi0)
                        else:
                            w0, w0B, wi0 = acquire(("ewout", j, n, 0))
                            w1, w1B, wi1 = acquire(("ewout", j, n, 1))
                            pt, pB = nextps()
                            for kc in range(16):
                                wsrc, wsB = (w0, w0B) if kc < 8 else (w1, w1B)
                                src, srcB = (Z, ZB) if kc < 8 else (Lo, LoB)
                                k8 = kc % 8
                                sy.op("pe", lambda e, kc=kc, k8=k8, wsrc=wsrc, src=src: e.matmul(
                                    pt[:, :], lhsT=wsrc[:, k8 * 128:(k8 + 1) * 128], rhs=src[:, k8, :],
                                    start=(kc == 0), stop=(kc == 15)),
                                    reads=[wsB, srcB[k8]], writes=[pB], inc=(kc == 15))
                            release(wi0)
                            release(wi1)
                        sy.op("act", lambda e: e.activation(out=big[:, n, :], in_=pt[:], func=AF.Identity),
                              reads=[pB], writes=[bigB[n]])
                    dump("y0", big[:, 0, :], [bigB[0]])
                    postnorm(l, s, t0, big, bigB, sqt, sqB, tp, tpt)
                sy.fence()

        def odd_layer(l, s):
            j = l // 2
            with ExitStack() as ol:
                def sbl(name, shape, dt):
                    return ol.enter_context(nc.sbuf_tensor(f"{name}_{l}_{s}", shape, dt))
                Zs = sbl("o_Zs", [128, KC, S], BF16)
                ZsB = [[Buf(f"o_Zs{c}_{b}") for b in range(NB)] for c in range(KC)]
                cqn = sbl("o_cqn", [128, 2, S], BF16)
                cqnB = [Buf(f"o_cqn{b}") for b in range(NB)]
                ckvn = sbl("o_ckvn", [128, 2, S], BF16)
                ckvnB = [Buf(f"o_ckvn{b}") for b in range(NB)]
                kr = sbl("o_kr", [128, S], BF16)
                krB = Buf("o_kr")
                COS = sbl("o_cos", [128, S], BF16)
                SIN = sbl("o_sin", [128, S], BF16)
                csB = Buf("o_cs")
                tp = mk_tmp_pool(ol, "o_tf", 4, F32)
                tp_stats[0] = mk_tmp_pool(ol, "o_ts", 3, F32)

                with ExitStack() as rl:
                    ang = rl.enter_context(nc.sbuf_tensor(f"o_ang_{l}_{s}", [128, S], F32))
                    wk = rl.enter_context(nc.sbuf_tensor(f"o_wk_{l}_{s}", [128, S], F32))
                    wk2 = rl.enter_context(nc.sbuf_tensor(f"o_wk2_{l}_{s}", [128, S], F32))
                    ki = rl.enter_context(nc.sbuf_tensor(f"o_ki_{l}_{s}", [128, S], I32))
                    posi = ki
                    rB = Buf("o_rope")
                    R = slice(64, 96)
                    src = AP(pos_d, s * S, [[0, 32], [1, S]])
                    sy.dma("sp", posi[R, :], src, writes=[rB])
                    sy.op("dve", lambda e: e.tensor_copy(out=ang[R, :], in_=posi[R, :]), reads=[rB], writes=[rB])
                    sy.op("dve", lambda e: e.tensor_scalar(out=ang[R, :], in0=ang[R, :], scalar1=pvt[R, pcols["inv"]:pcols["inv"] + 1],
                                                           scalar2=None, op0=ALU.mult), sreads=[pvB], writes=[rB])
                    for which in range(2):
                        if which == 0:
                            sy.op("dve", lambda e: e.tensor_scalar(out=wk2[R, :], in0=ang[R, :], scalar1=math.pi / 2,
                                                                   scalar2=None, op0=ALU.add), writes=[rB])
                            a_in = wk2
                        else:
                            a_in = ang
                        sy.op("dve", lambda e: e.tensor_scalar(out=wk[R, :], in0=a_in[R, :], scalar1=1.0 / TWO_PI,
                                                               scalar2=None, op0=ALU.mult), writes=[rB])
                        sy.op("dve", lambda e: e.tensor_copy(out=ki[R, :], in_=wk[R, :]), writes=[rB])
                        sy.op("dve", lambda e: e.tensor_copy(out=wk[R, :], in_=ki[R, :]), writes=[rB])
                        sy.op("dve", lambda e: e.scalar_tensor_tensor(out=wk2[R, :], in0=wk[R, :], scalar=-PI_HI,
                                                                      in1=a_in[R, :], op0=ALU.mult, op1=ALU.add),
                              writes=[rB])
                        sy.op("dve", lambda e: e.scalar_tensor_tensor(out=wk2[R, :], in0=wk[R, :], scalar=-PI_LO,
                                                                      in1=wk2[R, :], op0=ALU.mult, op1=ALU.add),
                              writes=[rB])
                        sy.op("dve", lambda e: e.tensor_scalar(out=wk[R, :], in0=wk2[R, :], scalar1=math.pi,
                                                               scalar2=-TWO_PI, op0=ALU.is_gt, op1=ALU.mult), writes=[rB])
                        sy.op("dve", lambda e: e.tensor_tensor(out=wk2[R, :], in0=wk2[R, :], in1=wk[R, :], op=ALU.add),
                              writes=[rB])
                        sy.op("dve", lambda e: e.tensor_scalar(out=wk[R, :], in0=wk2[R, :], scalar1=-math.pi,
                                                               scalar2=TWO_PI, op0=ALU.is_lt, op1=ALU.mult), writes=[rB])
                        sy.op("dve", lambda e: e.tensor_tensor(out=wk2[R, :], in0=wk2[R, :], in1=wk[R, :], op=ALU.add),
                              writes=[rB])
                        sy.op("dve", lambda e: e.tensor_scalar(out=wk2[R, :], in0=wk2[R, :], scalar1=3.1415925,
                                                               scalar2=-3.1415925, op0=ALU.min, op1=ALU.max), writes=[rB])
                        if which == 0:
                            sy.op("act", lambda e: e.activation(out=COS[R, :], in_=wk2[R, :], func=AF.Sin),
                                  reads=[rB], writes=[csB])
                        else:
                            sy.op("dve", lambda e: e.tensor_scalar(out=wk2[R, :], in0=wk2[R, :],
                                                                   scalar1=pvt[R, pcols["sgn"]:pcols["sgn"] + 1],
                                                                   scalar2=None, op0=ALU.mult), sreads=[pvB], writes=[rB])
                            sy.op("act", lambda e: e.activation(out=SIN[R, :], in_=wk2[R, :], func=AF.Sin),
                                  reads=[rB], writes=[csB])
                    sy.fence()

                with ExitStack() as p1:
                    hn = p1.enter_context(nc.sbuf_tensor(f"o_hn_{l}_{s}", [128, KC, 1024], BF16))
                    hnB2 = [[Buf(f"o_hn{c}_{b}") for c in range(KC)] for b in range(2)]
                    raw = p1.enter_context(nc.sbuf_tensor(f"o_raw_{l}_{s}", [128, 2, 1024], F32))
                    rawB = [Buf(f"o_raw{b}") for b in range(2)]
                    krA = p1.enter_context(nc.sbuf_tensor(f"o_krA_{l}_{s}", [128, 1024], F32))
                    krAB = [Buf(f"o_krA{b}") for b in range(2)]
                    sqt = p1.enter_context(nc.sbuf_tensor(f"o_sq_{l}_{s}", [128, KC, 512], BF16))
                    sqB = Buf("o_sq")
                    R = slice(64, 96)
                    for t in range(S // 1024):
                        for b in range(2):
                            t0 = t * 1024 + b * 512
                            prenorm(l, s, t0, lambda kc, b=b: hn[:, kc, b * 512:(b + 1) * 512], hnB2[b], sqt, sqB, tp)
                        for grp, (dst, dstB, nrm) in enumerate(((cqn, cqnB, "q_norm"), (ckvn, ckvnB, "kv_norm"))):
                            for c in range(2):
                                wt, wB, wi = acquire(("owin", j, grp * 2 + c))
                                for b in range(2):
                                    pt, pB = mm8(wt, wB, lambda kc, b=b: hn[:, kc, b * 512:(b + 1) * 512], hnB2[b])
                                    sy.op("act", lambda e: e.activation(out=raw[:, c, b * 512:(b + 1) * 512], in_=pt[:],
                                                                        func=AF.Identity),
                                          reads=[pB], writes=[rawB[b]])
                                release(wi)
                            for b in range(2):
                                gb = t * 2 + b
                                rs, rsB = rms_stats(lambda b=b: raw[:, :, b * 512:(b + 1) * 512], [rawB[b]], 2, od256,
                                                    sqt, sqB, tp)
                                for c in range(2):
                                    sy.op("dve", lambda e, c=c: e.scalar_tensor_tensor(
                                        out=dst[:, c, gb * 512:(gb + 1) * 512], in0=raw[:, c, b * 512:(b + 1) * 512],
                                        scalar=pv((nrm, j), c), in1=rs[:], op0=ALU.mult, op1=ALU.mult),
                                        reads=[rawB[b], rsB], sreads=[pvB], writes=[dstB[gb]])
                        wt, wB, wi = acquire(("owin", j, 4))
                        for b in range(2):
                            pt, pB = mm8(wt, wB, lambda kc, b=b: hn[:, kc, b * 512:(b + 1) * 512], hnB2[b])
                            sy.op("act", lambda e: e.activation(out=krA[R, b * 512:(b + 1) * 512], in_=pt[R, :],
                                                                func=AF.Identity), reads=[pB], writes=[krAB[b]])
                        release(wi)
                        wt, wB, wi = acquire(("owin", j, 5))
                        for b in range(2):
                            gb = t * 2 + b
                            tk = slice(gb * 512, (gb + 1) * 512)
                            pt, pB = mm8(wt, wB, lambda kc, b=b: hn[:, kc, b * 512:(b + 1) * 512], hnB2[b])
                            t1, t1B = tp()
                            sy.op("dve", lambda e: e.tensor_tensor(out=t1[R, :], in0=krA[R, b * 512:(b + 1) * 512],
                                                                   in1=COS[R, tk], op=ALU.mult),
                                  reads=[krAB[b], csB], writes=[t1B])
                            t2, t2B = tp()
                            sy.op("dve", lambda e: e.tensor_tensor(out=t2[R, :], in0=pt[R, :], in1=SIN[R, tk], op=ALU.mult),
                                  reads=[pB, csB], writes=[t2B])
                            sy.op("dve", lambda e: e.tensor_tensor(out=kr[R, tk], in0=t1[R, :], in1=t2[R, :], op=ALU.add),
                                  reads=[t1B, t2B], writes=[krB])
                        release(wi)
                        for c in range(8):
                            wt, wB, wi = acquire(("owin", j, 6 + c))
                            for b in range(2):
                                gb = t * 2 + b
                                pt, pB = mm8(wt, wB, lambda kc, b=b: hn[:, kc, b * 512:(b + 1) * 512], hnB2[b])
                                sy.op("act", lambda e: e.activation(out=Zs[:, c, gb * 512:(gb + 1) * 512], in_=pt[:],
                                                                    func=AF.Silu), reads=[pB], writes=[ZsB[c][gb]])
                            release(wi)
                    dump("cqn", cqn[:, 0, 0:512], [cqnB[0]])
                    dump("ckvn", ckvn[:, 0, 0:512], [ckvnB[0]])
                    dump("kr", kr[:, 0:512], [krB])
                    dump("cos", COS[:, 0:512], [csB])
                    dump("sin", SIN[:, 0:512], [csB])
                    dump("zs", Zs[:, 0, 0:512], [ZsB[0][0]])
                    sy.fence()

                with ExitStack() as p2:
                    QT = [p2.enter_context(nc.sbuf_tensor(f"o_QT{i}_{l}_{s}", [128, S], BF16)) for i in range(2)]
                    KT = [p2.enter_context(nc.sbuf_tensor(f"o_KT{i}_{l}_{s}", [128, S], BF16)) for i in range(2)]
                    V = [p2.enter_context(nc.sbuf_tensor(f"o_V{i}_{l}_{s}", [128, S // 128, 128], BF16)) for i in range(2)]
                    QTB = [Buf(f"o_QT{i}") for i in range(2)]
                    KTB = [Buf(f"o_KT{i}") for i in range(2)]
                    VB = [Buf(f"o_V{i}") for i in range(2)]
                    NPT = 6
                    PT = [p2.enter_context(nc.sbuf_tensor(f"o_PT{i}_{l}_{s}", [128, 512], BF16)) for i in range(NPT)]
                    PTB = [Buf(f"o_PT{i}") for i in range(NPT)]
                    rec = p2.enter_context(nc.sbuf_tensor(f"o_rec_{l}_{s}", [128, 512], F32))
                    recB = Buf("o_rec")
                    tpg = mk_tmp_pool(p2, "o_tg", 2, F32)
                    R = slice(64, 96)
                    sy.op("dve", lambda e: e.memset(V[0][:, :, 64:128], 1.0), writes=[VB[0]])
                    sy.op("dve", lambda e: e.memset(V[1][:, :, 0:64], 1.0), writes=[VB[1]])
                    pt_i = {"i": 0}

                    def gen_head(h):
                        par = h % 2
                        qt, qB = QT[par], QTB[par]
                        kt, kB = KT[par], KTB[par]
                        vt, vB = V[par], VB[par]
                        wt, wB, wi = acquire(("ouq", j, h))
                        for b in range(NB):
                            tk = slice(b * 512, (b + 1) * 512)
                            pa, paB = nextps()
                            pb, pbB = nextps()
                            for kc in range(2):
                                sy.op("pe", lambda e, kc=kc: e.matmul(pa[0:96, :], lhsT=wt[:, kc * 192:kc * 192 + 96],
                                                                      rhs=cqn[:, kc, tk], start=(kc == 0), stop=(kc == 1)),
                                      reads=[wB, cqnB[b]], writes=[paB], inc=(kc == 1))
                            for kc in range(2):
                                sy.op("pe", lambda e, kc=kc: e.matmul(pb[0:96, :], lhsT=wt[:, kc * 192 + 96:kc * 192 + 192],
                                                                      rhs=cqn[:, kc, tk], start=(kc == 0), stop=(kc == 1)),
                                      reads=[wB, cqnB[b]], writes=[pbB], inc=(kc == 1))
                            yield
                            sy.op("act", lambda e: e.activation(out=qt[0:64, tk], in_=pa[0:64, :], func=AF.Identity),
                                  reads=[paB], writes=[qB])
                            t1, t1B = tpg()
                            sy.op("dve", lambda e: e.tensor_tensor(out=t1[R, :], in0=pa[R, :], in1=COS[R, tk], op=ALU.mult),
                                  reads=[paB, csB], writes=[t1B])
                            t2, t2B = tpg()
                            sy.op("dve", lambda e: e.tensor_tensor(out=t2[R, :], in0=pb[R, :], in1=SIN[R, tk], op=ALU.mult),
                                  reads=[pbB, csB], writes=[t2B])
                            yield
                            sy.op("dve", lambda e: e.tensor_tensor(out=qt[R, tk], in0=t1[R, :], in1=t2[R, :], op=ALU.add),
                                  reads=[t1B, t2B], writes=[qB])
                            yield
                        release(wi)
                        wt, wB, wi = acquire(("oukv", j, h))
                        for b in range(NB):
                            tk = slice(b * 512, (b + 1) * 512)
                            pa, paB = nextps()
                            for kc in range(2):
                                sy.op("pe", lambda e, kc=kc: e.matmul(pa[0:64, :], lhsT=wt[:, kc * 128:kc * 128 + 64],
                                                                      rhs=ckvn[:, kc, tk], start=(kc == 0), stop=(kc == 1)),
                                      reads=[wB, ckvnB[b]], writes=[paB], inc=(kc == 1))
                            yield
                            sy.op("act", lambda e: e.activation(out=kt[0:64, tk], in_=pa[0:64, :], func=AF.Identity),
                                  reads=[paB], writes=[kB])
                            yield
                        sy.op("pool", lambda e: e.tensor_copy(out=kt[R, :], in_=kr[R, :]), reads=[krB], writes=[kB])
                        vo = 0 if par == 0 else 64
                        for g8 in range(S // 1024):
                            pa, paB = nextps()
                            for i8 in range(8):
                                kb = g8 * 8 + i8
                                for kc in range(2):
                                    sy.op("pe", lambda e, kc=kc, kb=kb, i8=i8: e.matmul(
                                        pa[:, i8 * 64:(i8 + 1) * 64], lhsT=ckvn[:, kc, kb * 128:(kb + 1) * 128],
                                        rhs=wt[:, kc * 128 + 64:kc * 128 + 128], start=(kc == 0), stop=(kc == 1)),
                                        reads=[wB, ckvnB[kb // 4]], writes=[paB], inc=(kc == 1 and i8 == 7))
                            yield
                            sy.op("act", lambda e: e.activation(
                                out=vt[:, g8 * 8:(g8 + 1) * 8, vo:vo + 64],
                                in_=pa[:, :].rearrange("p (a b) -> p a b", b=64), func=AF.Identity),
                                reads=[paB], writes=[vB])
                            yield
                        release(wi)

                    def attn_head(h):
                        par = h % 2
                        hp = h // 2
                        qt, qB = QT[par], QTB[par]
                        kt, kB = KT[par], KTB[par]
                        vt, vB = V[par], VB[par]
                        if h == 0:
                            dump("qt", qt[:, 0:512], [qB])
                            dump("kt", kt[:, 0:512], [kB])
                            dump("v0", vt[:, 0, :], [vB])
                        items = []
                        for g in range(NB):
                            for kb in range(4 * g + 4):
                                items.append((g, kb))
                        LA = 3
                        inflight = {}
                        for i in range(len(items) + LA):
                            if i < len(items):
                                g, kb = items[i]
                                d = kb - 4 * g
                                c0 = max(0, d) * 128
                                ncols = 512 - c0
                                sp_, spB = nextps()
                                sy.op("pe", lambda e, kb=kb, g=g, c0=c0, ncols=ncols, sp_=sp_: e.matmul(
                                    sp_[:, 0:ncols], lhsT=kt[0:96, kb * 128:(kb + 1) * 128],
                                    rhs=qt[0:96, g * 512 + c0:(g + 1) * 512], start=True, stop=(d < 0)),
                                    reads=[kB, qB], writes=[spB], inc=(d < 0))
                                if d >= 0:
                                    sy.op("pe", lambda e, sp_=sp_: e.matmul(sp_[:, 0:128], lhsT=ident_b[:], rhs=maskT[:],
                                                                            start=False, stop=True),
                                          reads=[constB], writes=[spB])
                                inflight[i] = (sp_, spB, c0, ncols)
                            if i >= LA:
                                ii = i - LA
                                g, kb = items[ii]
                                sp_, spB, c0, ncols = inflight.pop(ii)
                                pi = pt_i["i"] % NPT
                                pt_i["i"] += 1
                                ptile, ptB = PT[pi], PTB[pi]
                                sy.op("act", lambda e, sp_=sp_, ncols=ncols, ptile=ptile: e.activation(
                                    out=ptile[:, 0:ncols], in_=sp_[:, 0:ncols], func=AF.Exp, scale=ATT_SCALE),
                                    reads=[spB], writes=[ptB])
                                op_, opB = psum[g % 2], psB[g % 2]
                                nkb = 4 * g + 4
                                sy.op("pe", lambda e, kb=kb, c0=c0, ncols=ncols, ptile=ptile, op_=op_, nkb=nkb: e.matmul(
                                    op_[:, c0:512], lhsT=vt[:, kb, :], rhs=ptile[:, 0:ncols],
                                    start=(kb == 0), stop=(kb == nkb - 1)),
                                    reads=[vB, ptB], writes=[opB], inc=True)
                                if kb == nkb - 1:
                                    if par == 0:
                                        num, den = slice(0, 64), slice(64, 128)
                                    else:
                                        num, den = slice(64, 128), slice(0, 64)
                                    tk = slice(g * 512, (g + 1) * 512)
                                    sy.op("act", lambda e, op_=op_: e.activation(out=rec[num, :], in_=op_[den, :], func=AF.Ln),
                                          reads=[opB], writes=[recB])
                                    sy.op("act", lambda e: e.activation(out=rec[num, :], in_=rec[num, :], func=AF.Exp, scale=-1.0),
                                          writes=[recB])
                                    o1, o1B = tp()
                                    sy.op("dve", lambda e, op_=op_, o1=o1: e.tensor_tensor(out=o1[num, :], in0=op_[num, :],
                                                                                         in1=rec[num, :], op=ALU.mult),
                                          reads=[opB, recB], writes=[o1B])
                                    sy.op("dve", lambda e, o1=o1: e.tensor_tensor(out=Zs[num, hp, tk], in0=o1[num, :],
                                                                                in1=Zs[num, hp, tk], op=ALU.mult),
                                          reads=[o1B], writes=[ZsB[hp][g]])
                            yield

                    for _ in gen_head(0):
                        pass
                    for h in range(16):
                        gens = [attn_head(h)]
                        if h + 1 < 16:
                            gens.append(gen_head(h + 1))
                        while gens:
                            for g_ in list(gens):
                                try:
                                    next(g_)
                                except StopIteration:
                                    gens.remove(g_)
                    sy.fence()

                dump("og", Zs[:, 0, 0:512], [ZsB[0][0]])
                with ExitStack() as p3:
                    big = p3.enter_context(nc.sbuf_tensor(f"o_big_{l}_{s}", [128, KC, 512], F32))
                    bigB = [Buf(f"o_big{c}") for c in range(KC)]
                    sqt = p3.enter_context(nc.sbuf_tensor(f"o_sq3_{l}_{s}", [128, KC, 512], BF16))
                    sqB = Buf("o_sq3")
                    wl = [acquire(("owout", j, n)) for n in range(8)]
                    for b in range(NB):
                        tk = slice(b * 512, (b + 1) * 512)
                        for n in range(8):
                            wt, wB, _ = wl[n]
                            pt, pB = mm8(wt, wB, lambda kc: Zs[:, kc, tk], [ZsB[c][b] for c in range(KC)])
                            sy.op("act", lambda e: e.activation(out=big[:, n, :], in_=pt[:], func=AF.Identity),
                                  reads=[pB], writes=[bigB[n]])
                        postnorm(l, s, b * 512, big, bigB, sqt, sqB, tp)
                    for _w in wl:
                        release(_w[2])
                    sy.fence()
                tp_stats[0] = None

        allxs = [xsB[c][b] for c in range(KC) for b in range(NB)]
        outB = Buf("outst")
        for s in range(NSEQ):
            for c in range(KC):
                sy.dma("sp", xs[:, c, :], x_d.ap()[s, :, c, :], writes=xsB[c])
            for l in LAYERS:
                if l % 2 == 0:
                    even_layer(l, s)
                else:
                    odd_layer(l, s)
            for c in range(KC):
                sy.dma("sp", out_d.ap()[s, :, c, :], xs[:, c, :], reads=xsB[c], writes=[outB])
        nc.sync.wait_ge(outB.dsem, outB.dcnt)
        if DEBUG and dbgB.dsem is not None:
            nc.sync.wait_ge(dbgB.dsem, dbgB.dcnt)
        if RECORD:
            return ws["order"]
        assert ws["acq"] == len(ws["order"]) == ws["issued"], (ws["acq"], len(ws["order"]), ws["issued"])
        build.nins = sy.nins
    return nc


_CACHE = {}


def kernel(**inp):
    inp = {k: np.asarray(v) for k, v in inp.items()}
    x = inp["x"].astype(np.float32, copy=False)
    B, S, Dm = x.shape
    nseq = B // NCORES
    wts = pack_weights(inp)
    pvv = pack_pv(inp)
    key = (S, nseq)
    if key not in _CACHE:
        _CACHE[key] = build(S=S, NSEQ=nseq)
    nc = _CACHE[key]
    in_maps = []
    for cid in range(NCORES):
        bs = slice(cid * nseq, (cid + 1) * nseq)
        xf = np.ascontiguousarray(x[bs].reshape(nseq, S, KC, 128).transpose(0, 3, 2, 1))
        cf = np.ascontiguousarray(inp["c"][bs].astype(np.float32).reshape(nseq, KC, 128).transpose(2, 1, 0))
        pos = np.ascontiguousarray(inp["positions"][bs].astype(np.int32))
        in_maps.append({"x": xf, "c": cf, "pos": pos, "wts": wts, "pv": pvv})
    res = run_bass_kernel_spmd(nc, in_maps, core_ids=list(range(NCORES)))
    outs = []
    for cid in range(NCORES):
        o = np.asarray(res.results[cid]["out"]).reshape(nseq, 128, KC, S)
        outs.append(o.transpose(0, 3, 2, 1).reshape(nseq, S, Dm))
    return np.ascontiguousarray(np.concatenate(outs, axis=0).astype(np.float32))
```

```python
import math
from contextlib import ExitStack
import numpy as np
import concourse.bass as bass
import concourse.mybir as mybir
from concourse.ap import AP
from concourse.bass_utils import run_bass_kernel_spmd

F32 = mybir.dt.float32
BF16 = mybir.dt.bfloat16
I32 = mybir.dt.int32
AF = mybir.ActivationFunctionType
ALU = mybir.AluOpType

D = 1024
KC = 8
NCORES = 8
EPS = 1e-6
SLOT = 1024
RING = 10
SAME_ENG_WINDOW = 3
EVEN_STAGGER = 9
ATT_SCALE = 96.0 ** -0.5
TWO_PI = 2.0 * math.pi
PI_HI = 6.28125
PI_LO = TWO_PI - 6.28125


def weight_plan():
    plan = {}
    off = 0

    def add(name, F):
        nonlocal off
        plan[name] = (off, F)
        off += 128 * F

    for l in range(4):
        for n in range(24):
            add(("ada", l, n), 1024)
    for j in range(2):
        for c in range(40):
            add(("ewin", j, c), 1024)
        for kc in range(8):
            add(("egate", j, kc), 256)
        for n in range(8):
            add(("ewout", j, n, 0), 1024)
            add(("ewout", j, n, 1), 1024)
    for j in range(2):
        for c in range(14):
            add(("owin", j, c), 1024)
        for h in range(16):
            add(("ouq", j, h), 384)
            add(("oukv", j, h), 256)
        for n in range(8):
            add(("owout", j, n), 1024)
    return plan, off


def pv_plan():
    cols = {}
    off = 0

    def add(name, n):
        nonlocal off
        cols[name] = off
        off += n

    for l in range(4):
        add(("pre_g", l), 8)
        add(("post_g", l), 8)
        add(("ada_b", l), 24)
    for j in range(2):
        add(("conv_w", j), 8 * 31)
        add(("conv_b", j), 8)
        add(("ln_g", j), 8)
        add(("ln_b", j), 8)
        add(("lconv_w", j), 8 * 4)
        add(("lconv_b", j), 8)
        add(("ba", j), 8)
        add(("bx", j), 8)
        add(("lam", j), 8)
        add(("q_norm", j), 2)
        add(("kv_norm", j), 2)
    add("inv", 1)
    add("sgn", 1)
    return cols, off


def chunkify(W):
    K, N = W.shape
    return np.ascontiguousarray(W.reshape(K // 128, 128, N // 128, 128).transpose(2, 1, 0, 3))


def vec_pc(v):
    return np.ascontiguousarray(v.reshape(-1, 128).T)


def pack_weights(inp):
    plan, total = weight_plan()
    flat = np.zeros(total, np.float32)

    def put(name, arr):
        off, F = plan[name]
        a = np.ascontiguousarray(arr, dtype=np.float32).reshape(128, F)
        flat[off:off + 128 * F] = a.reshape(-1)

    for l in range(4):
        ch = chunkify(inp["ada_w"][l])
        for n in range(24):
            put(("ada", l, n), ch[n])
    for j in range(2):
        ch = chunkify(inp["ev_w_in"][j])
        for c in range(40):
            put(("ewin", j, c), ch[c])
        wa, wx = inp["ev_lru_wa"][j], inp["ev_lru_wx"][j]
        for kc in range(8):
            g = np.zeros((128, 2, 128), np.float32)
            for hh in range(2):
                g[hh * 64:(hh + 1) * 64, 0, hh * 64:(hh + 1) * 64] = wa[2 * kc + hh]
                g[hh * 64:(hh + 1) * 64, 1, hh * 64:(hh + 1) * 64] = wx[2 * kc + hh]
            put(("egate", j, kc), g)
        wo = inp["ev_w_out"][j]
        c0 = chunkify(wo[:1024])
        c1 = chunkify(wo[1024:])
        for n in range(8):
            put(("ewout", j, n, 0), c0[n])
            put(("ewout", j, n, 1), c1[n])
    for j in range(2):
        w = inp["od_w_in"][j]
        z64 = np.zeros((1024, 64), np.float32)
        z32 = np.zeros((1024, 32), np.float32)
        kr = w[:, 512:544]
        kr1 = np.concatenate([z64, kr, z32], axis=1)
        kr2 = np.concatenate([z64, kr[:, 16:32], kr[:, 0:16], z32], axis=1)
        wcat = np.concatenate([w[:, 0:512], kr1, kr2, w[:, 544:1568]], axis=1)
        ch = chunkify(wcat)
        for c in range(14):
            put(("owin", j, c), ch[c])
        uq = inp["od_w_uq"][j].reshape(2, 128, 16, 96)
        ukv = inp["od_w_ukv"][j].reshape(2, 128, 16, 128)
        for h in range(16):
            a = uq[:, :, h, :]
            sw = np.concatenate([a[:, :, 0:64], a[:, :, 80:96], a[:, :, 64:80]], axis=2)
            both = np.concatenate([a, sw], axis=2)
            put(("ouq", j, h), both.transpose(1, 0, 2))
            put(("oukv", j, h), ukv[:, :, h, :].transpose(1, 0, 2))
        ch = chunkify(inp["od_w_out"][j])
        for n in range(8):
            put(("owout", j, n), ch[n])
    return flat


def pack_pv(inp):
    cols, n = pv_plan()
    pv = np.zeros((128, n), np.float32)

    def put(name, arr):
        a = np.asarray(arr, np.float32)
        pv[:, cols[name]:cols[name] + a.shape[1]] = a

    for l in range(4):
        put(("pre_g", l), vec_pc(inp["pre_g"][l]))
        put(("post_g", l), vec_pc(inp["post_g"][l]))
        put(("ada_b", l), vec_pc(inp["ada_b"][l]))
    for j in range(2):
        cw = inp["ev_conv_w"][j]
        put(("conv_w", j), cw.reshape(31, 8, 128).transpose(2, 1, 0).reshape(128, 8 * 31))
        put(("conv_b", j), vec_pc(inp["ev_conv_b"][j]))
        put(("ln_g", j), vec_pc(inp["ev_ln_g"][j]))
        put(("ln_b", j), vec_pc(inp["ev_ln_b"][j]))
        lw = inp["ev_lru_conv_w"][j]
        put(("lconv_w", j), lw.reshape(4, 8, 128).transpose(2, 1, 0).reshape(128, 8 * 4))
        put(("lconv_b", j), vec_pc(inp["ev_lru_conv_b"][j]))
        put(("ba", j), vec_pc(inp["ev_lru_ba"][j]))
        put(("bx", j), vec_pc(inp["ev_lru_bx"][j]))
        put(("lam", j), vec_pc(inp["ev_lru_lam"][j]))
        put(("q_norm", j), vec_pc(inp["od_q_norm"][j]))
        put(("kv_norm", j), vec_pc(inp["od_kv_norm"][j]))
    inv = (10000.0 ** (-np.arange(0, 32, 2, dtype=np.float32) / 32.0)).astype(np.float32)
    iv = np.zeros((128, 1), np.float32)
    sg = np.ones((128, 1), np.float32)
    for i in range(32):
        iv[64 + i, 0] = inv[i % 16]
        sg[64 + i, 0] = -1.0 if i < 16 else 1.0
    put("inv", iv)
    put("sgn", sg)
    return pv


_UID = [0]


class Buf:
    __slots__ = ("name", "w", "r", "dsem", "dcnt")

    def __init__(self, name):
        _UID[0] += 1
        self.name = f"{name}_{_UID[0]}"
        self.w = None
        self.r = {}
        self.dsem = None
        self.dcnt = 0


class Sync:
    ENG = ("pe", "act", "dve", "pool", "sp")

    def __init__(self, nc, es):
        self.nc = nc
        self.es = es
        self.eng = {"pe": nc.tensor, "act": nc.scalar, "dve": nc.vector, "pool": nc.gpsimd, "sp": nc.sync}
        self.sem = {k: es.enter_context(nc.semaphore("s_" + k)) for k in self.ENG}
        self.cnt = {k: 0 for k in self.ENG}
        self.known = {k: {} for k in self.ENG}
        self.nins = 0

    def _need(self, E, dep, strict):
        key, sem, val, src = dep
        if src == E and not strict and E != "pool":
            if E == "pe" or (self.cnt[E] - val) >= SAME_ENG_WINDOW:
                return
        if self.known[E].get(key, 0) >= val:
            return
        self.eng[E].wait_ge(sem, val)
        self.known[E][key] = val
        self.nins += 1

    def _deps(self, E, reads, writes, sreads, strict_all=False):
        for b in reads:
            if b.w is not None:
                self._need(E, b.w, strict_all)
        for b in sreads:
            if b.w is not None:
                self._need(E, b.w, True)
        for b in writes:
            if b.w is not None:
                self._need(E, b.w, strict_all)
            for d in b.r.values():
                self._need(E, d, strict_all)

    def op(self, E, fn, reads=(), writes=(), sreads=(), inc=True):
        self._deps(E, reads, writes, sreads)
        ins = fn(self.eng[E])
        self.nins += 1
        if inc:
            self.cnt[E] += 1
            ins.then_inc(self.sem[E], 1)
            me = (E, self.sem[E], self.cnt[E], E)
        else:
            me = (E, self.sem[E], self.cnt[E] + 1, E)
        for b in writes:
            b.w = me
            b.r = {}
        for b in reads:
            b.r[E] = me
        for b in sreads:
            b.r[E] = me
        return ins

    def dma(self, Q, out, in_, reads=(), writes=()):
        self._deps(Q, reads, writes, (), strict_all=True)
        tgt = writes[0] if writes else reads[0]
        if tgt.dsem is None:
            tgt.dsem = self.es.enter_context(self.nc.semaphore("d_" + tgt.name))
        tgt.dcnt += 16
        self.eng[Q].dma_start(out=out, in_=in_).then_inc(tgt.dsem, 16)
        self.nins += 1
        me = ("d_" + tgt.name, tgt.dsem, tgt.dcnt, None)
        for b in writes:
            b.w = me
            b.r = {}
        for b in reads:
            b.r["dma_" + tgt.name] = me
        return me

    def fence(self):
        for E in ("pe", "act", "dve", "pool", "sp"):
            for Fg in ("pe", "act", "dve", "pool"):
                if Fg != E and self.cnt[Fg] > 0:
                    self._need(E, (Fg, self.sem[Fg], self.cnt[Fg], Fg), True)


def build(S=2048, NSEQ=2, LAYERS=(0, 1, 2, 3), DEBUG=False):
    order = _build(S, NSEQ, LAYERS, DEBUG, None)
    return _build(S, NSEQ, LAYERS, DEBUG, order)


def _build(S, NSEQ, LAYERS, DEBUG, ORDER):
    RECORD = ORDER is None
    nc = bass.Bass("TRN2", target_bir_lowering=False)
    plan, wtotal = weight_plan()
    pcols, npv = pv_plan()
    NB = S // 512

    x_d = nc.dram_tensor("x", [NSEQ, 128, KC, S], F32, kind="ExternalInput")
    c_d = nc.dram_tensor("c", [128, KC, NSEQ], F32, kind="ExternalInput")
    pos_d = nc.dram_tensor("pos", [NSEQ, S], I32, kind="ExternalInput")
    w_d = nc.dram_tensor("wts", [wtotal], F32, kind="ExternalInput")
    pv_d = nc.dram_tensor("pv", [128, npv], F32, kind="ExternalInput")
    out_d = nc.dram_tensor("out", [NSEQ, 128, KC, S], F32, kind="ExternalOutput")
    dgd = nc.dram_tensor("dgd", [2, 8, 128, 31 * 128], BF16, kind="Internal")

    dbg_d = nc.dram_tensor("dbg", [128, 16384], F32, kind="ExternalOutput") if DEBUG else None
    dbg_cols = {}
    build.dbg_cols = dbg_cols
    dbg_state = {"c": 0}

    with ExitStack() as es:
        sy = Sync(nc, es)
        dbgB = Buf("dbg")

        def dump(name, ap, bufs):
            if not DEBUG or name in dbg_cols:
                return
            n = ap.shape[-1]
            p0 = 0
            c0 = dbg_state["c"]
            dbg_cols[name] = (c0, n)
            dbg_state["c"] += n
            sy.dma("pool", dbg_d.ap()[0:ap.shape[0], c0:c0 + n], ap, reads=bufs, writes=[dbgB])

        def sb(name, shape, dt):
            return es.enter_context(nc.sbuf_tensor(name, shape, dt))

        xs = sb("xs", [128, KC, S], F32)
        xsB = [[Buf(f"xs{c}_{b}") for b in range(NB)] for c in range(KC)]
        pvt = sb("pvt", [128, npv], F32)
        pvB = Buf("pv")
        ring = sb("ring", [128, RING, SLOT], BF16)
        ringB = [Buf(f"ring{i}") for i in range(RING)]
        ident_f = sb("ident_f", [128, 128], F32)
        ident_b = sb("ident_b", [128, 128], BF16)
        od1024 = sb("od1024", [128, 128], BF16)
        od256 = sb("od256", [128, 128], BF16)
        maskT = sb("maskT", [128, 128], BF16)
        constB = Buf("const")
        cin = sb("cin", [128, KC, NSEQ], F32)
        cact = sb("cact", [128, KC, NSEQ], BF16)
        cB = Buf("c")
        modt = sb("modt", [128, 96, NSEQ], F32)
        modB = Buf("mod")
        drvA = sb("drvA", [128, 4, NSEQ, 8], F32)
        drvG = sb("drvG", [128, 4, NSEQ, 8], F32)
        drvB = Buf("drv")
        nsp = sb("nsp", [128, 2, 8], F32)
        nspB = Buf("nsp")
        carry = sb("carry", [128, 8], F32)
        carryB = [Buf(f"carry{j}") for j in range(8)]
        xbh = sb("xbh", [128, 8, 4], BF16)
        xbhB = [Buf(f"xbh{j}") for j in range(8)]

        psum = [es.enter_context(nc.psum_tensor(f"ps{i}", [128, 512], F32)) for i in range(8)]
        psB = [Buf(f"ps{i}") for i in range(8)]
        ps_state = {"i": 0}

        def nextps():
            i = 2 + ps_state["i"] % 6
            ps_state["i"] += 1
            return psum[i], psB[i]

        def pv(name, a, b=None):
            c0 = pcols[name]
            if b is None:
                return pvt[:, c0 + a:c0 + a + 1]
            return pvt[:, c0 + a:c0 + b]

        ws = {"order": [] if RECORD else ORDER, "issued": 0, "acq": 0, "rel": 0, "done": set()}

        def w_ap(name):
            off, Fw = plan[name]
            return AP(w_d, off, [[Fw, 128], [1, Fw]]), Fw

        def ws_issue():
            if RECORD:
                return
            while ws["issued"] < len(ws["order"]) and ws["issued"] < ws["rel"] + RING:
                i = ws["issued"]
                src, Fw = w_ap(ws["order"][i])
                slot = i % RING
                sy.dma("pool", ring[:, slot, 0:Fw], src, writes=[ringB[slot]])
                ws["issued"] += 1

        def acquire(name):
            i = ws["acq"]
            ws["acq"] += 1
            if RECORD:
                ws["order"].append(name)
                return ring[:, 0, :], ringB[0], i
            assert ws["order"][i] == name, (ws["order"][i], name)
            assert ws["issued"] > i, "weight ring deadlock (acquire beyond issued)"
            slot = i % RING
            return ring[:, slot, :], ringB[slot], i

        def release(i):
            ws["done"].add(i)
            while ws["rel"] in ws["done"]:
                ws["done"].remove(ws["rel"])
                ws["rel"] += 1
            ws_issue()

        sy.dma("sp", pvt[:], pv_d.ap()[:, :], writes=[pvB])
        sy.dma("sp", cin[:], c_d.ap()[:, :, :], writes=[cB])
        ws_issue()
        sy.op("pool", lambda e: e.memset(ident_f[:], 1.0), writes=[constB])
        sy.op("pool", lambda e: e.affine_select(out=ident_f[:], in_=ident_f[:], pattern=[[-1, 128]], base=0,
                                                channel_multiplier=1, compare_op=ALU.is_equal, fill=0.0),
              writes=[constB])
        sy.op("pool", lambda e: e.tensor_copy(out=ident_b[:], in_=ident_f[:]), writes=[constB])
        sy.op("pool", lambda e: e.memset(od1024[:], 1.0 / 1024), writes=[constB])
        sy.op("pool", lambda e: e.memset(od256[:], 1.0 / 256), writes=[constB])
        sy.op("pool", lambda e: e.memset(maskT[:], 0.0), writes=[constB])
        sy.op("pool", lambda e: e.affine_select(out=maskT[:], in_=maskT[:], pattern=[[1, 128]], base=0,
                                                channel_multiplier=-1, compare_op=ALU.is_ge, fill=-30000.0),
              writes=[constB])
        sy.op("act", lambda e: e.activation(out=cact[:], in_=cin[:], func=AF.Silu), reads=[cB], writes=[cB])

        dgdB = [[Buf(f"dgd{j}_{jj}") for jj in range(8)] for j in range(2)]
        with ExitStack() as st0:
            dgs = [st0.enter_context(nc.sbuf_tensor(f"dgs{i}", [128, 31, 128], BF16)) for i in range(2)]
            dgsB = [Buf(f"dgs{i}") for i in range(2)]
            for j in range(2):
                if (2 * j) not in LAYERS:
                    continue
                for jj in range(8):
                    i = jj % 2
                    cw0 = pcols[("conv_w", j)] + jj * 31
                    sy.op("pool", lambda e: e.tensor_tensor(
                        out=dgs[i][:], in0=ident_b[:].unsqueeze(1).to_broadcast([128, 31, 128]),
                        in1=pvt[:, cw0:cw0 + 31].unsqueeze(2).to_broadcast([128, 31, 128]), op=ALU.mult),
                        reads=[constB, pvB], writes=[dgsB[i]])
                    sy.dma("sp", dgd.ap()[j, jj, :, :], dgs[i][:].rearrange("p a b -> p (a b)"),
                           reads=[dgsB[i]], writes=[dgdB[j][jj]])
            sy.fence()
            for j in range(2):
                for jj in range(8):
                    if dgdB[j][jj].w is not None:
                        for E_ in ("pe", "act", "dve", "pool", "sp"):
                            sy._need(E_, dgdB[j][jj].w, True)

        for l in range(4):
            for n in range(24):
                wt, wB, wi = acquire(("ada", l, n))
                pt, pB = nextps()
                for kc in range(KC):
                    sy.op("pe", lambda e, kc=kc: e.matmul(pt[:, 0:NSEQ], lhsT=wt[:, kc * 128:(kc + 1) * 128],
                                                          rhs=cact[:, kc, :], start=(kc == 0), stop=(kc == KC - 1)),
                          reads=[wB, cB], writes=[pB], inc=(kc == KC - 1))
                release(wi)
                sy.op("act", lambda e: e.activation(out=modt[:, l * 24 + n, :], in_=pt[:, 0:NSEQ], func=AF.Identity,
                                                    bias=pv(("ada_b", l), n), scale=1.0),
                      reads=[pB], sreads=[pvB], writes=[modB])
        for l in range(4):
            for s in range(NSEQ):
                sy.op("dve", lambda e: e.tensor_scalar(out=drvA[:, l, s, :], in0=modt[:, l * 24 + 8:l * 24 + 16, s],
                                                       scalar1=1.0, scalar2=None, op0=ALU.add),
                      reads=[modB], writes=[drvB])
                sy.op("dve", lambda e: e.tensor_tensor(out=drvA[:, l, s, :], in0=drvA[:, l, s, :],
                                                       in1=pv(("pre_g", l), 0, 8), op=ALU.mult),
                      reads=[pvB], writes=[drvB])
                sy.op("dve", lambda e: e.tensor_tensor(out=drvG[:, l, s, :], in0=modt[:, l * 24 + 16:l * 24 + 24, s],
                                                       in1=pv(("post_g", l), 0, 8), op=ALU.mult),
                      reads=[pvB, modB], writes=[drvB])
        spt = [sb(f"spt{i}", [128, 8], F32) for i in range(4)]
        for j in range(2):
            lam_ap = pv(("lam", j), 0, 8)
            al, ee, ww, w2 = spt
            sy.op("act", lambda e: e.activation(out=al[:], in_=lam_ap, func=AF.Abs),
                  reads=[pvB], writes=[nspB])
            sy.op("act", lambda e: e.activation(out=ee[:], in_=al[:], func=AF.Exp, scale=-1.0),
                  reads=[nspB], writes=[nspB])
            sy.op("dve", lambda e: e.tensor_scalar(out=ww[:], in0=ee[:], scalar1=2.0, scalar2=None, op0=ALU.add),
                  reads=[nspB], writes=[nspB])
            sy.op("dve", lambda e: e.reciprocal(out=ww[:], in_=ww[:]), writes=[nspB])
            sy.op("dve", lambda e: e.tensor_tensor(out=ww[:], in0=ww[:], in1=ee[:], op=ALU.mult), writes=[nspB])
            sy.op("dve", lambda e: e.tensor_tensor(out=w2[:], in0=ww[:], in1=ww[:], op=ALU.mult), writes=[nspB])
            sy.op("dve", lambda e: e.tensor_scalar(out=al[:], in0=w2[:], scalar1=1.0 / 11, scalar2=1.0 / 9, op0=ALU.mult,
                                                   op1=ALU.add), writes=[nspB])
            for cf in (1.0 / 7, 1.0 / 5, 1.0 / 3, 1.0):
                sy.op("dve", lambda e: e.tensor_tensor(out=al[:], in0=al[:], in1=w2[:], op=ALU.mult), writes=[nspB])
                sy.op("dve", lambda e, cf=cf: e.tensor_scalar(out=al[:], in0=al[:], scalar1=cf, scalar2=None, op0=ALU.add),
                      writes=[nspB])
            sy.op("dve", lambda e: e.tensor_tensor(out=al[:], in0=al[:], in1=ww[:], op=ALU.mult), writes=[nspB])
            sy.op("dve", lambda e: e.tensor_scalar(out=ee[:], in0=lam_ap, scalar1=-1.0, scalar2=0.0, op0=ALU.mult,
                                                   op1=ALU.max), reads=[pvB], writes=[nspB])
            sy.op("dve", lambda e: e.scalar_tensor_tensor(out=al[:], in0=al[:], scalar=2.0, in1=ee[:], op0=ALU.mult,
                                                          op1=ALU.add), writes=[nspB])
            sy.op("dve", lambda e: e.tensor_scalar(out=nsp[:, j, :], in0=al[:], scalar1=-8.0, scalar2=None,
                                                   op0=ALU.mult), writes=[nspB])
        dump("modt", modt[:].rearrange("p a b -> p (a b)"), [modB])
        dump("drvA", drvA[:].rearrange("p a b c -> p (a b c)"), [drvB])
        dump("drvG", drvG[:].rearrange("p a b c -> p (a b c)"), [drvB])
        dump("nsp", nsp[:].rearrange("p a b -> p (a b)"), [nspB])
        tp_stats = [None]
        def mm8(wt, wB, rhs_fn, rB, M=128, wcol0=0, prow=None):
            pt, pB = nextps()
            for kc in range(KC):
                sy.op("pe", lambda e, kc=kc: e.matmul(pt[0:M, :], lhsT=wt[:, kc * 128 + wcol0:kc * 128 + wcol0 + M],
                                                      rhs=rhs_fn(kc), start=(kc == 0), stop=(kc == KC - 1)),
                      reads=[wB] + rB, writes=[pB], inc=(kc == KC - 1))
            return pt, pB

        def rms_stats(src_fn, srcB, nch, onesm, sqt, sqB, tp):
            tp = tp_stats[0] or tp
            sy.op("act", lambda e: e.activation(out=sqt[:, 0:nch, :], in_=src_fn(), func=AF.Square),
                  reads=srcB, writes=[sqB])
            pt, pB = nextps()
            for kc in range(nch):
                sy.op("pe", lambda e, kc=kc: e.matmul(pt[:, :], lhsT=onesm[:], rhs=sqt[:, kc, :], start=(kc == 0),
                                                      stop=(kc == nch - 1)),
                      reads=[constB, sqB], writes=[pB], inc=(kc == nch - 1))
            sd, sdB = tp()
            sy.op("act", lambda e: e.activation(out=sd[:], in_=pt[:], func=AF.Ln, bias=EPS, scale=1.0),
                  reads=[pB], writes=[sdB])
            rs, rsB = tp()
            sy.op("act", lambda e: e.activation(out=rs[:], in_=sd[:], func=AF.Exp, scale=-0.5), reads=[sdB], writes=[rsB])
            return rs, rsB

        def prenorm(l, s, t0, hn_fn, hnB, sqt, sqB, tp, tpt=None):
            tpt = tpt or tp
            b = t0 // 512
            rs, rsB = rms_stats(lambda: xs[:, :, t0:t0 + 512], [xsB[c][b] for c in range(KC)], KC, od1024, sqt, sqB, tp)
            for kc in range(KC):
                tt, ttB = tpt()
                sy.op("dve", lambda e: e.tensor_tensor(out=tt[:], in0=xs[:, kc, t0:t0 + 512], in1=rs[:], op=ALU.mult),
                      reads=[xsB[kc][b], rsB], writes=[ttB])
                sy.op("act", lambda e: e.activation(out=hn_fn(kc), in_=tt[:], func=AF.Identity,
                                                    scale=drvA[:, l, s, kc:kc + 1], bias=modt[:, l * 24 + kc, s:s + 1]),
                      reads=[ttB], sreads=[drvB, modB], writes=[hnB[kc]])

        def postnorm(l, s, t0, y, yB, sqt, sqB, tp, tpt=None):
            tpt = tpt or tp
            b = t0 // 512
            rs, rsB = rms_stats(lambda: y[:, :, :], yB, KC, od1024, sqt, sqB, tp)
            for n in range(KC):
                tt, ttB = tpt()
                sy.op("dve", lambda e: e.tensor_tensor(out=tt[:], in0=y[:, n, :], in1=rs[:], op=ALU.mult),
                      reads=[yB[n], rsB], writes=[ttB])
                sy.op("dve", lambda e: e.scalar_tensor_tensor(out=xs[:, n, t0:t0 + 512], in0=tt[:],
                                                              scalar=drvG[:, l, s, n:n + 1], in1=xs[:, n, t0:t0 + 512],
                                                              op0=ALU.mult, op1=ALU.add),
                      reads=[ttB], sreads=[drvB], writes=[xsB[n][b]])

        def mk_tmp_pool(ess, name, n, dt=F32, w=512):
            _UID[0] += 1
            tiles = [ess.enter_context(nc.sbuf_tensor(f"{name}{i}_{_UID[0]}", [128, w], dt)) for i in range(n)]
            bufs = [Buf(f"{name}{i}") for i in range(n)]
            st = {"i": 0}

            def get():
                i = st["i"] % n
                st["i"] += 1
                return tiles[i], bufs[i]
            return get

        def even_layer(l, s):
            j = l // 2
            with ExitStack() as el:
                def sbl(name, shape, dt):
                    return el.enter_context(nc.sbuf_tensor(f"{name}_{l}_{s}", shape, dt))
                hn = sbl("e_hn", [128, KC, 512], BF16)
                hnB = [Buf(f"e_hn{c}") for c in range(KC)]
                G = sbl("e_G", [128, KC, 544], BF16)
                GB = [Buf(f"e_G{c}") for c in range(KC)]
                Z = sbl("e_Z", [128, KC, 512], BF16)
                ZB = [Buf(f"e_Z{c}") for c in range(KC)]
                big = sbl("e_big", [128, KC, 512], F32)
                bigB = [Buf(f"e_big{c}") for c in range(KC)]
                Lo = sbl("e_Lo", [128, KC, 512], BF16)
                LoB = [Buf(f"e_Lo{c}") for c in range(KC)]
                sqt = sbl("e_sq", [128, KC, 512], BF16)
                sqB = Buf("e_sq")
                dg = sbl("e_dg", [128, 31, 128], BF16)
                dgB = Buf("e_dg")
                d4 = [sbl(f"e_d4{i}", [128, 4, 128], BF16) for i in range(2)]
                d4B = [Buf(f"e_d4{i}") for i in range(2)]
                XB = [sbl(f"e_XB{i}", [128, 516], BF16) for i in range(2)]
                XBB = [Buf(f"e_XB{i}") for i in range(2)]
                tp = mk_tmp_pool(el, "e_tf", 5, F32)
                tpt = mk_tmp_pool(el, "e_tt", 3, F32)
                tpb = mk_tmp_pool(el, "e_tb", 2, BF16)
                lt = [[sbl(f"e_lt{p}{i}", [128, 512], F32) for i in range(5)] for p in range(2)]
                ltB = [[Buf(f"e_lt{p}{i}") for i in range(5)] for p in range(2)]
                sgt = [sbl(f"e_sgt{p}", [128, 512], BF16) for p in range(2)]
                sgtB = [Buf(f"e_sgt{p}") for p in range(2)]
                zbt = [sbl(f"e_zbt{p}", [128, 512], BF16) for p in range(2)]
                zbtB = [Buf(f"e_zbt{p}") for p in range(2)]
                xcbt = [sbl(f"e_xcbt{p}", [128, 512], BF16) for p in range(2)]
                xcbtB = [Buf(f"e_xcbt{p}") for p in range(2)]

                sy.op("dve", lambda e: e.memset(G[:, :, 0:32], 0.0), writes=GB)
                sy.op("dve", lambda e: e.memset(carry[:], 0.0), writes=carryB)
                sy.op("dve", lambda e: e.memset(xbh[:], 0.0), writes=xbhB)
                hrhs = lambda kc: hn[:, kc, :]

                def w_mm8(name):
                    wt, wB, wi = acquire(name)
                    pt, pB = mm8(wt, wB, hrhs, hnB)
                    release(wi)
                    return pt, pB

                def conv_stage(jj):
                    p = jj % 2
                    pt, pB = w_mm8(("ewin", j, 8 + jj))
                    yield
                    sy.op("act", lambda e: e.activation(out=sgt[p][:], in_=pt[:], func=AF.Sigmoid),
                          reads=[pB], writes=[sgtB[p]])
                    pt, pB = w_mm8(("ewin", j, jj))
                    yield
                    sy.op("dve", lambda e: e.tensor_tensor(out=G[:, jj, 32:544], in0=pt[:], in1=sgt[p][:], op=ALU.mult),
                          reads=[pB, sgtB[p]], writes=[GB[jj]])
                    pt, pB = w_mm8(("ewin", j, 16 + jj))
                    yield
                    sy.op("act", lambda e: e.activation(out=Z[:, jj, :], in_=pt[:], func=AF.Silu),
                          reads=[pB], writes=[ZB[jj]])
                    sy.dma("sp", dg[:].rearrange("p a b -> p (a b)"), dgd.ap()[j, jj, :, :],
                           reads=[dgdB[j][jj]], writes=[dgB])
                    yield
                    yield
                    pt, pB = nextps()
                    for k in range(31):
                        sy.op("pe", lambda e, k=k: e.matmul(pt[:, :], lhsT=dg[:, k, :], rhs=G[:, jj, k + 2:k + 514],
                                                            start=(k == 0), stop=(k == 30)),
                              reads=[dgB, GB[jj]], writes=[pB], inc=(k == 30))
                        if k % 8 == 7:
                            yield
                    sy.op("act", lambda e: e.activation(out=big[:, jj, :], in_=pt[:], func=AF.Identity,
                                                        bias=pv(("conv_b", j), jj), scale=1.0),
                          reads=[pB], sreads=[pvB], writes=[bigB[jj]])
                    if jj == 0:
                        dump("G0", G[:, 0, 32:544], [GB[0]])
                        dump("aconv0", big[:, 0, :], [bigB[0]])
                        dump("Zraw0", Z[:, 0, :], [ZB[0]])
                    yield
                    sy.op("dve", lambda e: e.tensor_copy(out=G[:, jj, 0:32], in_=G[:, jj, 512:544]),
                          reads=[GB[jj]], writes=[GB[jj]])

                def lru_stage(jj):
                    p = jj % 2
                    xbt, xbB = XB[p], XBB[p]
                    pt, pB = w_mm8(("ewin", j, 24 + jj))
                    yield
                    sy.op("act", lambda e: e.activation(out=xbt[:, 4:516], in_=pt[:], func=AF.Identity),
                          reads=[pB], writes=[xbB])
                    sy.op("dve", lambda e: e.tensor_copy(out=xbt[:, 0:4], in_=xbh[:, jj, :]),
                          reads=[xbhB[jj]], writes=[xbB])
                    pt, pB = w_mm8(("ewin", j, 32 + jj))
                    yield
                    sy.op("act", lambda e: e.activation(out=zbt[p][:], in_=pt[:], func=AF.Silu), reads=[pB],
                          writes=[zbtB[p]])
                    lw0 = pcols[("lconv_w", j)] + jj * 4
                    sy.op("dve", lambda e: e.tensor_tensor(
                        out=d4[p][:], in0=ident_b[:].unsqueeze(1).to_broadcast([128, 4, 128]),
                        in1=pvt[:, lw0:lw0 + 4].unsqueeze(2).to_broadcast([128, 4, 128]), op=ALU.mult),
                        reads=[constB, pvB], writes=[d4B[p]])
                    yield
                    pt, pB = nextps()
                    for k in range(4):
                        sy.op("pe", lambda e, k=k: e.matmul(pt[:, :], lhsT=d4[p][:, k, :], rhs=xbt[:, k + 1:k + 513],
                                                            start=(k == 0), stop=(k == 3)),
                              reads=[d4B[p], xbB], writes=[pB], inc=(k == 3))
                    yield
                    xc, xcB = lt[p][0], ltB[p][0]
                    rg, rgB = lt[p][1], ltB[p][1]
                    ig, igB = lt[p][2], ltB[p][2]
                    at, atB = lt[p][3], ltB[p][3]
                    hh_, hhB = lt[p][4], ltB[p][4]
                    sy.op("act", lambda e: e.activation(out=xc[:], in_=pt[:], func=AF.Identity,
                                                        bias=pv(("lconv_b", j), jj), scale=1.0),
                          reads=[pB], sreads=[pvB], writes=[xcB])
                    yield
                    sy.op("dve", lambda e: e.tensor_copy(out=xcbt[p][:], in_=xc[:]), reads=[xcB], writes=[xcbtB[p]])
                    sy.op("dve", lambda e: e.tensor_copy(out=xbh[:, jj, :], in_=xbt[:, 512:516]),
                          reads=[xbB], writes=[xbhB[jj]])
                    yield
                    wt, wB, wi = acquire(("egate", j, jj))
                    pr, prB = nextps()
                    sy.op("pe", lambda e: e.matmul(pr[:, :], lhsT=wt[:, 0:128], rhs=xcbt[p][:], start=True, stop=True),
                          reads=[wB, xcbtB[p]], writes=[prB])
                    pi_, piB = nextps()
                    sy.op("pe", lambda e: e.matmul(pi_[:, :], lhsT=wt[:, 128:256], rhs=xcbt[p][:], start=True, stop=True),
                          reads=[wB, xcbtB[p]], writes=[piB])
                    release(wi)
                    yield
                    sy.op("act", lambda e: e.activation(out=rg[:], in_=pr[:], func=AF.Sigmoid,
                                                        bias=pv(("ba", j), jj), scale=1.0),
                          reads=[prB], sreads=[pvB], writes=[rgB])
                    yield
                    sy.op("act", lambda e: e.activation(out=ig[:], in_=pi_[:], func=AF.Sigmoid,
                                                        bias=pv(("bx", j), jj), scale=1.0),
                          reads=[piB], sreads=[pvB], writes=[igB])
                    yield
                    sy.op("act", lambda e: e.activation(out=at[:], in_=rg[:], func=AF.Exp,
                                                        scale=nsp[:, j, jj:jj + 1]),
                          reads=[rgB], sreads=[nspB], writes=[atB])
                    sy.op("dve", lambda e: e.tensor_tensor(out=ig[:], in0=ig[:], in1=xc[:], op=ALU.mult),
                          reads=[xcB], writes=[igB])
                    yield
                    sy.op("dve", lambda e: e.tensor_tensor(out=rg[:], in0=at[:], in1=at[:], op=ALU.mult),
                          reads=[atB], writes=[rgB])
                    yield
                    sy.op("act", lambda e: e.activation(out=rg[:], in_=rg[:], func=AF.Sqrt, bias=1.0, scale=-1.0),
                          writes=[rgB])
                    yield
                    sy.op("dve", lambda e: e.tensor_tensor(out=ig[:], in0=ig[:], in1=rg[:], op=ALU.mult),
                          reads=[rgB], writes=[igB])
                    yield
                    sy.op("dve", lambda e: e.tensor_tensor_scan(out=hh_[:], data0=at[:], data1=ig[:],
                                                                initial=carry[:, jj:jj + 1], op0=ALU.mult,
                                                                op1=ALU.add),
                          reads=[atB, igB], sreads=[carryB[jj]], writes=[hhB])
                    yield
                    sy.op("act", lambda e: e.activation(out=carry[:, jj:jj + 1], in_=hh_[:, 511:512],
                                                        func=AF.Identity),
                          reads=[hhB], writes=[carryB[jj]])
                    sy.op("dve", lambda e: e.tensor_tensor(out=Lo[:, jj, :], in0=hh_[:], in1=zbt[p][:], op=ALU.mult),
                          reads=[hhB, zbtB[p]], writes=[LoB[jj]])

                prenorm(l, s, 0, lambda kc: hn[:, kc, :], hnB, sqt, sqB, tp, tpt)
                dump("hn0", hn[:, 0, :], [hnB[0]])
                for t in range(NB):
                    t0 = t * 512
                    active = []
                    for jj in range(8):
                        active += [conv_stage(jj), lru_stage(jj)]
                        steps = 0
                        while active and (jj == 7 or steps < EVEN_STAGGER):
                            for g_ in list(active):
                                try:
                                    next(g_)
                                except StopIteration:
                                    active.remove(g_)
                            steps += 1
                    if t + 1 < NB:
                        prenorm(l, s, t0 + 512, lambda kc: hn[:, kc, :], hnB, sqt, sqB, tp, tpt)
                    w1s = [acquire(("ewout", j, n, 1)) for n in range(6)]
                    acc = [nextps() for n in range(6)]
                    for k8 in range(8):
                        for n in range(6):
                            sy.op("pe", lambda e, k8=k8, n=n: e.matmul(acc[n][0][:, :], lhsT=w1s[n][0][:, k8 * 128:(k8 + 1) * 128],
                                                                      rhs=Lo[:, k8, :], start=(k8 == 0), stop=False),
                                  reads=[w1s[n][1], LoB[k8]], writes=[acc[n][1]], inc=(k8 == 7))
                    for w_ in w1s:
                        release(w_[2])
                    sy.op("dve", lambda e: e.tensor_copy(out=sqt[:], in_=big[:]), reads=bigB, writes=[sqB])
                    pm, pmB = psum[0], psB[0]
                    for kc in range(KC):
                        sy.op("pe", lambda e, kc=kc: e.matmul(pm[:, :], lhsT=od1024[:], rhs=sqt[:, kc, :],
                                                              start=(kc == 0), stop=(kc == KC - 1)),
                              reads=[constB, sqB], writes=[pmB], inc=(kc == KC - 1))
                    sy.op("act", lambda e: e.activation(out=sqt[:], in_=big[:], func=AF.Square),
                          reads=bigB, writes=[sqB])
                    p2, p2B = psum[1], psB[1]
                    for kc in range(KC):
                        sy.op("pe", lambda e, kc=kc: e.matmul(p2[:, :], lhsT=od1024[:], rhs=sqt[:, kc, :],
                                                              start=(kc == 0), stop=(kc == KC - 1)),
                              reads=[constB, sqB], writes=[p2B], inc=(kc == KC - 1))
                    mean, meanB = tp()
                    sy.op("act", lambda e: e.activation(out=mean[:], in_=pm[:], func=AF.Identity), reads=[pmB],
                          writes=[meanB])
                    var, varB = tp()
                    sy.op("dve", lambda e: e.tensor_tensor(out=var[:], in0=mean[:], in1=mean[:], op=ALU.mult),
                          reads=[meanB], writes=[varB])
                    sy.op("dve", lambda e: e.tensor_tensor(out=var[:], in0=p2[:], in1=var[:], op=ALU.subtract),
                          reads=[p2B], writes=[varB])
                    sy.op("dve", lambda e: e.tensor_scalar(out=var[:], in0=var[:], scalar1=0.0, scalar2=None,
                                                           op0=ALU.max), writes=[varB])
                    sd, sdB = tp()
                    sy.op("act", lambda e: e.activation(out=sd[:], in_=var[:], func=AF.Ln, bias=EPS, scale=1.0),
                          reads=[varB], writes=[sdB])
                    rs, rsB = tp()
                    sy.op("act", lambda e: e.activation(out=rs[:], in_=sd[:], func=AF.Exp, scale=-0.5), reads=[sdB], writes=[rsB])
                    mr, mrB = tp()
                    sy.op("dve", lambda e: e.tensor_tensor(out=mr[:], in0=mean[:], in1=rs[:], op=ALU.mult),
                          reads=[meanB, rsB], writes=[mrB])
                    w0s = [acquire(("ewout", j, n, 0)) for n in range(6)]
                    for jj in range(8):
                        t1, t1B = tpt()
                        sy.op("dve", lambda e: e.tensor_tensor(out=t1[:], in0=big[:, jj, :], in1=rs[:], op=ALU.mult),
                              reads=[bigB[jj], rsB], writes=[t1B])
                        sy.op("dve", lambda e: e.tensor_tensor(out=t1[:], in0=t1[:], in1=mr[:], op=ALU.subtract),
                              reads=[mrB], writes=[t1B])
                        s1, s1B = tpb()
                        sy.op("act", lambda e: e.activation(out=s1[:], in_=t1[:], func=AF.Silu,
                                                            scale=pv(("ln_g", j), jj), bias=pv(("ln_b", j), jj)),
                              reads=[t1B], sreads=[pvB], writes=[s1B])
                        sy.op("dve", lambda e: e.tensor_tensor(out=Z[:, jj, :], in0=s1[:], in1=Z[:, jj, :], op=ALU.mult),
                              reads=[s1B], writes=[ZB[jj]])
                        for n in range(6):
                            sy.op("pe", lambda e, jj=jj, n=n: e.matmul(acc[n][0][:, :], lhsT=w0s[n][0][:, jj * 128:(jj + 1) * 128],
                                                                      rhs=Z[:, jj, :], start=False, stop=(jj == 7)),
                                  reads=[w0s[n][1], ZB[jj]], writes=[acc[n][1]], inc=(jj == 7 or n == 5))
                    for w_ in w0s:
                        release(w_[2])
                    dump("Aout0", Z[:, 0, :], [ZB[0]])
                    dump("Lo0", Lo[:, 0, :], [LoB[0]])
                    for n in range(8):
                        if n < 6:
                            pt, pB = acc[n]
                        else:
                            w0, w0B, wi0 = acquire(("ewout", j, n, 0))
                            w1, w1B, wi1 = acquire(("ewout", j, n, 1))
                            pt, pB = nextps()
                            for kc in range(16):
                                wsrc, wsB = (w0, w0B) if kc < 8 else (w1, w1B)
                                src, srcB = (Z, ZB) if kc < 8 else (Lo, LoB)
                                k8 = kc % 8
                                sy.op("pe", lambda e, kc=kc, k8=k8, wsrc=wsrc, src=src: e.matmul(
                                    pt[:, :], lhsT=wsrc[:, k8 * 128:(k8 + 1) * 128], rhs=src[:, k8, :],
                                    start=(kc == 0), stop=(kc == 15)),
                                    reads=[wsB, srcB[k8]], writes=[pB], inc=(kc == 15))
                            release(wi0)
                            release(wi1)
                        sy.op("act", lambda e: e.activation(out=big[:, n, :], in_=pt[:], func=AF.Identity),
                              reads=[pB], writes=[bigB[n]])
                    dump("y0", big[:, 0, :], [bigB[0]])
                    postnorm(l, s, t0, big, bigB, sqt, sqB, tp, tpt)
                sy.fence()

        def odd_layer(l, s):
            j = l // 2
            with ExitStack() as ol:
                def sbl(name, shape, dt):
                    return ol.enter_context(nc.sbuf_tensor(f"{name}_{l}_{s}", shape, dt))
                Zs = sbl("o_Zs", [128, KC, S], BF16)
                ZsB = [[Buf(f"o_Zs{c}_{b}") for b in range(NB)] for c in range(KC)]
                cqn = sbl("o_cqn", [128, 2, S], BF16)
                cqnB = [Buf(f"o_cqn{b}") for b in range(NB)]
                ckvn = sbl("o_ckvn", [128, 2, S], BF16)
                ckvnB = [Buf(f"o_ckvn{b}") for b in range(NB)]
                kr = sbl("o_kr", [128, S], BF16)
                krB = Buf("o_kr")
                COS = sbl("o_cos", [128, S], BF16)
                SIN = sbl("o_sin", [128, S], BF16)
                csB = Buf("o_cs")
                tp = mk_tmp_pool(ol, "o_tf", 4, F32)
                tp_stats[0] = mk_tmp_pool(ol, "o_ts", 3, F32)

                with ExitStack() as rl:
                    ang = rl.enter_context(nc.sbuf_tensor(f"o_ang_{l}_{s}", [128, S], F32))
                    wk = rl.enter_context(nc.sbuf_tensor(f"o_wk_{l}_{s}", [128, S], F32))
                    wk2 = rl.enter_context(nc.sbuf_tensor(f"o_wk2_{l}_{s}", [128, S], F32))
                    ki = rl.enter_context(nc.sbuf_tensor(f"o_ki_{l}_{s}", [128, S], I32))
                    posi = ki
                    rB = Buf("o_rope")
                    R = slice(64, 96)
                    src = AP(pos_d, s * S, [[0, 32], [1, S]])
                    sy.dma("sp", posi[R, :], src, writes=[rB])
                    sy.op("dve", lambda e: e.tensor_copy(out=ang[R, :], in_=posi[R, :]), reads=[rB], writes=[rB])
                    sy.op("dve", lambda e: e.tensor_scalar(out=ang[R, :], in0=ang[R, :], scalar1=pvt[R, pcols["inv"]:pcols["inv"] + 1],
                                                           scalar2=None, op0=ALU.mult), sreads=[pvB], writes=[rB])
                    for which in range(2):
                        if which == 0:
                            sy.op("dve", lambda e: e.tensor_scalar(out=wk2[R, :], in0=ang[R, :], scalar1=math.pi / 2,
                                                                   scalar2=None, op0=ALU.add), writes=[rB])
                            a_in = wk2
                        else:
                            a_in = ang
                        sy.op("dve", lambda e: e.tensor_scalar(out=wk[R, :], in0=a_in[R, :], scalar1=1.0 / TWO_PI,
                                                               scalar2=None, op0=ALU.mult), writes=[rB])
                        sy.op("dve", lambda e: e.tensor_copy(out=ki[R, :], in_=wk[R, :]), writes=[rB])
                        sy.op("dve", lambda e: e.tensor_copy(out=wk[R, :], in_=ki[R, :]), writes=[rB])
                        sy.op("dve", lambda e: e.scalar_tensor_tensor(out=wk2[R, :], in0=wk[R, :], scalar=-PI_HI,
                                                                      in1=a_in[R, :], op0=ALU.mult, op1=ALU.add),
                              writes=[rB])
                        sy.op("dve", lambda e: e.scalar_tensor_tensor(out=wk2[R, :], in0=wk[R, :], scalar=-PI_LO,
                                                                      in1=wk2[R, :], op0=ALU.mult, op1=ALU.add),
                              writes=[rB])
                        sy.op("dve", lambda e: e.tensor_scalar(out=wk[R, :], in0=wk2[R, :], scalar1=math.pi,
                                                               scalar2=-TWO_PI, op0=ALU.is_gt, op1=ALU.mult), writes=[rB])
                        sy.op("dve", lambda e: e.tensor_tensor(out=wk2[R, :], in0=wk2[R, :], in1=wk[R, :], op=ALU.add),
                              writes=[rB])
                        sy.op("dve", lambda e: e.tensor_scalar(out=wk[R, :], in0=wk2[R, :], scalar1=-math.pi,
                                                               scalar2=TWO_PI, op0=ALU.is_lt, op1=ALU.mult), writes=[rB])
                        sy.op("dve", lambda e: e.tensor_tensor(out=wk2[R, :], in0=wk2[R, :], in1=wk[R, :], op=ALU.add),
                              writes=[rB])
                        sy.op("dve", lambda e: e.tensor_scalar(out=wk2[R, :], in0=wk2[R, :], scalar1=3.1415925,
                                                               scalar2=-3.1415925, op0=ALU.min, op1=ALU.max), writes=[rB])
                        if which == 0:
                            sy.op("act", lambda e: e.activation(out=COS[R, :], in_=wk2[R, :], func=AF.Sin),
                                  reads=[rB], writes=[csB])
                        else:
                            sy.op("dve", lambda e: e.tensor_scalar(out=wk2[R, :], in0=wk2[R, :],
                                                                   scalar1=pvt[R, pcols["sgn"]:pcols["sgn"] + 1],
                                                                   scalar2=None, op0=ALU.mult), sreads=[pvB], writes=[rB])
                            sy.op("act", lambda e: e.activation(out=SIN[R, :], in_=wk2[R, :], func=AF.Sin),
                                  reads=[rB], writes=[csB])
                    sy.fence()

                with ExitStack() as p1:
                    hn = p1.enter_context(nc.sbuf_tensor(f"o_hn_{l}_{s}", [128, KC, 1024], BF16))
                    hnB2 = [[Buf(f"o_hn{c}_{b}") for c in range(KC)] for b in range(2)]
                    raw = p1.enter_context(nc.sbuf_tensor(f"o_raw_{l}_{s}", [128, 2, 1024], F32))
                    rawB = [Buf(f"o_raw{b}") for b in range(2)]
                    krA = p1.enter_context(nc.sbuf_tensor(f"o_krA_{l}_{s}", [128, 1024], F32))
                    krAB = [Buf(f"o_krA{b}") for b in range(2)]
                    sqt = p1.enter_context(nc.sbuf_tensor(f"o_sq_{l}_{s}", [128, KC, 512], BF16))
                    sqB = Buf("o_sq")
                    R = slice(64, 96)
                    for t in range(S // 1024):
                        for b in range(2):
                            t0 = t * 1024 + b * 512
                            prenorm(l, s, t0, lambda kc, b=b: hn[:, kc, b * 512:(b + 1) * 512], hnB2[b], sqt, sqB, tp)
                        for grp, (dst, dstB, nrm) in enumerate(((cqn, cqnB, "q_norm"), (ckvn, ckvnB, "kv_norm"))):
                            for c in range(2):
                                wt, wB, wi = acquire(("owin", j, grp * 2 + c))
                                for b in range(2):
                                    pt, pB = mm8(wt, wB, lambda kc, b=b: hn[:, kc, b * 512:(b + 1) * 512], hnB2[b])
                                    sy.op("act", lambda e: e.activation(out=raw[:, c, b * 512:(b + 1) * 512], in_=pt[:],
                                                                        func=AF.Identity),
                                          reads=[pB], writes=[rawB[b]])
                                release(wi)
                            for b in range(2):
                                gb = t * 2 + b
                                rs, rsB = rms_stats(lambda b=b: raw[:, :, b * 512:(b + 1) * 512], [rawB[b]], 2, od256,
                                                    sqt, sqB, tp)
                                for c in range(2):
                                    sy.op("dve", lambda e, c=c: e.scalar_tensor_tensor(
                                        out=dst[:, c, gb * 512:(gb + 1) * 512], in0=raw[:, c, b * 512:(b + 1) * 512],
                                        scalar=pv((nrm, j), c), in1=rs[:], op0=ALU.mult, op1=ALU.mult),
                                        reads=[rawB[b], rsB], sreads=[pvB], writes=[dstB[gb]])
                        wt, wB, wi = acquire(("owin", j, 4))
                        for b in range(2):
                            pt, pB = mm8(wt, wB, lambda kc, b=b: hn[:, kc, b * 512:(b + 1) * 512], hnB2[b])
                            sy.op("act", lambda e: e.activation(out=krA[R, b * 512:(b + 1) * 512], in_=pt[R, :],
                                                                func=AF.Identity), reads=[pB], writes=[krAB[b]])
                        release(wi)
                        wt, wB, wi = acquire(("owin", j, 5))
                        for b in range(2):
                            gb = t * 2 + b
                            tk = slice(gb * 512, (gb + 1) * 512)
                            pt, pB = mm8(wt, wB, lambda kc, b=b: hn[:, kc, b * 512:(b + 1) * 512], hnB2[b])
                            t1, t1B = tp()
                            sy.op("dve", lambda e: e.tensor_tensor(out=t1[R, :], in0=krA[R, b * 512:(b + 1) * 512],
                                                                   in1=COS[R, tk], op=ALU.mult),
                                  reads=[krAB[b], csB], writes=[t1B])
                            t2, t2B = tp()
                            sy.op("dve", lambda e: e.tensor_tensor(out=t2[R, :], in0=pt[R, :], in1=SIN[R, tk], op=ALU.mult),
                                  reads=[pB, csB], writes=[t2B])
                            sy.op("dve", lambda e: e.tensor_tensor(out=kr[R, tk], in0=t1[R, :], in1=t2[R, :], op=ALU.add),
                                  reads=[t1B, t2B], writes=[krB])
                        release(wi)
                        for c in range(8):
                            wt, wB, wi = acquire(("owin", j, 6 + c))
                            for b in range(2):
                                gb = t * 2 + b
                                pt, pB = mm8(wt, wB, lambda kc, b=b: hn[:, kc, b * 512:(b + 1) * 512], hnB2[b])
                                sy.op("act", lambda e: e.activation(out=Zs[:, c, gb * 512:(gb + 1) * 512], in_=pt[:],
                                                                    func=AF.Silu), reads=[pB], writes=[ZsB[c][gb]])
                            release(wi)
                    dump("cqn", cqn[:, 0, 0:512], [cqnB[0]])
                    dump("ckvn", ckvn[:, 0, 0:512], [ckvnB[0]])
                    dump("kr", kr[:, 0:512], [krB])
                    dump("cos", COS[:, 0:512], [csB])
                    dump("sin", SIN[:, 0:512], [csB])
                    dump("zs", Zs[:, 0, 0:512], [ZsB[0][0]])
                    sy.fence()

                with ExitStack() as p2:
                    QT = [p2.enter_context(nc.sbuf_tensor(f"o_QT{i}_{l}_{s}", [128, S], BF16)) for i in range(2)]
                    KT = [p2.enter_context(nc.sbuf_tensor(f"o_KT{i}_{l}_{s}", [128, S], BF16)) for i in range(2)]
                    V = [p2.enter_context(nc.sbuf_tensor(f"o_V{i}_{l}_{s}", [128, S // 128, 128], BF16)) for i in range(2)]
                    QTB = [Buf(f"o_QT{i}") for i in range(2)]
                    KTB = [Buf(f"o_KT{i}") for i in range(2)]
                    VB = [Buf(f"o_V{i}") for i in range(2)]
                    NPT = 6
                    PT = [p2.enter_context(nc.sbuf_tensor(f"o_PT{i}_{l}_{s}", [128, 512], BF16)) for i in range(NPT)]
                    PTB = [Buf(f"o_PT{i}") for i in range(NPT)]
                    rec = p2.enter_context(nc.sbuf_tensor(f"o_rec_{l}_{s}", [128, 512], F32))
                    recB = Buf("o_rec")
                    tpg = mk_tmp_pool(p2, "o_tg", 2, F32)
                    R = slice(64, 96)
                    sy.op("dve", lambda e: e.memset(V[0][:, :, 64:128], 1.0), writes=[VB[0]])
                    sy.op("dve", lambda e: e.memset(V[1][:, :, 0:64], 1.0), writes=[VB[1]])
                    pt_i = {"i": 0}

                    def gen_head(h):
                        par = h % 2
                        qt, qB = QT[par], QTB[par]
                        kt, kB = KT[par], KTB[par]
                        vt, vB = V[par], VB[par]
                        wt, wB, wi = acquire(("ouq", j, h))
                        for b in range(NB):
                            tk = slice(b * 512, (b + 1) * 512)
                            pa, paB = nextps()
                            pb, pbB = nextps()
                            for kc in range(2):
                                sy.op("pe", lambda e, kc=kc: e.matmul(pa[0:96, :], lhsT=wt[:, kc * 192:kc * 192 + 96],
                                                                      rhs=cqn[:, kc, tk], start=(kc == 0), stop=(kc == 1)),
                                      reads=[wB, cqnB[b]], writes=[paB], inc=(kc == 1))
                            for kc in range(2):
                                sy.op("pe", lambda e, kc=kc: e.matmul(pb[0:96, :], lhsT=wt[:, kc * 192 + 96:kc * 192 + 192],
                                                                      rhs=cqn[:, kc, tk], start=(kc == 0), stop=(kc == 1)),
                                      reads=[wB, cqnB[b]], writes=[pbB], inc=(kc == 1))
                            yield
                            sy.op("act", lambda e: e.activation(out=qt[0:64, tk], in_=pa[0:64, :], func=AF.Identity),
                                  reads=[paB], writes=[qB])
                            t1, t1B = tpg()
                            sy.op("dve", lambda e: e.tensor_tensor(out=t1[R, :], in0=pa[R, :], in1=COS[R, tk], op=ALU.mult),
                                  reads=[paB, csB], writes=[t1B])
                            t2, t2B = tpg()
                            sy.op("dve", lambda e: e.tensor_tensor(out=t2[R, :], in0=pb[R, :], in1=SIN[R, tk], op=ALU.mult),
                                  reads=[pbB, csB], writes=[t2B])
                            yield
                            sy.op("dve", lambda e: e.tensor_tensor(out=qt[R, tk], in0=t1[R, :], in1=t2[R, :], op=ALU.add),
                                  reads=[t1B, t2B], writes=[qB])
                            yield
                        release(wi)
                        wt, wB, wi = acquire(("oukv", j, h))
                        for b in range(NB):
                            tk = slice(b * 512, (b + 1) * 512)
                            pa, paB = nextps()
                            for kc in range(2):
                                sy.op("pe", lambda e, kc=kc: e.matmul(pa[0:64, :], lhsT=wt[:, kc * 128:kc * 128 + 64],
                                                                      rhs=ckvn[:, kc, tk], start=(kc == 0), stop=(kc == 1)),
                                      reads=[wB, ckvnB[b]], writes=[paB], inc=(kc == 1))
                            yield
                            sy.op("act", lambda e: e.activation(out=kt[0:64, tk], in_=pa[0:64, :], func=AF.Identity),
                                  reads=[paB], writes=[kB])
                            yield
                        sy.op("pool", lambda e: e.tensor_copy(out=kt[R, :], in_=kr[R, :]), reads=[krB], writes=[kB])
                        vo = 0 if par == 0 else 64
                        for g8 in range(S // 1024):
                            pa, paB = nextps()
                            for i8 in range(8):
                                kb = g8 * 8 + i8
                                for kc in range(2):
                                    sy.op("pe", lambda e, kc=kc, kb=kb, i8=i8: e.matmul(
                                        pa[:, i8 * 64:(i8 + 1) * 64], lhsT=ckvn[:, kc, kb * 128:(kb + 1) * 128],
                                        rhs=wt[:, kc * 128 + 64:kc * 128 + 128], start=(kc == 0), stop=(kc == 1)),
                                        reads=[wB, ckvnB[kb // 4]], writes=[paB], inc=(kc == 1 and i8 == 7))
                            yield
                            sy.op("act", lambda e: e.activation(
                                out=vt[:, g8 * 8:(g8 + 1) * 8, vo:vo + 64],
                                in_=pa[:, :].rearrange("p (a b) -> p a b", b=64), func=AF.Identity),
                                reads=[paB], writes=[vB])
                            yield
                        release(wi)

                    def attn_head(h):
                        par = h % 2
                        hp = h // 2
                        qt, qB = QT[par], QTB[par]
                        kt, kB = KT[par], KTB[par]
                        vt, vB = V[par], VB[par]
                        if h == 0:
                            dump("qt", qt[:, 0:512], [qB])
                            dump("kt", kt[:, 0:512], [kB])
                            dump("v0", vt[:, 0, :], [vB])
                        items = []
                        for g in range(NB):
                            for kb in range(4 * g + 4):
                                items.append((g, kb))
                        LA = 3
                        inflight = {}
                        for i in range(len(items) + LA):
                            if i < len(items):
                                g, kb = items[i]
                                d = kb - 4 * g
                                c0 = max(0, d) * 128
                                ncols = 512 - c0
                                sp_, spB = nextps()
                                sy.op("pe", lambda e, kb=kb, g=g, c0=c0, ncols=ncols, sp_=sp_: e.matmul(
                                    sp_[:, 0:ncols], lhsT=kt[0:96, kb * 128:(kb + 1) * 128],
                                    rhs=qt[0:96, g * 512 + c0:(g + 1) * 512], start=True, stop=(d < 0)),
                                    reads=[kB, qB], writes=[spB], inc=(d < 0))
                                if d >= 0:
                                    sy.op("pe", lambda e, sp_=sp_: e.matmul(sp_[:, 0:128], lhsT=ident_b[:], rhs=maskT[:],
                                                                            start=False, stop=True),
                                          reads=[constB], writes=[spB])
                                inflight[i] = (sp_, spB, c0, ncols)
                            if i >= LA:
                                ii = i - LA
                                g, kb = items[ii]
                                sp_, spB, c0, ncols = inflight.pop(ii)
                                pi = pt_i["i"] % NPT
                                pt_i["i"] += 1
                                ptile, ptB = PT[pi], PTB[pi]
                                sy.op("act", lambda e, sp_=sp_, ncols=ncols, ptile=ptile: e.activation(
                                    out=ptile[:, 0:ncols], in_=sp_[:, 0:ncols], func=AF.Exp, scale=ATT_SCALE),
                                    reads=[spB], writes=[ptB])
                                op_, opB = psum[g % 2], psB[g % 2]
                                nkb = 4 * g + 4
                                sy.op("pe", lambda e, kb=kb, c0=c0, ncols=ncols, ptile=ptile, op_=op_, nkb=nkb: e.matmul(
                                    op_[:, c0:512], lhsT=vt[:, kb, :], rhs=ptile[:, 0:ncols],
                                    start=(kb == 0), stop=(kb == nkb - 1)),
                                    reads=[vB, ptB], writes=[opB], inc=True)
                                if kb == nkb - 1:
                                    if par == 0:
                                        num, den = slice(0, 64), slice(64, 128)
                                    else:
                                        num, den = slice(64, 128), slice(0, 64)
                                    tk = slice(g * 512, (g + 1) * 512)
                                    sy.op("act", lambda e, op_=op_: e.activation(out=rec[num, :], in_=op_[den, :], func=AF.Ln),
                                          reads=[opB], writes=[recB])
                                    sy.op("act", lambda e: e.activation(out=rec[num, :], in_=rec[num, :], func=AF.Exp, scale=-1.0),
                                          writes=[recB])
                                    o1, o1B = tp()
                                    sy.op("dve", lambda e, op_=op_, o1=o1: e.tensor_tensor(out=o1[num, :], in0=op_[num, :],
                                                                                         in1=rec[num, :], op=ALU.mult),
                                          reads=[opB, recB], writes=[o1B])
                                    sy.op("dve", lambda e, o1=o1: e.tensor_tensor(out=Zs[num, hp, tk], in0=o1[num, :],
                                                                                in1=Zs[num, hp, tk], op=ALU.mult),
                                          reads=[o1B], writes=[ZsB[hp][g]])
                            yield

                    for _ in gen_head(0):
                        pass
                    for h in range(16):
                        gens = [attn_head(h)]
                        if h + 1 < 16:
                            gens.append(gen_head(h + 1))
                        while gens:
                            for g_ in list(gens):
                                try:
                                    next(g_)
                                except StopIteration:
                                    gens.remove(g_)
                    sy.fence()

                dump("og", Zs[:, 0, 0:512], [ZsB[0][0]])
                with ExitStack() as p3:
                    big = p3.enter_context(nc.sbuf_tensor(f"o_big_{l}_{s}", [128, KC, 512], F32))
                    bigB = [Buf(f"o_big{c}") for c in range(KC)]
                    sqt = p3.enter_context(nc.sbuf_tensor(f"o_sq3_{l}_{s}", [128, KC, 512], BF16))
                    sqB = Buf("o_sq3")
                    wl = [acquire(("owout", j, n)) for n in range(8)]
                    for b in range(NB):
                        tk = slice(b * 512, (b + 1) * 512)
                        for n in range(8):
                            wt, wB, _ = wl[n]
                            pt, pB = mm8(wt, wB, lambda kc: Zs[:, kc, tk], [ZsB[c][b] for c in range(KC)])
                            sy.op("act", lambda e: e.activation(out=big[:, n, :], in_=pt[:], func=AF.Identity),
                                  reads=[pB], writes=[bigB[n]])
                        postnorm(l, s, b * 512, big, bigB, sqt, sqB, tp)
                    for _w in wl:
                        release(_w[2])
                    sy.fence()
                tp_stats[0] = None

        allxs = [xsB[c][b] for c in range(KC) for b in range(NB)]
        outB = Buf("outst")
        for s in range(NSEQ):
            for c in range(KC):
                sy.dma("sp", xs[:, c, :], x_d.ap()[s, :, c, :], writes=xsB[c])
            for l in LAYERS:
                if l % 2 == 0:
                    even_layer(l, s)
                else:
                    odd_layer(l, s)
            for c in range(KC):
                sy.dma("sp", out_d.ap()[s, :, c, :], xs[:, c, :], reads=xsB[c], writes=[outB])
        nc.sync.wait_ge(outB.dsem, outB.dcnt)
        if DEBUG and dbgB.dsem is not None:
            nc.sync.wait_ge(dbgB.dsem, dbgB.dcnt)
        if RECORD:
            return ws["order"]
        assert ws["acq"] == len(ws["order"]) == ws["issued"], (ws["acq"], len(ws["order"]), ws["issued"])
        build.nins = sy.nins
    return nc


_CACHE = {}


def kernel(**inp):
    inp = {k: np.asarray(v) for k, v in inp.items()}
    x = inp["x"].astype(np.float32, copy=False)
    B, S, Dm = x.shape
    nseq = B // NCORES
    wts = pack_weights(inp)
    pvv = pack_pv(inp)
    key = (S, nseq)
    if key not in _CACHE:
        _CACHE[key] = build(S=S, NSEQ=nseq)
    nc = _CACHE[key]
    in_maps = []
    for cid in range(NCORES):
        bs = slice(cid * nseq, (cid + 1) * nseq)
        xf = np.ascontiguousarray(x[bs].reshape(nseq, S, KC, 128).transpose(0, 3, 2, 1))
        cf = np.ascontiguousarray(inp["c"][bs].astype(np.float32).reshape(nseq, KC, 128).transpose(2, 1, 0))
        pos = np.ascontiguousarray(inp["positions"][bs].astype(np.int32))
        in_maps.append({"x": xf, "c": cf, "pos": pos, "wts": wts, "pv": pvv})
    res = run_bass_kernel_spmd(nc, in_maps, core_ids=list(range(NCORES)))
    outs = []
    for cid in range(NCORES):
        o = np.asarray(res.results[cid]["out"]).reshape(nseq, 128, KC, S)
        outs.append(o.transpose(0, 3, 2, 1).reshape(nseq, S, Dm))
    return np.ascontiguousarray(np.concatenate(outs, axis=0).astype(np.float32))
```

```python
import math
from contextlib import ExitStack
import numpy as np
import concourse.bass as bass
import concourse.mybir as mybir
from concourse.ap import AP
from concourse.bass_utils import run_bass_kernel_spmd

F32 = mybir.dt.float32
BF16 = mybir.dt.bfloat16
I32 = mybir.dt.int32
AF = mybir.ActivationFunctionType
ALU = mybir.AluOpType

D = 1024
KC = 8
NCORES = 8
EPS = 1e-6
SLOT = 1024
RING = 10
SAME_ENG_WINDOW = 3
EVEN_STAGGER = 9
ATT_SCALE = 96.0 ** -0.5
TWO_PI = 2.0 * math.pi
PI_HI = 6.28125
PI_LO = TWO_PI - 6.28125


def weight_plan():
    plan = {}
    off = 0

    def add(name, F):
        nonlocal off
        plan[name] = (off, F)
        off += 128 * F

    for l in range(4):
        for n in range(24):
            add(("ada", l, n), 1024)
    for j in range(2):
        for c in range(40):
            add(("ewin", j, c), 1024)
        for kc in range(8):
            add(("egate", j, kc), 256)
        for n in range(8):
            add(("ewout", j, n, 0), 1024)
            add(("ewout", j, n, 1), 1024)
    for j in range(2):
        for c in range(14):
            add(("owin", j, c), 1024)
        for h in range(16):
            add(("ouq", j, h), 384)
            add(("oukv", j, h), 256)
        for n in range(8):
            add(("owout", j, n), 1024)
    return plan, off


def pv_plan():
    cols = {}
    off = 0

    def add(name, n):
        nonlocal off
        cols[name] = off
        off += n

    for l in range(4):
        add(("pre_g", l), 8)
        add(("post_g", l), 8)
        add(("ada_b", l), 24)
    for j in range(2):
        add(("conv_w", j), 8 * 31)
        add(("conv_b", j), 8)
        add(("ln_g", j), 8)
        add(("ln_b", j), 8)
        add(("lconv_w", j), 8 * 4)
        add(("lconv_b", j), 8)
        add(("ba", j), 8)
        add(("bx", j), 8)
        add(("lam", j), 8)
        add(("q_norm", j), 2)
        add(("kv_norm", j), 2)
    add("inv", 1)
    add("sgn", 1)
    return cols, off


def chunkify(W):
    K, N = W.shape
    return np.ascontiguousarray(W.reshape(K // 128, 128, N // 128, 128).transpose(2, 1, 0, 3))


def vec_pc(v):
    return np.ascontiguousarray(v.reshape(-1, 128).T)


def pack_weights(inp):
    plan, total = weight_plan()
    flat = np.zeros(total, np.float32)

    def put(name, arr):
        off, F = plan[name]
        a = np.ascontiguousarray(arr, dtype=np.float32).reshape(128, F)
        flat[off:off + 128 * F] = a.reshape(-1)

    for l in range(4):
        ch = chunkify(inp["ada_w"][l])
        for n in range(24):
            put(("ada", l, n), ch[n])
    for j in range(2):
        ch = chunkify(inp["ev_w_in"][j])
        for c in range(40):
            put(("ewin", j, c), ch[c])
        wa, wx = inp["ev_lru_wa"][j], inp["ev_lru_wx"][j]
        for kc in range(8):
            g = np.zeros((128, 2, 128), np.float32)
            for hh in range(2):
                g[hh * 64:(hh + 1) * 64, 0, hh * 64:(hh + 1) * 64] = wa[2 * kc + hh]
                g[hh * 64:(hh + 1) * 64, 1, hh * 64:(hh + 1) * 64] = wx[2 * kc + hh]
            put(("egate", j, kc), g)
        wo = inp["ev_w_out"][j]
        c0 = chunkify(wo[:1024])
        c1 = chunkify(wo[1024:])
        for n in range(8):
            put(("ewout", j, n, 0), c0[n])
            put(("ewout", j, n, 1), c1[n])
    for j in range(2):
        w = inp["od_w_in"][j]
        z64 = np.zeros((1024, 64), np.float32)
        z32 = np.zeros((1024, 32), np.float32)
        kr = w[:, 512:544]
        kr1 = np.concatenate([z64, kr, z32], axis=1)
        kr2 = np.concatenate([z64, kr[:, 16:32], kr[:, 0:16], z32], axis=1)
        wcat = np.concatenate([w[:, 0:512], kr1, kr2, w[:, 544:1568]], axis=1)
        ch = chunkify(wcat)
        for c in range(14):
            put(("owin", j, c), ch[c])
        uq = inp["od_w_uq"][j].reshape(2, 128, 16, 96)
        ukv = inp["od_w_ukv"][j].reshape(2, 128, 16, 128)
        for h in range(16):
            a = uq[:, :, h, :]
            sw = np.concatenate([a[:, :, 0:64], a[:, :, 80:96], a[:, :, 64:80]], axis=2)
            both = np.concatenate([a, sw], axis=2)
            put(("ouq", j, h), both.transpose(1, 0, 2))
            put(("oukv", j, h), ukv[:, :, h, :].transpose(1, 0, 2))
        ch = chunkify(inp["od_w_out"][j])
        for n in range(8):
            put(("owout", j, n), ch[n])
    return flat


def pack_pv(inp):
    cols, n = pv_plan()
    pv = np.zeros((128, n), np.float32)

    def put(name, arr):
        a = np.asarray(arr, np.float32)
        pv[:, cols[name]:cols[name] + a.shape[1]] = a

    for l in range(4):
        put(("pre_g", l), vec_pc(inp["pre_g"][l]))
        put(("post_g", l), vec_pc(inp["post_g"][l]))
        put(("ada_b", l), vec_pc(inp["ada_b"][l]))
    for j in range(2):
        cw = inp["ev_conv_w"][j]
        put(("conv_w", j), cw.reshape(31, 8, 128).transpose(2, 1, 0).reshape(128, 8 * 31))
        put(("conv_b", j), vec_pc(inp["ev_conv_b"][j]))
        put(("ln_g", j), vec_pc(inp["ev_ln_g"][j]))
        put(("ln_b", j), vec_pc(inp["ev_ln_b"][j]))
        lw = inp["ev_lru_conv_w"][j]
        put(("lconv_w", j), lw.reshape(4, 8, 128).transpose(2, 1, 0).reshape(128, 8 * 4))
        put(("lconv_b", j), vec_pc(inp["ev_lru_conv_b"][j]))
        put(("ba", j), vec_pc(inp["ev_lru_ba"][j]))
        put(("bx", j), vec_pc(inp["ev_lru_bx"][j]))
        put(("lam", j), vec_pc(inp["ev_lru_lam"][j]))
        put(("q_norm", j), vec_pc(inp["od_q_norm"][j]))
        put(("kv_norm", j), vec_pc(inp["od_kv_norm"][j]))
    inv = (10000.0 ** (-np.arange(0, 32, 2, dtype=np.float32) / 32.0)).astype(np.float32)
    iv = np.zeros((128, 1), np.float32)
    sg = np.ones((128, 1), np.float32)
    for i in range(32):
        iv[64 + i, 0] = inv[i % 16]
        sg[64 + i, 0] = -1.0 if i < 16 else 1.0
    put("inv", iv)
    put("sgn", sg)
    return pv


_UID = [0]


class Buf:
    __slots__ = ("name", "w", "r", "dsem", "dcnt")

    def __init__(self, name):
        _UID[0] += 1
        self.name = f"{name}_{_UID[0]}"
        self.w = None
        self.r = {}
        self.dsem = None
        self.dcnt = 0


class Sync:
    ENG = ("pe", "act", "dve", "pool", "sp")

    def __init__(self, nc, es):
        self.nc = nc
        self.es = es
        self.eng = {"pe": nc.tensor, "act": nc.scalar, "dve": nc.vector, "pool": nc.gpsimd, "sp": nc.sync}
        self.sem = {k: es.enter_context(nc.semaphore("s_" + k)) for k in self.ENG}
        self.cnt = {k: 0 for k in self.ENG}
        self.known = {k: {} for k in self.ENG}
        self.nins = 0

    def _need(self, E, dep, strict):
        key, sem, val, src = dep
        if src == E and not strict and E != "pool":
            if E == "pe" or (self.cnt[E] - val) >= SAME_ENG_WINDOW:
                return
        if self.known[E].get(key, 0) >= val:
            return
        self.eng[E].wait_ge(sem, val)
        self.known[E][key] = val
        self.nins += 1

    def _deps(self, E, reads, writes, sreads, strict_all=False):
        for b in reads:
            if b.w is not None:
                self._need(E, b.w, strict_all)
        for b in sreads:
            if b.w is not None:
                self._need(E, b.w, True)
        for b in writes:
            if b.w is not None:
                self._need(E, b.w, strict_all)
            for d in b.r.values():
                self._need(E, d, strict_all)

    def op(self, E, fn, reads=(), writes=(), sreads=(), inc=True):
        self._deps(E, reads, writes, sreads)
        ins = fn(self.eng[E])
        self.nins += 1
        if inc:
            self.cnt[E] += 1
            ins.then_inc(self.sem[E], 1)
            me = (E, self.sem[E], self.cnt[E], E)
        else:
            me = (E, self.sem[E], self.cnt[E] + 1, E)
        for b in writes:
            b.w = me
            b.r = {}
        for b in reads:
            b.r[E] = me
        for b in sreads:
            b.r[E] = me
        return ins

    def dma(self, Q, out, in_, reads=(), writes=()):
        self._deps(Q, reads, writes, (), strict_all=True)
        tgt = writes[0] if writes else reads[0]
        if tgt.dsem is None:
            tgt.dsem = self.es.enter_context(self.nc.semaphore("d_" + tgt.name))
        tgt.dcnt += 16
        self.eng[Q].dma_start(out=out, in_=in_).then_inc(tgt.dsem, 16)
        self.nins += 1
        me = ("d_" + tgt.name, tgt.dsem, tgt.dcnt, None)
        for b in writes:
            b.w = me
            b.r = {}
        for b in reads:
            b.r["dma_" + tgt.name] = me
        return me

    def fence(self):
        for E in ("pe", "act", "dve", "pool", "sp"):
            for Fg in ("pe", "act", "dve", "pool"):
                if Fg != E and self.cnt[Fg] > 0:
                    self._need(E, (Fg, self.sem[Fg], self.cnt[Fg], Fg), True)


def build(S=2048, NSEQ=2, LAYERS=(0, 1, 2, 3), DEBUG=False):
    order = _build(S, NSEQ, LAYERS, DEBUG, None)
    return _build(S, NSEQ, LAYERS, DEBUG, order)


def _build(S, NSEQ, LAYERS, DEBUG, ORDER):
    RECORD = ORDER is None
    nc = bass.Bass("TRN2", target_bir_lowering=False)
    plan, wtotal = weight_plan()
    pcols, npv = pv_plan()
    NB = S // 512

    x_d = nc.dram_tensor("x", [NSEQ, 128, KC, S], F32, kind="ExternalInput")
    c_d = nc.dram_tensor("c", [128, KC, NSEQ], F32, kind="ExternalInput")
    pos_d = nc.dram_tensor("pos", [NSEQ, S], I32, kind="ExternalInput")
    w_d = nc.dram_tensor("wts", [wtotal], F32, kind="ExternalInput")
    pv_d = nc.dram_tensor("pv", [128, npv], F32, kind="ExternalInput")
    out_d = nc.dram_tensor("out", [NSEQ, 128, KC, S], F32, kind="ExternalOutput")
    dgd = nc.dram_tensor("dgd", [2, 8, 128, 31 * 128], BF16, kind="Internal")

    dbg_d = nc.dram_tensor("dbg", [128, 16384], F32, kind="ExternalOutput") if DEBUG else None
    dbg_cols = {}
    build.dbg_cols = dbg_cols
    dbg_state = {"c": 0}

    with ExitStack() as es:
        sy = Sync(nc, es)
        dbgB = Buf("dbg")

        def dump(name, ap, bufs):
            if not DEBUG or name in dbg_cols:
                return
            n = ap.shape[-1]
            p0 = 0
            c0 = dbg_state["c"]
            dbg_cols[name] = (c0, n)
            dbg_state["c"] += n
            sy.dma("pool", dbg_d.ap()[0:ap.shape[0], c0:c0 + n], ap, reads=bufs, writes=[dbgB])

        def sb(name, shape, dt):
            return es.enter_context(nc.sbuf_tensor(name, shape, dt))

        xs = sb("xs", [128, KC, S], F32)
        xsB = [[Buf(f"xs{c}_{b}") for b in range(NB)] for c in range(KC)]
        pvt = sb("pvt", [128, npv], F32)
        pvB = Buf("pv")
        ring = sb("ring", [128, RING, SLOT], BF16)
        ringB = [Buf(f"ring{i}") for i in range(RING)]
        ident_f = sb("ident_f", [128, 128], F32)
        ident_b = sb("ident_b", [128, 128], BF16)
        od1024 = sb("od1024", [128, 128], BF16)
        od256 = sb("od256", [128, 128], BF16)
        maskT = sb("maskT", [128, 128], BF16)
        constB = Buf("const")
        cin = sb("cin", [128, KC, NSEQ], F32)
        cact = sb("cact", [128, KC, NSEQ], BF16)
        cB = Buf("c")
        modt = sb("modt", [128, 96, NSEQ], F32)
        modB = Buf("mod")
        drvA = sb("drvA", [128, 4, NSEQ, 8], F32)
        drvG = sb("drvG", [128, 4, NSEQ, 8], F32)
        drvB = Buf("drv")
        nsp = sb("nsp", [128, 2, 8], F32)
        nspB = Buf("nsp")
        carry = sb("carry", [128, 8], F32)
        carryB = [Buf(f"carry{j}") for j in range(8)]
        xbh = sb("xbh", [128, 8, 4], BF16)
        xbhB = [Buf(f"xbh{j}") for j in range(8)]

        psum = [es.enter_context(nc.psum_tensor(f"ps{i}", [128, 512], F32)) for i in range(8)]
        psB = [Buf(f"ps{i}") for i in range(8)]
        ps_state = {"i": 0}

        def nextps():
            i = 2 + ps_state["i"] % 6
            ps_state["i"] += 1
            return psum[i], psB[i]

        def pv(name, a, b=None):
            c0 = pcols[name]
            if b is None:
                return pvt[:, c0 + a:c0 + a + 1]
            return pvt[:, c0 + a:c0 + b]

        ws = {"order": [] if RECORD else ORDER, "issued": 0, "acq": 0, "rel": 0, "done": set()}

        def w_ap(name):
            off, Fw = plan[name]
            return AP(w_d, off, [[Fw, 128], [1, Fw]]), Fw

        def ws_issue():
            if RECORD:
                return
            while ws["issued"] < len(ws["order"]) and ws["issued"] < ws["rel"] + RING:
                i = ws["issued"]
                src, Fw = w_ap(ws["order"][i])
                slot = i % RING
                sy.dma("pool", ring[:, slot, 0:Fw], src, writes=[ringB[slot]])
                ws["issued"] += 1

        def acquire(name):
            i = ws["acq"]
            ws["acq"] += 1
            if RECORD:
                ws["order"].append(name)
                return ring[:, 0, :], ringB[0], i
            assert ws["order"][i] == name, (ws["order"][i], name)
            assert ws["issued"] > i, "weight ring deadlock (acquire beyond issued)"
            slot = i % RING
            return ring[:, slot, :], ringB[slot], i

        def release(i):
            ws["done"].add(i)
            while ws["rel"] in ws["done"]:
                ws["done"].remove(ws["rel"])
                ws["rel"] += 1
            ws_issue()

        sy.dma("sp", pvt[:], pv_d.ap()[:, :], writes=[pvB])
        sy.dma("sp", cin[:], c_d.ap()[:, :, :], writes=[cB])
        ws_issue()
        sy.op("pool", lambda e: e.memset(ident_f[:], 1.0), writes=[constB])
        sy.op("pool", lambda e: e.affine_select(out=ident_f[:], in_=ident_f[:], pattern=[[-1, 128]], base=0,
                                                channel_multiplier=1, compare_op=ALU.is_equal, fill=0.0),
              writes=[constB])
        sy.op("pool", lambda e: e.tensor_copy(out=ident_b[:], in_=ident_f[:]), writes=[constB])
        sy.op("pool", lambda e: e.memset(od1024[:], 1.0 / 1024), writes=[constB])
        sy.op("pool", lambda e: e.memset(od256[:], 1.0 / 256), writes=[constB])
        sy.op("pool", lambda e: e.memset(maskT[:], 0.0), writes=[constB])
        sy.op("pool", lambda e: e.affine_select(out=maskT[:], in_=maskT[:], pattern=[[1, 128]], base=0,
                                                channel_multiplier=-1, compare_op=ALU.is_ge, fill=-30000.0),
              writes=[constB])
        sy.op("act", lambda e: e.activation(out=cact[:], in_=cin[:], func=AF.Silu), reads=[cB], writes=[cB])

        dgdB = [[Buf(f"dgd{j}_{jj}") for jj in range(8)] for j in range(2)]
        with ExitStack() as st0:
            dgs = [st0.enter_context(nc.sbuf_tensor(f"dgs{i}", [128, 31, 128], BF16)) for i in range(2)]
            dgsB = [Buf(f"dgs{i}") for i in range(2)]
            for j in range(2):
                if (2 * j) not in LAYERS:
                    continue
                for jj in range(8):
                    i = jj % 2
                    cw0 = pcols[("conv_w", j)] + jj * 31
                    sy.op("pool", lambda e: e.tensor_tensor(
                        out=dgs[i][:], in0=ident_b[:].unsqueeze(1).to_broadcast([128, 31, 128]),
                        in1=pvt[:, cw0:cw0 + 31].unsqueeze(2).to_broadcast([128, 31, 128]), op=ALU.mult),
                        reads=[constB, pvB], writes=[dgsB[i]])
                    sy.dma("sp", dgd.ap()[j, jj, :, :], dgs[i][:].rearrange("p a b -> p (a b)"),
                           reads=[dgsB[i]], writes=[dgdB[j][jj]])
            sy.fence()
            for j in range(2):
                for jj in range(8):
                    if dgdB[j][jj].w is not None:
                        for E_ in ("pe", "act", "dve", "pool", "sp"):
                            sy._need(E_, dgdB[j][jj].w, True)

        for l in range(4):
            for n in range(24):
                wt, wB, wi = acquire(("ada", l, n))
                pt, pB = nextps()
                for kc in range(KC):
                    sy.op("pe", lambda e, kc=kc: e.matmul(pt[:, 0:NSEQ], lhsT=wt[:, kc * 128:(kc + 1) * 128],
                                                          rhs=cact[:, kc, :], start=(kc == 0), stop=(kc == KC - 1)),
                          reads=[wB, cB], writes=[pB], inc=(kc == KC - 1))
                release(wi)
                sy.op("act", lambda e: e.activation(out=modt[:, l * 24 + n, :], in_=pt[:, 0:NSEQ], func=AF.Identity,
                                                    bias=pv(("ada_b", l), n), scale=1.0),
                      reads=[pB], sreads=[pvB], writes=[modB])
        for l in range(4):
            for s in range(NSEQ):
                sy.op("dve", lambda e: e.tensor_scalar(out=drvA[:, l, s, :], in0=modt[:, l * 24 + 8:l * 24 + 16, s],
                                                       scalar1=1.0, scalar2=None, op0=ALU.add),
                      reads=[modB], writes=[drvB])
                sy.op("dve", lambda e: e.tensor_tensor(out=drvA[:, l, s, :], in0=drvA[:, l, s, :],
                                                       in1=pv(("pre_g", l), 0, 8), op=ALU.mult),
                      reads=[pvB], writes=[drvB])
                sy.op("dve", lambda e: e.tensor_tensor(out=drvG[:, l, s, :], in0=modt[:, l * 24 + 16:l * 24 + 24, s],
                                                       in1=pv(("post_g", l), 0, 8), op=ALU.mult),
                      reads=[pvB, modB], writes=[drvB])
        spt = [sb(f"spt{i}", [128, 8], F32) for i in range(4)]
        for j in range(2):
            lam_ap = pv(("lam", j), 0, 8)
            al, ee, ww, w2 = spt
            sy.op("act", lambda e: e.activation(out=al[:], in_=lam_ap, func=AF.Abs),
                  reads=[pvB], writes=[nspB])
            sy.op("act", lambda e: e.activation(out=ee[:], in_=al[:], func=AF.Exp, scale=-1.0),
                  reads=[nspB], writes=[nspB])
            sy.op("dve", lambda e: e.tensor_scalar(out=ww[:], in0=ee[:], scalar1=2.0, scalar2=None, op0=ALU.add),
                  reads=[nspB], writes=[nspB])
            sy.op("dve", lambda e: e.reciprocal(out=ww[:], in_=ww[:]), writes=[nspB])
            sy.op("dve", lambda e: e.tensor_tensor(out=ww[:], in0=ww[:], in1=ee[:], op=ALU.mult), writes=[nspB])
            sy.op("dve", lambda e: e.tensor_tensor(out=w2[:], in0=ww[:], in1=ww[:], op=ALU.mult), writes=[nspB])
            sy.op("dve", lambda e: e.tensor_scalar(out=al[:], in0=w2[:], scalar1=1.0 / 11, scalar2=1.0 / 9, op0=ALU.mult,
                                                   op1=ALU.add), writes=[nspB])
            for cf in (1.0 / 7, 1.0 / 5, 1.0 / 3, 1.0):
                sy.op("dve", lambda e: e.tensor_tensor(out=al[:], in0=al[:], in1=w2[:], op=ALU.mult), writes=[nspB])
                sy.op("dve", lambda e, cf=cf: e.tensor_scalar(out=al[:], in0=al[:], scalar1=cf, scalar2=None, op0=ALU.add),
                      writes=[nspB])
            sy.op("dve", lambda e: e.tensor_tensor(out=al[:], in0=al[:], in1=ww[:], op=ALU.mult), writes=[nspB])
            sy.op("dve", lambda e: e.tensor_scalar(out=ee[:], in0=lam_ap, scalar1=-1.0, scalar2=0.0, op0=ALU.mult,
                                                   op1=ALU.max), reads=[pvB], writes=[nspB])
            sy.op("dve", lambda e: e.scalar_tensor_tensor(out=al[:], in0=al[:], scalar=2.0, in1=ee[:], op0=ALU.mult,
                                                          op1=ALU.add), writes=[nspB])
            sy.op("dve", lambda e: e.tensor_scalar(out=nsp[:, j, :], in0=al[:], scalar1=-8.0, scalar2=None,
                                                   op0=ALU.mult), writes=[nspB])
        dump("modt", modt[:].rearrange("p a b -> p (a b)"), [modB])
        dump("drvA", drvA[:].rearrange("p a b c -> p (a b c)"), [drvB])
        dump("drvG", drvG[:].rearrange("p a b c -> p (a b c)"), [drvB])
        dump("nsp", nsp[:].rearrange("p a b -> p (a b)"), [nspB])
        tp_stats = [None]
        def mm8(wt, wB, rhs_fn, rB, M=128, wcol0=0, prow=None):
            pt, pB = nextps()
            for kc in range(KC):
                sy.op("pe", lambda e, kc=kc: e.matmul(pt[0:M, :], lhsT=wt[:, kc * 128 + wcol0:kc * 128 + wcol0 + M],
                                                      rhs=rhs_fn(kc), start=(kc == 0), stop=(kc == KC - 1)),
                      reads=[wB] + rB, writes=[pB], inc=(kc == KC - 1))
            return pt, pB

        def rms_stats(src_fn, srcB, nch, onesm, sqt, sqB, tp, bank=None):
            tp = tp_stats[0] or tp
            sy.op("act", lambda e: e.activation(out=sqt[:, 0:nch, :], in_=src_fn(), func=AF.Square),
                  reads=srcB, writes=[sqB])
            pt, pB = (psum[bank], psB[bank]) if bank is not None else nextps()
            for kc in range(nch):
                sy.op("pe", lambda e, kc=kc: e.matmul(pt[:, :], lhsT=onesm[:], rhs=sqt[:, kc, :], start=(kc == 0),
                                                      stop=(kc == nch - 1)),
                      reads=[constB, sqB], writes=[pB], inc=(kc == nch - 1))
            sd, sdB = tp()
            sy.op("act", lambda e: e.activation(out=sd[:], in_=pt[:], func=AF.Ln, bias=EPS, scale=1.0),
                  reads=[pB], writes=[sdB])
            rs, rsB = tp()
            sy.op("act", lambda e: e.activation(out=rs[:], in_=sd[:], func=AF.Exp, scale=-0.5), reads=[sdB], writes=[rsB])
            return rs, rsB

        def prenorm(l, s, t0, hn_fn, hnB, sqt, sqB, tp, tpt=None, bank=None):
            tpt = tpt or tp
            b = t0 // 512
            rs, rsB = rms_stats(lambda: xs[:, :, t0:t0 + 512], [xsB[c][b] for c in range(KC)], KC, od1024, sqt, sqB, tp, bank)
            for kc in range(KC):
                tt, ttB = tpt()
                sy.op("dve", lambda e: e.tensor_tensor(out=tt[:], in0=xs[:, kc, t0:t0 + 512], in1=rs[:], op=ALU.mult),
                      reads=[xsB[kc][b], rsB], writes=[ttB])
                sy.op("act", lambda e: e.activation(out=hn_fn(kc), in_=tt[:], func=AF.Identity,
                                                    scale=drvA[:, l, s, kc:kc + 1], bias=modt[:, l * 24 + kc, s:s + 1]),
                      reads=[ttB], sreads=[drvB, modB], writes=[hnB[kc]])

        def postnorm_gen(l, s, t0, y, yB, sqt, sqB, tp, tpt=None):
            tpt = tpt or tp
            b = t0 // 512
            rs, rsB = rms_stats(lambda: y[:, :, :], yB, KC, od1024, sqt, sqB, tp)
            yield
            for n in range(KC):
                tt, ttB = tpt()
                sy.op("dve", lambda e: e.tensor_tensor(out=tt[:], in0=y[:, n, :], in1=rs[:], op=ALU.mult),
                      reads=[yB[n], rsB], writes=[ttB])
                sy.op("dve", lambda e: e.scalar_tensor_tensor(out=xs[:, n, t0:t0 + 512], in0=tt[:],
                                                              scalar=drvG[:, l, s, n:n + 1], in1=xs[:, n, t0:t0 + 512],
                                                              op0=ALU.mult, op1=ALU.add),
                      reads=[ttB], sreads=[drvB], writes=[xsB[n][b]])
                yield

        def postnorm(l, s, t0, y, yB, sqt, sqB, tp, tpt=None):
            for _ in postnorm_gen(l, s, t0, y, yB, sqt, sqB, tp, tpt):
                pass

        def mk_tmp_pool(ess, name, n, dt=F32, w=512):
            _UID[0] += 1
            tiles = [ess.enter_context(nc.sbuf_tensor(f"{name}{i}_{_UID[0]}", [128, w], dt)) for i in range(n)]
            bufs = [Buf(f"{name}{i}") for i in range(n)]
            st = {"i": 0}

            def get():
                i = st["i"] % n
                st["i"] += 1
                return tiles[i], bufs[i]
            return get

        def even_layer(l, s):
            j = l // 2
            with ExitStack() as el:
                def sbl(name, shape, dt):
                    return el.enter_context(nc.sbuf_tensor(f"{name}_{l}_{s}", shape, dt))
                hn = sbl("e_hn", [128, KC, 512], BF16)
                hnB = [Buf(f"e_hn{c}") for c in range(KC)]
                G = sbl("e_G", [128, KC, 544], BF16)
                GB = [Buf(f"e_G{c}") for c in range(KC)]
                Z = sbl("e_Z", [128, KC, 512], BF16)
                ZB = [Buf(f"e_Z{c}") for c in range(KC)]
                big = sbl("e_big", [128, KC, 512], F32)
                bigB = [Buf(f"e_big{c}") for c in range(KC)]
                Lo = sbl("e_Lo", [128, KC, 512], BF16)
                LoB = [Buf(f"e_Lo{c}") for c in range(KC)]
                sqt = sbl("e_sq", [128, KC, 512], BF16)
                sqB = Buf("e_sq")
                dg = sbl("e_dg", [128, 31, 128], BF16)
                dgB = Buf("e_dg")
                d4 = [sbl(f"e_d4{i}", [128, 4, 128], BF16) for i in range(2)]
                d4B = [Buf(f"e_d4{i}") for i in range(2)]
                XB = [sbl(f"e_XB{i}", [128, 516], BF16) for i in range(2)]
                XBB = [Buf(f"e_XB{i}") for i in range(2)]
                tp = mk_tmp_pool(el, "e_tf", 5, F32)
                tpt = mk_tmp_pool(el, "e_tt", 3, F32)
                tpb = mk_tmp_pool(el, "e_tb", 2, BF16)
                lt = [[sbl(f"e_lt{p}{i}", [128, 512], F32) for i in range(5)] for p in range(2)]
                ltB = [[Buf(f"e_lt{p}{i}") for i in range(5)] for p in range(2)]
                sgt = [sbl(f"e_sgt{p}", [128, 512], BF16) for p in range(2)]
                sgtB = [Buf(f"e_sgt{p}") for p in range(2)]
                zbt = [sbl(f"e_zbt{p}", [128, 512], BF16) for p in range(2)]
                zbtB = [Buf(f"e_zbt{p}") for p in range(2)]
                xcbt = [sbl(f"e_xcbt{p}", [128, 512], BF16) for p in range(2)]
                xcbtB = [Buf(f"e_xcbt{p}") for p in range(2)]

                sy.op("dve", lambda e: e.memset(G[:, :, 0:32], 0.0), writes=GB)
                sy.op("dve", lambda e: e.memset(carry[:], 0.0), writes=carryB)
                sy.op("dve", lambda e: e.memset(xbh[:], 0.0), writes=xbhB)
                hrhs = lambda kc: hn[:, kc, :]

                def w_mm8(name):
                    wt, wB, wi = acquire(name)
                    pt, pB = mm8(wt, wB, hrhs, hnB)
                    release(wi)
                    return pt, pB

                def conv_stage(jj):
                    p = jj % 2
                    pt, pB = w_mm8(("ewin", j, 8 + jj))
                    yield
                    sy.op("act", lambda e: e.activation(out=sgt[p][:], in_=pt[:], func=AF.Sigmoid),
                          reads=[pB], writes=[sgtB[p]])
                    pt, pB = w_mm8(("ewin", j, jj))
                    yield
                    sy.op("dve", lambda e: e.tensor_tensor(out=G[:, jj, 32:544], in0=pt[:], in1=sgt[p][:], op=ALU.mult),
                          reads=[pB, sgtB[p]], writes=[GB[jj]])
                    pt, pB = w_mm8(("ewin", j, 16 + jj))
                    yield
                    sy.op("act", lambda e: e.activation(out=Z[:, jj, :], in_=pt[:], func=AF.Silu),
                          reads=[pB], writes=[ZB[jj]])
                    sy.dma("sp", dg[:].rearrange("p a b -> p (a b)"), dgd.ap()[j, jj, :, :],
                           reads=[dgdB[j][jj]], writes=[dgB])
                    yield
                    yield
                    pt, pB = nextps()
                    for k in range(31):
                        sy.op("pe", lambda e, k=k: e.matmul(pt[:, :], lhsT=dg[:, k, :], rhs=G[:, jj, k + 2:k + 514],
                                                            start=(k == 0), stop=(k == 30)),
                              reads=[dgB, GB[jj]], writes=[pB], inc=(k == 30))
                        if k % 8 == 7:
                            yield
                    sy.op("act", lambda e: e.activation(out=big[:, jj, :], in_=pt[:], func=AF.Identity,
                                                        bias=pv(("conv_b", j), jj), scale=1.0),
                          reads=[pB], sreads=[pvB], writes=[bigB[jj]])
                    if jj == 0:
                        dump("G0", G[:, 0, 32:544], [GB[0]])
                        dump("aconv0", big[:, 0, :], [bigB[0]])
                        dump("Zraw0", Z[:, 0, :], [ZB[0]])
                    yield
                    sy.op("dve", lambda e: e.tensor_copy(out=G[:, jj, 0:32], in_=G[:, jj, 512:544]),
                          reads=[GB[jj]], writes=[GB[jj]])

                def lru_stage(jj):
                    p = jj % 2
                    xbt, xbB = XB[p], XBB[p]
                    pt, pB = w_mm8(("ewin", j, 24 + jj))
                    yield
                    sy.op("act", lambda e: e.activation(out=xbt[:, 4:516], in_=pt[:], func=AF.Identity),
                          reads=[pB], writes=[xbB])
                    sy.op("dve", lambda e: e.tensor_copy(out=xbt[:, 0:4], in_=xbh[:, jj, :]),
                          reads=[xbhB[jj]], writes=[xbB])
                    pt, pB = w_mm8(("ewin", j, 32 + jj))
                    yield
                    sy.op("act", lambda e: e.activation(out=zbt[p][:], in_=pt[:], func=AF.Silu), reads=[pB],
                          writes=[zbtB[p]])
                    lw0 = pcols[("lconv_w", j)] + jj * 4
                    sy.op("dve", lambda e: e.tensor_tensor(
                        out=d4[p][:], in0=ident_b[:].unsqueeze(1).to_broadcast([128, 4, 128]),
                        in1=pvt[:, lw0:lw0 + 4].unsqueeze(2).to_broadcast([128, 4, 128]), op=ALU.mult),
                        reads=[constB, pvB], writes=[d4B[p]])
                    yield
                    pt, pB = nextps()
                    for k in range(4):
                        sy.op("pe", lambda e, k=k: e.matmul(pt[:, :], lhsT=d4[p][:, k, :], rhs=xbt[:, k + 1:k + 513],
                                                            start=(k == 0), stop=(k == 3)),
                              reads=[d4B[p], xbB], writes=[pB], inc=(k == 3))
                    yield
                    xc, xcB = lt[p][0], ltB[p][0]
                    rg, rgB = lt[p][1], ltB[p][1]
                    ig, igB = lt[p][2], ltB[p][2]
                    at, atB = lt[p][3], ltB[p][3]
                    hh_, hhB = lt[p][4], ltB[p][4]
                    sy.op("act", lambda e: e.activation(out=xc[:], in_=pt[:], func=AF.Identity,
                                                        bias=pv(("lconv_b", j), jj), scale=1.0),
                          reads=[pB], sreads=[pvB], writes=[xcB])
                    yield
                    sy.op("dve", lambda e: e.tensor_copy(out=xcbt[p][:], in_=xc[:]), reads=[xcB], writes=[xcbtB[p]])
                    sy.op("dve", lambda e: e.tensor_copy(out=xbh[:, jj, :], in_=xbt[:, 512:516]),
                          reads=[xbB], writes=[xbhB[jj]])
                    yield
                    wt, wB, wi = acquire(("egate", j, jj))
                    pr, prB = nextps()
                    sy.op("pe", lambda e: e.matmul(pr[:, :], lhsT=wt[:, 0:128], rhs=xcbt[p][:], start=True, stop=True),
                          reads=[wB, xcbtB[p]], writes=[prB])
                    pi_, piB = nextps()
                    sy.op("pe", lambda e: e.matmul(pi_[:, :], lhsT=wt[:, 128:256], rhs=xcbt[p][:], start=True, stop=True),
                          reads=[wB, xcbtB[p]], writes=[piB])
                    release(wi)
                    yield
                    sy.op("act", lambda e: e.activation(out=rg[:], in_=pr[:], func=AF.Sigmoid,
                                                        bias=pv(("ba", j), jj), scale=1.0),
                          reads=[prB], sreads=[pvB], writes=[rgB])
                    yield
                    sy.op("act", lambda e: e.activation(out=ig[:], in_=pi_[:], func=AF.Sigmoid,
                                                        bias=pv(("bx", j), jj), scale=1.0),
                          reads=[piB], sreads=[pvB], writes=[igB])
                    yield
                    sy.op("act", lambda e: e.activation(out=at[:], in_=rg[:], func=AF.Exp,
                                                        scale=nsp[:, j, jj:jj + 1]),
                          reads=[rgB], sreads=[nspB], writes=[atB])
                    sy.op("dve", lambda e: e.tensor_tensor(out=ig[:], in0=ig[:], in1=xc[:], op=ALU.mult),
                          reads=[xcB], writes=[igB])
                    yield
                    sy.op("dve", lambda e: e.tensor_tensor(out=rg[:], in0=at[:], in1=at[:], op=ALU.mult),
                          reads=[atB], writes=[rgB])
                    yield
                    sy.op("act", lambda e: e.activation(out=rg[:], in_=rg[:], func=AF.Sqrt, bias=1.0, scale=-1.0),
                          writes=[rgB])
                    yield
                    sy.op("dve", lambda e: e.tensor_tensor(out=ig[:], in0=ig[:], in1=rg[:], op=ALU.mult),
                          reads=[rgB], writes=[igB])
                    yield
                    sy.op("dve", lambda e: e.tensor_tensor_scan(out=hh_[:], data0=at[:], data1=ig[:],
                                                                initial=carry[:, jj:jj + 1], op0=ALU.mult,
                                                                op1=ALU.add),
                          reads=[atB, igB], sreads=[carryB[jj]], writes=[hhB])
                    yield
                    sy.op("act", lambda e: e.activation(out=carry[:, jj:jj + 1], in_=hh_[:, 511:512],
                                                        func=AF.Identity),
                          reads=[hhB], writes=[carryB[jj]])
                    sy.op("dve", lambda e: e.tensor_tensor(out=Lo[:, jj, :], in0=hh_[:], in1=zbt[p][:], op=ALU.mult),
                          reads=[hhB, zbtB[p]], writes=[LoB[jj]])

                prenorm(l, s, 0, lambda kc: hn[:, kc, :], hnB, sqt, sqB, tp, tpt)
                dump("hn0", hn[:, 0, :], [hnB[0]])
                pending_post = None
                for t in range(NB):
                    t0 = t * 512
                    active = [pending_post] if pending_post is not None else []
                    pending_post = None
                    for jj in range(8):
                        active += [conv_stage(jj), lru_stage(jj)]
                        steps = 0
                        while active and (jj == 7 or steps < EVEN_STAGGER):
                            for g_ in list(active):
                                try:
                                    next(g_)
                                except StopIteration:
                                    active.remove(g_)
                            steps += 1
                    sy.op("dve", lambda e: e.tensor_copy(out=sqt[:], in_=big[:]), reads=bigB, writes=[sqB])
                    pm, pmB = psum[0], psB[0]
                    for kc in range(KC):
                        sy.op("pe", lambda e, kc=kc: e.matmul(pm[:, :], lhsT=od1024[:], rhs=sqt[:, kc, :],
                                                              start=(kc == 0), stop=(kc == KC - 1)),
                              reads=[constB, sqB], writes=[pmB], inc=(kc == KC - 1))
                    sy.op("act", lambda e: e.activation(out=sqt[:], in_=big[:], func=AF.Square),
                          reads=bigB, writes=[sqB])
                    p2, p2B = psum[1], psB[1]
                    for kc in range(KC):
                        sy.op("pe", lambda e, kc=kc: e.matmul(p2[:, :], lhsT=od1024[:], rhs=sqt[:, kc, :],
                                                              start=(kc == 0), stop=(kc == KC - 1)),
                              reads=[constB, sqB], writes=[p2B], inc=(kc == KC - 1))
                    w1s = [acquire(("ewout", j, n, 1)) for n in range(6)]
                    acc = [nextps() for n in range(6)]
                    for k8 in range(8):
                        for n in range(6):
                            sy.op("pe", lambda e, k8=k8, n=n: e.matmul(acc[n][0][:, :], lhsT=w1s[n][0][:, k8 * 128:(k8 + 1) * 128],
                                                                      rhs=Lo[:, k8, :], start=(k8 == 0), stop=False),
                                  reads=[w1s[n][1], LoB[k8]], writes=[acc[n][1]], inc=(k8 == 7))
                    for w_ in w1s:
                        release(w_[2])
                    mean, meanB = tp()
                    sy.op("act", lambda e: e.activation(out=mean[:], in_=pm[:], func=AF.Identity), reads=[pmB],
                          writes=[meanB])
                    var, varB = tp()
                    sy.op("dve", lambda e: e.tensor_tensor(out=var[:], in0=mean[:], in1=mean[:], op=ALU.mult),
                          reads=[meanB], writes=[varB])
                    sy.op("dve", lambda e: e.tensor_tensor(out=var[:], in0=p2[:], in1=var[:], op=ALU.subtract),
                          reads=[p2B], writes=[varB])
                    sy.op("dve", lambda e: e.tensor_scalar(out=var[:], in0=var[:], scalar1=0.0, scalar2=None,
                                                           op0=ALU.max), writes=[varB])
                    sd, sdB = tp()
                    sy.op("act", lambda e: e.activation(out=sd[:], in_=var[:], func=AF.Ln, bias=EPS, scale=1.0),
                          reads=[varB], writes=[sdB])
                    rs, rsB = tp()
                    sy.op("act", lambda e: e.activation(out=rs[:], in_=sd[:], func=AF.Exp, scale=-0.5), reads=[sdB], writes=[rsB])
                    mr, mrB = tp()
                    sy.op("dve", lambda e: e.tensor_tensor(out=mr[:], in0=mean[:], in1=rs[:], op=ALU.mult),
                          reads=[meanB, rsB], writes=[mrB])
                    w0s = [acquire(("ewout", j, n, 0)) for n in range(6)]
                    s1s = {}

                    def ln_front(jj):
                        t1, t1B = tpt()
                        sy.op("dve", lambda e: e.tensor_tensor(out=t1[:], in0=big[:, jj, :], in1=rs[:], op=ALU.mult),
                              reads=[bigB[jj], rsB], writes=[t1B])
                        sy.op("dve", lambda e: e.tensor_tensor(out=t1[:], in0=t1[:], in1=mr[:], op=ALU.subtract),
                              reads=[mrB], writes=[t1B])
                        s1, s1B = tpb()
                        sy.op("act", lambda e: e.activation(out=s1[:], in_=t1[:], func=AF.Silu,
                                                            scale=pv(("ln_g", j), jj), bias=pv(("ln_b", j), jj)),
                              reads=[t1B], sreads=[pvB], writes=[s1B])
                        s1s[jj] = (s1, s1B)

                    def ln_back(jj):
                        s1, s1B = s1s.pop(jj)
                        sy.op("dve", lambda e: e.tensor_tensor(out=Z[:, jj, :], in0=s1[:], in1=Z[:, jj, :], op=ALU.mult),
                              reads=[s1B], writes=[ZB[jj]])
                        for n in range(6):
                            sy.op("pe", lambda e, n=n: e.matmul(acc[n][0][:, :], lhsT=w0s[n][0][:, jj * 128:(jj + 1) * 128],
                                                                rhs=Z[:, jj, :], start=False, stop=(jj == 7)),
                                  reads=[w0s[n][1], ZB[jj]], writes=[acc[n][1]], inc=(jj == 7 or n == 5))

                    ln_front(0)
                    for jj in range(8):
                        if jj + 1 < 8:
                            ln_front(jj + 1)
                        ln_back(jj)
                    for w_ in w0s:
                        release(w_[2])
                    dump("Aout0", Z[:, 0, :], [ZB[0]])
                    dump("Lo0", Lo[:, 0, :], [LoB[0]])
                    if t + 1 < NB:
                        prenorm(l, s, t0 + 512, lambda kc: hn[:, kc, :], hnB, sqt, sqB, tp, tpt, bank=0)
                    for n in range(8):
                        if n < 6:
                            pt, pB = acc[n]
                        else:
                            w0, w0B, wi0 = acquire(("ewout", j, n, 0))
                            w1, w1B, wi1 = acquire(("ewout", j, n, 1))
                            pt, pB = nextps()
                            for kc in range(16):
                                wsrc, wsB = (w0, w0B) if kc < 8 else (w1, w1B)
                                src, srcB = (Z, ZB) if kc < 8 else (Lo, LoB)
                                k8 = kc % 8
                                sy.op("pe", lambda e, kc=kc, k8=k8, wsrc=wsrc, src=src: e.matmul(
                                    pt[:, :], lhsT=wsrc[:, k8 * 128:(k8 + 1) * 128], rhs=src[:, k8, :],
                                    start=(kc == 0), stop=(kc == 15)),
                                    reads=[wsB, srcB[k8]], writes=[pB], inc=(kc == 15))
                            release(wi0)
                            release(wi1)
                        sy.op("act", lambda e: e.activation(out=big[:, n, :], in_=pt[:], func=AF.Identity),
                              reads=[pB], writes=[bigB[n]])
                    dump("y0", big[:, 0, :], [bigB[0]])
                    pending_post = postnorm_gen(l, s, t0, big, bigB, sqt, sqB, tp, tpt)
                for _ in pending_post:
                    pass
                sy.fence()

        def odd_layer(l, s):
            j = l // 2
            with ExitStack() as ol:
                def sbl(name, shape, dt):
                    return ol.enter_context(nc.sbuf_tensor(f"{name}_{l}_{s}", shape, dt))
                Zs = sbl("o_Zs", [128, KC, S], BF16)
                ZsB = [[Buf(f"o_Zs{c}_{b}") for b in range(NB)] for c in range(KC)]
                cqn = sbl("o_cqn", [128, 2, S], BF16)
                cqnB = [Buf(f"o_cqn{b}") for b in range(NB)]
                ckvn = sbl("o_ckvn", [128, 2, S], BF16)
                ckvnB = [Buf(f"o_ckvn{b}") for b in range(NB)]
                kr = sbl("o_kr", [128, S], BF16)
                krB = Buf("o_kr")
                COS = sbl("o_cos", [128, S], BF16)
                SIN = sbl("o_sin", [128, S], BF16)
                csB = Buf("o_cs")
                tp = mk_tmp_pool(ol, "o_tf", 4, F32)
                tp_stats[0] = mk_tmp_pool(ol, "o_ts", 3, F32)

                with ExitStack() as rl:
                    ang = rl.enter_context(nc.sbuf_tensor(f"o_ang_{l}_{s}", [128, S], F32))
                    wk = rl.enter_context(nc.sbuf_tensor(f"o_wk_{l}_{s}", [128, S], F32))
                    wk2 = rl.enter_context(nc.sbuf_tensor(f"o_wk2_{l}_{s}", [128, S], F32))
                    ki = rl.enter_context(nc.sbuf_tensor(f"o_ki_{l}_{s}", [128, S], I32))
                    posi = ki
                    rB = Buf("o_rope")
                    R = slice(64, 96)
                    src = AP(pos_d, s * S, [[0, 32], [1, S]])
                    sy.dma("sp", posi[R, :], src, writes=[rB])
                    sy.op("dve", lambda e: e.tensor_copy(out=ang[R, :], in_=posi[R, :]), reads=[rB], writes=[rB])
                    sy.op("dve", lambda e: e.tensor_scalar(out=ang[R, :], in0=ang[R, :], scalar1=pvt[R, pcols["inv"]:pcols["inv"] + 1],
                                                           scalar2=None, op0=ALU.mult), sreads=[pvB], writes=[rB])
                    for which in range(2):
                        if which == 0:
                            sy.op("dve", lambda e: e.tensor_scalar(out=wk2[R, :], in0=ang[R, :], scalar1=math.pi / 2,
                                                                   scalar2=None, op0=ALU.add), writes=[rB])
                            a_in = wk2
                        else:
                            a_in = ang
                        sy.op("dve", lambda e: e.tensor_scalar(out=wk[R, :], in0=a_in[R, :], scalar1=1.0 / TWO_PI,
                                                               scalar2=None, op0=ALU.mult), writes=[rB])
                        sy.op("dve", lambda e: e.tensor_copy(out=ki[R, :], in_=wk[R, :]), writes=[rB])
                        sy.op("dve", lambda e: e.tensor_copy(out=wk[R, :], in_=ki[R, :]), writes=[rB])
                        sy.op("dve", lambda e: e.scalar_tensor_tensor(out=wk2[R, :], in0=wk[R, :], scalar=-PI_HI,
                                                                      in1=a_in[R, :], op0=ALU.mult, op1=ALU.add),
                              writes=[rB])
                        sy.op("dve", lambda e: e.scalar_tensor_tensor(out=wk2[R, :], in0=wk[R, :], scalar=-PI_LO,
                                                                      in1=wk2[R, :], op0=ALU.mult, op1=ALU.add),
                              writes=[rB])
                        sy.op("dve", lambda e: e.tensor_scalar(out=wk[R, :], in0=wk2[R, :], scalar1=math.pi,
                                                               scalar2=-TWO_PI, op0=ALU.is_gt, op1=ALU.mult), writes=[rB])
                        sy.op("dve", lambda e: e.tensor_tensor(out=wk2[R, :], in0=wk2[R, :], in1=wk[R, :], op=ALU.add),
                              writes=[rB])
                        sy.op("dve", lambda e: e.tensor_scalar(out=wk[R, :], in0=wk2[R, :], scalar1=-math.pi,
                                                               scalar2=TWO_PI, op0=ALU.is_lt, op1=ALU.mult), writes=[rB])
                        sy.op("dve", lambda e: e.tensor_tensor(out=wk2[R, :], in0=wk2[R, :], in1=wk[R, :], op=ALU.add),
                              writes=[rB])
                        sy.op("dve", lambda e: e.tensor_scalar(out=wk2[R, :], in0=wk2[R, :], scalar1=3.1415925,
                                                               scalar2=-3.1415925, op0=ALU.min, op1=ALU.max), writes=[rB])
                        if which == 0:
                            sy.op("act", lambda e: e.activation(out=COS[R, :], in_=wk2[R, :], func=AF.Sin),
                                  reads=[rB], writes=[csB])
                        else:
                            sy.op("dve", lambda e: e.tensor_scalar(out=wk2[R, :], in0=wk2[R, :],
                                                                   scalar1=pvt[R, pcols["sgn"]:pcols["sgn"] + 1],
                                                                   scalar2=None, op0=ALU.mult), sreads=[pvB], writes=[rB])
                            sy.op("act", lambda e: e.activation(out=SIN[R, :], in_=wk2[R, :], func=AF.Sin),
                                  reads=[rB], writes=[csB])
                    sy.fence()

                with ExitStack() as p1:
                    hn = p1.enter_context(nc.sbuf_tensor(f"o_hn_{l}_{s}", [128, KC, 1024], BF16))
                    hnB2 = [[Buf(f"o_hn{c}_{b}") for c in range(KC)] for b in range(2)]
                    raw = p1.enter_context(nc.sbuf_tensor(f"o_raw_{l}_{s}", [128, 2, 1024], F32))
                    rawB = [Buf(f"o_raw{b}") for b in range(2)]
                    krA = p1.enter_context(nc.sbuf_tensor(f"o_krA_{l}_{s}", [128, 1024], F32))
                    krAB = [Buf(f"o_krA{b}") for b in range(2)]
                    sqt = p1.enter_context(nc.sbuf_tensor(f"o_sq_{l}_{s}", [128, KC, 512], BF16))
                    sqB = Buf("o_sq")
                    R = slice(64, 96)
                    for t in range(S // 1024):
                        for b in range(2):
                            t0 = t * 1024 + b * 512
                            prenorm(l, s, t0, lambda kc, b=b: hn[:, kc, b * 512:(b + 1) * 512], hnB2[b], sqt, sqB, tp)
                        for grp, (dst, dstB, nrm) in enumerate(((cqn, cqnB, "q_norm"), (ckvn, ckvnB, "kv_norm"))):
                            for c in range(2):
                                wt, wB, wi = acquire(("owin", j, grp * 2 + c))
                                for b in range(2):
                                    pt, pB = mm8(wt, wB, lambda kc, b=b: hn[:, kc, b * 512:(b + 1) * 512], hnB2[b])
                                    sy.op("act", lambda e: e.activation(out=raw[:, c, b * 512:(b + 1) * 512], in_=pt[:],
                                                                        func=AF.Identity),
                                          reads=[pB], writes=[rawB[b]])
                                release(wi)
                            for b in range(2):
                                gb = t * 2 + b
                                rs, rsB = rms_stats(lambda b=b: raw[:, :, b * 512:(b + 1) * 512], [rawB[b]], 2, od256,
                                                    sqt, sqB, tp)
                                for c in range(2):
                                    sy.op("dve", lambda e, c=c: e.scalar_tensor_tensor(
                                        out=dst[:, c, gb * 512:(gb + 1) * 512], in0=raw[:, c, b * 512:(b + 1) * 512],
                                        scalar=pv((nrm, j), c), in1=rs[:], op0=ALU.mult, op1=ALU.mult),
                                        reads=[rawB[b], rsB], sreads=[pvB], writes=[dstB[gb]])
                        wt, wB, wi = acquire(("owin", j, 4))
                        for b in range(2):
                            pt, pB = mm8(wt, wB, lambda kc, b=b: hn[:, kc, b * 512:(b + 1) * 512], hnB2[b])
                            sy.op("act", lambda e: e.activation(out=krA[R, b * 512:(b + 1) * 512], in_=pt[R, :],
                                                                func=AF.Identity), reads=[pB], writes=[krAB[b]])
                        release(wi)
                        wt, wB, wi = acquire(("owin", j, 5))
                        for b in range(2):
                            gb = t * 2 + b
                            tk = slice(gb * 512, (gb + 1) * 512)
                            pt, pB = mm8(wt, wB, lambda kc, b=b: hn[:, kc, b * 512:(b + 1) * 512], hnB2[b])
                            t1, t1B = tp()
                            sy.op("dve", lambda e: e.tensor_tensor(out=t1[R, :], in0=krA[R, b * 512:(b + 1) * 512],
                                                                   in1=COS[R, tk], op=ALU.mult),
                                  reads=[krAB[b], csB], writes=[t1B])
                            t2, t2B = tp()
                            sy.op("dve", lambda e: e.tensor_tensor(out=t2[R, :], in0=pt[R, :], in1=SIN[R, tk], op=ALU.mult),
                                  reads=[pB, csB], writes=[t2B])
                            sy.op("dve", lambda e: e.tensor_tensor(out=kr[R, tk], in0=t1[R, :], in1=t2[R, :], op=ALU.add),
                                  reads=[t1B, t2B], writes=[krB])
                        release(wi)
                        for c in range(8):
                            wt, wB, wi = acquire(("owin", j, 6 + c))
                            for b in range(2):
                                gb = t * 2 + b
                                pt, pB = mm8(wt, wB, lambda kc, b=b: hn[:, kc, b * 512:(b + 1) * 512], hnB2[b])
                                sy.op("act", lambda e: e.activation(out=Zs[:, c, gb * 512:(gb + 1) * 512], in_=pt[:],
                                                                    func=AF.Silu), reads=[pB], writes=[ZsB[c][gb]])
                            release(wi)
                    dump("cqn", cqn[:, 0, 0:512], [cqnB[0]])
                    dump("ckvn", ckvn[:, 0, 0:512], [ckvnB[0]])
                    dump("kr", kr[:, 0:512], [krB])
                    dump("cos", COS[:, 0:512], [csB])
                    dump("sin", SIN[:, 0:512], [csB])
                    dump("zs", Zs[:, 0, 0:512], [ZsB[0][0]])
                    sy.fence()

                with ExitStack() as p2:
                    QT = [p2.enter_context(nc.sbuf_tensor(f"o_QT{i}_{l}_{s}", [128, S], BF16)) for i in range(2)]
                    KT = [p2.enter_context(nc.sbuf_tensor(f"o_KT{i}_{l}_{s}", [128, S], BF16)) for i in range(2)]
                    V = [p2.enter_context(nc.sbuf_tensor(f"o_V{i}_{l}_{s}", [128, S // 128, 128], BF16)) for i in range(2)]
                    QTB = [Buf(f"o_QT{i}") for i in range(2)]
                    KTB = [Buf(f"o_KT{i}") for i in range(2)]
                    VB = [Buf(f"o_V{i}") for i in range(2)]
                    NPT = 6
                    PT = [p2.enter_context(nc.sbuf_tensor(f"o_PT{i}_{l}_{s}", [128, 512], BF16)) for i in range(NPT)]
                    PTB = [Buf(f"o_PT{i}") for i in range(NPT)]
                    rec = p2.enter_context(nc.sbuf_tensor(f"o_rec_{l}_{s}", [128, 512], F32))
                    recB = Buf("o_rec")
                    tpg = mk_tmp_pool(p2, "o_tg", 2, F32)
                    R = slice(64, 96)
                    sy.op("dve", lambda e: e.memset(V[0][:, :, 64:128], 1.0), writes=[VB[0]])
                    sy.op("dve", lambda e: e.memset(V[1][:, :, 0:64], 1.0), writes=[VB[1]])
                    pt_i = {"i": 0}

                    def gen_head(h):
                        par = h % 2
                        qt, qB = QT[par], QTB[par]
                        kt, kB = KT[par], KTB[par]
                        vt, vB = V[par], VB[par]
                        wt, wB, wi = acquire(("ouq", j, h))
                        for b in range(NB):
                            tk = slice(b * 512, (b + 1) * 512)
                            pa, paB = nextps()
                            pb, pbB = nextps()
                            for kc in range(2):
                                sy.op("pe", lambda e, kc=kc: e.matmul(pa[0:96, :], lhsT=wt[:, kc * 192:kc * 192 + 96],
                                                                      rhs=cqn[:, kc, tk], start=(kc == 0), stop=(kc == 1)),
                                      reads=[wB, cqnB[b]], writes=[paB], inc=(kc == 1))
                            for kc in range(2):
                                sy.op("pe", lambda e, kc=kc: e.matmul(pb[0:96, :], lhsT=wt[:, kc * 192 + 96:kc * 192 + 192],
                                                                      rhs=cqn[:, kc, tk], start=(kc == 0), stop=(kc == 1)),
                                      reads=[wB, cqnB[b]], writes=[pbB], inc=(kc == 1))
                            yield
                            sy.op("act", lambda e: e.activation(out=qt[0:64, tk], in_=pa[0:64, :], func=AF.Identity),
                                  reads=[paB], writes=[qB])
                            t1, t1B = tpg()
                            sy.op("dve", lambda e: e.tensor_tensor(out=t1[R, :], in0=pa[R, :], in1=COS[R, tk], op=ALU.mult),
                                  reads=[paB, csB], writes=[t1B])
                            t2, t2B = tpg()
                            sy.op("dve", lambda e: e.tensor_tensor(out=t2[R, :], in0=pb[R, :], in1=SIN[R, tk], op=ALU.mult),
                                  reads=[pbB, csB], writes=[t2B])
                            yield
                            sy.op("dve", lambda e: e.tensor_tensor(out=qt[R, tk], in0=t1[R, :], in1=t2[R, :], op=ALU.add),
                                  reads=[t1B, t2B], writes=[qB])
                            yield
                        release(wi)
                        wt, wB, wi = acquire(("oukv", j, h))
                        for b in range(NB):
                            tk = slice(b * 512, (b + 1) * 512)
                            pa, paB = nextps()
                            for kc in range(2):
                                sy.op("pe", lambda e, kc=kc: e.matmul(pa[0:64, :], lhsT=wt[:, kc * 128:kc * 128 + 64],
                                                                      rhs=ckvn[:, kc, tk], start=(kc == 0), stop=(kc == 1)),
                                      reads=[wB, ckvnB[b]], writes=[paB], inc=(kc == 1))
                            yield
                            sy.op("act", lambda e: e.activation(out=kt[0:64, tk], in_=pa[0:64, :], func=AF.Identity),
                                  reads=[paB], writes=[kB])
                            yield
                        sy.op("pool", lambda e: e.tensor_copy(out=kt[R, :], in_=kr[R, :]), reads=[krB], writes=[kB])
                        vo = 0 if par == 0 else 64
                        for g8 in range(S // 1024):
                            pa, paB = nextps()
                            for i8 in range(8):
                                kb = g8 * 8 + i8
                                for kc in range(2):
                                    sy.op("pe", lambda e, kc=kc, kb=kb, i8=i8: e.matmul(
                                        pa[:, i8 * 64:(i8 + 1) * 64], lhsT=ckvn[:, kc, kb * 128:(kb + 1) * 128],
                                        rhs=wt[:, kc * 128 + 64:kc * 128 + 128], start=(kc == 0), stop=(kc == 1)),
                                        reads=[wB, ckvnB[kb // 4]], writes=[paB], inc=(kc == 1 and i8 == 7))
                            yield
                            sy.op("act", lambda e: e.activation(
                                out=vt[:, g8 * 8:(g8 + 1) * 8, vo:vo + 64],
                                in_=pa[:, :].rearrange("p (a b) -> p a b", b=64), func=AF.Identity),
                                reads=[paB], writes=[vB])
                            yield
                        release(wi)

                    def attn_head(h):
                        par = h % 2
                        hp = h // 2
                        qt, qB = QT[par], QTB[par]
                        kt, kB = KT[par], KTB[par]
                        vt, vB = V[par], VB[par]
                        if h == 0:
                            dump("qt", qt[:, 0:512], [qB])
                            dump("kt", kt[:, 0:512], [kB])
                            dump("v0", vt[:, 0, :], [vB])
                        items = []
                        for g in range(NB):
                            for kb in range(4 * g + 4):
                                items.append((g, kb))
                        LA = 3
                        inflight = {}
                        for i in range(len(items) + LA):
                            if i < len(items):
                                g, kb = items[i]
                                d = kb - 4 * g
                                c0 = max(0, d) * 128
                                ncols = 512 - c0
                                sp_, spB = nextps()
                                sy.op("pe", lambda e, kb=kb, g=g, c0=c0, ncols=ncols, sp_=sp_: e.matmul(
                                    sp_[:, 0:ncols], lhsT=kt[0:96, kb * 128:(kb + 1) * 128],
                                    rhs=qt[0:96, g * 512 + c0:(g + 1) * 512], start=True, stop=(d < 0)),
                                    reads=[kB, qB], writes=[spB], inc=(d < 0))
                                if d >= 0:
                                    sy.op("pe", lambda e, sp_=sp_: e.matmul(sp_[:, 0:128], lhsT=ident_b[:], rhs=maskT[:],
                                                                            start=False, stop=True),
                                          reads=[constB], writes=[spB])
                                inflight[i] = (sp_, spB, c0, ncols)
                            if i >= LA:
                                ii = i - LA
                                g, kb = items[ii]
                                sp_, spB, c0, ncols = inflight.pop(ii)
                                pi = pt_i["i"] % NPT
                                pt_i["i"] += 1
                                ptile, ptB = PT[pi], PTB[pi]
                                sy.op("act", lambda e, sp_=sp_, ncols=ncols, ptile=ptile: e.activation(
                                    out=ptile[:, 0:ncols], in_=sp_[:, 0:ncols], func=AF.Exp, scale=ATT_SCALE),
                                    reads=[spB], writes=[ptB])
                                op_, opB = psum[g % 2], psB[g % 2]
                                nkb = 4 * g + 4
                                sy.op("pe", lambda e, kb=kb, c0=c0, ncols=ncols, ptile=ptile, op_=op_, nkb=nkb: e.matmul(
                                    op_[:, c0:512], lhsT=vt[:, kb, :], rhs=ptile[:, 0:ncols],
                                    start=(kb == 0), stop=(kb == nkb - 1)),
                                    reads=[vB, ptB], writes=[opB], inc=True)
                                if kb == nkb - 1:
                                    if par == 0:
                                        num, den = slice(0, 64), slice(64, 128)
                                    else:
                                        num, den = slice(64, 128), slice(0, 64)
                                    tk = slice(g * 512, (g + 1) * 512)
                                    sy.op("act", lambda e, op_=op_: e.activation(out=rec[num, :], in_=op_[den, :], func=AF.Ln),
                                          reads=[opB], writes=[recB])
                                    sy.op("act", lambda e: e.activation(out=rec[num, :], in_=rec[num, :], func=AF.Exp, scale=-1.0),
                                          writes=[recB])
                                    o1, o1B = tp()
                                    sy.op("dve", lambda e, op_=op_, o1=o1: e.tensor_tensor(out=o1[num, :], in0=op_[num, :],
                                                                                         in1=rec[num, :], op=ALU.mult),
                                          reads=[opB, recB], writes=[o1B])
                                    sy.op("dve", lambda e, o1=o1: e.tensor_tensor(out=Zs[num, hp, tk], in0=o1[num, :],
                                                                                in1=Zs[num, hp, tk], op=ALU.mult),
                                          reads=[o1B], writes=[ZsB[hp][g]])
                            yield

                    for _ in gen_head(0):
                        pass
                    for h in range(16):
                        gens = [attn_head(h)]
                        if h + 1 < 16:
                            gens.append(gen_head(h + 1))
                        while gens:
                            for g_ in list(gens):
                                try:
                                    next(g_)
                                except StopIteration:
                                    gens.remove(g_)
                    sy.fence()

                dump("og", Zs[:, 0, 0:512], [ZsB[0][0]])
                with ExitStack() as p3:
                    big = p3.enter_context(nc.sbuf_tensor(f"o_big_{l}_{s}", [128, KC, 512], F32))
                    bigB = [Buf(f"o_big{c}") for c in range(KC)]
                    sqt = p3.enter_context(nc.sbuf_tensor(f"o_sq3_{l}_{s}", [128, KC, 512], BF16))
                    sqB = Buf("o_sq3")
                    wl = [acquire(("owout", j, n)) for n in range(8)]
                    for b in range(NB):
                        tk = slice(b * 512, (b + 1) * 512)
                        for n in range(8):
                            wt, wB, _ = wl[n]
                            pt, pB = mm8(wt, wB, lambda kc: Zs[:, kc, tk], [ZsB[c][b] for c in range(KC)])
                            sy.op("act", lambda e: e.activation(out=big[:, n, :], in_=pt[:], func=AF.Identity),
                                  reads=[pB], writes=[bigB[n]])
                        postnorm(l, s, b * 512, big, bigB, sqt, sqB, tp)
                    for _w in wl:
                        release(_w[2])
                    sy.fence()
                tp_stats[0] = None

        allxs = [xsB[c][b] for c in range(KC) for b in range(NB)]
        outB = Buf("outst")
        for s in range(NSEQ):
            for c in range(KC):
                sy.dma("sp", xs[:, c, :], x_d.ap()[s, :, c, :], writes=xsB[c])
            for l in LAYERS:
                if l % 2 == 0:
                    even_layer(l, s)
                else:
                    odd_layer(l, s)
            for c in range(KC):
                sy.dma("sp", out_d.ap()[s, :, c, :], xs[:, c, :], reads=xsB[c], writes=[outB])
        nc.sync.wait_ge(outB.dsem, outB.dcnt)
        if DEBUG and dbgB.dsem is not None:
            nc.sync.wait_ge(dbgB.dsem, dbgB.dcnt)
        if RECORD:
            return ws["order"]
        assert ws["acq"] == len(ws["order"]) == ws["issued"], (ws["acq"], len(ws["order"]), ws["issued"])
        build.nins = sy.nins
    return nc


_CACHE = {}


def kernel(**inp):
    inp = {k: np.asarray(v) for k, v in inp.items()}
    x = inp["x"].astype(np.float32, copy=False)
    B, S, Dm = x.shape
    nseq = B // NCORES
    wts = pack_weights(inp)
    pvv = pack_pv(inp)
    key = (S, nseq)
    if key not in _CACHE:
        _CACHE[key] = build(S=S, NSEQ=nseq)
    nc = _CACHE[key]
    in_maps = []
    for cid in range(NCORES):
        bs = slice(cid * nseq, (cid + 1) * nseq)
        xf = np.ascontiguousarray(x[bs].reshape(nseq, S, KC, 128).transpose(0, 3, 2, 1))
        cf = np.ascontiguousarray(inp["c"][bs].astype(np.float32).reshape(nseq, KC, 128).transpose(2, 1, 0))
        pos = np.ascontiguousarray(inp["positions"][bs].astype(np.int32))
        in_maps.append({"x": xf, "c": cf, "pos": pos, "wts": wts, "pv": pvv})
    res = run_bass_kernel_spmd(nc, in_maps, core_ids=list(range(NCORES)))
    outs = []
    for cid in range(NCORES):
        o = np.asarray(res.results[cid]["out"]).reshape(nseq, 128, KC, S)
        outs.append(o.transpose(0, 3, 2, 1).reshape(nseq, S, Dm))
    return np.ascontiguousarray(np.concatenate(outs, axis=0).astype(np.float32))
```

```python
import math
from contextlib import ExitStack
import numpy as np
import concourse.bass as bass
import concourse.mybir as mybir
from concourse.ap import AP
from concourse.bass_utils import run_bass_kernel_spmd

F32 = mybir.dt.float32
BF16 = mybir.dt.bfloat16
I32 = mybir.dt.int32
AF = mybir.ActivationFunctionType
ALU = mybir.AluOpType

D = 1024
KC = 8
NCORES = 8
EPS = 1e-6
SLOT = 1024
RING = 10
SAME_ENG_WINDOW = 3
EVEN_STAGGER = 9
ATT_SCALE = 96.0 ** -0.5
TWO_PI = 2.0 * math.pi
PI_HI = 6.28125
PI_LO = TWO_PI - 6.28125


def weight_plan():
    plan = {}
    off = 0

    def add(name, F):
        nonlocal off
        plan[name] = (off, F)
        off += 128 * F

    for l in range(4):
        for n in range(24):
            add(("ada", l, n), 1024)
    for j in range(2):
        for c in range(40):
            add(("ewin", j, c), 1024)
        for kc in range(8):
            add(("egate", j, kc), 256)
        for n in range(8):
            add(("ewout", j, n, 0), 1024)
            add(("ewout", j, n, 1), 1024)
    for j in range(2):
        for c in range(14):
            add(("owin", j, c), 1024)
        for h in range(16):
            add(("ouq", j, h), 384)
            add(("oukv", j, h), 256)
        for n in range(8):
            add(("owout", j, n), 1024)
    return plan, off


def pv_plan():
    cols = {}
    off = 0

    def add(name, n):
        nonlocal off
        cols[name] = off
        off += n

    for l in range(4):
        add(("pre_g", l), 8)
        add(("post_g", l), 8)
        add(("ada_b", l), 24)
    for j in range(2):
        add(("conv_w", j), 8 * 31)
        add(("conv_b", j), 8)
        add(("ln_g", j), 8)
        add(("ln_b", j), 8)
        add(("lconv_w", j), 8 * 4)
        add(("lconv_b", j), 8)
        add(("ba", j), 8)
        add(("bx", j), 8)
        add(("lam", j), 8)
        add(("q_norm", j), 2)
        add(("kv_norm", j), 2)
    add("inv", 1)
    add("sgn", 1)
    return cols, off


def chunkify(W):
    K, N = W.shape
    return np.ascontiguousarray(W.reshape(K // 128, 128, N // 128, 128).transpose(2, 1, 0, 3))


def vec_pc(v):
    return np.ascontiguousarray(v.reshape(-1, 128).T)


def pack_weights(inp):
    plan, total = weight_plan()
    flat = np.zeros(total, np.float32)

    def put(name, arr):
        off, F = plan[name]
        a = np.ascontiguousarray(arr, dtype=np.float32).reshape(128, F)
        flat[off:off + 128 * F] = a.reshape(-1)

    for l in range(4):
        ch = chunkify(inp["ada_w"][l])
        for n in range(24):
            put(("ada", l, n), ch[n])
    for j in range(2):
        ch = chunkify(inp["ev_w_in"][j])
        for c in range(40):
            put(("ewin", j, c), ch[c])
        wa, wx = inp["ev_lru_wa"][j], inp["ev_lru_wx"][j]
        for kc in range(8):
            g = np.zeros((128, 2, 128), np.float32)
            for hh in range(2):
                g[hh * 64:(hh + 1) * 64, 0, hh * 64:(hh + 1) * 64] = wa[2 * kc + hh]
                g[hh * 64:(hh + 1) * 64, 1, hh * 64:(hh + 1) * 64] = wx[2 * kc + hh]
            put(("egate", j, kc), g)
        wo = inp["ev_w_out"][j]
        c0 = chunkify(wo[:1024])
        c1 = chunkify(wo[1024:])
        for n in range(8):
            put(("ewout", j, n, 0), c0[n])
            put(("ewout", j, n, 1), c1[n])
    for j in range(2):
        w = inp["od_w_in"][j]
        z64 = np.zeros((1024, 64), np.float32)
        z32 = np.zeros((1024, 32), np.float32)
        kr = w[:, 512:544]
        kr1 = np.concatenate([z64, kr, z32], axis=1)
        kr2 = np.concatenate([z64, kr[:, 16:32], kr[:, 0:16], z32], axis=1)
        wcat = np.concatenate([w[:, 0:512], kr1, kr2, w[:, 544:1568]], axis=1)
        ch = chunkify(wcat)
        for c in range(14):
            put(("owin", j, c), ch[c])
        uq = inp["od_w_uq"][j].reshape(2, 128, 16, 96)
        ukv = inp["od_w_ukv"][j].reshape(2, 128, 16, 128)
        for h in range(16):
            a = uq[:, :, h, :]
            sw = np.concatenate([a[:, :, 0:64], a[:, :, 80:96], a[:, :, 64:80]], axis=2)
            both = np.concatenate([a, sw], axis=2)
            put(("ouq", j, h), both.transpose(1, 0, 2))
            put(("oukv", j, h), ukv[:, :, h, :].transpose(1, 0, 2))
        ch = chunkify(inp["od_w_out"][j])
        for n in range(8):
            put(("owout", j, n), ch[n])
    return flat


def pack_pv(inp):
    cols, n = pv_plan()
    pv = np.zeros((128, n), np.float32)

    def put(name, arr):
        a = np.asarray(arr, np.float32)
        pv[:, cols[name]:cols[name] + a.shape[1]] = a

    for l in range(4):
        put(("pre_g", l), vec_pc(inp["pre_g"][l]))
        put(("post_g", l), vec_pc(inp["post_g"][l]))
        put(("ada_b", l), vec_pc(inp["ada_b"][l]))
    for j in range(2):
        cw = inp["ev_conv_w"][j]
        put(("conv_w", j), cw.reshape(31, 8, 128).transpose(2, 1, 0).reshape(128, 8 * 31))
        put(("conv_b", j), vec_pc(inp["ev_conv_b"][j]))
        put(("ln_g", j), vec_pc(inp["ev_ln_g"][j]))
        put(("ln_b", j), vec_pc(inp["ev_ln_b"][j]))
        lw = inp["ev_lru_conv_w"][j]
        put(("lconv_w", j), lw.reshape(4, 8, 128).transpose(2, 1, 0).reshape(128, 8 * 4))
        put(("lconv_b", j), vec_pc(inp["ev_lru_conv_b"][j]))
        put(("ba", j), vec_pc(inp["ev_lru_ba"][j]))
        put(("bx", j), vec_pc(inp["ev_lru_bx"][j]))
        put(("lam", j), vec_pc(inp["ev_lru_lam"][j]))
        put(("q_norm", j), vec_pc(inp["od_q_norm"][j]))
        put(("kv_norm", j), vec_pc(inp["od_kv_norm"][j]))
    inv = (10000.0 ** (-np.arange(0, 32, 2, dtype=np.float32) / 32.0)).astype(np.float32)
    iv = np.zeros((128, 1), np.float32)
    sg = np.ones((128, 1), np.float32)
    for i in range(32):
        iv[64 + i, 0] = inv[i % 16]
        sg[64 + i, 0] = -1.0 if i < 16 else 1.0
    put("inv", iv)
    put("sgn", sg)
    return pv


_UID = [0]


class Buf:
    __slots__ = ("name", "w", "r", "dsem", "dcnt")

    def __init__(self, name):
        _UID[0] += 1
        self.name = f"{name}_{_UID[0]}"
        self.w = None
        self.r = {}
        self.dsem = None
        self.dcnt = 0


class Sync:
    ENG = ("pe", "act", "dve", "pool", "sp")

    def __init__(self, nc, es):
        self.nc = nc
        self.es = es
        self.eng = {"pe": nc.tensor, "act": nc.scalar, "dve": nc.vector, "pool": nc.gpsimd, "sp": nc.sync}
        self.sem = {k: es.enter_context(nc.semaphore("s_" + k)) for k in self.ENG}
        self.cnt = {k: 0 for k in self.ENG}
        self.known = {k: {} for k in self.ENG}
        self.nins = 0

    def _need(self, E, dep, strict):
        key, sem, val, src = dep
        if src == E and not strict and E != "pool":
            if E == "pe" or (self.cnt[E] - val) >= SAME_ENG_WINDOW:
                return
        if self.known[E].get(key, 0) >= val:
            return
        self.eng[E].wait_ge(sem, val)
        self.known[E][key] = val
        self.nins += 1

    def _deps(self, E, reads, writes, sreads, strict_all=False):
        for b in reads:
            if b.w is not None:
                self._need(E, b.w, strict_all)
        for b in sreads:
            if b.w is not None:
                self._need(E, b.w, True)
        for b in writes:
            if b.w is not None:
                self._need(E, b.w, strict_all)
            for d in b.r.values():
                self._need(E, d, strict_all)

    def op(self, E, fn, reads=(), writes=(), sreads=(), inc=True):
        self._deps(E, reads, writes, sreads)
        ins = fn(self.eng[E])
        self.nins += 1
        if inc:
            self.cnt[E] += 1
            ins.then_inc(self.sem[E], 1)
            me = (E, self.sem[E], self.cnt[E], E)
        else:
            me = (E, self.sem[E], self.cnt[E] + 1, E)
        for b in writes:
            b.w = me
            b.r = {}
        for b in reads:
            b.r[E] = me
        for b in sreads:
            b.r[E] = me
        return ins

    def dma(self, Q, out, in_, reads=(), writes=()):
        self._deps(Q, reads, writes, (), strict_all=True)
        tgt = writes[0] if writes else reads[0]
        if tgt.dsem is None:
            tgt.dsem = self.es.enter_context(self.nc.semaphore("d_" + tgt.name))
        tgt.dcnt += 16
        self.eng[Q].dma_start(out=out, in_=in_).then_inc(tgt.dsem, 16)
        self.nins += 1
        me = ("d_" + tgt.name, tgt.dsem, tgt.dcnt, None)
        for b in writes:
            b.w = me
            b.r = {}
        for b in reads:
            b.r["dma_" + tgt.name] = me
        return me

    def fence(self):
        for E in ("pe", "act", "dve", "pool", "sp"):
            for Fg in ("pe", "act", "dve", "pool"):
                if Fg != E and self.cnt[Fg] > 0:
                    self._need(E, (Fg, self.sem[Fg], self.cnt[Fg], Fg), True)


def build(S=2048, NSEQ=2, LAYERS=(0, 1, 2, 3), DEBUG=False):
    order = _build(S, NSEQ, LAYERS, DEBUG, None)
    return _build(S, NSEQ, LAYERS, DEBUG, order)


def _build(S, NSEQ, LAYERS, DEBUG, ORDER):
    RECORD = ORDER is None
    nc = bass.Bass("TRN2", target_bir_lowering=False)
    plan, wtotal = weight_plan()
    pcols, npv = pv_plan()
    NB = S // 512

    x_d = nc.dram_tensor("x", [NSEQ, 128, KC, S], F32, kind="ExternalInput")
    c_d = nc.dram_tensor("c", [128, KC, NSEQ], F32, kind="ExternalInput")
    pos_d = nc.dram_tensor("pos", [NSEQ, S], I32, kind="ExternalInput")
    w_d = nc.dram_tensor("wts", [wtotal], F32, kind="ExternalInput")
    pv_d = nc.dram_tensor("pv", [128, npv], F32, kind="ExternalInput")
    out_d = nc.dram_tensor("out", [NSEQ, 128, KC, S], F32, kind="ExternalOutput")
    dgd = nc.dram_tensor("dgd", [2, 8, 128, 31 * 128], BF16, kind="Internal")

    dbg_d = nc.dram_tensor("dbg", [128, 16384], F32, kind="ExternalOutput") if DEBUG else None
    dbg_cols = {}
    build.dbg_cols = dbg_cols
    dbg_state = {"c": 0}

    with ExitStack() as es:
        sy = Sync(nc, es)
        dbgB = Buf("dbg")

        def dump(name, ap, bufs):
            if not DEBUG or name in dbg_cols:
                return
            n = ap.shape[-1]
            p0 = 0
            c0 = dbg_state["c"]
            dbg_cols[name] = (c0, n)
            dbg_state["c"] += n
            sy.dma("pool", dbg_d.ap()[0:ap.shape[0], c0:c0 + n], ap, reads=bufs, writes=[dbgB])

        def sb(name, shape, dt):
            return es.enter_context(nc.sbuf_tensor(name, shape, dt))

        xs = sb("xs", [128, KC, S], F32)
        xsB = [[Buf(f"xs{c}_{b}") for b in range(NB)] for c in range(KC)]
        pvt = sb("pvt", [128, npv], F32)
        pvB = Buf("pv")
        ring = sb("ring", [128, RING, SLOT], BF16)
        ringB = [Buf(f"ring{i}") for i in range(RING)]
        ident_f = sb("ident_f", [128, 128], F32)
        ident_b = sb("ident_b", [128, 128], BF16)
        od1024 = sb("od1024", [128, 128], BF16)
        od256 = sb("od256", [128, 128], BF16)
        maskT = sb("maskT", [128, 128], BF16)
        constB = Buf("const")
        cin = sb("cin", [128, KC, NSEQ], F32)
        cact = sb("cact", [128, KC, NSEQ], BF16)
        cB = Buf("c")
        modt = sb("modt", [128, 96, NSEQ], F32)
        modB = Buf("mod")
        drvA = sb("drvA", [128, 4, NSEQ, 8], F32)
        drvG = sb("drvG", [128, 4, NSEQ, 8], F32)
        drvB = Buf("drv")
        nsp = sb("nsp", [128, 2, 8], F32)
        nspB = Buf("nsp")
        carry = sb("carry", [128, 8], F32)
        carryB = [Buf(f"carry{j}") for j in range(8)]
        xbh = sb("xbh", [128, 8, 4], BF16)
        xbhB = [Buf(f"xbh{j}") for j in range(8)]

        psum = [es.enter_context(nc.psum_tensor(f"ps{i}", [128, 512], F32)) for i in range(8)]
        psB = [Buf(f"ps{i}") for i in range(8)]
        ps_state = {"i": 0}

        def nextps():
            i = 2 + ps_state["i"] % 6
            ps_state["i"] += 1
            return psum[i], psB[i]

        def pv(name, a, b=None):
            c0 = pcols[name]
            if b is None:
                return pvt[:, c0 + a:c0 + a + 1]
            return pvt[:, c0 + a:c0 + b]

        ws = {"order": [] if RECORD else ORDER, "issued": 0, "acq": 0, "rel": 0, "done": set()}

        def w_ap(name):
            off, Fw = plan[name]
            return AP(w_d, off, [[Fw, 128], [1, Fw]]), Fw

        def ws_issue():
            if RECORD:
                return
            while ws["issued"] < len(ws["order"]) and ws["issued"] < ws["rel"] + RING:
                i = ws["issued"]
                src, Fw = w_ap(ws["order"][i])
                slot = i % RING
                sy.dma("pool", ring[:, slot, 0:Fw], src, writes=[ringB[slot]])
                ws["issued"] += 1

        def acquire(name):
            i = ws["acq"]
            ws["acq"] += 1
            if RECORD:
                ws["order"].append(name)
                return ring[:, 0, :], ringB[0], i
            assert ws["order"][i] == name, (ws["order"][i], name)
            assert ws["issued"] > i, "weight ring deadlock (acquire beyond issued)"
            slot = i % RING
            return ring[:, slot, :], ringB[slot], i

        def release(i):
            ws["done"].add(i)
            while ws["rel"] in ws["done"]:
                ws["done"].remove(ws["rel"])
                ws["rel"] += 1
            ws_issue()

        sy.dma("sp", pvt[:], pv_d.ap()[:, :], writes=[pvB])
        sy.dma("sp", cin[:], c_d.ap()[:, :, :], writes=[cB])
        ws_issue()
        sy.op("pool", lambda e: e.memset(ident_f[:], 1.0), writes=[constB])
        sy.op("pool", lambda e: e.affine_select(out=ident_f[:], in_=ident_f[:], pattern=[[-1, 128]], base=0,
                                                channel_multiplier=1, compare_op=ALU.is_equal, fill=0.0),
              writes=[constB])
        sy.op("pool", lambda e: e.tensor_copy(out=ident_b[:], in_=ident_f[:]), writes=[constB])
        sy.op("pool", lambda e: e.memset(od1024[:], 1.0 / 1024), writes=[constB])
        sy.op("pool", lambda e: e.memset(od256[:], 1.0 / 256), writes=[constB])
        sy.op("pool", lambda e: e.memset(maskT[:], 0.0), writes=[constB])
        sy.op("pool", lambda e: e.affine_select(out=maskT[:], in_=maskT[:], pattern=[[1, 128]], base=0,
                                                channel_multiplier=-1, compare_op=ALU.is_ge, fill=-30000.0),
              writes=[constB])
        sy.op("act", lambda e: e.activation(out=cact[:], in_=cin[:], func=AF.Silu), reads=[cB], writes=[cB])

        dgdB = [[Buf(f"dgd{j}_{jj}") for jj in range(8)] for j in range(2)]
        with ExitStack() as st0:
            dgs = [st0.enter_context(nc.sbuf_tensor(f"dgs{i}", [128, 31, 128], BF16)) for i in range(2)]
            dgsB = [Buf(f"dgs{i}") for i in range(2)]
            for j in range(2):
                if (2 * j) not in LAYERS:
                    continue
                for jj in range(8):
                    i = jj % 2
                    cw0 = pcols[("conv_w", j)] + jj * 31
                    sy.op("pool", lambda e: e.tensor_tensor(
                        out=dgs[i][:], in0=ident_b[:].unsqueeze(1).to_broadcast([128, 31, 128]),
                        in1=pvt[:, cw0:cw0 + 31].unsqueeze(2).to_broadcast([128, 31, 128]), op=ALU.mult),
                        reads=[constB, pvB], writes=[dgsB[i]])
                    sy.dma("sp", dgd.ap()[j, jj, :, :], dgs[i][:].rearrange("p a b -> p (a b)"),
                           reads=[dgsB[i]], writes=[dgdB[j][jj]])
            sy.fence()
            for j in range(2):
                for jj in range(8):
                    if dgdB[j][jj].w is not None:
                        for E_ in ("pe", "act", "dve", "pool", "sp"):
                            sy._need(E_, dgdB[j][jj].w, True)

        for l in range(4):
            for n in range(24):
                wt, wB, wi = acquire(("ada", l, n))
                pt, pB = nextps()
                for kc in range(KC):
                    sy.op("pe", lambda e, kc=kc: e.matmul(pt[:, 0:NSEQ], lhsT=wt[:, kc * 128:(kc + 1) * 128],
                                                          rhs=cact[:, kc, :], start=(kc == 0), stop=(kc == KC - 1)),
                          reads=[wB, cB], writes=[pB], inc=(kc == KC - 1))
                release(wi)
                sy.op("act", lambda e: e.activation(out=modt[:, l * 24 + n, :], in_=pt[:, 0:NSEQ], func=AF.Identity,
                                                    bias=pv(("ada_b", l), n), scale=1.0),
                      reads=[pB], sreads=[pvB], writes=[modB])
        for l in range(4):
            for s in range(NSEQ):
                sy.op("dve", lambda e: e.tensor_scalar(out=drvA[:, l, s, :], in0=modt[:, l * 24 + 8:l * 24 + 16, s],
                                                       scalar1=1.0, scalar2=None, op0=ALU.add),
                      reads=[modB], writes=[drvB])
                sy.op("dve", lambda e: e.tensor_tensor(out=drvA[:, l, s, :], in0=drvA[:, l, s, :],
                                                       in1=pv(("pre_g", l), 0, 8), op=ALU.mult),
                      reads=[pvB], writes=[drvB])
                sy.op("dve", lambda e: e.tensor_tensor(out=drvG[:, l, s, :], in0=modt[:, l * 24 + 16:l * 24 + 24, s],
                                                       in1=pv(("post_g", l), 0, 8), op=ALU.mult),
                      reads=[pvB, modB], writes=[drvB])
        spt = [sb(f"spt{i}", [128, 8], F32) for i in range(4)]
        for j in range(2):
            lam_ap = pv(("lam", j), 0, 8)
            al, ee, ww, w2 = spt
            sy.op("act", lambda e: e.activation(out=al[:], in_=lam_ap, func=AF.Abs),
                  reads=[pvB], writes=[nspB])
            sy.op("act", lambda e: e.activation(out=ee[:], in_=al[:], func=AF.Exp, scale=-1.0),
                  reads=[nspB], writes=[nspB])
            sy.op("dve", lambda e: e.tensor_scalar(out=ww[:], in0=ee[:], scalar1=2.0, scalar2=None, op0=ALU.add),
                  reads=[nspB], writes=[nspB])
            sy.op("dve", lambda e: e.reciprocal(out=ww[:], in_=ww[:]), writes=[nspB])
            sy.op("dve", lambda e: e.tensor_tensor(out=ww[:], in0=ww[:], in1=ee[:], op=ALU.mult), writes=[nspB])
            sy.op("dve", lambda e: e.tensor_tensor(out=w2[:], in0=ww[:], in1=ww[:], op=ALU.mult), writes=[nspB])
            sy.op("dve", lambda e: e.tensor_scalar(out=al[:], in0=w2[:], scalar1=1.0 / 11, scalar2=1.0 / 9, op0=ALU.mult,
                                                   op1=ALU.add), writes=[nspB])
            for cf in (1.0 / 7, 1.0 / 5, 1.0 / 3, 1.0):
                sy.op("dve", lambda e: e.tensor_tensor(out=al[:], in0=al[:], in1=w2[:], op=ALU.mult), writes=[nspB])
                sy.op("dve", lambda e, cf=cf: e.tensor_scalar(out=al[:], in0=al[:], scalar1=cf, scalar2=None, op0=ALU.add),
                      writes=[nspB])
            sy.op("dve", lambda e: e.tensor_tensor(out=al[:], in0=al[:], in1=ww[:], op=ALU.mult), writes=[nspB])
            sy.op("dve", lambda e: e.tensor_scalar(out=ee[:], in0=lam_ap, scalar1=-1.0, scalar2=0.0, op0=ALU.mult,
                                                   op1=ALU.max), reads=[pvB], writes=[nspB])
            sy.op("dve", lambda e: e.scalar_tensor_tensor(out=al[:], in0=al[:], scalar=2.0, in1=ee[:], op0=ALU.mult,
                                                          op1=ALU.add), writes=[nspB])
            sy.op("dve", lambda e: e.tensor_scalar(out=nsp[:, j, :], in0=al[:], scalar1=-8.0, scalar2=None,
                                                   op0=ALU.mult), writes=[nspB])
        dump("modt", modt[:].rearrange("p a b -> p (a b)"), [modB])
        dump("drvA", drvA[:].rearrange("p a b c -> p (a b c)"), [drvB])
        dump("drvG", drvG[:].rearrange("p a b c -> p (a b c)"), [drvB])
        dump("nsp", nsp[:].rearrange("p a b -> p (a b)"), [nspB])
        tp_stats = [None]
        def mm8(wt, wB, rhs_fn, rB, M=128, wcol0=0, prow=None):
            pt, pB = nextps()
            for kc in range(KC):
                sy.op("pe", lambda e, kc=kc: e.matmul(pt[0:M, :], lhsT=wt[:, kc * 128 + wcol0:kc * 128 + wcol0 + M],
                                                      rhs=rhs_fn(kc), start=(kc == 0), stop=(kc == KC - 1)),
                      reads=[wB] + rB, writes=[pB], inc=(kc == KC - 1))
            return pt, pB

        def rms_stats(src_fn, srcB, nch, onesm, sqt, sqB, tp, bank=None):
            tp = tp_stats[0] or tp
            sy.op("act", lambda e: e.activation(out=sqt[:, 0:nch, :], in_=src_fn(), func=AF.Square),
                  reads=srcB, writes=[sqB])
            pt, pB = (psum[bank], psB[bank]) if bank is not None else nextps()
            for kc in range(nch):
                sy.op("pe", lambda e, kc=kc: e.matmul(pt[:, :], lhsT=onesm[:], rhs=sqt[:, kc, :], start=(kc == 0),
                                                      stop=(kc == nch - 1)),
                      reads=[constB, sqB], writes=[pB], inc=(kc == nch - 1))
            sd, sdB = tp()
            sy.op("act", lambda e: e.activation(out=sd[:], in_=pt[:], func=AF.Ln, bias=EPS, scale=1.0),
                  reads=[pB], writes=[sdB])
            rs, rsB = tp()
            sy.op("act", lambda e: e.activation(out=rs[:], in_=sd[:], func=AF.Exp, scale=-0.5), reads=[sdB], writes=[rsB])
            return rs, rsB

        def prenorm(l, s, t0, hn_fn, hnB, sqt, sqB, tp, tpt=None, bank=None):
            tpt = tpt or tp
            b = t0 // 512
            rs, rsB = rms_stats(lambda: xs[:, :, t0:t0 + 512], [xsB[c][b] for c in range(KC)], KC, od1024, sqt, sqB, tp, bank)
            for kc in range(KC):
                tt, ttB = tpt()
                sy.op("dve", lambda e: e.tensor_tensor(out=tt[:], in0=xs[:, kc, t0:t0 + 512], in1=rs[:], op=ALU.mult),
                      reads=[xsB[kc][b], rsB], writes=[ttB])
                sy.op("act", lambda e: e.activation(out=hn_fn(kc), in_=tt[:], func=AF.Identity,
                                                    scale=drvA[:, l, s, kc:kc + 1], bias=modt[:, l * 24 + kc, s:s + 1]),
                      reads=[ttB], sreads=[drvB, modB], writes=[hnB[kc]])

        def postnorm_gen(l, s, t0, y, yB, sqt, sqB, tp, tpt=None):
            tpt = tpt or tp
            b = t0 // 512
            rs, rsB = rms_stats(lambda: y[:, :, :], yB, KC, od1024, sqt, sqB, tp)
            yield
            for n in range(KC):
                tt, ttB = tpt()
                sy.op("dve", lambda e: e.tensor_tensor(out=tt[:], in0=y[:, n, :], in1=rs[:], op=ALU.mult),
                      reads=[yB[n], rsB], writes=[ttB])
                sy.op("dve", lambda e: e.scalar_tensor_tensor(out=xs[:, n, t0:t0 + 512], in0=tt[:],
                                                              scalar=drvG[:, l, s, n:n + 1], in1=xs[:, n, t0:t0 + 512],
                                                              op0=ALU.mult, op1=ALU.add),
                      reads=[ttB], sreads=[drvB], writes=[xsB[n][b]])
                yield

        def postnorm(l, s, t0, y, yB, sqt, sqB, tp, tpt=None):
            for _ in postnorm_gen(l, s, t0, y, yB, sqt, sqB, tp, tpt):
                pass

        def mk_tmp_pool(ess, name, n, dt=F32, w=512):
            _UID[0] += 1
            tiles = [ess.enter_context(nc.sbuf_tensor(f"{name}{i}_{_UID[0]}", [128, w], dt)) for i in range(n)]
            bufs = [Buf(f"{name}{i}") for i in range(n)]
            st = {"i": 0}

            def get():
                i = st["i"] % n
                st["i"] += 1
                return tiles[i], bufs[i]
            return get

        def even_layer(l, s):
            j = l // 2
            with ExitStack() as el:
                def sbl(name, shape, dt):
                    return el.enter_context(nc.sbuf_tensor(f"{name}_{l}_{s}", shape, dt))
                hn = sbl("e_hn", [128, KC, 512], BF16)
                hnB = [Buf(f"e_hn{c}") for c in range(KC)]
                G = sbl("e_G", [128, KC, 544], BF16)
                GB = [Buf(f"e_G{c}") for c in range(KC)]
                Z = sbl("e_Z", [128, KC, 512], BF16)
                ZB = [Buf(f"e_Z{c}") for c in range(KC)]
                big = sbl("e_big", [128, KC, 512], F32)
                bigB = [Buf(f"e_big{c}") for c in range(KC)]
                Lo = sbl("e_Lo", [128, KC, 512], BF16)
                LoB = [Buf(f"e_Lo{c}") for c in range(KC)]
                sqt = sbl("e_sq", [128, KC, 512], BF16)
                sqB = Buf("e_sq")
                dg = sbl("e_dg", [128, 31, 128], BF16)
                dgB = Buf("e_dg")
                d4 = [sbl(f"e_d4{i}", [128, 4, 128], BF16) for i in range(2)]
                d4B = [Buf(f"e_d4{i}") for i in range(2)]
                XB = [sbl(f"e_XB{i}", [128, 516], BF16) for i in range(2)]
                XBB = [Buf(f"e_XB{i}") for i in range(2)]
                tp = mk_tmp_pool(el, "e_tf", 5, F32)
                tpt = mk_tmp_pool(el, "e_tt", 3, F32)
                tpb = mk_tmp_pool(el, "e_tb", 2, BF16)
                lt = [[sbl(f"e_lt{p}{i}", [128, 512], F32) for i in range(5)] for p in range(2)]
                ltB = [[Buf(f"e_lt{p}{i}") for i in range(5)] for p in range(2)]
                sgt = [sbl(f"e_sgt{p}", [128, 512], BF16) for p in range(2)]
                sgtB = [Buf(f"e_sgt{p}") for p in range(2)]
                zbt = [sbl(f"e_zbt{p}", [128, 512], BF16) for p in range(2)]
                zbtB = [Buf(f"e_zbt{p}") for p in range(2)]
                xcbt = [sbl(f"e_xcbt{p}", [128, 512], BF16) for p in range(2)]
                xcbtB = [Buf(f"e_xcbt{p}") for p in range(2)]

                sy.op("dve", lambda e: e.memset(G[:, :, 0:32], 0.0), writes=GB)
                sy.op("dve", lambda e: e.memset(carry[:], 0.0), writes=carryB)
                sy.op("dve", lambda e: e.memset(xbh[:], 0.0), writes=xbhB)
                hrhs = lambda kc: hn[:, kc, :]

                def w_mm8(name):
                    wt, wB, wi = acquire(name)
                    pt, pB = mm8(wt, wB, hrhs, hnB)
                    release(wi)
                    return pt, pB

                def conv_stage(jj):
                    p = jj % 2
                    pt, pB = w_mm8(("ewin", j, 8 + jj))
                    yield
                    sy.op("act", lambda e: e.activation(out=sgt[p][:], in_=pt[:], func=AF.Sigmoid),
                          reads=[pB], writes=[sgtB[p]])
                    pt, pB = w_mm8(("ewin", j, jj))
                    yield
                    sy.op("dve", lambda e: e.tensor_tensor(out=G[:, jj, 32:544], in0=pt[:], in1=sgt[p][:], op=ALU.mult),
                          reads=[pB, sgtB[p]], writes=[GB[jj]])
                    pt, pB = w_mm8(("ewin", j, 16 + jj))
                    yield
                    sy.op("act", lambda e: e.activation(out=Z[:, jj, :], in_=pt[:], func=AF.Silu),
                          reads=[pB], writes=[ZB[jj]])
                    sy.dma("sp", dg[:].rearrange("p a b -> p (a b)"), dgd.ap()[j, jj, :, :],
                           reads=[dgdB[j][jj]], writes=[dgB])
                    yield
                    yield
                    pt, pB = nextps()
                    for k in range(31):
                        sy.op("pe", lambda e, k=k: e.matmul(pt[:, :], lhsT=dg[:, k, :], rhs=G[:, jj, k + 2:k + 514],
                                                            start=(k == 0), stop=(k == 30)),
                              reads=[dgB, GB[jj]], writes=[pB], inc=(k == 30))
                        if k % 8 == 7:
                            yield
                    sy.op("act", lambda e: e.activation(out=big[:, jj, :], in_=pt[:], func=AF.Identity,
                                                        bias=pv(("conv_b", j), jj), scale=1.0),
                          reads=[pB], sreads=[pvB], writes=[bigB[jj]])
                    if jj == 0:
                        dump("G0", G[:, 0, 32:544], [GB[0]])
                        dump("aconv0", big[:, 0, :], [bigB[0]])
                        dump("Zraw0", Z[:, 0, :], [ZB[0]])
                    yield
                    sy.op("dve", lambda e: e.tensor_copy(out=G[:, jj, 0:32], in_=G[:, jj, 512:544]),
                          reads=[GB[jj]], writes=[GB[jj]])

                def lru_stage(jj):
                    p = jj % 2
                    xbt, xbB = XB[p], XBB[p]
                    pt, pB = w_mm8(("ewin", j, 24 + jj))
                    yield
                    sy.op("act", lambda e: e.activation(out=xbt[:, 4:516], in_=pt[:], func=AF.Identity),
                          reads=[pB], writes=[xbB])
                    sy.op("dve", lambda e: e.tensor_copy(out=xbt[:, 0:4], in_=xbh[:, jj, :]),
                          reads=[xbhB[jj]], writes=[xbB])
                    pt, pB = w_mm8(("ewin", j, 32 + jj))
                    yield
                    sy.op("act", lambda e: e.activation(out=zbt[p][:], in_=pt[:], func=AF.Silu), reads=[pB],
                          writes=[zbtB[p]])
                    lw0 = pcols[("lconv_w", j)] + jj * 4
                    sy.op("dve", lambda e: e.tensor_tensor(
                        out=d4[p][:], in0=ident_b[:].unsqueeze(1).to_broadcast([128, 4, 128]),
                        in1=pvt[:, lw0:lw0 + 4].unsqueeze(2).to_broadcast([128, 4, 128]), op=ALU.mult),
                        reads=[constB, pvB], writes=[d4B[p]])
                    yield
                    pt, pB = nextps()
                    for k in range(4):
                        sy.op("pe", lambda e, k=k: e.matmul(pt[:, :], lhsT=d4[p][:, k, :], rhs=xbt[:, k + 1:k + 513],
                                                            start=(k == 0), stop=(k == 3)),
                              reads=[d4B[p], xbB], writes=[pB], inc=(k == 3))
                    yield
                    xc, xcB = lt[p][0], ltB[p][0]
                    rg, rgB = lt[p][1], ltB[p][1]
                    ig, igB = lt[p][2], ltB[p][2]
                    at, atB = lt[p][3], ltB[p][3]
                    hh_, hhB = lt[p][4], ltB[p][4]
                    sy.op("act", lambda e: e.activation(out=xc[:], in_=pt[:], func=AF.Identity,
                                                        bias=pv(("lconv_b", j), jj), scale=1.0),
                          reads=[pB], sreads=[pvB], writes=[xcB])
                    yield
                    sy.op("dve", lambda e: e.tensor_copy(out=xcbt[p][:], in_=xc[:]), reads=[xcB], writes=[xcbtB[p]])
                    sy.op("dve", lambda e: e.tensor_copy(out=xbh[:, jj, :], in_=xbt[:, 512:516]),
                          reads=[xbB], writes=[xbhB[jj]])
                    yield
                    wt, wB, wi = acquire(("egate", j, jj))
                    pr, prB = nextps()
                    sy.op("pe", lambda e: e.matmul(pr[:, :], lhsT=wt[:, 0:128], rhs=xcbt[p][:], start=True, stop=True),
                          reads=[wB, xcbtB[p]], writes=[prB])
                    pi_, piB = nextps()
                    sy.op("pe", lambda e: e.matmul(pi_[:, :], lhsT=wt[:, 128:256], rhs=xcbt[p][:], start=True, stop=True),
                          reads=[wB, xcbtB[p]], writes=[piB])
                    release(wi)
                    yield
                    sy.op("act", lambda e: e.activation(out=rg[:], in_=pr[:], func=AF.Sigmoid,
                                                        bias=pv(("ba", j), jj), scale=1.0),
                          reads=[prB], sreads=[pvB], writes=[rgB])
                    yield
                    sy.op("act", lambda e: e.activation(out=ig[:], in_=pi_[:], func=AF.Sigmoid,
                                                        bias=pv(("bx", j), jj), scale=1.0),
                          reads=[piB], sreads=[pvB], writes=[igB])
                    yield
                    sy.op("act", lambda e: e.activation(out=at[:], in_=rg[:], func=AF.Exp,
                                                        scale=nsp[:, j, jj:jj + 1]),
                          reads=[rgB], sreads=[nspB], writes=[atB])
                    sy.op("dve", lambda e: e.tensor_tensor(out=ig[:], in0=ig[:], in1=xc[:], op=ALU.mult),
                          reads=[xcB], writes=[igB])
                    yield
                    sy.op("dve", lambda e: e.tensor_tensor(out=rg[:], in0=at[:], in1=at[:], op=ALU.mult),
                          reads=[atB], writes=[rgB])
                    yield
                    sy.op("act", lambda e: e.activation(out=rg[:], in_=rg[:], func=AF.Sqrt, bias=1.0, scale=-1.0),
                          writes=[rgB])
                    yield
                    sy.op("dve", lambda e: e.tensor_tensor(out=ig[:], in0=ig[:], in1=rg[:], op=ALU.mult),
                          reads=[rgB], writes=[igB])
                    yield
                    sy.op("dve", lambda e: e.tensor_tensor_scan(out=hh_[:], data0=at[:], data1=ig[:],
                                                                initial=carry[:, jj:jj + 1], op0=ALU.mult,
                                                                op1=ALU.add),
                          reads=[atB, igB], sreads=[carryB[jj]], writes=[hhB])
                    yield
                    sy.op("act", lambda e: e.activation(out=carry[:, jj:jj + 1], in_=hh_[:, 511:512],
                                                        func=AF.Identity),
                          reads=[hhB], writes=[carryB[jj]])
                    sy.op("dve", lambda e: e.tensor_tensor(out=Lo[:, jj, :], in0=hh_[:], in1=zbt[p][:], op=ALU.mult),
                          reads=[hhB, zbtB[p]], writes=[LoB[jj]])

                prenorm(l, s, 0, lambda kc: hn[:, kc, :], hnB, sqt, sqB, tp, tpt)
                dump("hn0", hn[:, 0, :], [hnB[0]])
                pending_post = None
                for t in range(NB):
                    t0 = t * 512
                    active = [pending_post] if pending_post is not None else []
                    pending_post = None
                    for jj in range(8):
                        active += [conv_stage(jj), lru_stage(jj)]
                        steps = 0
                        while active and (jj == 7 or steps < EVEN_STAGGER):
                            for g_ in list(active):
                                try:
                                    next(g_)
                                except StopIteration:
                                    active.remove(g_)
                            steps += 1
                    w1s = [acquire(("ewout", j, n, 1)) for n in range(6)]
                    acc = [nextps() for n in range(6)]

                    def lo_half(k8):
                        for n in range(6):
                            sy.op("pe", lambda e, n=n: e.matmul(acc[n][0][:, :], lhsT=w1s[n][0][:, k8 * 128:(k8 + 1) * 128],
                                                                rhs=Lo[:, k8, :], start=(k8 == 0), stop=False),
                                  reads=[w1s[n][1], LoB[k8]], writes=[acc[n][1]], inc=(n == 5))
                    for k8 in range(7):
                        lo_half(k8)
                    sy.op("dve", lambda e: e.tensor_copy(out=sqt[:], in_=big[:]), reads=bigB, writes=[sqB])
                    pm, pmB = psum[0], psB[0]
                    for kc in range(KC):
                        sy.op("pe", lambda e, kc=kc: e.matmul(pm[:, :], lhsT=od1024[:], rhs=sqt[:, kc, :],
                                                              start=(kc == 0), stop=(kc == KC - 1)),
                              reads=[constB, sqB], writes=[pmB], inc=(kc == KC - 1))
                    sy.op("act", lambda e: e.activation(out=sqt[:], in_=big[:], func=AF.Square),
                          reads=bigB, writes=[sqB])
                    p2, p2B = psum[1], psB[1]
                    for kc in range(KC):
                        sy.op("pe", lambda e, kc=kc: e.matmul(p2[:, :], lhsT=od1024[:], rhs=sqt[:, kc, :],
                                                              start=(kc == 0), stop=(kc == KC - 1)),
                              reads=[constB, sqB], writes=[p2B], inc=(kc == KC - 1))
                    lo_half(7)
                    for w_ in w1s:
                        release(w_[2])
                    mean, meanB = tp()
                    sy.op("act", lambda e: e.activation(out=mean[:], in_=pm[:], func=AF.Identity), reads=[pmB],
                          writes=[meanB])
                    var, varB = tp()
                    sy.op("dve", lambda e: e.tensor_tensor(out=var[:], in0=mean[:], in1=mean[:], op=ALU.mult),
                          reads=[meanB], writes=[varB])
                    sy.op("dve", lambda e: e.tensor_tensor(out=var[:], in0=p2[:], in1=var[:], op=ALU.subtract),
                          reads=[p2B], writes=[varB])
                    sy.op("dve", lambda e: e.tensor_scalar(out=var[:], in0=var[:], scalar1=0.0, scalar2=None,
                                                           op0=ALU.max), writes=[varB])
                    sd, sdB = tp()
                    sy.op("act", lambda e: e.activation(out=sd[:], in_=var[:], func=AF.Ln, bias=EPS, scale=1.0),
                          reads=[varB], writes=[sdB])
                    rs, rsB = tp()
                    sy.op("act", lambda e: e.activation(out=rs[:], in_=sd[:], func=AF.Exp, scale=-0.5), reads=[sdB], writes=[rsB])
                    mr, mrB = tp()
                    sy.op("dve", lambda e: e.tensor_tensor(out=mr[:], in0=mean[:], in1=rs[:], op=ALU.mult),
                          reads=[meanB, rsB], writes=[mrB])
                    w0s = [acquire(("ewout", j, n, 0)) for n in range(6)]
                    s1s = {}

                    def ln_front(jj):
                        t1, t1B = tpt()
                        sy.op("dve", lambda e: e.tensor_tensor(out=t1[:], in0=big[:, jj, :], in1=rs[:], op=ALU.mult),
                              reads=[bigB[jj], rsB], writes=[t1B])
                        sy.op("dve", lambda e: e.tensor_tensor(out=t1[:], in0=t1[:], in1=mr[:], op=ALU.subtract),
                              reads=[mrB], writes=[t1B])
                        s1, s1B = tpb()
                        sy.op("act", lambda e: e.activation(out=s1[:], in_=t1[:], func=AF.Silu,
                                                            scale=pv(("ln_g", j), jj), bias=pv(("ln_b", j), jj)),
                              reads=[t1B], sreads=[pvB], writes=[s1B])
                        s1s[jj] = (s1, s1B)

                    def ln_back(jj):
                        s1, s1B = s1s.pop(jj)
                        sy.op("dve", lambda e: e.tensor_tensor(out=Z[:, jj, :], in0=s1[:], in1=Z[:, jj, :], op=ALU.mult),
                              reads=[s1B], writes=[ZB[jj]])
                        for n in range(6):
                            sy.op("pe", lambda e, n=n: e.matmul(acc[n][0][:, :], lhsT=w0s[n][0][:, jj * 128:(jj + 1) * 128],
                                                                rhs=Z[:, jj, :], start=False, stop=(jj == 7)),
                                  reads=[w0s[n][1], ZB[jj]], writes=[acc[n][1]], inc=(jj == 7 or n == 5))

                    ln_front(0)
                    for jj in range(8):
                        if jj + 1 < 8:
                            ln_front(jj + 1)
                        ln_back(jj)
                    for w_ in w0s:
                        release(w_[2])
                    dump("Aout0", Z[:, 0, :], [ZB[0]])
                    dump("Lo0", Lo[:, 0, :], [LoB[0]])
                    for n in range(6):
                        pt, pB = acc[n]
                        sy.op("act", lambda e: e.activation(out=big[:, n, :], in_=pt[:], func=AF.Identity),
                              reads=[pB], writes=[bigB[n]])
                    late = []
                    for n in (6, 7):
                        w0, w0B, wi0 = acquire(("ewout", j, n, 0))
                        w1, w1B, wi1 = acquire(("ewout", j, n, 1))
                        pt, pB = nextps()
                        for kc in range(16):
                            wsrc, wsB = (w0, w0B) if kc < 8 else (w1, w1B)
                            src, srcB = (Z, ZB) if kc < 8 else (Lo, LoB)
                            k8 = kc % 8
                            sy.op("pe", lambda e, kc=kc, k8=k8, wsrc=wsrc, src=src: e.matmul(
                                pt[:, :], lhsT=wsrc[:, k8 * 128:(k8 + 1) * 128], rhs=src[:, k8, :],
                                start=(kc == 0), stop=(kc == 15)),
                                reads=[wsB, srcB[k8]], writes=[pB], inc=(kc == 15))
                        release(wi0)
                        release(wi1)
                        late.append((n, pt, pB))
                    if t + 1 < NB:
                        prenorm(l, s, t0 + 512, lambda kc: hn[:, kc, :], hnB, sqt, sqB, tp, tpt, bank=0)
                    for n, pt, pB in late:
                        sy.op("act", lambda e: e.activation(out=big[:, n, :], in_=pt[:], func=AF.Identity),
                              reads=[pB], writes=[bigB[n]])
                    dump("y0", big[:, 0, :], [bigB[0]])
                    pending_post = postnorm_gen(l, s, t0, big, bigB, sqt, sqB, tp, tpt)
                for _ in pending_post:
                    pass
                sy.fence()

        def odd_layer(l, s):
            j = l // 2
            with ExitStack() as ol:
                def sbl(name, shape, dt):
                    return ol.enter_context(nc.sbuf_tensor(f"{name}_{l}_{s}", shape, dt))
                Zs = sbl("o_Zs", [128, KC, S], BF16)
                ZsB = [[Buf(f"o_Zs{c}_{b}") for b in range(NB)] for c in range(KC)]
                cqn = sbl("o_cqn", [128, 2, S], BF16)
                cqnB = [Buf(f"o_cqn{b}") for b in range(NB)]
                ckvn = sbl("o_ckvn", [128, 2, S], BF16)
                ckvnB = [Buf(f"o_ckvn{b}") for b in range(NB)]
                kr = sbl("o_kr", [128, S], BF16)
                krB = Buf("o_kr")
                COS = sbl("o_cos", [128, S], BF16)
                SIN = sbl("o_sin", [128, S], BF16)
                csB = Buf("o_cs")
                tp = mk_tmp_pool(ol, "o_tf", 4, F32)
                tp_stats[0] = mk_tmp_pool(ol, "o_ts", 3, F32)

                with ExitStack() as rl:
                    ang = rl.enter_context(nc.sbuf_tensor(f"o_ang_{l}_{s}", [128, S], F32))
                    wk = rl.enter_context(nc.sbuf_tensor(f"o_wk_{l}_{s}", [128, S], F32))
                    wk2 = rl.enter_context(nc.sbuf_tensor(f"o_wk2_{l}_{s}", [128, S], F32))
                    ki = rl.enter_context(nc.sbuf_tensor(f"o_ki_{l}_{s}", [128, S], I32))
                    posi = ki
                    rB = Buf("o_rope")
                    R = slice(64, 96)
                    src = AP(pos_d, s * S, [[0, 32], [1, S]])
                    sy.dma("sp", posi[R, :], src, writes=[rB])
                    sy.op("dve", lambda e: e.tensor_copy(out=ang[R, :], in_=posi[R, :]), reads=[rB], writes=[rB])
                    sy.op("dve", lambda e: e.tensor_scalar(out=ang[R, :], in0=ang[R, :], scalar1=pvt[R, pcols["inv"]:pcols["inv"] + 1],
                                                           scalar2=None, op0=ALU.mult), sreads=[pvB], writes=[rB])
                    for which in range(2):
                        if which == 0:
                            sy.op("dve", lambda e: e.tensor_scalar(out=wk2[R, :], in0=ang[R, :], scalar1=math.pi / 2,
                                                                   scalar2=None, op0=ALU.add), writes=[rB])
                            a_in = wk2
                        else:
                            a_in = ang
                        sy.op("dve", lambda e: e.tensor_scalar(out=wk[R, :], in0=a_in[R, :], scalar1=1.0 / TWO_PI,
                                                               scalar2=None, op0=ALU.mult), writes=[rB])
                        sy.op("dve", lambda e: e.tensor_copy(out=ki[R, :], in_=wk[R, :]), writes=[rB])
                        sy.op("dve", lambda e: e.tensor_copy(out=wk[R, :], in_=ki[R, :]), writes=[rB])
                        sy.op("dve", lambda e: e.scalar_tensor_tensor(out=wk2[R, :], in0=wk[R, :], scalar=-PI_HI,
                                                                      in1=a_in[R, :], op0=ALU.mult, op1=ALU.add),
                              writes=[rB])
                        sy.op("dve", lambda e: e.scalar_tensor_tensor(out=wk2[R, :], in0=wk[R, :], scalar=-PI_LO,
                                                                      in1=wk2[R, :], op0=ALU.mult, op1=ALU.add),
                              writes=[rB])
                        sy.op("dve", lambda e: e.tensor_scalar(out=wk[R, :], in0=wk2[R, :], scalar1=math.pi,
                                                               scalar2=-TWO_PI, op0=ALU.is_gt, op1=ALU.mult), writes=[rB])
                        sy.op("dve", lambda e: e.tensor_tensor(out=wk2[R, :], in0=wk2[R, :], in1=wk[R, :], op=ALU.add),
                              writes=[rB])
                        sy.op("dve", lambda e: e.tensor_scalar(out=wk[R, :], in0=wk2[R, :], scalar1=-math.pi,
                                                               scalar2=TWO_PI, op0=ALU.is_lt, op1=ALU.mult), writes=[rB])
                        sy.op("dve", lambda e: e.tensor_tensor(out=wk2[R, :], in0=wk2[R, :], in1=wk[R, :], op=ALU.add),
                              writes=[rB])
                        sy.op("dve", lambda e: e.tensor_scalar(out=wk2[R, :], in0=wk2[R, :], scalar1=3.1415925,
                                                               scalar2=-3.1415925, op0=ALU.min, op1=ALU.max), writes=[rB])
                        if which == 0:
                            sy.op("act", lambda e: e.activation(out=COS[R, :], in_=wk2[R, :], func=AF.Sin),
                                  reads=[rB], writes=[csB])
                        else:
                            sy.op("dve", lambda e: e.tensor_scalar(out=wk2[R, :], in0=wk2[R, :],
                                                                   scalar1=pvt[R, pcols["sgn"]:pcols["sgn"] + 1],
                                                                   scalar2=None, op0=ALU.mult), sreads=[pvB], writes=[rB])
                            sy.op("act", lambda e: e.activation(out=SIN[R, :], in_=wk2[R, :], func=AF.Sin),
                                  reads=[rB], writes=[csB])
                    sy.fence()

                with ExitStack() as p1:
                    hn = p1.enter_context(nc.sbuf_tensor(f"o_hn_{l}_{s}", [128, KC, 1024], BF16))
                    hnB2 = [[Buf(f"o_hn{c}_{b}") for c in range(KC)] for b in range(2)]
                    raw = p1.enter_context(nc.sbuf_tensor(f"o_raw_{l}_{s}", [128, 2, 1024], F32))
                    rawB = [Buf(f"o_raw{b}") for b in range(2)]
                    krA = p1.enter_context(nc.sbuf_tensor(f"o_krA_{l}_{s}", [128, 1024], F32))
                    krAB = [Buf(f"o_krA{b}") for b in range(2)]
                    sqt = p1.enter_context(nc.sbuf_tensor(f"o_sq_{l}_{s}", [128, KC, 512], BF16))
                    sqB = Buf("o_sq")
                    R = slice(64, 96)
                    for t in range(S // 1024):
                        for b in range(2):
                            t0 = t * 1024 + b * 512
                            prenorm(l, s, t0, lambda kc, b=b: hn[:, kc, b * 512:(b + 1) * 512], hnB2[b], sqt, sqB, tp)
                        for grp, (dst, dstB, nrm) in enumerate(((cqn, cqnB, "q_norm"), (ckvn, ckvnB, "kv_norm"))):
                            for c in range(2):
                                wt, wB, wi = acquire(("owin", j, grp * 2 + c))
                                for b in range(2):
                                    pt, pB = mm8(wt, wB, lambda kc, b=b: hn[:, kc, b * 512:(b + 1) * 512], hnB2[b])
                                    sy.op("act", lambda e: e.activation(out=raw[:, c, b * 512:(b + 1) * 512], in_=pt[:],
                                                                        func=AF.Identity),
                                          reads=[pB], writes=[rawB[b]])
                                release(wi)
                            for b in range(2):
                                gb = t * 2 + b
                                rs, rsB = rms_stats(lambda b=b: raw[:, :, b * 512:(b + 1) * 512], [rawB[b]], 2, od256,
                                                    sqt, sqB, tp)
                                for c in range(2):
                                    sy.op("dve", lambda e, c=c: e.scalar_tensor_tensor(
                                        out=dst[:, c, gb * 512:(gb + 1) * 512], in0=raw[:, c, b * 512:(b + 1) * 512],
                                        scalar=pv((nrm, j), c), in1=rs[:], op0=ALU.mult, op1=ALU.mult),
                                        reads=[rawB[b], rsB], sreads=[pvB], writes=[dstB[gb]])
                        wt, wB, wi = acquire(("owin", j, 4))
                        for b in range(2):
                            pt, pB = mm8(wt, wB, lambda kc, b=b: hn[:, kc, b * 512:(b + 1) * 512], hnB2[b])
                            sy.op("act", lambda e: e.activation(out=krA[R, b * 512:(b + 1) * 512], in_=pt[R, :],
                                                                func=AF.Identity), reads=[pB], writes=[krAB[b]])
                        release(wi)
                        wt, wB, wi = acquire(("owin", j, 5))
                        for b in range(2):
                            gb = t * 2 + b
                            tk = slice(gb * 512, (gb + 1) * 512)
                            pt, pB = mm8(wt, wB, lambda kc, b=b: hn[:, kc, b * 512:(b + 1) * 512], hnB2[b])
                            t1, t1B = tp()
                            sy.op("dve", lambda e: e.tensor_tensor(out=t1[R, :], in0=krA[R, b * 512:(b + 1) * 512],
                                                                   in1=COS[R, tk], op=ALU.mult),
                                  reads=[krAB[b], csB], writes=[t1B])
                            t2, t2B = tp()
                            sy.op("dve", lambda e: e.tensor_tensor(out=t2[R, :], in0=pt[R, :], in1=SIN[R, tk], op=ALU.mult),
                                  reads=[pB, csB], writes=[t2B])
                            sy.op("dve", lambda e: e.tensor_tensor(out=kr[R, tk], in0=t1[R, :], in1=t2[R, :], op=ALU.add),
                                  reads=[t1B, t2B], writes=[krB])
                        release(wi)
                        for c in range(8):
                            wt, wB, wi = acquire(("owin", j, 6 + c))
                            for b in range(2):
                                gb = t * 2 + b
                                pt, pB = mm8(wt, wB, lambda kc, b=b: hn[:, kc, b * 512:(b + 1) * 512], hnB2[b])
                                sy.op("act", lambda e: e.activation(out=Zs[:, c, gb * 512:(gb + 1) * 512], in_=pt[:],
                                                                    func=AF.Silu), reads=[pB], writes=[ZsB[c][gb]])
                            release(wi)
                    dump("cqn", cqn[:, 0, 0:512], [cqnB[0]])
                    dump("ckvn", ckvn[:, 0, 0:512], [ckvnB[0]])
                    dump("kr", kr[:, 0:512], [krB])
                    dump("cos", COS[:, 0:512], [csB])
                    dump("sin", SIN[:, 0:512], [csB])
                    dump("zs", Zs[:, 0, 0:512], [ZsB[0][0]])
                    sy.fence()

                with ExitStack() as p2:
                    QT = [p2.enter_context(nc.sbuf_tensor(f"o_QT{i}_{l}_{s}", [128, S], BF16)) for i in range(2)]
                    KT = [p2.enter_context(nc.sbuf_tensor(f"o_KT{i}_{l}_{s}", [128, S], BF16)) for i in range(2)]
                    V = [p2.enter_context(nc.sbuf_tensor(f"o_V{i}_{l}_{s}", [128, S // 128, 128], BF16)) for i in range(2)]
                    QTB = [Buf(f"o_QT{i}") for i in range(2)]
                    KTB = [Buf(f"o_KT{i}") for i in range(2)]
                    VB = [Buf(f"o_V{i}") for i in range(2)]
                    NPT = 6
                    PT = [p2.enter_context(nc.sbuf_tensor(f"o_PT{i}_{l}_{s}", [128, 512], BF16)) for i in range(NPT)]
                    PTB = [Buf(f"o_PT{i}") for i in range(NPT)]
                    rec = p2.enter_context(nc.sbuf_tensor(f"o_rec_{l}_{s}", [128, 512], F32))
                    recB = Buf("o_rec")
                    tpg = mk_tmp_pool(p2, "o_tg", 2, F32)
                    R = slice(64, 96)
                    sy.op("dve", lambda e: e.memset(V[0][:, :, 64:128], 1.0), writes=[VB[0]])
                    sy.op("dve", lambda e: e.memset(V[1][:, :, 0:64], 1.0), writes=[VB[1]])
                    pt_i = {"i": 0}

                    def gen_head(h):
                        par = h % 2
                        qt, qB = QT[par], QTB[par]
                        kt, kB = KT[par], KTB[par]
                        vt, vB = V[par], VB[par]
                        wt, wB, wi = acquire(("ouq", j, h))
                        for b in range(NB):
                            tk = slice(b * 512, (b + 1) * 512)
                            pa, paB = nextps()
                            pb, pbB = nextps()
                            for kc in range(2):
                                sy.op("pe", lambda e, kc=kc: e.matmul(pa[0:96, :], lhsT=wt[:, kc * 192:kc * 192 + 96],
                                                                      rhs=cqn[:, kc, tk], start=(kc == 0), stop=(kc == 1)),
                                      reads=[wB, cqnB[b]], writes=[paB], inc=(kc == 1))
                            for kc in range(2):
                                sy.op("pe", lambda e, kc=kc: e.matmul(pb[0:96, :], lhsT=wt[:, kc * 192 + 96:kc * 192 + 192],
                                                                      rhs=cqn[:, kc, tk], start=(kc == 0), stop=(kc == 1)),
                                      reads=[wB, cqnB[b]], writes=[pbB], inc=(kc == 1))
                            yield
                            sy.op("dve", lambda e: e.tensor_copy(out=qt[0:64, tk], in_=pa[0:64, :]),
                                  reads=[paB], writes=[qB])
                            t1, t1B = tpg()
                            sy.op("dve", lambda e: e.tensor_tensor(out=t1[R, :], in0=pa[R, :], in1=COS[R, tk], op=ALU.mult),
                                  reads=[paB, csB], writes=[t1B])
                            t2, t2B = tpg()
                            sy.op("dve", lambda e: e.tensor_tensor(out=t2[R, :], in0=pb[R, :], in1=SIN[R, tk], op=ALU.mult),
                                  reads=[pbB, csB], writes=[t2B])
                            yield
                            sy.op("dve", lambda e: e.tensor_tensor(out=qt[R, tk], in0=t1[R, :], in1=t2[R, :], op=ALU.add),
                                  reads=[t1B, t2B], writes=[qB])
                            yield
                        release(wi)
                        wt, wB, wi = acquire(("oukv", j, h))
                        for b in range(NB):
                            tk = slice(b * 512, (b + 1) * 512)
                            pa, paB = nextps()
                            for kc in range(2):
                                sy.op("pe", lambda e, kc=kc: e.matmul(pa[0:64, :], lhsT=wt[:, kc * 128:kc * 128 + 64],
                                                                      rhs=ckvn[:, kc, tk], start=(kc == 0), stop=(kc == 1)),
                                      reads=[wB, ckvnB[b]], writes=[paB], inc=(kc == 1))
                            yield
                            sy.op("dve", lambda e: e.tensor_copy(out=kt[0:64, tk], in_=pa[0:64, :]),
                                  reads=[paB], writes=[kB])
                            yield
                        sy.op("pool", lambda e: e.tensor_copy(out=kt[R, :], in_=kr[R, :]), reads=[krB], writes=[kB])
                        vo = 0 if par == 0 else 64
                        for g8 in range(S // 1024):
                            pa, paB = nextps()
                            for i8 in range(8):
                                kb = g8 * 8 + i8
                                for kc in range(2):
                                    sy.op("pe", lambda e, kc=kc, kb=kb, i8=i8: e.matmul(
                                        pa[:, i8 * 64:(i8 + 1) * 64], lhsT=ckvn[:, kc, kb * 128:(kb + 1) * 128],
                                        rhs=wt[:, kc * 128 + 64:kc * 128 + 128], start=(kc == 0), stop=(kc == 1)),
                                        reads=[wB, ckvnB[kb // 4]], writes=[paB], inc=(kc == 1 and i8 == 7))
                            yield
                            sy.op("dve", lambda e: e.tensor_copy(
                                out=vt[:, g8 * 8:(g8 + 1) * 8, vo:vo + 64],
                                in_=pa[:, :].rearrange("p (a b) -> p a b", b=64)),
                                reads=[paB], writes=[vB])
                            yield
                        release(wi)

                    def attn_head(h):
                        par = h % 2
                        hp = h // 2
                        qt, qB = QT[par], QTB[par]
                        kt, kB = KT[par], KTB[par]
                        vt, vB = V[par], VB[par]
                        if h == 0:
                            dump("qt", qt[:, 0:512], [qB])
                            dump("kt", kt[:, 0:512], [kB])
                            dump("v0", vt[:, 0, :], [vB])
                        items = []
                        for g in range(NB):
                            for kb in range(4 * g + 4):
                                items.append((g, kb))
                        LA = 3
                        inflight = {}
                        for i in range(len(items) + LA):
                            if i < len(items):
                                g, kb = items[i]
                                d = kb - 4 * g
                                c0 = max(0, d) * 128
                                ncols = 512 - c0
                                sp_, spB = nextps()
                                sy.op("pe", lambda e, kb=kb, g=g, c0=c0, ncols=ncols, sp_=sp_: e.matmul(
                                    sp_[:, 0:ncols], lhsT=kt[0:96, kb * 128:(kb + 1) * 128],
                                    rhs=qt[0:96, g * 512 + c0:(g + 1) * 512], start=True, stop=(d < 0)),
                                    reads=[kB, qB], writes=[spB], inc=(d < 0))
                                if d >= 0:
                                    sy.op("pe", lambda e, sp_=sp_: e.matmul(sp_[:, 0:128], lhsT=ident_b[:], rhs=maskT[:],
                                                                            start=False, stop=True),
                                          reads=[constB], writes=[spB])
                                inflight[i] = (sp_, spB, c0, ncols)
                            if i >= LA:
                                ii = i - LA
                                g, kb = items[ii]
                                sp_, spB, c0, ncols = inflight.pop(ii)
                                pi = pt_i["i"] % NPT
                                pt_i["i"] += 1
                                ptile, ptB = PT[pi], PTB[pi]
                                sy.op("act", lambda e, sp_=sp_, ncols=ncols, ptile=ptile: e.activation(
                                    out=ptile[:, 0:ncols], in_=sp_[:, 0:ncols], func=AF.Exp, scale=ATT_SCALE),
                                    reads=[spB], writes=[ptB])
                                op_, opB = psum[g % 2], psB[g % 2]
                                nkb = 4 * g + 4
                                sy.op("pe", lambda e, kb=kb, c0=c0, ncols=ncols, ptile=ptile, op_=op_, nkb=nkb: e.matmul(
                                    op_[:, c0:512], lhsT=vt[:, kb, :], rhs=ptile[:, 0:ncols],
                                    start=(kb == 0), stop=(kb == nkb - 1)),
                                    reads=[vB, ptB], writes=[opB], inc=True)
                                if kb == nkb - 1:
                                    if par == 0:
                                        num, den = slice(0, 64), slice(64, 128)
                                    else:
                                        num, den = slice(64, 128), slice(0, 64)
                                    tk = slice(g * 512, (g + 1) * 512)
                                    sy.op("act", lambda e, op_=op_: e.activation(out=rec[num, :], in_=op_[den, :], func=AF.Ln),
                                          reads=[opB], writes=[recB])
                                    sy.op("act", lambda e: e.activation(out=rec[num, :], in_=rec[num, :], func=AF.Exp, scale=-1.0),
                                          writes=[recB])
                                    o1, o1B = tp()
                                    sy.op("dve", lambda e, op_=op_, o1=o1: e.tensor_tensor(out=o1[num, :], in0=op_[num, :],
                                                                                         in1=rec[num, :], op=ALU.mult),
                                          reads=[opB, recB], writes=[o1B])
                                    sy.op("dve", lambda e, o1=o1: e.tensor_tensor(out=Zs[num, hp, tk], in0=o1[num, :],
                                                                                in1=Zs[num, hp, tk], op=ALU.mult),
                                          reads=[o1B], writes=[ZsB[hp][g]])
                            yield

                    for _ in gen_head(0):
                        pass
                    for h in range(16):
                        gens = [attn_head(h)]
                        if h + 1 < 16:
                            gens.append(gen_head(h + 1))
                        while gens:
                            for g_ in list(gens):
                                try:
                                    next(g_)
                                except StopIteration:
                                    gens.remove(g_)
                    sy.fence()

                dump("og", Zs[:, 0, 0:512], [ZsB[0][0]])
                with ExitStack() as p3:
                    big = p3.enter_context(nc.sbuf_tensor(f"o_big_{l}_{s}", [128, KC, 512], F32))
                    bigB = [Buf(f"o_big{c}") for c in range(KC)]
                    big2 = p3.enter_context(nc.sbuf_tensor(f"o_big2_{l}_{s}", [128, KC, 512], F32))
                    big2B = [Buf(f"o_big2{c}") for c in range(KC)]
                    sqt = p3.enter_context(nc.sbuf_tensor(f"o_sq3_{l}_{s}", [128, KC, 512], BF16))
                    sqB = Buf("o_sq3")
                    wl = [acquire(("owout", j, n)) for n in range(8)]
                    bigs = [(big, bigB), (big2, big2B)]
                    pend = None
                    for b in range(NB):
                        tk = slice(b * 512, (b + 1) * 512)
                        bg, bgB = bigs[b % 2]
                        for n in range(8):
                            wt, wB, _ = wl[n]
                            pt, pB = mm8(wt, wB, lambda kc: Zs[:, kc, tk], [ZsB[c][b] for c in range(KC)])
                            sy.op("act", lambda e: e.activation(out=bg[:, n, :], in_=pt[:], func=AF.Identity),
                                  reads=[pB], writes=[bgB[n]])
                            if pend is not None:
                                try:
                                    next(pend)
                                    next(pend)
                                except StopIteration:
                                    pend = None
                        if pend is not None:
                            for _ in pend:
                                pass
                        pend = postnorm_gen(l, s, b * 512, bg, bgB, sqt, sqB, tp)
                    for _ in pend:
                        pass
                    for _w in wl:
                        release(_w[2])
                    sy.fence()
                tp_stats[0] = None

        allxs = [xsB[c][b] for c in range(KC) for b in range(NB)]
        outB = Buf("outst")
        for s in range(NSEQ):
            for c in range(KC):
                sy.dma("sp", xs[:, c, :], x_d.ap()[s, :, c, :], writes=xsB[c])
            for l in LAYERS:
                if l % 2 == 0:
                    even_layer(l, s)
                else:
                    odd_layer(l, s)
            for c in range(KC):
                sy.dma("sp", out_d.ap()[s, :, c, :], xs[:, c, :], reads=xsB[c], writes=[outB])
        nc.sync.wait_ge(outB.dsem, outB.dcnt)
        if DEBUG and dbgB.dsem is not None:
            nc.sync.wait_ge(dbgB.dsem, dbgB.dcnt)
        if RECORD:
            return ws["order"]
        assert ws["acq"] == len(ws["order"]) == ws["issued"], (ws["acq"], len(ws["order"]), ws["issued"])
        build.nins = sy.nins
    return nc


_CACHE = {}


def kernel(**inp):
    inp = {k: np.asarray(v) for k, v in inp.items()}
    x = inp["x"].astype(np.float32, copy=False)
    B, S, Dm = x.shape
    nseq = B // NCORES
    wts = pack_weights(inp)
    pvv = pack_pv(inp)
    key = (S, nseq)
    if key not in _CACHE:
        _CACHE[key] = build(S=S, NSEQ=nseq)
    nc = _CACHE[key]
    in_maps = []
    for cid in range(NCORES):
        bs = slice(cid * nseq, (cid + 1) * nseq)
        xf = np.ascontiguousarray(x[bs].reshape(nseq, S, KC, 128).transpose(0, 3, 2, 1))
        cf = np.ascontiguousarray(inp["c"][bs].astype(np.float32).reshape(nseq, KC, 128).transpose(2, 1, 0))
        pos = np.ascontiguousarray(inp["positions"][bs].astype(np.int32))
        in_maps.append({"x": xf, "c": cf, "pos": pos, "wts": wts, "pv": pvv})
    res = run_bass_kernel_spmd(nc, in_maps, core_ids=list(range(NCORES)))
    outs = []
    for cid in range(NCORES):
        o = np.asarray(res.results[cid]["out"]).reshape(nseq, 128, KC, S)
        outs.append(o.transpose(0, 3, 2, 1).reshape(nseq, S, Dm))
    return np.ascontiguousarray(np.concatenate(outs, axis=0).astype(np.float32))
```

```python
import math
from contextlib import ExitStack
import numpy as np
import concourse.bass as bass
import concourse.mybir as mybir
from concourse.ap import AP
from concourse.bass_utils import run_bass_kernel_spmd

F32 = mybir.dt.float32
BF16 = mybir.dt.bfloat16
I32 = mybir.dt.int32
AF = mybir.ActivationFunctionType
ALU = mybir.AluOpType

D = 1024
KC = 8
NCORES = 8
EPS = 1e-6
SLOT = 1024
RING = 10
SAME_ENG_WINDOW = 3
EVEN_STAGGER = 9
ATT_SCALE = 96.0 ** -0.5
TWO_PI = 2.0 * math.pi
PI_HI = 6.28125
PI_LO = TWO_PI - 6.28125


def weight_plan():
    plan = {}
    off = 0

    def add(name, F):
        nonlocal off
        plan[name] = (off, F)
        off += 128 * F

    for l in range(4):
        for n in range(24):
            add(("ada", l, n), 1024)
    for j in range(2):
        for c in range(40):
            add(("ewin", j, c), 1024)
        for kc in range(8):
            add(("egate", j, kc), 256)
        for n in range(8):
            add(("ewout", j, n, 0), 1024)
            add(("ewout", j, n, 1), 1024)
    for j in range(2):
        for c in range(14):
            add(("owin", j, c), 1024)
        for h in range(16):
            add(("ouq", j, h), 384)
            add(("oukv", j, h), 256)
        for n in range(8):
            add(("owout", j, n), 1024)
    return plan, off


def pv_plan():
    cols = {}
    off = 0

    def add(name, n):
        nonlocal off
        cols[name] = off
        off += n

    for l in range(4):
        add(("pre_g", l), 8)
        add(("post_g", l), 8)
        add(("ada_b", l), 24)
    for j in range(2):
        add(("conv_w", j), 8 * 31)
        add(("conv_b", j), 8)
        add(("ln_g", j), 8)
        add(("ln_b", j), 8)
        add(("lconv_w", j), 8 * 4)
        add(("lconv_b", j), 8)
        add(("ba", j), 8)
        add(("bx", j), 8)
        add(("lam", j), 8)
        add(("q_norm", j), 2)
        add(("kv_norm", j), 2)
    add("inv", 1)
    add("sgn", 1)
    add("inv4", 1)
    add("sgn4", 1)
    return cols, off


def chunkify(W):
    K, N = W.shape
    return np.ascontiguousarray(W.reshape(K // 128, 128, N // 128, 128).transpose(2, 1, 0, 3))


def vec_pc(v):
    return np.ascontiguousarray(v.reshape(-1, 128).T)


def pack_weights(inp):
    plan, total = weight_plan()
    flat = np.zeros(total, np.float32)

    def put(name, arr):
        off, F = plan[name]
        a = np.ascontiguousarray(arr, dtype=np.float32).reshape(128, F)
        flat[off:off + 128 * F] = a.reshape(-1)

    for l in range(4):
        ch = chunkify(inp["ada_w"][l])
        for n in range(24):
            put(("ada", l, n), ch[n])
    for j in range(2):
        ch = chunkify(inp["ev_w_in"][j])
        for c in range(40):
            put(("ewin", j, c), ch[c])
        wa, wx = inp["ev_lru_wa"][j], inp["ev_lru_wx"][j]
        for kc in range(8):
            g = np.zeros((128, 2, 128), np.float32)
            for hh in range(2):
                g[hh * 64:(hh + 1) * 64, 0, hh * 64:(hh + 1) * 64] = wa[2 * kc + hh]
                g[hh * 64:(hh + 1) * 64, 1, hh * 64:(hh + 1) * 64] = wx[2 * kc + hh]
            put(("egate", j, kc), g)
        wo = inp["ev_w_out"][j]
        c0 = chunkify(wo[:1024])
        c1 = chunkify(wo[1024:])
        for n in range(8):
            put(("ewout", j, n, 0), c0[n])
            put(("ewout", j, n, 1), c1[n])
    for j in range(2):
        w = inp["od_w_in"][j]
        z64 = np.zeros((1024, 64), np.float32)
        z32 = np.zeros((1024, 32), np.float32)
        kr = w[:, 512:544]
        kr1 = np.concatenate([z64, kr, z32], axis=1)
        kr2 = np.concatenate([z64, kr[:, 16:32], kr[:, 0:16], z32], axis=1)
        wcat = np.concatenate([w[:, 0:512], kr1, kr2, w[:, 544:1568]], axis=1)
        ch = chunkify(wcat)
        for c in range(14):
            put(("owin", j, c), ch[c])
        uq = inp["od_w_uq"][j].reshape(2, 128, 16, 96)
        ukv = inp["od_w_ukv"][j].reshape(2, 128, 16, 128)
        for h in range(16):
            a = uq[:, :, h, :]
            sw = np.concatenate([a[:, :, 0:64], a[:, :, 80:96], a[:, :, 64:80]], axis=2)
            both = np.concatenate([a, sw], axis=2)
            put(("ouq", j, h), both.transpose(1, 0, 2))
            put(("oukv", j, h), ukv[:, :, h, :].transpose(1, 0, 2))
        ch = chunkify(inp["od_w_out"][j])
        for n in range(8):
            put(("owout", j, n), ch[n])
    return flat


def pack_pv(inp):
    cols, n = pv_plan()
    pv = np.zeros((128, n), np.float32)

    def put(name, arr):
        a = np.asarray(arr, np.float32)
        pv[:, cols[name]:cols[name] + a.shape[1]] = a

    for l in range(4):
        put(("pre_g", l), vec_pc(inp["pre_g"][l]))
        put(("post_g", l), vec_pc(inp["post_g"][l]))
        put(("ada_b", l), vec_pc(inp["ada_b"][l]))
    for j in range(2):
        cw = inp["ev_conv_w"][j]
        put(("conv_w", j), cw.reshape(31, 8, 128).transpose(2, 1, 0).reshape(128, 8 * 31))
        put(("conv_b", j), vec_pc(inp["ev_conv_b"][j]))
        put(("ln_g", j), vec_pc(inp["ev_ln_g"][j]))
        put(("ln_b", j), vec_pc(inp["ev_ln_b"][j]))
        lw = inp["ev_lru_conv_w"][j]
        put(("lconv_w", j), lw.reshape(4, 8, 128).transpose(2, 1, 0).reshape(128, 8 * 4))
        put(("lconv_b", j), vec_pc(inp["ev_lru_conv_b"][j]))
        put(("ba", j), vec_pc(inp["ev_lru_ba"][j]))
        put(("bx", j), vec_pc(inp["ev_lru_bx"][j]))
        put(("lam", j), vec_pc(inp["ev_lru_lam"][j]))
        put(("q_norm", j), vec_pc(inp["od_q_norm"][j]))
        put(("kv_norm", j), vec_pc(inp["od_kv_norm"][j]))
    inv = (10000.0 ** (-np.arange(0, 32, 2, dtype=np.float32) / 32.0)).astype(np.float32)
    iv = np.zeros((128, 1), np.float32)
    sg = np.ones((128, 1), np.float32)
    for i in range(32):
        iv[64 + i, 0] = inv[i % 16]
        sg[64 + i, 0] = -1.0 if i < 16 else 1.0
    put("inv", iv)
    put("sgn", sg)
    iv4 = np.zeros((128, 1), np.float32)
    sg4 = np.ones((128, 1), np.float32)
    for p in range(128):
        iv4[p, 0] = inv[(p % 32) % 16]
        sg4[p, 0] = -1.0 if (p % 32) < 16 else 1.0
    put("inv4", iv4)
    put("sgn4", sg4)
    return pv


_UID = [0]


class Buf:
    __slots__ = ("name", "w", "r", "dsem", "dcnt")

    def __init__(self, name):
        _UID[0] += 1
        self.name = f"{name}_{_UID[0]}"
        self.w = None
        self.r = {}
        self.dsem = None
        self.dcnt = 0


class Sync:
    ENG = ("pe", "act", "dve", "pool", "sp")

    def __init__(self, nc, es):
        self.nc = nc
        self.es = es
        self.eng = {"pe": nc.tensor, "act": nc.scalar, "dve": nc.vector, "pool": nc.gpsimd, "sp": nc.sync}
        self.sem = {k: es.enter_context(nc.semaphore("s_" + k)) for k in self.ENG}
        self.cnt = {k: 0 for k in self.ENG}
        self.known = {k: {} for k in self.ENG}
        self.nins = 0

    def _need(self, E, dep, strict):
        key, sem, val, src = dep
        if src == E and not strict and E != "pool":
            if E == "pe" or (self.cnt[E] - val) >= SAME_ENG_WINDOW:
                return
        if self.known[E].get(key, 0) >= val:
            return
        self.eng[E].wait_ge(sem, val)
        self.known[E][key] = val
        self.nins += 1

    def _deps(self, E, reads, writes, sreads, strict_all=False):
        for b in reads:
            if b.w is not None:
                self._need(E, b.w, strict_all)
        for b in sreads:
            if b.w is not None:
                self._need(E, b.w, True)
        for b in writes:
            if b.w is not None:
                self._need(E, b.w, strict_all)
            for d in b.r.values():
                self._need(E, d, strict_all)

    def op(self, E, fn, reads=(), writes=(), sreads=(), inc=True):
        self._deps(E, reads, writes, sreads)
        ins = fn(self.eng[E])
        self.nins += 1
        if inc:
            self.cnt[E] += 1
            ins.then_inc(self.sem[E], 1)
            me = (E, self.sem[E], self.cnt[E], E)
        else:
            me = (E, self.sem[E], self.cnt[E] + 1, E)
        for b in writes:
            b.w = me
            b.r = {}
        for b in reads:
            b.r[E] = me
        for b in sreads:
            b.r[E] = me
        return ins

    def dma(self, Q, out, in_, reads=(), writes=()):
        self._deps(Q, reads, writes, (), strict_all=True)
        tgt = writes[0] if writes else reads[0]
        if tgt.dsem is None:
            tgt.dsem = self.es.enter_context(self.nc.semaphore("d_" + tgt.name))
        tgt.dcnt += 16
        self.eng[Q].dma_start(out=out, in_=in_).then_inc(tgt.dsem, 16)
        self.nins += 1
        me = ("d_" + tgt.name, tgt.dsem, tgt.dcnt, None)
        for b in writes:
            b.w = me
            b.r = {}
        for b in reads:
            b.r["dma_" + tgt.name] = me
        return me

    def fence(self):
        for E in ("pe", "act", "dve", "pool", "sp"):
            for Fg in ("pe", "act", "dve", "pool"):
                if Fg != E and self.cnt[Fg] > 0:
                    self._need(E, (Fg, self.sem[Fg], self.cnt[Fg], Fg), True)


def build(S=2048, NSEQ=2, LAYERS=(0, 1, 2, 3), DEBUG=False):
    order = _build(S, NSEQ, LAYERS, DEBUG, None)
    return _build(S, NSEQ, LAYERS, DEBUG, order)


def _build(S, NSEQ, LAYERS, DEBUG, ORDER):
    RECORD = ORDER is None
    nc = bass.Bass("TRN2", target_bir_lowering=False)
    plan, wtotal = weight_plan()
    pcols, npv = pv_plan()
    NB = S // 512

    x_d = nc.dram_tensor("x", [NSEQ, 128, KC, S], F32, kind="ExternalInput")
    c_d = nc.dram_tensor("c", [128, KC, NSEQ], F32, kind="ExternalInput")
    pos_d = nc.dram_tensor("pos", [NSEQ, S], I32, kind="ExternalInput")
    w_d = nc.dram_tensor("wts", [wtotal], F32, kind="ExternalInput")
    pv_d = nc.dram_tensor("pv", [128, npv], F32, kind="ExternalInput")
    out_d = nc.dram_tensor("out", [NSEQ, 128, KC, S], F32, kind="ExternalOutput")
    dgd = nc.dram_tensor("dgd", [2, 8, 128, 31 * 128], BF16, kind="Internal")

    dbg_d = nc.dram_tensor("dbg", [128, 16384], F32, kind="ExternalOutput") if DEBUG else None
    dbg_cols = {}
    build.dbg_cols = dbg_cols
    dbg_state = {"c": 0}

    with ExitStack() as es:
        sy = Sync(nc, es)
        dbgB = Buf("dbg")

        def dump(name, ap, bufs):
            if not DEBUG or name in dbg_cols:
                return
            n = ap.shape[-1]
            p0 = 0
            c0 = dbg_state["c"]
            dbg_cols[name] = (c0, n)
            dbg_state["c"] += n
            sy.dma("pool", dbg_d.ap()[0:ap.shape[0], c0:c0 + n], ap, reads=bufs, writes=[dbgB])

        def sb(name, shape, dt):
            return es.enter_context(nc.sbuf_tensor(name, shape, dt))

        xs = sb("xs", [128, KC, S], F32)
        xsB = [[Buf(f"xs{c}_{b}") for b in range(NB)] for c in range(KC)]
        pvt = sb("pvt", [128, npv], F32)
        pvB = Buf("pv")
        ring = sb("ring", [128, RING, SLOT], BF16)
        ringB = [Buf(f"ring{i}") for i in range(RING)]
        ident_f = sb("ident_f", [128, 128], F32)
        ident_b = sb("ident_b", [128, 128], BF16)
        od1024 = sb("od1024", [128, 128], BF16)
        od256 = sb("od256", [128, 128], BF16)
        maskT = sb("maskT", [128, 128], BF16)
        constB = Buf("const")
        cin = sb("cin", [128, KC, NSEQ], F32)
        cact = sb("cact", [128, KC, NSEQ], BF16)
        cB = Buf("c")
        modt = sb("modt", [128, 96, NSEQ], F32)
        modB = Buf("mod")
        drvA = sb("drvA", [128, 4, NSEQ, 8], F32)
        drvG = sb("drvG", [128, 4, NSEQ, 8], F32)
        drvB = Buf("drv")
        nsp = sb("nsp", [128, 2, 8], F32)
        nspB = Buf("nsp")
        carry = sb("carry", [128, 8], F32)
        carryB = [Buf(f"carry{j}") for j in range(8)]
        xbh = sb("xbh", [128, 8, 4], BF16)
        xbhB = [Buf(f"xbh{j}") for j in range(8)]

        psum = [es.enter_context(nc.psum_tensor(f"ps{i}", [128, 512], F32)) for i in range(8)]
        psB = [Buf(f"ps{i}") for i in range(8)]
        ps_state = {"i": 0}

        def nextps():
            i = 2 + ps_state["i"] % 6
            ps_state["i"] += 1
            return psum[i], psB[i]

        def pv(name, a, b=None):
            c0 = pcols[name]
            if b is None:
                return pvt[:, c0 + a:c0 + a + 1]
            return pvt[:, c0 + a:c0 + b]

        ws = {"order": [] if RECORD else ORDER, "issued": 0, "acq": 0, "rel": 0, "done": set()}

        def w_ap(name):
            off, Fw = plan[name]
            return AP(w_d, off, [[Fw, 128], [1, Fw]]), Fw

        def ws_issue():
            if RECORD:
                return
            while ws["issued"] < len(ws["order"]) and ws["issued"] < ws["rel"] + RING:
                i = ws["issued"]
                src, Fw = w_ap(ws["order"][i])
                slot = i % RING
                sy.dma("pool", ring[:, slot, 0:Fw], src, writes=[ringB[slot]])
                ws["issued"] += 1

        def acquire(name):
            i = ws["acq"]
            ws["acq"] += 1
            if RECORD:
                ws["order"].append(name)
                return ring[:, 0, :], ringB[0], i
            assert ws["order"][i] == name, (ws["order"][i], name)
            assert ws["issued"] > i, "weight ring deadlock (acquire beyond issued)"
            slot = i % RING
            return ring[:, slot, :], ringB[slot], i

        def release(i):
            ws["done"].add(i)
            while ws["rel"] in ws["done"]:
                ws["done"].remove(ws["rel"])
                ws["rel"] += 1
            ws_issue()

        sy.dma("sp", pvt[:], pv_d.ap()[:, :], writes=[pvB])
        sy.dma("sp", cin[:], c_d.ap()[:, :, :], writes=[cB])
        ws_issue()
        sy.op("pool", lambda e: e.memset(ident_f[:], 1.0), writes=[constB])
        sy.op("pool", lambda e: e.affine_select(out=ident_f[:], in_=ident_f[:], pattern=[[-1, 128]], base=0,
                                                channel_multiplier=1, compare_op=ALU.is_equal, fill=0.0),
              writes=[constB])
        sy.op("pool", lambda e: e.tensor_copy(out=ident_b[:], in_=ident_f[:]), writes=[constB])
        sy.op("pool", lambda e: e.memset(od1024[:], 1.0 / 1024), writes=[constB])
        sy.op("pool", lambda e: e.memset(od256[:], 1.0 / 256), writes=[constB])
        sy.op("pool", lambda e: e.memset(maskT[:], 0.0), writes=[constB])
        sy.op("pool", lambda e: e.affine_select(out=maskT[:], in_=maskT[:], pattern=[[1, 128]], base=0,
                                                channel_multiplier=-1, compare_op=ALU.is_ge, fill=-30000.0),
              writes=[constB])
        sy.op("act", lambda e: e.activation(out=cact[:], in_=cin[:], func=AF.Silu), reads=[cB], writes=[cB])

        dgdB = [[Buf(f"dgd{j}_{jj}") for jj in range(8)] for j in range(2)]
        with ExitStack() as st0:
            dgs = [st0.enter_context(nc.sbuf_tensor(f"dgs{i}", [128, 31, 128], BF16)) for i in range(2)]
            dgsB = [Buf(f"dgs{i}") for i in range(2)]
            for j in range(2):
                if (2 * j) not in LAYERS:
                    continue
                for jj in range(8):
                    i = jj % 2
                    cw0 = pcols[("conv_w", j)] + jj * 31
                    sy.op("pool", lambda e: e.tensor_tensor(
                        out=dgs[i][:], in0=ident_b[:].unsqueeze(1).to_broadcast([128, 31, 128]),
                        in1=pvt[:, cw0:cw0 + 31].unsqueeze(2).to_broadcast([128, 31, 128]), op=ALU.mult),
                        reads=[constB, pvB], writes=[dgsB[i]])
                    sy.dma("sp", dgd.ap()[j, jj, :, :], dgs[i][:].rearrange("p a b -> p (a b)"),
                           reads=[dgsB[i]], writes=[dgdB[j][jj]])
            sy.fence()
            for j in range(2):
                for jj in range(8):
                    if dgdB[j][jj].w is not None:
                        for E_ in ("pe", "act", "dve", "pool", "sp"):
                            sy._need(E_, dgdB[j][jj].w, True)

        for l in range(4):
            for n in range(24):
                wt, wB, wi = acquire(("ada", l, n))
                pt, pB = nextps()
                for kc in range(KC):
                    sy.op("pe", lambda e, kc=kc: e.matmul(pt[:, 0:NSEQ], lhsT=wt[:, kc * 128:(kc + 1) * 128],
                                                          rhs=cact[:, kc, :], start=(kc == 0), stop=(kc == KC - 1)),
                          reads=[wB, cB], writes=[pB], inc=(kc == KC - 1))
                release(wi)
                sy.op("act", lambda e: e.activation(out=modt[:, l * 24 + n, :], in_=pt[:, 0:NSEQ], func=AF.Identity,
                                                    bias=pv(("ada_b", l), n), scale=1.0),
                      reads=[pB], sreads=[pvB], writes=[modB])
        for l in range(4):
            for s in range(NSEQ):
                sy.op("dve", lambda e: e.tensor_scalar(out=drvA[:, l, s, :], in0=modt[:, l * 24 + 8:l * 24 + 16, s],
                                                       scalar1=1.0, scalar2=None, op0=ALU.add),
                      reads=[modB], writes=[drvB])
                sy.op("dve", lambda e: e.tensor_tensor(out=drvA[:, l, s, :], in0=drvA[:, l, s, :],
                                                       in1=pv(("pre_g", l), 0, 8), op=ALU.mult),
                      reads=[pvB], writes=[drvB])
                sy.op("dve", lambda e: e.tensor_tensor(out=drvG[:, l, s, :], in0=modt[:, l * 24 + 16:l * 24 + 24, s],
                                                       in1=pv(("post_g", l), 0, 8), op=ALU.mult),
                      reads=[pvB, modB], writes=[drvB])
        spt = [sb(f"spt{i}", [128, 8], F32) for i in range(4)]
        for j in range(2):
            lam_ap = pv(("lam", j), 0, 8)
            al, ee, ww, w2 = spt
            sy.op("act", lambda e: e.activation(out=al[:], in_=lam_ap, func=AF.Abs),
                  reads=[pvB], writes=[nspB])
            sy.op("act", lambda e: e.activation(out=ee[:], in_=al[:], func=AF.Exp, scale=-1.0),
                  reads=[nspB], writes=[nspB])
            sy.op("dve", lambda e: e.tensor_scalar(out=ww[:], in0=ee[:], scalar1=2.0, scalar2=None, op0=ALU.add),
                  reads=[nspB], writes=[nspB])
            sy.op("dve", lambda e: e.reciprocal(out=ww[:], in_=ww[:]), writes=[nspB])
            sy.op("dve", lambda e: e.tensor_tensor(out=ww[:], in0=ww[:], in1=ee[:], op=ALU.mult), writes=[nspB])
            sy.op("dve", lambda e: e.tensor_tensor(out=w2[:], in0=ww[:], in1=ww[:], op=ALU.mult), writes=[nspB])
            sy.op("dve", lambda e: e.tensor_scalar(out=al[:], in0=w2[:], scalar1=1.0 / 11, scalar2=1.0 / 9, op0=ALU.mult,
                                                   op1=ALU.add), writes=[nspB])
            for cf in (1.0 / 7, 1.0 / 5, 1.0 / 3, 1.0):
                sy.op("dve", lambda e: e.tensor_tensor(out=al[:], in0=al[:], in1=w2[:], op=ALU.mult), writes=[nspB])
                sy.op("dve", lambda e, cf=cf: e.tensor_scalar(out=al[:], in0=al[:], scalar1=cf, scalar2=None, op0=ALU.add),
                      writes=[nspB])
            sy.op("dve", lambda e: e.tensor_tensor(out=al[:], in0=al[:], in1=ww[:], op=ALU.mult), writes=[nspB])
            sy.op("dve", lambda e: e.tensor_scalar(out=ee[:], in0=lam_ap, scalar1=-1.0, scalar2=0.0, op0=ALU.mult,
                                                   op1=ALU.max), reads=[pvB], writes=[nspB])
            sy.op("dve", lambda e: e.scalar_tensor_tensor(out=al[:], in0=al[:], scalar=2.0, in1=ee[:], op0=ALU.mult,
                                                          op1=ALU.add), writes=[nspB])
            sy.op("dve", lambda e: e.tensor_scalar(out=nsp[:, j, :], in0=al[:], scalar1=-8.0, scalar2=None,
                                                   op0=ALU.mult), writes=[nspB])
        dump("modt", modt[:].rearrange("p a b -> p (a b)"), [modB])
        dump("drvA", drvA[:].rearrange("p a b c -> p (a b c)"), [drvB])
        dump("drvG", drvG[:].rearrange("p a b c -> p (a b c)"), [drvB])
        dump("nsp", nsp[:].rearrange("p a b -> p (a b)"), [nspB])
        tp_stats = [None]
        def mm8(wt, wB, rhs_fn, rB, M=128, wcol0=0, prow=None):
            pt, pB = nextps()
            for kc in range(KC):
                sy.op("pe", lambda e, kc=kc: e.matmul(pt[0:M, :], lhsT=wt[:, kc * 128 + wcol0:kc * 128 + wcol0 + M],
                                                      rhs=rhs_fn(kc), start=(kc == 0), stop=(kc == KC - 1)),
                      reads=[wB] + rB, writes=[pB], inc=(kc == KC - 1))
            return pt, pB

        def rms_stats(src_fn, srcB, nch, onesm, sqt, sqB, tp, bank=None):
            tp = tp_stats[0] or tp
            sy.op("act", lambda e: e.activation(out=sqt[:, 0:nch, :], in_=src_fn(), func=AF.Square),
                  reads=srcB, writes=[sqB])
            pt, pB = (psum[bank], psB[bank]) if bank is not None else nextps()
            for kc in range(nch):
                sy.op("pe", lambda e, kc=kc: e.matmul(pt[:, :], lhsT=onesm[:], rhs=sqt[:, kc, :], start=(kc == 0),
                                                      stop=(kc == nch - 1)),
                      reads=[constB, sqB], writes=[pB], inc=(kc == nch - 1))
            sd, sdB = tp()
            sy.op("act", lambda e: e.activation(out=sd[:], in_=pt[:], func=AF.Ln, bias=EPS, scale=1.0),
                  reads=[pB], writes=[sdB])
            rs, rsB = tp()
            sy.op("act", lambda e: e.activation(out=rs[:], in_=sd[:], func=AF.Exp, scale=-0.5), reads=[sdB], writes=[rsB])
            return rs, rsB

        def prenorm(l, s, t0, hn_fn, hnB, sqt, sqB, tp, tpt=None, bank=None):
            tpt = tpt or tp
            b = t0 // 512
            rs, rsB = rms_stats(lambda: xs[:, :, t0:t0 + 512], [xsB[c][b] for c in range(KC)], KC, od1024, sqt, sqB, tp, bank)
            for kc in range(KC):
                tt, ttB = tpt()
                sy.op("dve", lambda e: e.tensor_tensor(out=tt[:], in0=xs[:, kc, t0:t0 + 512], in1=rs[:], op=ALU.mult),
                      reads=[xsB[kc][b], rsB], writes=[ttB])
                sy.op("act", lambda e: e.activation(out=hn_fn(kc), in_=tt[:], func=AF.Identity,
                                                    scale=drvA[:, l, s, kc:kc + 1], bias=modt[:, l * 24 + kc, s:s + 1]),
                      reads=[ttB], sreads=[drvB, modB], writes=[hnB[kc]])

        def postnorm_gen(l, s, t0, y, yB, sqt, sqB, tp, tpt=None):
            tpt = tpt or tp
            b = t0 // 512
            rs, rsB = rms_stats(lambda: y[:, :, :], yB, KC, od1024, sqt, sqB, tp)
            yield
            for n in range(KC):
                tt, ttB = tpt()
                sy.op("dve", lambda e: e.tensor_tensor(out=tt[:], in0=y[:, n, :], in1=rs[:], op=ALU.mult),
                      reads=[yB[n], rsB], writes=[ttB])
                sy.op("dve", lambda e: e.scalar_tensor_tensor(out=xs[:, n, t0:t0 + 512], in0=tt[:],
                                                              scalar=drvG[:, l, s, n:n + 1], in1=xs[:, n, t0:t0 + 512],
                                                              op0=ALU.mult, op1=ALU.add),
                      reads=[ttB], sreads=[drvB], writes=[xsB[n][b]])
                yield

        def postnorm(l, s, t0, y, yB, sqt, sqB, tp, tpt=None):
            for _ in postnorm_gen(l, s, t0, y, yB, sqt, sqB, tp, tpt):
                pass

        def mk_tmp_pool(ess, name, n, dt=F32, w=512):
            _UID[0] += 1
            tiles = [ess.enter_context(nc.sbuf_tensor(f"{name}{i}_{_UID[0]}", [128, w], dt)) for i in range(n)]
            bufs = [Buf(f"{name}{i}") for i in range(n)]
            st = {"i": 0}

            def get():
                i = st["i"] % n
                st["i"] += 1
                return tiles[i], bufs[i]
            return get

        def even_layer(l, s):
            j = l // 2
            with ExitStack() as el:
                def sbl(name, shape, dt):
                    return el.enter_context(nc.sbuf_tensor(f"{name}_{l}_{s}", shape, dt))
                hn = sbl("e_hn", [128, KC, 512], BF16)
                hnB = [Buf(f"e_hn{c}") for c in range(KC)]
                G = sbl("e_G", [128, KC, 544], BF16)
                GB = [Buf(f"e_G{c}") for c in range(KC)]
                Z = sbl("e_Z", [128, KC, 512], BF16)
                ZB = [Buf(f"e_Z{c}") for c in range(KC)]
                big = sbl("e_big", [128, KC, 512], F32)
                bigB = [Buf(f"e_big{c}") for c in range(KC)]
                Lo = sbl("e_Lo", [128, KC, 512], BF16)
                LoB = [Buf(f"e_Lo{c}") for c in range(KC)]
                sqt = sbl("e_sq", [128, KC, 512], BF16)
                sqB = Buf("e_sq")
                dg = sbl("e_dg", [128, 31, 128], BF16)
                dgB = Buf("e_dg")
                d4 = [sbl(f"e_d4{i}", [128, 4, 128], BF16) for i in range(2)]
                d4B = [Buf(f"e_d4{i}") for i in range(2)]
                XB = [sbl(f"e_XB{i}", [128, 516], BF16) for i in range(2)]
                XBB = [Buf(f"e_XB{i}") for i in range(2)]
                tp = mk_tmp_pool(el, "e_tf", 5, F32)
                tpt = mk_tmp_pool(el, "e_tt", 3, F32)
                tpb = mk_tmp_pool(el, "e_tb", 2, BF16)
                lt = [[sbl(f"e_lt{p}{i}", [128, 512], F32) for i in range(5)] for p in range(2)]
                ltB = [[Buf(f"e_lt{p}{i}") for i in range(5)] for p in range(2)]
                sgt = [sbl(f"e_sgt{p}", [128, 512], BF16) for p in range(2)]
                sgtB = [Buf(f"e_sgt{p}") for p in range(2)]
                zbt = [sbl(f"e_zbt{p}", [128, 512], BF16) for p in range(2)]
                zbtB = [Buf(f"e_zbt{p}") for p in range(2)]
                xcbt = [sbl(f"e_xcbt{p}", [128, 512], BF16) for p in range(2)]
                xcbtB = [Buf(f"e_xcbt{p}") for p in range(2)]

                sy.op("dve", lambda e: e.memset(G[:, :, 0:32], 0.0), writes=GB)
                sy.op("dve", lambda e: e.memset(carry[:], 0.0), writes=carryB)
                sy.op("dve", lambda e: e.memset(xbh[:], 0.0), writes=xbhB)
                hrhs = lambda kc: hn[:, kc, :]

                def w_mm8(name):
                    wt, wB, wi = acquire(name)
                    pt, pB = mm8(wt, wB, hrhs, hnB)
                    release(wi)
                    return pt, pB

                def conv_stage(jj):
                    p = jj % 2
                    pt, pB = w_mm8(("ewin", j, 8 + jj))
                    yield
                    sy.op("act", lambda e: e.activation(out=sgt[p][:], in_=pt[:], func=AF.Sigmoid),
                          reads=[pB], writes=[sgtB[p]])
                    pt, pB = w_mm8(("ewin", j, jj))
                    yield
                    sy.op("dve", lambda e: e.tensor_tensor(out=G[:, jj, 32:544], in0=pt[:], in1=sgt[p][:], op=ALU.mult),
                          reads=[pB, sgtB[p]], writes=[GB[jj]])
                    pt, pB = w_mm8(("ewin", j, 16 + jj))
                    yield
                    sy.op("act", lambda e: e.activation(out=Z[:, jj, :], in_=pt[:], func=AF.Silu),
                          reads=[pB], writes=[ZB[jj]])
                    sy.dma("sp", dg[:].rearrange("p a b -> p (a b)"), dgd.ap()[j, jj, :, :],
                           reads=[dgdB[j][jj]], writes=[dgB])
                    yield
                    yield
                    pt, pB = nextps()
                    for k in range(31):
                        sy.op("pe", lambda e, k=k: e.matmul(pt[:, :], lhsT=dg[:, k, :], rhs=G[:, jj, k + 2:k + 514],
                                                            start=(k == 0), stop=(k == 30)),
                              reads=[dgB, GB[jj]], writes=[pB], inc=(k == 30))
                        if k % 8 == 7:
                            yield
                    sy.op("act", lambda e: e.activation(out=big[:, jj, :], in_=pt[:], func=AF.Identity,
                                                        bias=pv(("conv_b", j), jj), scale=1.0),
                          reads=[pB], sreads=[pvB], writes=[bigB[jj]])
                    if jj == 0:
                        dump("G0", G[:, 0, 32:544], [GB[0]])
                        dump("aconv0", big[:, 0, :], [bigB[0]])
                        dump("Zraw0", Z[:, 0, :], [ZB[0]])
                    yield
                    sy.op("dve", lambda e: e.tensor_copy(out=G[:, jj, 0:32], in_=G[:, jj, 512:544]),
                          reads=[GB[jj]], writes=[GB[jj]])

                def lru_stage(jj):
                    p = jj % 2
                    xbt, xbB = XB[p], XBB[p]
                    pt, pB = w_mm8(("ewin", j, 24 + jj))
                    yield
                    sy.op("act", lambda e: e.activation(out=xbt[:, 4:516], in_=pt[:], func=AF.Identity),
                          reads=[pB], writes=[xbB])
                    sy.op("dve", lambda e: e.tensor_copy(out=xbt[:, 0:4], in_=xbh[:, jj, :]),
                          reads=[xbhB[jj]], writes=[xbB])
                    pt, pB = w_mm8(("ewin", j, 32 + jj))
                    yield
                    sy.op("act", lambda e: e.activation(out=zbt[p][:], in_=pt[:], func=AF.Silu), reads=[pB],
                          writes=[zbtB[p]])
                    lw0 = pcols[("lconv_w", j)] + jj * 4
                    sy.op("dve", lambda e: e.tensor_tensor(
                        out=d4[p][:], in0=ident_b[:].unsqueeze(1).to_broadcast([128, 4, 128]),
                        in1=pvt[:, lw0:lw0 + 4].unsqueeze(2).to_broadcast([128, 4, 128]), op=ALU.mult),
                        reads=[constB, pvB], writes=[d4B[p]])
                    yield
                    pt, pB = nextps()
                    for k in range(4):
                        sy.op("pe", lambda e, k=k: e.matmul(pt[:, :], lhsT=d4[p][:, k, :], rhs=xbt[:, k + 1:k + 513],
                                                            start=(k == 0), stop=(k == 3)),
                              reads=[d4B[p], xbB], writes=[pB], inc=(k == 3))
                    yield
                    xc, xcB = lt[p][0], ltB[p][0]
                    rg, rgB = lt[p][1], ltB[p][1]
                    ig, igB = lt[p][2], ltB[p][2]
                    at, atB = lt[p][3], ltB[p][3]
                    hh_, hhB = lt[p][4], ltB[p][4]
                    sy.op("act", lambda e: e.activation(out=xc[:], in_=pt[:], func=AF.Identity,
                                                        bias=pv(("lconv_b", j), jj), scale=1.0),
                          reads=[pB], sreads=[pvB], writes=[xcB])
                    yield
                    sy.op("dve", lambda e: e.tensor_copy(out=xcbt[p][:], in_=xc[:]), reads=[xcB], writes=[xcbtB[p]])
                    sy.op("dve", lambda e: e.tensor_copy(out=xbh[:, jj, :], in_=xbt[:, 512:516]),
                          reads=[xbB], writes=[xbhB[jj]])
                    yield
                    wt, wB, wi = acquire(("egate", j, jj))
                    pr, prB = nextps()
                    sy.op("pe", lambda e: e.matmul(pr[:, :], lhsT=wt[:, 0:128], rhs=xcbt[p][:], start=True, stop=True),
                          reads=[wB, xcbtB[p]], writes=[prB])
                    pi_, piB = nextps()
                    sy.op("pe", lambda e: e.matmul(pi_[:, :], lhsT=wt[:, 128:256], rhs=xcbt[p][:], start=True, stop=True),
                          reads=[wB, xcbtB[p]], writes=[piB])
                    release(wi)
                    yield
                    sy.op("act", lambda e: e.activation(out=rg[:], in_=pr[:], func=AF.Sigmoid,
                                                        bias=pv(("ba", j), jj), scale=1.0),
                          reads=[prB], sreads=[pvB], writes=[rgB])
                    yield
                    sy.op("act", lambda e: e.activation(out=ig[:], in_=pi_[:], func=AF.Sigmoid,
                                                        bias=pv(("bx", j), jj), scale=1.0),
                          reads=[piB], sreads=[pvB], writes=[igB])
                    yield
                    sy.op("act", lambda e: e.activation(out=at[:], in_=rg[:], func=AF.Exp,
                                                        scale=nsp[:, j, jj:jj + 1]),
                          reads=[rgB], sreads=[nspB], writes=[atB])
                    sy.op("dve", lambda e: e.tensor_tensor(out=ig[:], in0=ig[:], in1=xc[:], op=ALU.mult),
                          reads=[xcB], writes=[igB])
                    yield
                    sy.op("dve", lambda e: e.tensor_tensor(out=rg[:], in0=at[:], in1=at[:], op=ALU.mult),
                          reads=[atB], writes=[rgB])
                    yield
                    sy.op("act", lambda e: e.activation(out=rg[:], in_=rg[:], func=AF.Sqrt, bias=1.0, scale=-1.0),
                          writes=[rgB])
                    yield
                    sy.op("dve", lambda e: e.tensor_tensor(out=ig[:], in0=ig[:], in1=rg[:], op=ALU.mult),
                          reads=[rgB], writes=[igB])
                    yield
                    sy.op("dve", lambda e: e.tensor_tensor_scan(out=hh_[:], data0=at[:], data1=ig[:],
                                                                initial=carry[:, jj:jj + 1], op0=ALU.mult,
                                                                op1=ALU.add),
                          reads=[atB, igB], sreads=[carryB[jj]], writes=[hhB])
                    yield
                    sy.op("act", lambda e: e.activation(out=carry[:, jj:jj + 1], in_=hh_[:, 511:512],
                                                        func=AF.Identity),
                          reads=[hhB], writes=[carryB[jj]])
                    sy.op("dve", lambda e: e.tensor_tensor(out=Lo[:, jj, :], in0=hh_[:], in1=zbt[p][:], op=ALU.mult),
                          reads=[hhB, zbtB[p]], writes=[LoB[jj]])

                prenorm(l, s, 0, lambda kc: hn[:, kc, :], hnB, sqt, sqB, tp, tpt)
                dump("hn0", hn[:, 0, :], [hnB[0]])
                pending_post = None
                for t in range(NB):
                    t0 = t * 512
                    active = [pending_post] if pending_post is not None else []
                    pending_post = None
                    for jj in range(8):
                        active += [conv_stage(jj), lru_stage(jj)]
                        steps = 0
                        while active and (jj == 7 or steps < EVEN_STAGGER):
                            for g_ in list(active):
                                try:
                                    next(g_)
                                except StopIteration:
                                    active.remove(g_)
                            steps += 1
                    w1s = [acquire(("ewout", j, n, 1)) for n in range(6)]
                    acc = [nextps() for n in range(6)]

                    def lo_half(k8):
                        for n in range(6):
                            sy.op("pe", lambda e, n=n: e.matmul(acc[n][0][:, :], lhsT=w1s[n][0][:, k8 * 128:(k8 + 1) * 128],
                                                                rhs=Lo[:, k8, :], start=(k8 == 0), stop=False),
                                  reads=[w1s[n][1], LoB[k8]], writes=[acc[n][1]], inc=(n == 5))
                    for k8 in range(7):
                        lo_half(k8)
                    sy.op("dve", lambda e: e.tensor_copy(out=sqt[:], in_=big[:]), reads=bigB, writes=[sqB])
                    pm, pmB = psum[0], psB[0]
                    for kc in range(KC):
                        sy.op("pe", lambda e, kc=kc: e.matmul(pm[:, :], lhsT=od1024[:], rhs=sqt[:, kc, :],
                                                              start=(kc == 0), stop=(kc == KC - 1)),
                              reads=[constB, sqB], writes=[pmB], inc=(kc == KC - 1))
                    sy.op("act", lambda e: e.activation(out=sqt[:], in_=big[:], func=AF.Square),
                          reads=bigB, writes=[sqB])
                    p2, p2B = psum[1], psB[1]
                    for kc in range(KC):
                        sy.op("pe", lambda e, kc=kc: e.matmul(p2[:, :], lhsT=od1024[:], rhs=sqt[:, kc, :],
                                                              start=(kc == 0), stop=(kc == KC - 1)),
                              reads=[constB, sqB], writes=[p2B], inc=(kc == KC - 1))
                    lo_half(7)
                    for w_ in w1s:
                        release(w_[2])
                    mean, meanB = tp()
                    sy.op("act", lambda e: e.activation(out=mean[:], in_=pm[:], func=AF.Identity), reads=[pmB],
                          writes=[meanB])
                    var, varB = tp()
                    sy.op("dve", lambda e: e.tensor_tensor(out=var[:], in0=mean[:], in1=mean[:], op=ALU.mult),
                          reads=[meanB], writes=[varB])
                    sy.op("dve", lambda e: e.tensor_tensor(out=var[:], in0=p2[:], in1=var[:], op=ALU.subtract),
                          reads=[p2B], writes=[varB])
                    sy.op("dve", lambda e: e.tensor_scalar(out=var[:], in0=var[:], scalar1=0.0, scalar2=None,
                                                           op0=ALU.max), writes=[varB])
                    sd, sdB = tp()
                    sy.op("act", lambda e: e.activation(out=sd[:], in_=var[:], func=AF.Ln, bias=EPS, scale=1.0),
                          reads=[varB], writes=[sdB])
                    rs, rsB = tp()
                    sy.op("act", lambda e: e.activation(out=rs[:], in_=sd[:], func=AF.Exp, scale=-0.5), reads=[sdB], writes=[rsB])
                    mr, mrB = tp()
                    sy.op("dve", lambda e: e.tensor_tensor(out=mr[:], in0=mean[:], in1=rs[:], op=ALU.mult),
                          reads=[meanB, rsB], writes=[mrB])
                    w0s = [acquire(("ewout", j, n, 0)) for n in range(6)]
                    s1s = {}

                    def ln_front(jj):
                        t1, t1B = tpt()
                        sy.op("dve", lambda e: e.tensor_tensor(out=t1[:], in0=big[:, jj, :], in1=rs[:], op=ALU.mult),
                              reads=[bigB[jj], rsB], writes=[t1B])
                        sy.op("dve", lambda e: e.tensor_tensor(out=t1[:], in0=t1[:], in1=mr[:], op=ALU.subtract),
                              reads=[mrB], writes=[t1B])
                        s1, s1B = tpb()
                        sy.op("act", lambda e: e.activation(out=s1[:], in_=t1[:], func=AF.Silu,
                                                            scale=pv(("ln_g", j), jj), bias=pv(("ln_b", j), jj)),
                              reads=[t1B], sreads=[pvB], writes=[s1B])
                        s1s[jj] = (s1, s1B)

                    def ln_back(jj):
                        s1, s1B = s1s.pop(jj)
                        sy.op("dve", lambda e: e.tensor_tensor(out=Z[:, jj, :], in0=s1[:], in1=Z[:, jj, :], op=ALU.mult),
                              reads=[s1B], writes=[ZB[jj]])
                        for n in range(6):
                            sy.op("pe", lambda e, n=n: e.matmul(acc[n][0][:, :], lhsT=w0s[n][0][:, jj * 128:(jj + 1) * 128],
                                                                rhs=Z[:, jj, :], start=False, stop=(jj == 7)),
                                  reads=[w0s[n][1], ZB[jj]], writes=[acc[n][1]], inc=(jj == 7 or n == 5))

                    ln_front(0)
                    for jj in range(8):
                        if jj + 1 < 8:
                            ln_front(jj + 1)
                        ln_back(jj)
                    for w_ in w0s:
                        release(w_[2])
                    dump("Aout0", Z[:, 0, :], [ZB[0]])
                    dump("Lo0", Lo[:, 0, :], [LoB[0]])
                    for n in range(6):
                        pt, pB = acc[n]
                        sy.op("act", lambda e: e.activation(out=big[:, n, :], in_=pt[:], func=AF.Identity),
                              reads=[pB], writes=[bigB[n]])
                    late = []
                    for n in (6, 7):
                        w0, w0B, wi0 = acquire(("ewout", j, n, 0))
                        w1, w1B, wi1 = acquire(("ewout", j, n, 1))
                        pt, pB = nextps()
                        for kc in range(16):
                            wsrc, wsB = (w0, w0B) if kc < 8 else (w1, w1B)
                            src, srcB = (Z, ZB) if kc < 8 else (Lo, LoB)
                            k8 = kc % 8
                            sy.op("pe", lambda e, kc=kc, k8=k8, wsrc=wsrc, src=src: e.matmul(
                                pt[:, :], lhsT=wsrc[:, k8 * 128:(k8 + 1) * 128], rhs=src[:, k8, :],
                                start=(kc == 0), stop=(kc == 15)),
                                reads=[wsB, srcB[k8]], writes=[pB], inc=(kc == 15))
                        release(wi0)
                        release(wi1)
                        late.append((n, pt, pB))
                    if t + 1 < NB:
                        prenorm(l, s, t0 + 512, lambda kc: hn[:, kc, :], hnB, sqt, sqB, tp, tpt, bank=0)
                    for n, pt, pB in late:
                        sy.op("act", lambda e: e.activation(out=big[:, n, :], in_=pt[:], func=AF.Identity),
                              reads=[pB], writes=[bigB[n]])
                    dump("y0", big[:, 0, :], [bigB[0]])
                    pending_post = postnorm_gen(l, s, t0, big, bigB, sqt, sqB, tp, tpt)
                for _ in pending_post:
                    pass
                sy.fence()

        def odd_layer(l, s):
            j = l // 2
            with ExitStack() as ol:
                def sbl(name, shape, dt):
                    return ol.enter_context(nc.sbuf_tensor(f"{name}_{l}_{s}", shape, dt))
                Zs = sbl("o_Zs", [128, KC, S], BF16)
                ZsB = [[Buf(f"o_Zs{c}_{b}") for b in range(NB)] for c in range(KC)]
                cqn = sbl("o_cqn", [128, 2, S], BF16)
                cqnB = [Buf(f"o_cqn{b}") for b in range(NB)]
                ckvn = sbl("o_ckvn", [128, 2, S], BF16)
                ckvnB = [Buf(f"o_ckvn{b}") for b in range(NB)]
                kr = sbl("o_kr", [128, S], BF16)
                krB = Buf("o_kr")
                COS = sbl("o_cos", [128, S], BF16)
                SIN = sbl("o_sin", [128, S], BF16)
                csB = Buf("o_cs")
                tp = mk_tmp_pool(ol, "o_tf", 4, F32)
                tp_stats[0] = mk_tmp_pool(ol, "o_ts", 3, F32)

                with ExitStack() as rl:
                    Q4 = S // 4
                    ang = rl.enter_context(nc.sbuf_tensor(f"o_ang_{l}_{s}", [128, Q4], F32))
                    wk = rl.enter_context(nc.sbuf_tensor(f"o_wk_{l}_{s}", [128, Q4], F32))
                    wk2 = rl.enter_context(nc.sbuf_tensor(f"o_wk2_{l}_{s}", [128, Q4], F32))
                    ki = rl.enter_context(nc.sbuf_tensor(f"o_ki_{l}_{s}", [128, Q4], I32))
                    posi = ki
                    rB = Buf("o_rope")
                    R = slice(64, 96)
                    for q4 in range(4):
                        src = AP(pos_d, s * S + q4 * Q4, [[0, 32], [1, Q4]])
                        sy.dma("sp", posi[q4 * 32:(q4 + 1) * 32, :], src, writes=[rB])
                    sy.op("dve", lambda e: e.tensor_copy(out=ang[:], in_=posi[:]), reads=[rB], writes=[rB])
                    sy.op("dve", lambda e: e.tensor_scalar(out=ang[:], in0=ang[:], scalar1=pvt[:, pcols["inv4"]:pcols["inv4"] + 1],
                                                           scalar2=None, op0=ALU.mult), sreads=[pvB], writes=[rB])
                    for which in range(2):
                        if which == 0:
                            sy.op("dve", lambda e: e.tensor_scalar(out=wk2[:], in0=ang[:], scalar1=math.pi / 2,
                                                                   scalar2=None, op0=ALU.add), writes=[rB])
                            a_in = wk2
                        else:
                            a_in = ang
                        sy.op("dve", lambda e: e.tensor_scalar(out=wk[:], in0=a_in[:], scalar1=1.0 / TWO_PI,
                                                               scalar2=None, op0=ALU.mult), writes=[rB])
                        sy.op("dve", lambda e: e.tensor_copy(out=ki[:], in_=wk[:]), writes=[rB])
                        sy.op("dve", lambda e: e.tensor_copy(out=wk[:], in_=ki[:]), writes=[rB])
                        sy.op("dve", lambda e: e.scalar_tensor_tensor(out=wk2[:], in0=wk[:], scalar=-PI_HI,
                                                                      in1=a_in[:], op0=ALU.mult, op1=ALU.add),
                              writes=[rB])
                        sy.op("dve", lambda e: e.scalar_tensor_tensor(out=wk2[:], in0=wk[:], scalar=-PI_LO,
                                                                      in1=wk2[:], op0=ALU.mult, op1=ALU.add),
                              writes=[rB])
                        sy.op("dve", lambda e: e.tensor_scalar(out=wk[:], in0=wk2[:], scalar1=math.pi,
                                                               scalar2=-TWO_PI, op0=ALU.is_gt, op1=ALU.mult), writes=[rB])
                        sy.op("dve", lambda e: e.tensor_tensor(out=wk2[:], in0=wk2[:], in1=wk[:], op=ALU.add),
                              writes=[rB])
                        sy.op("dve", lambda e: e.tensor_scalar(out=wk[:], in0=wk2[:], scalar1=-math.pi,
                                                               scalar2=TWO_PI, op0=ALU.is_lt, op1=ALU.mult), writes=[rB])
                        sy.op("dve", lambda e: e.tensor_tensor(out=wk2[:], in0=wk2[:], in1=wk[:], op=ALU.add),
                              writes=[rB])
                        sy.op("dve", lambda e: e.tensor_scalar(out=wk2[:], in0=wk2[:], scalar1=3.1415925,
                                                               scalar2=-3.1415925, op0=ALU.min, op1=ALU.max), writes=[rB])
                        if which == 1:
                            sy.op("dve", lambda e: e.tensor_scalar(out=wk2[:], in0=wk2[:],
                                                                   scalar1=pvt[:, pcols["sgn4"]:pcols["sgn4"] + 1],
                                                                   scalar2=None, op0=ALU.mult), sreads=[pvB], writes=[rB])
                        dstT = COS if which == 0 else SIN
                        for q4 in range(4):
                            sy.op("act", lambda e, q4=q4: e.activation(out=dstT[R, q4 * Q4:(q4 + 1) * Q4],
                                                                       in_=wk2[q4 * 32:(q4 + 1) * 32, :], func=AF.Sin),
                                  reads=[rB], writes=[csB])
                    sy.fence()

                with ExitStack() as p1:
                    hn = p1.enter_context(nc.sbuf_tensor(f"o_hn_{l}_{s}", [128, KC, 1024], BF16))
                    hnB2 = [[Buf(f"o_hn{c}_{b}") for c in range(KC)] for b in range(2)]
                    raw = p1.enter_context(nc.sbuf_tensor(f"o_raw_{l}_{s}", [128, 2, 1024], F32))
                    rawB = [Buf(f"o_raw{b}") for b in range(2)]
                    krA = p1.enter_context(nc.sbuf_tensor(f"o_krA_{l}_{s}", [128, 1024], F32))
                    krAB = [Buf(f"o_krA{b}") for b in range(2)]
                    sqt = p1.enter_context(nc.sbuf_tensor(f"o_sq_{l}_{s}", [128, KC, 512], BF16))
                    sqB = Buf("o_sq")
                    R = slice(64, 96)
                    for t in range(S // 1024):
                        for b in range(2):
                            t0 = t * 1024 + b * 512
                            prenorm(l, s, t0, lambda kc, b=b: hn[:, kc, b * 512:(b + 1) * 512], hnB2[b], sqt, sqB, tp)
                        def chain_gen(t=t):
                            for grp, (dst, dstB, nrm) in enumerate(((cqn, cqnB, "q_norm"), (ckvn, ckvnB, "kv_norm"))):
                                for c in range(2):
                                    wt, wB, wi = acquire(("owin", j, grp * 2 + c))
                                    for b in range(2):
                                        pt, pB = mm8(wt, wB, lambda kc, b=b: hn[:, kc, b * 512:(b + 1) * 512], hnB2[b])
                                        sy.op("act", lambda e: e.activation(out=raw[:, c, b * 512:(b + 1) * 512], in_=pt[:],
                                                                            func=AF.Identity),
                                              reads=[pB], writes=[rawB[b]])
                                        yield
                                    release(wi)
                                for b in range(2):
                                    gb = t * 2 + b
                                    rs, rsB = rms_stats(lambda b=b: raw[:, :, b * 512:(b + 1) * 512], [rawB[b]], 2, od256,
                                                        sqt, sqB, tp)
                                    yield
                                    for c in range(2):
                                        sy.op("dve", lambda e, c=c: e.scalar_tensor_tensor(
                                            out=dst[:, c, gb * 512:(gb + 1) * 512], in0=raw[:, c, b * 512:(b + 1) * 512],
                                            scalar=pv((nrm, j), c), in1=rs[:], op0=ALU.mult, op1=ALU.mult),
                                            reads=[rawB[b], rsB], sreads=[pvB], writes=[dstB[gb]])
                                    yield
                            wt, wB, wi = acquire(("owin", j, 4))
                            for b in range(2):
                                pt, pB = mm8(wt, wB, lambda kc, b=b: hn[:, kc, b * 512:(b + 1) * 512], hnB2[b])
                                sy.op("act", lambda e: e.activation(out=krA[R, b * 512:(b + 1) * 512], in_=pt[R, :],
                                                                    func=AF.Identity), reads=[pB], writes=[krAB[b]])
                                yield
                            release(wi)
                            wt, wB, wi = acquire(("owin", j, 5))
                            for b in range(2):
                                gb = t * 2 + b
                                tk = slice(gb * 512, (gb + 1) * 512)
                                pt, pB = mm8(wt, wB, lambda kc, b=b: hn[:, kc, b * 512:(b + 1) * 512], hnB2[b])
                                t1, t1B = tp()
                                sy.op("dve", lambda e: e.tensor_tensor(out=t1[R, :], in0=krA[R, b * 512:(b + 1) * 512],
                                                                       in1=COS[R, tk], op=ALU.mult),
                                      reads=[krAB[b], csB], writes=[t1B])
                                t2, t2B = tp()
                                sy.op("dve", lambda e: e.tensor_tensor(out=t2[R, :], in0=pt[R, :], in1=SIN[R, tk], op=ALU.mult),
                                      reads=[pB, csB], writes=[t2B])
                                sy.op("dve", lambda e: e.tensor_tensor(out=kr[R, tk], in0=t1[R, :], in1=t2[R, :], op=ALU.add),
                                      reads=[t1B, t2B], writes=[krB])
                                yield
                            release(wi)

                        def z_gen(t=t):
                            for c in range(8):
                                wt, wB, wi = acquire(("owin", j, 6 + c))
                                for b in range(2):
                                    gb = t * 2 + b
                                    pt, pB = mm8(wt, wB, lambda kc, b=b: hn[:, kc, b * 512:(b + 1) * 512], hnB2[b])
                                    sy.op("act", lambda e: e.activation(out=Zs[:, c, gb * 512:(gb + 1) * 512], in_=pt[:],
                                                                        func=AF.Silu), reads=[pB], writes=[ZsB[c][gb]])
                                    yield
                                release(wi)

                        gens = [chain_gen(), z_gen()]
                        while gens:
                            for g_ in list(gens):
                                try:
                                    next(g_)
                                except StopIteration:
                                    gens.remove(g_)
                    dump("cqn", cqn[:, 0, 0:512], [cqnB[0]])
                    dump("ckvn", ckvn[:, 0, 0:512], [ckvnB[0]])
                    dump("kr", kr[:, 0:512], [krB])
                    dump("cos", COS[:, 0:512], [csB])
                    dump("sin", SIN[:, 0:512], [csB])
                    dump("zs", Zs[:, 0, 0:512], [ZsB[0][0]])
                    sy.fence()

                with ExitStack() as p2:
                    QT = [p2.enter_context(nc.sbuf_tensor(f"o_QT{i}_{l}_{s}", [128, S], BF16)) for i in range(2)]
                    KT = [p2.enter_context(nc.sbuf_tensor(f"o_KT{i}_{l}_{s}", [128, S], BF16)) for i in range(2)]
                    V = [p2.enter_context(nc.sbuf_tensor(f"o_V{i}_{l}_{s}", [128, S // 128, 128], BF16)) for i in range(2)]
                    QTB = [Buf(f"o_QT{i}") for i in range(2)]
                    KTB = [Buf(f"o_KT{i}") for i in range(2)]
                    VB = [Buf(f"o_V{i}") for i in range(2)]
                    NPT = 6
                    PT = [p2.enter_context(nc.sbuf_tensor(f"o_PT{i}_{l}_{s}", [128, 512], BF16)) for i in range(NPT)]
                    PTB = [Buf(f"o_PT{i}") for i in range(NPT)]
                    rec = p2.enter_context(nc.sbuf_tensor(f"o_rec_{l}_{s}", [128, 512], F32))
                    recB = Buf("o_rec")
                    tpg = mk_tmp_pool(p2, "o_tg", 2, F32)
                    R = slice(64, 96)
                    sy.op("dve", lambda e: e.memset(V[0][:, :, 64:128], 1.0), writes=[VB[0]])
                    sy.op("dve", lambda e: e.memset(V[1][:, :, 0:64], 1.0), writes=[VB[1]])
                    pt_i = {"i": 0}

                    def gen_head(h):
                        par = h % 2
                        qt, qB = QT[par], QTB[par]
                        kt, kB = KT[par], KTB[par]
                        vt, vB = V[par], VB[par]
                        wt, wB, wi = acquire(("ouq", j, h))
                        for b in range(NB):
                            tk = slice(b * 512, (b + 1) * 512)
                            pa, paB = nextps()
                            pb, pbB = nextps()
                            for kc in range(2):
                                sy.op("pe", lambda e, kc=kc: e.matmul(pa[0:96, :], lhsT=wt[:, kc * 192:kc * 192 + 96],
                                                                      rhs=cqn[:, kc, tk], start=(kc == 0), stop=(kc == 1)),
                                      reads=[wB, cqnB[b]], writes=[paB], inc=(kc == 1))
                            for kc in range(2):
                                sy.op("pe", lambda e, kc=kc: e.matmul(pb[0:96, :], lhsT=wt[:, kc * 192 + 96:kc * 192 + 192],
                                                                      rhs=cqn[:, kc, tk], start=(kc == 0), stop=(kc == 1)),
                                      reads=[wB, cqnB[b]], writes=[pbB], inc=(kc == 1))
                            yield
                            sy.op("dve", lambda e: e.tensor_copy(out=qt[0:64, tk], in_=pa[0:64, :]),
                                  reads=[paB], writes=[qB])
                            t1, t1B = tpg()
                            sy.op("dve", lambda e: e.tensor_tensor(out=t1[R, :], in0=pa[R, :], in1=COS[R, tk], op=ALU.mult),
                                  reads=[paB, csB], writes=[t1B])
                            t2, t2B = tpg()
                            sy.op("dve", lambda e: e.tensor_tensor(out=t2[R, :], in0=pb[R, :], in1=SIN[R, tk], op=ALU.mult),
                                  reads=[pbB, csB], writes=[t2B])
                            yield
                            sy.op("dve", lambda e: e.tensor_tensor(out=qt[R, tk], in0=t1[R, :], in1=t2[R, :], op=ALU.add),
                                  reads=[t1B, t2B], writes=[qB])
                            yield
                        release(wi)
                        wt, wB, wi = acquire(("oukv", j, h))
                        for b in range(NB):
                            tk = slice(b * 512, (b + 1) * 512)
                            pa, paB = nextps()
                            for kc in range(2):
                                sy.op("pe", lambda e, kc=kc: e.matmul(pa[0:64, :], lhsT=wt[:, kc * 128:kc * 128 + 64],
                                                                      rhs=ckvn[:, kc, tk], start=(kc == 0), stop=(kc == 1)),
                                      reads=[wB, ckvnB[b]], writes=[paB], inc=(kc == 1))
                            yield
                            sy.op("dve", lambda e: e.tensor_copy(out=kt[0:64, tk], in_=pa[0:64, :]),
                                  reads=[paB], writes=[kB])
                            yield
                        sy.op("pool", lambda e: e.tensor_copy(out=kt[R, :], in_=kr[R, :]), reads=[krB], writes=[kB])
                        vo = 0 if par == 0 else 64
                        for g8 in range(S // 1024):
                            pa, paB = nextps()
                            for i8 in range(8):
                                kb = g8 * 8 + i8
                                for kc in range(2):
                                    sy.op("pe", lambda e, kc=kc, kb=kb, i8=i8: e.matmul(
                                        pa[:, i8 * 64:(i8 + 1) * 64], lhsT=ckvn[:, kc, kb * 128:(kb + 1) * 128],
                                        rhs=wt[:, kc * 128 + 64:kc * 128 + 128], start=(kc == 0), stop=(kc == 1)),
                                        reads=[wB, ckvnB[kb // 4]], writes=[paB], inc=(kc == 1 and i8 == 7))
                            yield
                            sy.op("dve", lambda e: e.tensor_copy(
                                out=vt[:, g8 * 8:(g8 + 1) * 8, vo:vo + 64],
                                in_=pa[:, :].rearrange("p (a b) -> p a b", b=64)),
                                reads=[paB], writes=[vB])
                            yield
                        release(wi)

                    def attn_head(h):
                        par = h % 2
                        hp = h // 2
                        qt, qB = QT[par], QTB[par]
                        kt, kB = KT[par], KTB[par]
                        vt, vB = V[par], VB[par]
                        if h == 0:
                            dump("qt", qt[:, 0:512], [qB])
                            dump("kt", kt[:, 0:512], [kB])
                            dump("v0", vt[:, 0, :], [vB])
                        items = []
                        for g in range(NB):
                            for kb in range(4 * g + 4):
                                items.append((g, kb))
                        LA = 3
                        inflight = {}
                        for i in range(len(items) + LA):
                            if i < len(items):
                                g, kb = items[i]
                                d = kb - 4 * g
                                c0 = max(0, d) * 128
                                ncols = 512 - c0
                                sp_, spB = nextps()
                                sy.op("pe", lambda e, kb=kb, g=g, c0=c0, ncols=ncols, sp_=sp_: e.matmul(
                                    sp_[:, 0:ncols], lhsT=kt[0:96, kb * 128:(kb + 1) * 128],
                                    rhs=qt[0:96, g * 512 + c0:(g + 1) * 512], start=True, stop=(d < 0)),
                                    reads=[kB, qB], writes=[spB], inc=(d < 0))
                                if d >= 0:
                                    sy.op("pe", lambda e, sp_=sp_: e.matmul(sp_[:, 0:128], lhsT=ident_b[:], rhs=maskT[:],
                                                                            start=False, stop=True),
                                          reads=[constB], writes=[spB])
                                inflight[i] = (sp_, spB, c0, ncols)
                            if i >= LA:
                                ii = i - LA
                                g, kb = items[ii]
                                sp_, spB, c0, ncols = inflight.pop(ii)
                                pi = pt_i["i"] % NPT
                                pt_i["i"] += 1
                                ptile, ptB = PT[pi], PTB[pi]
                                sy.op("act", lambda e, sp_=sp_, ncols=ncols, ptile=ptile: e.activation(
                                    out=ptile[:, 0:ncols], in_=sp_[:, 0:ncols], func=AF.Exp, scale=ATT_SCALE),
                                    reads=[spB], writes=[ptB])
                                op_, opB = psum[g % 2], psB[g % 2]
                                nkb = 4 * g + 4
                                sy.op("pe", lambda e, kb=kb, c0=c0, ncols=ncols, ptile=ptile, op_=op_, nkb=nkb: e.matmul(
                                    op_[:, c0:512], lhsT=vt[:, kb, :], rhs=ptile[:, 0:ncols],
                                    start=(kb == 0), stop=(kb == nkb - 1)),
                                    reads=[vB, ptB], writes=[opB], inc=True)
                                if kb == nkb - 1:
                                    if par == 0:
                                        num, den = slice(0, 64), slice(64, 128)
                                    else:
                                        num, den = slice(64, 128), slice(0, 64)
                                    tk = slice(g * 512, (g + 1) * 512)
                                    sy.op("act", lambda e, op_=op_: e.activation(out=rec[num, :], in_=op_[den, :], func=AF.Ln),
                                          reads=[opB], writes=[recB])
                                    sy.op("act", lambda e: e.activation(out=rec[num, :], in_=rec[num, :], func=AF.Exp, scale=-1.0),
                                          writes=[recB])
                                    o1, o1B = tp()
                                    sy.op("dve", lambda e, op_=op_, o1=o1: e.tensor_tensor(out=o1[num, :], in0=op_[num, :],
                                                                                         in1=rec[num, :], op=ALU.mult),
                                          reads=[opB, recB], writes=[o1B])
                                    sy.op("dve", lambda e, o1=o1: e.tensor_tensor(out=Zs[num, hp, tk], in0=o1[num, :],
                                                                                in1=Zs[num, hp, tk], op=ALU.mult),
                                          reads=[o1B], writes=[ZsB[hp][g]])
                            yield

                    for _ in gen_head(0):
                        pass
                    for h in range(16):
                        gens = [attn_head(h)]
                        if h + 1 < 16:
                            gens.append(gen_head(h + 1))
                        while gens:
                            for g_ in list(gens):
                                try:
                                    next(g_)
                                except StopIteration:
                                    gens.remove(g_)
                    sy.fence()

                dump("og", Zs[:, 0, 0:512], [ZsB[0][0]])
                with ExitStack() as p3:
                    big = p3.enter_context(nc.sbuf_tensor(f"o_big_{l}_{s}", [128, KC, 512], F32))
                    bigB = [Buf(f"o_big{c}") for c in range(KC)]
                    big2 = p3.enter_context(nc.sbuf_tensor(f"o_big2_{l}_{s}", [128, KC, 512], F32))
                    big2B = [Buf(f"o_big2{c}") for c in range(KC)]
                    sqt = p3.enter_context(nc.sbuf_tensor(f"o_sq3_{l}_{s}", [128, KC, 512], BF16))
                    sqB = Buf("o_sq3")
                    wl = [acquire(("owout", j, n)) for n in range(8)]
                    bigs = [(big, bigB), (big2, big2B)]
                    pend = None
                    for b in range(NB):
                        tk = slice(b * 512, (b + 1) * 512)
                        bg, bgB = bigs[b % 2]
                        for n in range(8):
                            wt, wB, _ = wl[n]
                            pt, pB = mm8(wt, wB, lambda kc: Zs[:, kc, tk], [ZsB[c][b] for c in range(KC)])
                            sy.op("act", lambda e: e.activation(out=bg[:, n, :], in_=pt[:], func=AF.Identity),
                                  reads=[pB], writes=[bgB[n]])
                            if pend is not None:
                                try:
                                    next(pend)
                                    next(pend)
                                except StopIteration:
                                    pend = None
                        if pend is not None:
                            for _ in pend:
                                pass
                        pend = postnorm_gen(l, s, b * 512, bg, bgB, sqt, sqB, tp)
                    for _ in pend:
                        pass
                    for _w in wl:
                        release(_w[2])
                    sy.fence()
                tp_stats[0] = None

        allxs = [xsB[c][b] for c in range(KC) for b in range(NB)]
        outB = Buf("outst")
        for s in range(NSEQ):
            for c in range(KC):
                sy.dma("sp", xs[:, c, :], x_d.ap()[s, :, c, :], writes=xsB[c])
            for l in LAYERS:
                if l % 2 == 0:
                    even_layer(l, s)
                else:
                    odd_layer(l, s)
            for c in range(KC):
                sy.dma("sp", out_d.ap()[s, :, c, :], xs[:, c, :], reads=xsB[c], writes=[outB])
        nc.sync.wait_ge(outB.dsem, outB.dcnt)
        if DEBUG and dbgB.dsem is not None:
            nc.sync.wait_ge(dbgB.dsem, dbgB.dcnt)
        if RECORD:
            return ws["order"]
        assert ws["acq"] == len(ws["order"]) == ws["issued"], (ws["acq"], len(ws["order"]), ws["issued"])
        build.nins = sy.nins
    return nc


_CACHE = {}


def kernel(**inp):
    inp = {k: np.asarray(v) for k, v in inp.items()}
    x = inp["x"].astype(np.float32, copy=False)
    B, S, Dm = x.shape
    nseq = B // NCORES
    wts = pack_weights(inp)
    pvv = pack_pv(inp)
    key = (S, nseq)
    if key not in _CACHE:
        _CACHE[key] = build(S=S, NSEQ=nseq)
    nc = _CACHE[key]
    in_maps = []
    for cid in range(NCORES):
        bs = slice(cid * nseq, (cid + 1) * nseq)
        xf = np.ascontiguousarray(x[bs].reshape(nseq, S, KC, 128).transpose(0, 3, 2, 1))
        cf = np.ascontiguousarray(inp["c"][bs].astype(np.float32).reshape(nseq, KC, 128).transpose(2, 1, 0))
        pos = np.ascontiguousarray(inp["positions"][bs].astype(np.int32))
        in_maps.append({"x": xf, "c": cf, "pos": pos, "wts": wts, "pv": pvv})
    res = run_bass_kernel_spmd(nc, in_maps, core_ids=list(range(NCORES)))
    outs = []
    for cid in range(NCORES):
        o = np.asarray(res.results[cid]["out"]).reshape(nseq, 128, KC, S)
        outs.append(o.transpose(0, 3, 2, 1).reshape(nseq, S, Dm))
    return np.ascontiguousarray(np.concatenate(outs, axis=0).astype(np.float32))
```

```python
import math
from contextlib import ExitStack
import numpy as np
import concourse.bass as bass
import concourse.mybir as mybir
from concourse.ap import AP
from concourse.bass_utils import run_bass_kernel_spmd

F32 = mybir.dt.float32
BF16 = mybir.dt.bfloat16
I32 = mybir.dt.int32
AF = mybir.ActivationFunctionType
ALU = mybir.AluOpType

D = 1024
KC = 8
NCORES = 8
EPS = 1e-6
SLOT = 1024
RING = 10
SAME_ENG_WINDOW = 3
EVEN_STAGGER = 9
ATT_SCALE = 96.0 ** -0.5
TWO_PI = 2.0 * math.pi
PI_HI = 6.28125
PI_LO = TWO_PI - 6.28125


def weight_plan():
    plan = {}
    off = 0

    def add(name, F):
        nonlocal off
        plan[name] = (off, F)
        off += 128 * F

    for l in range(4):
        for n in range(24):
            add(("ada", l, n), 1024)
    for j in range(2):
        for c in range(40):
            add(("ewin", j, c), 1024)
        for kc in range(8):
            add(("egate", j, kc), 256)
        for n in range(8):
            add(("ewout", j, n, 0), 1024)
            add(("ewout", j, n, 1), 1024)
    for j in range(2):
        for c in range(14):
            add(("owin", j, c), 1024)
        for h in range(16):
            add(("ouq", j, h), 384)
            add(("oukv", j, h), 256)
        for n in range(8):
            add(("owout", j, n), 1024)
    return plan, off


def pv_plan():
    cols = {}
    off = 0

    def add(name, n):
        nonlocal off
        cols[name] = off
        off += n

    for l in range(4):
        add(("pre_g", l), 8)
        add(("post_g", l), 8)
        add(("ada_b", l), 24)
    for j in range(2):
        add(("conv_w", j), 8 * 31)
        add(("conv_b", j), 8)
        add(("ln_g", j), 8)
        add(("ln_b", j), 8)
        add(("lconv_w", j), 8 * 4)
        add(("lconv_b", j), 8)
        add(("ba", j), 8)
        add(("bx", j), 8)
        add(("lam", j), 8)
        add(("q_norm", j), 2)
        add(("kv_norm", j), 2)
    add("inv", 1)
    add("sgn", 1)
    add("inv4", 1)
    add("sgn4", 1)
    return cols, off


def chunkify(W):
    K, N = W.shape
    return np.ascontiguousarray(W.reshape(K // 128, 128, N // 128, 128).transpose(2, 1, 0, 3))


def vec_pc(v):
    return np.ascontiguousarray(v.reshape(-1, 128).T)


def pack_weights(inp):
    plan, total = weight_plan()
    flat = np.zeros(total, np.float32)

    def put(name, arr):
        off, F = plan[name]
        a = np.ascontiguousarray(arr, dtype=np.float32).reshape(128, F)
        flat[off:off + 128 * F] = a.reshape(-1)

    for l in range(4):
        ch = chunkify(inp["ada_w"][l])
        for n in range(24):
            put(("ada", l, n), ch[n])
    for j in range(2):
        ch = chunkify(inp["ev_w_in"][j])
        for c in range(40):
            put(("ewin", j, c), ch[c])
        wa, wx = inp["ev_lru_wa"][j], inp["ev_lru_wx"][j]
        for kc in range(8):
            g = np.zeros((128, 2, 128), np.float32)
            for hh in range(2):
                g[hh * 64:(hh + 1) * 64, 0, hh * 64:(hh + 1) * 64] = wa[2 * kc + hh]
                g[hh * 64:(hh + 1) * 64, 1, hh * 64:(hh + 1) * 64] = wx[2 * kc + hh]
            put(("egate", j, kc), g)
        wo = inp["ev_w_out"][j]
        c0 = chunkify(wo[:1024])
        c1 = chunkify(wo[1024:])
        for n in range(8):
            put(("ewout", j, n, 0), c0[n])
            put(("ewout", j, n, 1), c1[n])
    for j in range(2):
        w = inp["od_w_in"][j]
        z64 = np.zeros((1024, 64), np.float32)
        z32 = np.zeros((1024, 32), np.float32)
        kr = w[:, 512:544]
        kr1 = np.concatenate([z64, kr, z32], axis=1)
        kr2 = np.concatenate([z64, kr[:, 16:32], kr[:, 0:16], z32], axis=1)
        wcat = np.concatenate([w[:, 0:512], kr1, kr2, w[:, 544:1568]], axis=1)
        ch = chunkify(wcat)
        for c in range(14):
            put(("owin", j, c), ch[c])
        uq = inp["od_w_uq"][j].reshape(2, 128, 16, 96)
        ukv = inp["od_w_ukv"][j].reshape(2, 128, 16, 128)
        for h in range(16):
            a = uq[:, :, h, :]
            sw = np.concatenate([a[:, :, 0:64], a[:, :, 80:96], a[:, :, 64:80]], axis=2)
            both = np.concatenate([a, sw], axis=2)
            put(("ouq", j, h), both.transpose(1, 0, 2))
            put(("oukv", j, h), ukv[:, :, h, :].transpose(1, 0, 2))
        ch = chunkify(inp["od_w_out"][j])
        for n in range(8):
            put(("owout", j, n), ch[n])
    return flat


def pack_pv(inp):
    cols, n = pv_plan()
    pv = np.zeros((128, n), np.float32)

    def put(name, arr):
        a = np.asarray(arr, np.float32)
        pv[:, cols[name]:cols[name] + a.shape[1]] = a

    for l in range(4):
        put(("pre_g", l), vec_pc(inp["pre_g"][l]))
        put(("post_g", l), vec_pc(inp["post_g"][l]))
        put(("ada_b", l), vec_pc(inp["ada_b"][l]))
    for j in range(2):
        cw = inp["ev_conv_w"][j]
        put(("conv_w", j), cw.reshape(31, 8, 128).transpose(2, 1, 0).reshape(128, 8 * 31))
        put(("conv_b", j), vec_pc(inp["ev_conv_b"][j]))
        put(("ln_g", j), vec_pc(inp["ev_ln_g"][j]))
        put(("ln_b", j), vec_pc(inp["ev_ln_b"][j]))
        lw = inp["ev_lru_conv_w"][j]
        put(("lconv_w", j), lw.reshape(4, 8, 128).transpose(2, 1, 0).reshape(128, 8 * 4))
        put(("lconv_b", j), vec_pc(inp["ev_lru_conv_b"][j]))
        put(("ba", j), vec_pc(inp["ev_lru_ba"][j]))
        put(("bx", j), vec_pc(inp["ev_lru_bx"][j]))
        put(("lam", j), vec_pc(inp["ev_lru_lam"][j]))
        put(("q_norm", j), vec_pc(inp["od_q_norm"][j]))
        put(("kv_norm", j), vec_pc(inp["od_kv_norm"][j]))
    inv = (10000.0 ** (-np.arange(0, 32, 2, dtype=np.float32) / 32.0)).astype(np.float32)
    iv = np.zeros((128, 1), np.float32)
    sg = np.ones((128, 1), np.float32)
    for i in range(32):
        iv[64 + i, 0] = inv[i % 16]
        sg[64 + i, 0] = -1.0 if i < 16 else 1.0
    put("inv", iv)
    put("sgn", sg)
    iv4 = np.zeros((128, 1), np.float32)
    sg4 = np.ones((128, 1), np.float32)
    for p in range(128):
        iv4[p, 0] = inv[(p % 32) % 16]
        sg4[p, 0] = -1.0 if (p % 32) < 16 else 1.0
    put("inv4", iv4)
    put("sgn4", sg4)
    return pv


_UID = [0]


class Buf:
    __slots__ = ("name", "w", "r", "dsem", "dcnt")

    def __init__(self, name):
        _UID[0] += 1
        self.name = f"{name}_{_UID[0]}"
        self.w = None
        self.r = {}
        self.dsem = None
        self.dcnt = 0


class Sync:
    ENG = ("pe", "act", "dve", "pool", "sp")

    def __init__(self, nc, es):
        self.nc = nc
        self.es = es
        self.eng = {"pe": nc.tensor, "act": nc.scalar, "dve": nc.vector, "pool": nc.gpsimd, "sp": nc.sync}
        self.sem = {k: es.enter_context(nc.semaphore("s_" + k)) for k in self.ENG}
        self.cnt = {k: 0 for k in self.ENG}
        self.known = {k: {} for k in self.ENG}
        self.nins = 0

    def _need(self, E, dep, strict):
        key, sem, val, src = dep
        if src == E and not strict and E != "pool":
            if E == "pe" or (self.cnt[E] - val) >= SAME_ENG_WINDOW:
                return
        if self.known[E].get(key, 0) >= val:
            return
        self.eng[E].wait_ge(sem, val)
        self.known[E][key] = val
        self.nins += 1

    def _deps(self, E, reads, writes, sreads, strict_all=False):
        for b in reads:
            if b.w is not None:
                self._need(E, b.w, strict_all)
        for b in sreads:
            if b.w is not None:
                self._need(E, b.w, True)
        for b in writes:
            if b.w is not None:
                self._need(E, b.w, strict_all)
            for d in b.r.values():
                self._need(E, d, strict_all)

    def op(self, E, fn, reads=(), writes=(), sreads=(), inc=True):
        self._deps(E, reads, writes, sreads)
        ins = fn(self.eng[E])
        self.nins += 1
        if inc:
            self.cnt[E] += 1
            ins.then_inc(self.sem[E], 1)
            me = (E, self.sem[E], self.cnt[E], E)
        else:
            me = (E, self.sem[E], self.cnt[E] + 1, E)
        for b in writes:
            b.w = me
            b.r = {}
        for b in reads:
            b.r[E] = me
        for b in sreads:
            b.r[E] = me
        return ins

    def dma(self, Q, out, in_, reads=(), writes=()):
        self._deps(Q, reads, writes, (), strict_all=True)
        tgt = writes[0] if writes else reads[0]
        if tgt.dsem is None:
            tgt.dsem = self.es.enter_context(self.nc.semaphore("d_" + tgt.name))
        tgt.dcnt += 16
        self.eng[Q].dma_start(out=out, in_=in_).then_inc(tgt.dsem, 16)
        self.nins += 1
        me = ("d_" + tgt.name, tgt.dsem, tgt.dcnt, None)
        for b in writes:
            b.w = me
            b.r = {}
        for b in reads:
            b.r["dma_" + tgt.name] = me
        return me

    def fence(self):
        for E in ("pe", "act", "dve", "pool", "sp"):
            for Fg in ("pe", "act", "dve", "pool"):
                if Fg != E and self.cnt[Fg] > 0:
                    self._need(E, (Fg, self.sem[Fg], self.cnt[Fg], Fg), True)


def build(S=2048, NSEQ=2, LAYERS=(0, 1, 2, 3), DEBUG=False):
    order = _build(S, NSEQ, LAYERS, DEBUG, None)
    return _build(S, NSEQ, LAYERS, DEBUG, order)


def _build(S, NSEQ, LAYERS, DEBUG, ORDER):
    RECORD = ORDER is None
    nc = bass.Bass("TRN2", target_bir_lowering=False)
    plan, wtotal = weight_plan()
    pcols, npv = pv_plan()
    NB = S // 512

    x_d = nc.dram_tensor("x", [NSEQ, 128, KC, S], F32, kind="ExternalInput")
    c_d = nc.dram_tensor("c", [128, KC, NSEQ], F32, kind="ExternalInput")
    pos_d = nc.dram_tensor("pos", [NSEQ, S], I32, kind="ExternalInput")
    w_d = nc.dram_tensor("wts", [wtotal], F32, kind="ExternalInput")
    pv_d = nc.dram_tensor("pv", [128, npv], F32, kind="ExternalInput")
    out_d = nc.dram_tensor("out", [NSEQ, 128, KC, S], F32, kind="ExternalOutput")
    dgd = nc.dram_tensor("dgd", [2, 8, 128, 31 * 128], BF16, kind="Internal")

    dbg_d = nc.dram_tensor("dbg", [128, 16384], F32, kind="ExternalOutput") if DEBUG else None
    dbg_cols = {}
    build.dbg_cols = dbg_cols
    dbg_state = {"c": 0}

    with ExitStack() as es:
        sy = Sync(nc, es)
        dbgB = Buf("dbg")

        def dump(name, ap, bufs):
            if not DEBUG or name in dbg_cols:
                return
            n = ap.shape[-1]
            p0 = 0
            c0 = dbg_state["c"]
            dbg_cols[name] = (c0, n)
            dbg_state["c"] += n
            sy.dma("pool", dbg_d.ap()[0:ap.shape[0], c0:c0 + n], ap, reads=bufs, writes=[dbgB])

        def sb(name, shape, dt):
            return es.enter_context(nc.sbuf_tensor(name, shape, dt))

        xs = sb("xs", [128, KC, S], F32)
        xsB = [[Buf(f"xs{c}_{b}") for b in range(NB)] for c in range(KC)]
        pvt = sb("pvt", [128, npv], F32)
        pvB = Buf("pv")
        ring = sb("ring", [128, RING, SLOT], BF16)
        ringB = [Buf(f"ring{i}") for i in range(RING)]
        ident_f = sb("ident_f", [128, 128], F32)
        ident_b = sb("ident_b", [128, 128], BF16)
        od1024 = sb("od1024", [128, 128], BF16)
        od256 = sb("od256", [128, 128], BF16)
        maskT = sb("maskT", [128, 128], BF16)
        constB = Buf("const")
        cin = sb("cin", [128, KC, NSEQ], F32)
        cact = sb("cact", [128, KC, NSEQ], BF16)
        cB = Buf("c")
        modt = sb("modt", [128, 96, NSEQ], F32)
        modB = Buf("mod")
        drvA = sb("drvA", [128, 4, NSEQ, 8], F32)
        drvG = sb("drvG", [128, 4, NSEQ, 8], F32)
        drvB = Buf("drv")
        nsp = sb("nsp", [128, 2, 8], F32)
        nspB = Buf("nsp")
        carry = sb("carry", [128, 8], F32)
        carryB = [Buf(f"carry{j}") for j in range(8)]
        xbh = sb("xbh", [128, 8, 4], BF16)
        xbhB = [Buf(f"xbh{j}") for j in range(8)]

        psum = [es.enter_context(nc.psum_tensor(f"ps{i}", [128, 512], F32)) for i in range(8)]
        psB = [Buf(f"ps{i}") for i in range(8)]
        ps_state = {"i": 0}

        def nextps():
            i = 2 + ps_state["i"] % 6
            ps_state["i"] += 1
            return psum[i], psB[i]

        def pv(name, a, b=None):
            c0 = pcols[name]
            if b is None:
                return pvt[:, c0 + a:c0 + a + 1]
            return pvt[:, c0 + a:c0 + b]

        ws = {"order": [] if RECORD else ORDER, "issued": 0, "acq": 0, "rel": 0, "done": set()}

        def w_ap(name):
            off, Fw = plan[name]
            return AP(w_d, off, [[Fw, 128], [1, Fw]]), Fw

        def ws_issue():
            if RECORD:
                return
            while ws["issued"] < len(ws["order"]) and ws["issued"] < ws["rel"] + RING:
                i = ws["issued"]
                src, Fw = w_ap(ws["order"][i])
                slot = i % RING
                sy.dma("pool", ring[:, slot, 0:Fw], src, writes=[ringB[slot]])
                ws["issued"] += 1

        def acquire(name):
            i = ws["acq"]
            ws["acq"] += 1
            if RECORD:
                ws["order"].append(name)
                return ring[:, 0, :], ringB[0], i
            assert ws["order"][i] == name, (ws["order"][i], name)
            assert ws["issued"] > i, "weight ring deadlock (acquire beyond issued)"
            slot = i % RING
            return ring[:, slot, :], ringB[slot], i

        def release(i):
            ws["done"].add(i)
            while ws["rel"] in ws["done"]:
                ws["done"].remove(ws["rel"])
                ws["rel"] += 1
            ws_issue()

        sy.dma("sp", pvt[:], pv_d.ap()[:, :], writes=[pvB])
        sy.dma("sp", cin[:], c_d.ap()[:, :, :], writes=[cB])
        ws_issue()
        sy.op("pool", lambda e: e.memset(ident_f[:], 1.0), writes=[constB])
        sy.op("pool", lambda e: e.affine_select(out=ident_f[:], in_=ident_f[:], pattern=[[-1, 128]], base=0,
                                                channel_multiplier=1, compare_op=ALU.is_equal, fill=0.0),
              writes=[constB])
        sy.op("pool", lambda e: e.tensor_copy(out=ident_b[:], in_=ident_f[:]), writes=[constB])
        sy.op("pool", lambda e: e.memset(od1024[:], 1.0 / 1024), writes=[constB])
        sy.op("pool", lambda e: e.memset(od256[:], 1.0 / 256), writes=[constB])
        sy.op("pool", lambda e: e.memset(maskT[:], 0.0), writes=[constB])
        sy.op("pool", lambda e: e.affine_select(out=maskT[:], in_=maskT[:], pattern=[[1, 128]], base=0,
                                                channel_multiplier=-1, compare_op=ALU.is_ge, fill=-30000.0),
              writes=[constB])
        sy.op("act", lambda e: e.activation(out=cact[:], in_=cin[:], func=AF.Silu), reads=[cB], writes=[cB])

        spt = [sb(f"spt{i}", [128, 8], F32) for i in range(4)]
        dgdB = [[Buf(f"dgd{j}_{jj}") for jj in range(8)] for j in range(2)]
        st0 = ExitStack()
        dgs = [st0.enter_context(nc.sbuf_tensor(f"dgs{i}", [128, 31, 128], BF16)) for i in range(2)]
        dgsB = [Buf(f"dgs{i}") for i in range(2)]
        for j in range(2):
            if (2 * j) not in LAYERS:
                continue
            for jj in range(8):
                i = jj % 2
                cw0 = pcols[("conv_w", j)] + jj * 31
                sy.op("dve", lambda e: e.tensor_tensor(
                    out=dgs[i][:], in0=ident_b[:].unsqueeze(1).to_broadcast([128, 31, 128]),
                    in1=pvt[:, cw0:cw0 + 31].unsqueeze(2).to_broadcast([128, 31, 128]), op=ALU.mult),
                    reads=[constB, pvB], writes=[dgsB[i]])
                sy.dma("sp", dgd.ap()[j, jj, :, :], dgs[i][:].rearrange("p a b -> p (a b)"),
                       reads=[dgsB[i]], writes=[dgdB[j][jj]])

        for l in range(4):
            for n in range(24):
                wt, wB, wi = acquire(("ada", l, n))
                pt, pB = nextps()
                for kc in range(KC):
                    sy.op("pe", lambda e, kc=kc: e.matmul(pt[:, 0:NSEQ], lhsT=wt[:, kc * 128:(kc + 1) * 128],
                                                          rhs=cact[:, kc, :], start=(kc == 0), stop=(kc == KC - 1)),
                          reads=[wB, cB], writes=[pB], inc=(kc == KC - 1))
                release(wi)
                sy.op("act", lambda e: e.activation(out=modt[:, l * 24 + n, :], in_=pt[:, 0:NSEQ], func=AF.Identity,
                                                    bias=pv(("ada_b", l), n), scale=1.0),
                      reads=[pB], sreads=[pvB], writes=[modB])
        for l in range(4):
            for s in range(NSEQ):
                sy.op("dve", lambda e: e.tensor_scalar(out=drvA[:, l, s, :], in0=modt[:, l * 24 + 8:l * 24 + 16, s],
                                                       scalar1=1.0, scalar2=None, op0=ALU.add),
                      reads=[modB], writes=[drvB])
                sy.op("dve", lambda e: e.tensor_tensor(out=drvA[:, l, s, :], in0=drvA[:, l, s, :],
                                                       in1=pv(("pre_g", l), 0, 8), op=ALU.mult),
                      reads=[pvB], writes=[drvB])
                sy.op("dve", lambda e: e.tensor_tensor(out=drvG[:, l, s, :], in0=modt[:, l * 24 + 16:l * 24 + 24, s],
                                                       in1=pv(("post_g", l), 0, 8), op=ALU.mult),
                      reads=[pvB, modB], writes=[drvB])
        for j in range(2):
            lam_ap = pv(("lam", j), 0, 8)
            al, ee, ww, w2 = spt
            sy.op("act", lambda e: e.activation(out=al[:], in_=lam_ap, func=AF.Abs),
                  reads=[pvB], writes=[nspB])
            sy.op("act", lambda e: e.activation(out=ee[:], in_=al[:], func=AF.Exp, scale=-1.0),
                  reads=[nspB], writes=[nspB])
            sy.op("dve", lambda e: e.tensor_scalar(out=ww[:], in0=ee[:], scalar1=2.0, scalar2=None, op0=ALU.add),
                  reads=[nspB], writes=[nspB])
            sy.op("dve", lambda e: e.reciprocal(out=ww[:], in_=ww[:]), writes=[nspB])
            sy.op("dve", lambda e: e.tensor_tensor(out=ww[:], in0=ww[:], in1=ee[:], op=ALU.mult), writes=[nspB])
            sy.op("dve", lambda e: e.tensor_tensor(out=w2[:], in0=ww[:], in1=ww[:], op=ALU.mult), writes=[nspB])
            sy.op("dve", lambda e: e.tensor_scalar(out=al[:], in0=w2[:], scalar1=1.0 / 11, scalar2=1.0 / 9, op0=ALU.mult,
                                                   op1=ALU.add), writes=[nspB])
            for cf in (1.0 / 7, 1.0 / 5, 1.0 / 3, 1.0):
                sy.op("dve", lambda e: e.tensor_tensor(out=al[:], in0=al[:], in1=w2[:], op=ALU.mult), writes=[nspB])
                sy.op("dve", lambda e, cf=cf: e.tensor_scalar(out=al[:], in0=al[:], scalar1=cf, scalar2=None, op0=ALU.add),
                      writes=[nspB])
            sy.op("dve", lambda e: e.tensor_tensor(out=al[:], in0=al[:], in1=ww[:], op=ALU.mult), writes=[nspB])
            sy.op("dve", lambda e: e.tensor_scalar(out=ee[:], in0=lam_ap, scalar1=-1.0, scalar2=0.0, op0=ALU.mult,
                                                   op1=ALU.max), reads=[pvB], writes=[nspB])
            sy.op("dve", lambda e: e.scalar_tensor_tensor(out=al[:], in0=al[:], scalar=2.0, in1=ee[:], op0=ALU.mult,
                                                          op1=ALU.add), writes=[nspB])
            sy.op("dve", lambda e: e.tensor_scalar(out=nsp[:, j, :], in0=al[:], scalar1=-8.0, scalar2=None,
                                                   op0=ALU.mult), writes=[nspB])
        sy.fence()
        for j in range(2):
            for jj in range(8):
                if dgdB[j][jj].w is not None:
                    for E_ in ("pe", "act", "dve", "pool", "sp"):
                        sy._need(E_, dgdB[j][jj].w, True)
        st0.close()
        dump("modt", modt[:].rearrange("p a b -> p (a b)"), [modB])
        dump("drvA", drvA[:].rearrange("p a b c -> p (a b c)"), [drvB])
        dump("drvG", drvG[:].rearrange("p a b c -> p (a b c)"), [drvB])
        dump("nsp", nsp[:].rearrange("p a b -> p (a b)"), [nspB])
        tp_stats = [None]
        def mm8(wt, wB, rhs_fn, rB, M=128, wcol0=0, prow=None):
            pt, pB = nextps()
            for kc in range(KC):
                sy.op("pe", lambda e, kc=kc: e.matmul(pt[0:M, :], lhsT=wt[:, kc * 128 + wcol0:kc * 128 + wcol0 + M],
                                                      rhs=rhs_fn(kc), start=(kc == 0), stop=(kc == KC - 1)),
                      reads=[wB] + rB, writes=[pB], inc=(kc == KC - 1))
            return pt, pB

        def rms_stats(src_fn, srcB, nch, onesm, sqt, sqB, tp, bank=None):
            tp = tp_stats[0] or tp
            sy.op("act", lambda e: e.activation(out=sqt[:, 0:nch, :], in_=src_fn(), func=AF.Square),
                  reads=srcB, writes=[sqB])
            pt, pB = (psum[bank], psB[bank]) if bank is not None else nextps()
            for kc in range(nch):
                sy.op("pe", lambda e, kc=kc: e.matmul(pt[:, :], lhsT=onesm[:], rhs=sqt[:, kc, :], start=(kc == 0),
                                                      stop=(kc == nch - 1)),
                      reads=[constB, sqB], writes=[pB], inc=(kc == nch - 1))
            sd, sdB = tp()
            sy.op("act", lambda e: e.activation(out=sd[:], in_=pt[:], func=AF.Ln, bias=EPS, scale=1.0),
                  reads=[pB], writes=[sdB])
            rs, rsB = tp()
            sy.op("act", lambda e: e.activation(out=rs[:], in_=sd[:], func=AF.Exp, scale=-0.5), reads=[sdB], writes=[rsB])
            return rs, rsB

        def prenorm(l, s, t0, hn_fn, hnB, sqt, sqB, tp, tpt=None, bank=None):
            tpt = tpt or tp
            b = t0 // 512
            rs, rsB = rms_stats(lambda: xs[:, :, t0:t0 + 512], [xsB[c][b] for c in range(KC)], KC, od1024, sqt, sqB, tp, bank)
            for kc in range(KC):
                tt, ttB = tpt()
                sy.op("dve", lambda e: e.tensor_tensor(out=tt[:], in0=xs[:, kc, t0:t0 + 512], in1=rs[:], op=ALU.mult),
                      reads=[xsB[kc][b], rsB], writes=[ttB])
                sy.op("act", lambda e: e.activation(out=hn_fn(kc), in_=tt[:], func=AF.Identity,
                                                    scale=drvA[:, l, s, kc:kc + 1], bias=modt[:, l * 24 + kc, s:s + 1]),
                      reads=[ttB], sreads=[drvB, modB], writes=[hnB[kc]])

        def postnorm_gen(l, s, t0, y, yB, sqt, sqB, tp, tpt=None):
            tpt = tpt or tp
            b = t0 // 512
            rs, rsB = rms_stats(lambda: y[:, :, :], yB, KC, od1024, sqt, sqB, tp)
            yield
            for n in range(KC):
                tt, ttB = tpt()
                sy.op("dve", lambda e: e.tensor_tensor(out=tt[:], in0=y[:, n, :], in1=rs[:], op=ALU.mult),
                      reads=[yB[n], rsB], writes=[ttB])
                sy.op("dve", lambda e: e.scalar_tensor_tensor(out=xs[:, n, t0:t0 + 512], in0=tt[:],
                                                              scalar=drvG[:, l, s, n:n + 1], in1=xs[:, n, t0:t0 + 512],
                                                              op0=ALU.mult, op1=ALU.add),
                      reads=[ttB], sreads=[drvB], writes=[xsB[n][b]])
                yield

        def postnorm(l, s, t0, y, yB, sqt, sqB, tp, tpt=None):
            for _ in postnorm_gen(l, s, t0, y, yB, sqt, sqB, tp, tpt):
                pass

        def mk_tmp_pool(ess, name, n, dt=F32, w=512):
            _UID[0] += 1
            tiles = [ess.enter_context(nc.sbuf_tensor(f"{name}{i}_{_UID[0]}", [128, w], dt)) for i in range(n)]
            bufs = [Buf(f"{name}{i}") for i in range(n)]
            st = {"i": 0}

            def get():
                i = st["i"] % n
                st["i"] += 1
                return tiles[i], bufs[i]
            return get

        def even_layer(l, s):
            j = l // 2
            with ExitStack() as el:
                def sbl(name, shape, dt):
                    return el.enter_context(nc.sbuf_tensor(f"{name}_{l}_{s}", shape, dt))
                hn = sbl("e_hn", [128, KC, 512], BF16)
                hnB = [Buf(f"e_hn{c}") for c in range(KC)]
                G = sbl("e_G", [128, KC, 544], BF16)
                GB = [Buf(f"e_G{c}") for c in range(KC)]
                Z = sbl("e_Z", [128, KC, 512], BF16)
                ZB = [Buf(f"e_Z{c}") for c in range(KC)]
                big = sbl("e_big", [128, KC, 512], F32)
                bigB = [Buf(f"e_big{c}") for c in range(KC)]
                Lo = sbl("e_Lo", [128, KC, 512], BF16)
                LoB = [Buf(f"e_Lo{c}") for c in range(KC)]
                sqt = sbl("e_sq", [128, KC, 512], BF16)
                sqB = Buf("e_sq")
                dg = sbl("e_dg", [128, 31, 128], BF16)
                dgB = Buf("e_dg")
                d4 = [sbl(f"e_d4{i}", [128, 4, 128], BF16) for i in range(2)]
                d4B = [Buf(f"e_d4{i}") for i in range(2)]
                XB = [sbl(f"e_XB{i}", [128, 516], BF16) for i in range(2)]
                XBB = [Buf(f"e_XB{i}") for i in range(2)]
                tp = mk_tmp_pool(el, "e_tf", 5, F32)
                tpt = mk_tmp_pool(el, "e_tt", 3, F32)
                tpb = mk_tmp_pool(el, "e_tb", 2, BF16)
                lt = [[sbl(f"e_lt{p}{i}", [128, 512], F32) for i in range(5)] for p in range(2)]
                ltB = [[Buf(f"e_lt{p}{i}") for i in range(5)] for p in range(2)]
                sgt = [sbl(f"e_sgt{p}", [128, 512], BF16) for p in range(2)]
                sgtB = [Buf(f"e_sgt{p}") for p in range(2)]
                zbt = [sbl(f"e_zbt{p}", [128, 512], BF16) for p in range(2)]
                zbtB = [Buf(f"e_zbt{p}") for p in range(2)]
                xcbt = [sbl(f"e_xcbt{p}", [128, 512], BF16) for p in range(2)]
                xcbtB = [Buf(f"e_xcbt{p}") for p in range(2)]

                sy.op("dve", lambda e: e.memset(G[:, :, 0:32], 0.0), writes=GB)
                sy.op("dve", lambda e: e.memset(carry[:], 0.0), writes=carryB)
                sy.op("dve", lambda e: e.memset(xbh[:], 0.0), writes=xbhB)
                hrhs = lambda kc: hn[:, kc, :]

                def w_mm8(name):
                    wt, wB, wi = acquire(name)
                    pt, pB = mm8(wt, wB, hrhs, hnB)
                    release(wi)
                    return pt, pB

                def conv_stage(jj):
                    p = jj % 2
                    pt, pB = w_mm8(("ewin", j, 8 + jj))
                    yield
                    sy.op("act", lambda e: e.activation(out=sgt[p][:], in_=pt[:], func=AF.Sigmoid),
                          reads=[pB], writes=[sgtB[p]])
                    pt, pB = w_mm8(("ewin", j, jj))
                    yield
                    sy.op("dve", lambda e: e.tensor_tensor(out=G[:, jj, 32:544], in0=pt[:], in1=sgt[p][:], op=ALU.mult),
                          reads=[pB, sgtB[p]], writes=[GB[jj]])
                    pt, pB = w_mm8(("ewin", j, 16 + jj))
                    yield
                    sy.op("act", lambda e: e.activation(out=Z[:, jj, :], in_=pt[:], func=AF.Silu),
                          reads=[pB], writes=[ZB[jj]])
                    sy.dma("sp", dg[:].rearrange("p a b -> p (a b)"), dgd.ap()[j, jj, :, :],
                           reads=[dgdB[j][jj]], writes=[dgB])
                    yield
                    yield
                    pt, pB = nextps()
                    for k in range(31):
                        sy.op("pe", lambda e, k=k: e.matmul(pt[:, :], lhsT=dg[:, k, :], rhs=G[:, jj, k + 2:k + 514],
                                                            start=(k == 0), stop=(k == 30)),
                              reads=[dgB, GB[jj]], writes=[pB], inc=(k == 30))
                        if k % 8 == 7:
                            yield
                    sy.op("act", lambda e: e.activation(out=big[:, jj, :], in_=pt[:], func=AF.Identity,
                                                        bias=pv(("conv_b", j), jj), scale=1.0),
                          reads=[pB], sreads=[pvB], writes=[bigB[jj]])
                    if jj == 0:
                        dump("G0", G[:, 0, 32:544], [GB[0]])
                        dump("aconv0", big[:, 0, :], [bigB[0]])
                        dump("Zraw0", Z[:, 0, :], [ZB[0]])
                    yield
                    sy.op("dve", lambda e: e.tensor_copy(out=G[:, jj, 0:32], in_=G[:, jj, 512:544]),
                          reads=[GB[jj]], writes=[GB[jj]])

                def lru_stage(jj):
                    p = jj % 2
                    xbt, xbB = XB[p], XBB[p]
                    pt, pB = w_mm8(("ewin", j, 24 + jj))
                    yield
                    sy.op("act", lambda e: e.activation(out=xbt[:, 4:516], in_=pt[:], func=AF.Identity),
                          reads=[pB], writes=[xbB])
                    sy.op("dve", lambda e: e.tensor_copy(out=xbt[:, 0:4], in_=xbh[:, jj, :]),
                          reads=[xbhB[jj]], writes=[xbB])
                    pt, pB = w_mm8(("ewin", j, 32 + jj))
                    yield
                    sy.op("act", lambda e: e.activation(out=zbt[p][:], in_=pt[:], func=AF.Silu), reads=[pB],
                          writes=[zbtB[p]])
                    lw0 = pcols[("lconv_w", j)] + jj * 4
                    sy.op("dve", lambda e: e.tensor_tensor(
                        out=d4[p][:], in0=ident_b[:].unsqueeze(1).to_broadcast([128, 4, 128]),
                        in1=pvt[:, lw0:lw0 + 4].unsqueeze(2).to_broadcast([128, 4, 128]), op=ALU.mult),
                        reads=[constB, pvB], writes=[d4B[p]])
                    yield
                    pt, pB = nextps()
                    for k in range(4):
                        sy.op("pe", lambda e, k=k: e.matmul(pt[:, :], lhsT=d4[p][:, k, :], rhs=xbt[:, k + 1:k + 513],
                                                            start=(k == 0), stop=(k == 3)),
                              reads=[d4B[p], xbB], writes=[pB], inc=(k == 3))
                    yield
                    xc, xcB = lt[p][0], ltB[p][0]
                    rg, rgB = lt[p][1], ltB[p][1]
                    ig, igB = lt[p][2], ltB[p][2]
                    at, atB = lt[p][3], ltB[p][3]
                    hh_, hhB = lt[p][4], ltB[p][4]
                    sy.op("act", lambda e: e.activation(out=xc[:], in_=pt[:], func=AF.Identity,
                                                        bias=pv(("lconv_b", j), jj), scale=1.0),
                          reads=[pB], sreads=[pvB], writes=[xcB])
                    yield
                    sy.op("dve", lambda e: e.tensor_copy(out=xcbt[p][:], in_=xc[:]), reads=[xcB], writes=[xcbtB[p]])
                    sy.op("dve", lambda e: e.tensor_copy(out=xbh[:, jj, :], in_=xbt[:, 512:516]),
                          reads=[xbB], writes=[xbhB[jj]])
                    yield
                    wt, wB, wi = acquire(("egate", j, jj))
                    pr, prB = nextps()
                    sy.op("pe", lambda e: e.matmul(pr[:, :], lhsT=wt[:, 0:128], rhs=xcbt[p][:], start=True, stop=True),
                          reads=[wB, xcbtB[p]], writes=[prB])
                    pi_, piB = nextps()
                    sy.op("pe", lambda e: e.matmul(pi_[:, :], lhsT=wt[:, 128:256], rhs=xcbt[p][:], start=True, stop=True),
                          reads=[wB, xcbtB[p]], writes=[piB])
                    release(wi)
                    yield
                    sy.op("act", lambda e: e.activation(out=rg[:], in_=pr[:], func=AF.Sigmoid,
                                                        bias=pv(("ba", j), jj), scale=1.0),
                          reads=[prB], sreads=[pvB], writes=[rgB])
                    yield
                    sy.op("act", lambda e: e.activation(out=ig[:], in_=pi_[:], func=AF.Sigmoid,
                                                        bias=pv(("bx", j), jj), scale=1.0),
                          reads=[piB], sreads=[pvB], writes=[igB])
                    yield
                    sy.op("act", lambda e: e.activation(out=at[:], in_=rg[:], func=AF.Exp,
                                                        scale=nsp[:, j, jj:jj + 1]),
                          reads=[rgB], sreads=[nspB], writes=[atB])
                    sy.op("dve", lambda e: e.tensor_tensor(out=ig[:], in0=ig[:], in1=xc[:], op=ALU.mult),
                          reads=[xcB], writes=[igB])
                    yield
                    sy.op("dve", lambda e: e.tensor_tensor(out=rg[:], in0=at[:], in1=at[:], op=ALU.mult),
                          reads=[atB], writes=[rgB])
                    yield
                    sy.op("act", lambda e: e.activation(out=rg[:], in_=rg[:], func=AF.Sqrt, bias=1.0, scale=-1.0),
                          writes=[rgB])
                    yield
                    sy.op("dve", lambda e: e.tensor_tensor(out=ig[:], in0=ig[:], in1=rg[:], op=ALU.mult),
                          reads=[rgB], writes=[igB])
                    yield
                    sy.op("dve", lambda e: e.tensor_tensor_scan(out=hh_[:], data0=at[:], data1=ig[:],
                                                                initial=carry[:, jj:jj + 1], op0=ALU.mult,
                                                                op1=ALU.add),
                          reads=[atB, igB], sreads=[carryB[jj]], writes=[hhB])
                    yield
                    sy.op("act", lambda e: e.activation(out=carry[:, jj:jj + 1], in_=hh_[:, 511:512],
                                                        func=AF.Identity),
                          reads=[hhB], writes=[carryB[jj]])
                    sy.op("dve", lambda e: e.tensor_tensor(out=Lo[:, jj, :], in0=hh_[:], in1=zbt[p][:], op=ALU.mult),
                          reads=[hhB, zbtB[p]], writes=[LoB[jj]])

                prenorm(l, s, 0, lambda kc: hn[:, kc, :], hnB, sqt, sqB, tp, tpt)
                dump("hn0", hn[:, 0, :], [hnB[0]])
                pending_post = None
                for t in range(NB):
                    t0 = t * 512
                    active = [pending_post] if pending_post is not None else []
                    pending_post = None
                    for jj in range(8):
                        active += [conv_stage(jj), lru_stage(jj)]
                        steps = 0
                        while active and (jj == 7 or steps < EVEN_STAGGER):
                            for g_ in list(active):
                                try:
                                    next(g_)
                                except StopIteration:
                                    active.remove(g_)
                            steps += 1
                    w1s = [acquire(("ewout", j, n, 1)) for n in range(6)]
                    acc = [nextps() for n in range(6)]

                    def lo_half(k8):
                        for n in range(6):
                            sy.op("pe", lambda e, n=n: e.matmul(acc[n][0][:, :], lhsT=w1s[n][0][:, k8 * 128:(k8 + 1) * 128],
                                                                rhs=Lo[:, k8, :], start=(k8 == 0), stop=False),
                                  reads=[w1s[n][1], LoB[k8]], writes=[acc[n][1]], inc=(n == 5))
                    for k8 in range(7):
                        lo_half(k8)
                    sy.op("dve", lambda e: e.tensor_copy(out=sqt[:], in_=big[:]), reads=bigB, writes=[sqB])
                    pm, pmB = psum[0], psB[0]
                    for kc in range(KC):
                        sy.op("pe", lambda e, kc=kc: e.matmul(pm[:, :], lhsT=od1024[:], rhs=sqt[:, kc, :],
                                                              start=(kc == 0), stop=(kc == KC - 1)),
                              reads=[constB, sqB], writes=[pmB], inc=(kc == KC - 1))
                    sy.op("act", lambda e: e.activation(out=sqt[:], in_=big[:], func=AF.Square),
                          reads=bigB, writes=[sqB])
                    p2, p2B = psum[1], psB[1]
                    for kc in range(KC):
                        sy.op("pe", lambda e, kc=kc: e.matmul(p2[:, :], lhsT=od1024[:], rhs=sqt[:, kc, :],
                                                              start=(kc == 0), stop=(kc == KC - 1)),
                              reads=[constB, sqB], writes=[p2B], inc=(kc == KC - 1))
                    lo_half(7)
                    for w_ in w1s:
                        release(w_[2])
                    mean, meanB = tp()
                    sy.op("act", lambda e: e.activation(out=mean[:], in_=pm[:], func=AF.Identity), reads=[pmB],
                          writes=[meanB])
                    var, varB = tp()
                    sy.op("dve", lambda e: e.tensor_tensor(out=var[:], in0=mean[:], in1=mean[:], op=ALU.mult),
                          reads=[meanB], writes=[varB])
                    sy.op("dve", lambda e: e.tensor_tensor(out=var[:], in0=p2[:], in1=var[:], op=ALU.subtract),
                          reads=[p2B], writes=[varB])
                    sy.op("dve", lambda e: e.tensor_scalar(out=var[:], in0=var[:], scalar1=0.0, scalar2=None,
                                                           op0=ALU.max), writes=[varB])
                    sd, sdB = tp()
                    sy.op("act", lambda e: e.activation(out=sd[:], in_=var[:], func=AF.Ln, bias=EPS, scale=1.0),
                          reads=[varB], writes=[sdB])
                    rs, rsB = tp()
                    sy.op("act", lambda e: e.activation(out=rs[:], in_=sd[:], func=AF.Exp, scale=-0.5), reads=[sdB], writes=[rsB])
                    mr, mrB = tp()
                    sy.op("dve", lambda e: e.tensor_tensor(out=mr[:], in0=mean[:], in1=rs[:], op=ALU.mult),
                          reads=[meanB, rsB], writes=[mrB])
                    w0s = [acquire(("ewout", j, n, 0)) for n in range(6)]
                    s1s = {}

                    def ln_front(jj):
                        t1, t1B = tpt()
                        sy.op("dve", lambda e: e.tensor_tensor(out=t1[:], in0=big[:, jj, :], in1=rs[:], op=ALU.mult),
                              reads=[bigB[jj], rsB], writes=[t1B])
                        sy.op("dve", lambda e: e.tensor_tensor(out=t1[:], in0=t1[:], in1=mr[:], op=ALU.subtract),
                              reads=[mrB], writes=[t1B])
                        s1, s1B = tpb()
                        sy.op("act", lambda e: e.activation(out=s1[:], in_=t1[:], func=AF.Silu,
                                                            scale=pv(("ln_g", j), jj), bias=pv(("ln_b", j), jj)),
                              reads=[t1B], sreads=[pvB], writes=[s1B])
                        s1s[jj] = (s1, s1B)

                    def ln_back(jj):
                        s1, s1B = s1s.pop(jj)
                        sy.op("dve", lambda e: e.tensor_tensor(out=Z[:, jj, :], in0=s1[:], in1=Z[:, jj, :], op=ALU.mult),
                              reads=[s1B], writes=[ZB[jj]])
                        for n in range(6):
                            sy.op("pe", lambda e, n=n: e.matmul(acc[n][0][:, :], lhsT=w0s[n][0][:, jj * 128:(jj + 1) * 128],
                                                                rhs=Z[:, jj, :], start=False, stop=(jj == 7)),
                                  reads=[w0s[n][1], ZB[jj]], writes=[acc[n][1]], inc=(jj == 7 or n == 5))

                    ln_front(0)
                    for jj in range(8):
                        if jj + 1 < 8:
                            ln_front(jj + 1)
                        ln_back(jj)
                    for w_ in w0s:
                        release(w_[2])
                    dump("Aout0", Z[:, 0, :], [ZB[0]])
                    dump("Lo0", Lo[:, 0, :], [LoB[0]])
                    for n in range(6):
                        pt, pB = acc[n]
                        sy.op("act", lambda e: e.activation(out=big[:, n, :], in_=pt[:], func=AF.Identity),
                              reads=[pB], writes=[bigB[n]])
                    late = []
                    for n in (6, 7):
                        w0, w0B, wi0 = acquire(("ewout", j, n, 0))
                        w1, w1B, wi1 = acquire(("ewout", j, n, 1))
                        pt, pB = nextps()
                        for kc in range(16):
                            wsrc, wsB = (w0, w0B) if kc < 8 else (w1, w1B)
                            src, srcB = (Z, ZB) if kc < 8 else (Lo, LoB)
                            k8 = kc % 8
                            sy.op("pe", lambda e, kc=kc, k8=k8, wsrc=wsrc, src=src: e.matmul(
                                pt[:, :], lhsT=wsrc[:, k8 * 128:(k8 + 1) * 128], rhs=src[:, k8, :],
                                start=(kc == 0), stop=(kc == 15)),
                                reads=[wsB, srcB[k8]], writes=[pB], inc=(kc == 15))
                        release(wi0)
                        release(wi1)
                        late.append((n, pt, pB))
                    if t + 1 < NB:
                        prenorm(l, s, t0 + 512, lambda kc: hn[:, kc, :], hnB, sqt, sqB, tp, tpt, bank=0)
                    for n, pt, pB in late:
                        sy.op("act", lambda e: e.activation(out=big[:, n, :], in_=pt[:], func=AF.Identity),
                              reads=[pB], writes=[bigB[n]])
                    dump("y0", big[:, 0, :], [bigB[0]])
                    pending_post = postnorm_gen(l, s, t0, big, bigB, sqt, sqB, tp, tpt)
                for _ in pending_post:
                    pass
                sy.fence()

        def odd_layer(l, s):
            j = l // 2
            with ExitStack() as ol:
                def sbl(name, shape, dt):
                    return ol.enter_context(nc.sbuf_tensor(f"{name}_{l}_{s}", shape, dt))
                Zs = sbl("o_Zs", [128, KC, S], BF16)
                ZsB = [[Buf(f"o_Zs{c}_{b}") for b in range(NB)] for c in range(KC)]
                cqn = sbl("o_cqn", [128, 2, S], BF16)
                cqnB = [Buf(f"o_cqn{b}") for b in range(NB)]
                ckvn = sbl("o_ckvn", [128, 2, S], BF16)
                ckvnB = [Buf(f"o_ckvn{b}") for b in range(NB)]
                kr = sbl("o_kr", [128, S], BF16)
                krB = Buf("o_kr")
                COS = sbl("o_cos", [128, S], BF16)
                SIN = sbl("o_sin", [128, S], BF16)
                csB = Buf("o_cs")
                tp = mk_tmp_pool(ol, "o_tf", 4, F32)
                tp_stats[0] = mk_tmp_pool(ol, "o_ts", 3, F32)

                with ExitStack() as rl:
                    Q4 = S // 4
                    ang = rl.enter_context(nc.sbuf_tensor(f"o_ang_{l}_{s}", [128, Q4], F32))
                    wk = rl.enter_context(nc.sbuf_tensor(f"o_wk_{l}_{s}", [128, Q4], F32))
                    wk2 = rl.enter_context(nc.sbuf_tensor(f"o_wk2_{l}_{s}", [128, Q4], F32))
                    ki = rl.enter_context(nc.sbuf_tensor(f"o_ki_{l}_{s}", [128, Q4], I32))
                    posi = ki
                    rB = Buf("o_rope")
                    R = slice(64, 96)
                    for q4 in range(4):
                        src = AP(pos_d, s * S + q4 * Q4, [[0, 32], [1, Q4]])
                        sy.dma("sp", posi[q4 * 32:(q4 + 1) * 32, :], src, writes=[rB])
                    sy.op("dve", lambda e: e.tensor_copy(out=ang[:], in_=posi[:]), reads=[rB], writes=[rB])
                    sy.op("dve", lambda e: e.tensor_scalar(out=ang[:], in0=ang[:], scalar1=pvt[:, pcols["inv4"]:pcols["inv4"] + 1],
                                                           scalar2=None, op0=ALU.mult), sreads=[pvB], writes=[rB])
                    for which in range(2):
                        if which == 0:
                            sy.op("dve", lambda e: e.tensor_scalar(out=wk2[:], in0=ang[:], scalar1=math.pi / 2,
                                                                   scalar2=None, op0=ALU.add), writes=[rB])
                            a_in = wk2
                        else:
                            a_in = ang
                        sy.op("dve", lambda e: e.tensor_scalar(out=wk[:], in0=a_in[:], scalar1=1.0 / TWO_PI,
                                                               scalar2=None, op0=ALU.mult), writes=[rB])
                        sy.op("dve", lambda e: e.tensor_copy(out=ki[:], in_=wk[:]), writes=[rB])
                        sy.op("dve", lambda e: e.tensor_copy(out=wk[:], in_=ki[:]), writes=[rB])
                        sy.op("dve", lambda e: e.scalar_tensor_tensor(out=wk2[:], in0=wk[:], scalar=-PI_HI,
                                                                      in1=a_in[:], op0=ALU.mult, op1=ALU.add),
                              writes=[rB])
                        sy.op("dve", lambda e: e.scalar_tensor_tensor(out=wk2[:], in0=wk[:], scalar=-PI_LO,
                                                                      in1=wk2[:], op0=ALU.mult, op1=ALU.add),
                              writes=[rB])
                        sy.op("dve", lambda e: e.tensor_scalar(out=wk[:], in0=wk2[:], scalar1=math.pi,
                                                               scalar2=-TWO_PI, op0=ALU.is_gt, op1=ALU.mult), writes=[rB])
                        sy.op("dve", lambda e: e.tensor_tensor(out=wk2[:], in0=wk2[:], in1=wk[:], op=ALU.add),
                              writes=[rB])
                        sy.op("dve", lambda e: e.tensor_scalar(out=wk[:], in0=wk2[:], scalar1=-math.pi,
                                                               scalar2=TWO_PI, op0=ALU.is_lt, op1=ALU.mult), writes=[rB])
                        sy.op("dve", lambda e: e.tensor_tensor(out=wk2[:], in0=wk2[:], in1=wk[:], op=ALU.add),
                              writes=[rB])
                        sy.op("dve", lambda e: e.tensor_scalar(out=wk2[:], in0=wk2[:], scalar1=3.1415925,
                                                               scalar2=-3.1415925, op0=ALU.min, op1=ALU.max), writes=[rB])
                        if which == 1:
                            sy.op("dve", lambda e: e.tensor_scalar(out=wk2[:], in0=wk2[:],
                                                                   scalar1=pvt[:, pcols["sgn4"]:pcols["sgn4"] + 1],
                                                                   scalar2=None, op0=ALU.mult), sreads=[pvB], writes=[rB])
                        dstT = COS if which == 0 else SIN
                        for q4 in range(4):
                            sy.op("act", lambda e, q4=q4: e.activation(out=dstT[R, q4 * Q4:(q4 + 1) * Q4],
                                                                       in_=wk2[q4 * 32:(q4 + 1) * 32, :], func=AF.Sin),
                                  reads=[rB], writes=[csB])
                    sy.fence()

                with ExitStack() as p1:
                    hn = p1.enter_context(nc.sbuf_tensor(f"o_hn_{l}_{s}", [128, KC, 1024], BF16))
                    hnB2 = [[Buf(f"o_hn{c}_{b}") for c in range(KC)] for b in range(2)]
                    raw = p1.enter_context(nc.sbuf_tensor(f"o_raw_{l}_{s}", [128, 2, 1024], F32))
                    rawB = [Buf(f"o_raw{b}") for b in range(2)]
                    krA = p1.enter_context(nc.sbuf_tensor(f"o_krA_{l}_{s}", [128, 1024], F32))
                    krAB = [Buf(f"o_krA{b}") for b in range(2)]
                    sqt = p1.enter_context(nc.sbuf_tensor(f"o_sq_{l}_{s}", [128, KC, 512], BF16))
                    sqB = Buf("o_sq")
                    R = slice(64, 96)
                    for t in range(S // 1024):
                        for b in range(2):
                            t0 = t * 1024 + b * 512
                            prenorm(l, s, t0, lambda kc, b=b: hn[:, kc, b * 512:(b + 1) * 512], hnB2[b], sqt, sqB, tp)
                        def chain_gen(t=t):
                            for grp, (dst, dstB, nrm) in enumerate(((cqn, cqnB, "q_norm"), (ckvn, ckvnB, "kv_norm"))):
                                for c in range(2):
                                    wt, wB, wi = acquire(("owin", j, grp * 2 + c))
                                    for b in range(2):
                                        pt, pB = mm8(wt, wB, lambda kc, b=b: hn[:, kc, b * 512:(b + 1) * 512], hnB2[b])
                                        sy.op("act", lambda e: e.activation(out=raw[:, c, b * 512:(b + 1) * 512], in_=pt[:],
                                                                            func=AF.Identity),
                                              reads=[pB], writes=[rawB[b]])
                                        yield
                                    release(wi)
                                for b in range(2):
                                    gb = t * 2 + b
                                    rs, rsB = rms_stats(lambda b=b: raw[:, :, b * 512:(b + 1) * 512], [rawB[b]], 2, od256,
                                                        sqt, sqB, tp)
                                    yield
                                    for c in range(2):
                                        sy.op("dve", lambda e, c=c: e.scalar_tensor_tensor(
                                            out=dst[:, c, gb * 512:(gb + 1) * 512], in0=raw[:, c, b * 512:(b + 1) * 512],
                                            scalar=pv((nrm, j), c), in1=rs[:], op0=ALU.mult, op1=ALU.mult),
                                            reads=[rawB[b], rsB], sreads=[pvB], writes=[dstB[gb]])
                                    yield
                            wt, wB, wi = acquire(("owin", j, 4))
                            for b in range(2):
                                pt, pB = mm8(wt, wB, lambda kc, b=b: hn[:, kc, b * 512:(b + 1) * 512], hnB2[b])
                                sy.op("act", lambda e: e.activation(out=krA[R, b * 512:(b + 1) * 512], in_=pt[R, :],
                                                                    func=AF.Identity), reads=[pB], writes=[krAB[b]])
                                yield
                            release(wi)
                            wt, wB, wi = acquire(("owin", j, 5))
                            for b in range(2):
                                gb = t * 2 + b
                                tk = slice(gb * 512, (gb + 1) * 512)
                                pt, pB = mm8(wt, wB, lambda kc, b=b: hn[:, kc, b * 512:(b + 1) * 512], hnB2[b])
                                t1, t1B = tp()
                                sy.op("dve", lambda e: e.tensor_tensor(out=t1[R, :], in0=krA[R, b * 512:(b + 1) * 512],
                                                                       in1=COS[R, tk], op=ALU.mult),
                                      reads=[krAB[b], csB], writes=[t1B])
                                t2, t2B = tp()
                                sy.op("dve", lambda e: e.tensor_tensor(out=t2[R, :], in0=pt[R, :], in1=SIN[R, tk], op=ALU.mult),
                                      reads=[pB, csB], writes=[t2B])
                                sy.op("dve", lambda e: e.tensor_tensor(out=kr[R, tk], in0=t1[R, :], in1=t2[R, :], op=ALU.add),
                                      reads=[t1B, t2B], writes=[krB])
                                yield
                            release(wi)

                        def z_gen(t=t):
                            for c in range(8):
                                wt, wB, wi = acquire(("owin", j, 6 + c))
                                for b in range(2):
                                    gb = t * 2 + b
                                    pt, pB = mm8(wt, wB, lambda kc, b=b: hn[:, kc, b * 512:(b + 1) * 512], hnB2[b])
                                    sy.op("act", lambda e: e.activation(out=Zs[:, c, gb * 512:(gb + 1) * 512], in_=pt[:],
                                                                        func=AF.Silu), reads=[pB], writes=[ZsB[c][gb]])
                                    yield
                                release(wi)

                        gens = [chain_gen(), z_gen()]
                        while gens:
                            for g_ in list(gens):
                                try:
                                    next(g_)
                                except StopIteration:
                                    gens.remove(g_)
                    dump("cqn", cqn[:, 0, 0:512], [cqnB[0]])
                    dump("ckvn", ckvn[:, 0, 0:512], [ckvnB[0]])
                    dump("kr", kr[:, 0:512], [krB])
                    dump("cos", COS[:, 0:512], [csB])
                    dump("sin", SIN[:, 0:512], [csB])
                    dump("zs", Zs[:, 0, 0:512], [ZsB[0][0]])
                    sy.fence()

                with ExitStack() as p2:
                    QT = [p2.enter_context(nc.sbuf_tensor(f"o_QT{i}_{l}_{s}", [128, S], BF16)) for i in range(2)]
                    KT = [p2.enter_context(nc.sbuf_tensor(f"o_KT{i}_{l}_{s}", [128, S], BF16)) for i in range(2)]
                    V = [p2.enter_context(nc.sbuf_tensor(f"o_V{i}_{l}_{s}", [128, S // 128, 128], BF16)) for i in range(2)]
                    QTB = [Buf(f"o_QT{i}") for i in range(2)]
                    KTB = [Buf(f"o_KT{i}") for i in range(2)]
                    VB = [Buf(f"o_V{i}") for i in range(2)]
                    NPT = 6
                    PT = [p2.enter_context(nc.sbuf_tensor(f"o_PT{i}_{l}_{s}", [128, 512], BF16)) for i in range(NPT)]
                    PTB = [Buf(f"o_PT{i}") for i in range(NPT)]
                    rec = p2.enter_context(nc.sbuf_tensor(f"o_rec_{l}_{s}", [128, 512], F32))
                    recB = Buf("o_rec")
                    tpg = mk_tmp_pool(p2, "o_tg", 2, F32)
                    R = slice(64, 96)
                    sy.op("dve", lambda e: e.memset(V[0][:, :, 64:128], 1.0), writes=[VB[0]])
                    sy.op("dve", lambda e: e.memset(V[1][:, :, 0:64], 1.0), writes=[VB[1]])
                    pt_i = {"i": 0}

                    def gen_head(h):
                        par = h % 2
                        qt, qB = QT[par], QTB[par]
                        kt, kB = KT[par], KTB[par]
                        vt, vB = V[par], VB[par]
                        wt, wB, wi = acquire(("ouq", j, h))
                        for b in range(NB):
                            tk = slice(b * 512, (b + 1) * 512)
                            pa, paB = nextps()
                            pb, pbB = nextps()
                            for kc in range(2):
                                sy.op("pe", lambda e, kc=kc: e.matmul(pa[0:96, :], lhsT=wt[:, kc * 192:kc * 192 + 96],
                                                                      rhs=cqn[:, kc, tk], start=(kc == 0), stop=(kc == 1)),
                                      reads=[wB, cqnB[b]], writes=[paB], inc=(kc == 1))
                            for kc in range(2):
                                sy.op("pe", lambda e, kc=kc: e.matmul(pb[0:96, :], lhsT=wt[:, kc * 192 + 96:kc * 192 + 192],
                                                                      rhs=cqn[:, kc, tk], start=(kc == 0), stop=(kc == 1)),
                                      reads=[wB, cqnB[b]], writes=[pbB], inc=(kc == 1))
                            yield
                            sy.op("dve", lambda e: e.tensor_copy(out=qt[0:64, tk], in_=pa[0:64, :]),
                                  reads=[paB], writes=[qB])
                            t1, t1B = tpg()
                            sy.op("dve", lambda e: e.tensor_tensor(out=t1[R, :], in0=pa[R, :], in1=COS[R, tk], op=ALU.mult),
                                  reads=[paB, csB], writes=[t1B])
                            t2, t2B = tpg()
                            sy.op("dve", lambda e: e.tensor_tensor(out=t2[R, :], in0=pb[R, :], in1=SIN[R, tk], op=ALU.mult),
                                  reads=[pbB, csB], writes=[t2B])
                            yield
                            sy.op("dve", lambda e: e.tensor_tensor(out=qt[R, tk], in0=t1[R, :], in1=t2[R, :], op=ALU.add),
                                  reads=[t1B, t2B], writes=[qB])
                            yield
                        release(wi)
                        wt, wB, wi = acquire(("oukv", j, h))
                        for b in range(NB):
                            tk = slice(b * 512, (b + 1) * 512)
                            pa, paB = nextps()
                            for kc in range(2):
                                sy.op("pe", lambda e, kc=kc: e.matmul(pa[0:64, :], lhsT=wt[:, kc * 128:kc * 128 + 64],
                                                                      rhs=ckvn[:, kc, tk], start=(kc == 0), stop=(kc == 1)),
                                      reads=[wB, ckvnB[b]], writes=[paB], inc=(kc == 1))
                            yield
                            sy.op("dve", lambda e: e.tensor_copy(out=kt[0:64, tk], in_=pa[0:64, :]),
                                  reads=[paB], writes=[kB])
                            yield
                        sy.op("pool", lambda e: e.tensor_copy(out=kt[R, :], in_=kr[R, :]), reads=[krB], writes=[kB])
                        vo = 0 if par == 0 else 64
                        for g8 in range(S // 1024):
                            pa, paB = nextps()
                            for i8 in range(8):
                                kb = g8 * 8 + i8
                                for kc in range(2):
                                    sy.op("pe", lambda e, kc=kc, kb=kb, i8=i8: e.matmul(
                                        pa[:, i8 * 64:(i8 + 1) * 64], lhsT=ckvn[:, kc, kb * 128:(kb + 1) * 128],
                                        rhs=wt[:, kc * 128 + 64:kc * 128 + 128], start=(kc == 0), stop=(kc == 1)),
                                        reads=[wB, ckvnB[kb // 4]], writes=[paB], inc=(kc == 1 and i8 == 7))
                            yield
                            sy.op("dve", lambda e: e.tensor_copy(
                                out=vt[:, g8 * 8:(g8 + 1) * 8, vo:vo + 64],
                                in_=pa[:, :].rearrange("p (a b) -> p a b", b=64)),
                                reads=[paB], writes=[vB])
                            yield
                        release(wi)

                    def attn_head(h):
                        par = h % 2
                        hp = h // 2
                        qt, qB = QT[par], QTB[par]
                        kt, kB = KT[par], KTB[par]
                        vt, vB = V[par], VB[par]
                        if h == 0:
                            dump("qt", qt[:, 0:512], [qB])
                            dump("kt", kt[:, 0:512], [kB])
                            dump("v0", vt[:, 0, :], [vB])
                        items = []
                        for g in range(NB):
                            for kb in range(4 * g + 4):
                                items.append((g, kb))
                        LA = 3
                        inflight = {}
                        for i in range(len(items) + LA):
                            if i < len(items):
                                g, kb = items[i]
                                d = kb - 4 * g
                                c0 = max(0, d) * 128
                                ncols = 512 - c0
                                sp_, spB = nextps()
                                sy.op("pe", lambda e, kb=kb, g=g, c0=c0, ncols=ncols, sp_=sp_: e.matmul(
                                    sp_[:, 0:ncols], lhsT=kt[0:96, kb * 128:(kb + 1) * 128],
                                    rhs=qt[0:96, g * 512 + c0:(g + 1) * 512], start=True, stop=(d < 0)),
                                    reads=[kB, qB], writes=[spB], inc=(d < 0))
                                if d >= 0:
                                    sy.op("pe", lambda e, sp_=sp_: e.matmul(sp_[:, 0:128], lhsT=ident_b[:], rhs=maskT[:],
                                                                            start=False, stop=True),
                                          reads=[constB], writes=[spB])
                                inflight[i] = (sp_, spB, c0, ncols)
                            if i >= LA:
                                ii = i - LA
                                g, kb = items[ii]
                                sp_, spB, c0, ncols = inflight.pop(ii)
                                pi = pt_i["i"] % NPT
                                pt_i["i"] += 1
                                ptile, ptB = PT[pi], PTB[pi]
                                sy.op("act", lambda e, sp_=sp_, ncols=ncols, ptile=ptile: e.activation(
                                    out=ptile[:, 0:ncols], in_=sp_[:, 0:ncols], func=AF.Exp, scale=ATT_SCALE),
                                    reads=[spB], writes=[ptB])
                                op_, opB = psum[g % 2], psB[g % 2]
                                nkb = 4 * g + 4
                                sy.op("pe", lambda e, kb=kb, c0=c0, ncols=ncols, ptile=ptile, op_=op_, nkb=nkb: e.matmul(
                                    op_[:, c0:512], lhsT=vt[:, kb, :], rhs=ptile[:, 0:ncols],
                                    start=(kb == 0), stop=(kb == nkb - 1)),
                                    reads=[vB, ptB], writes=[opB], inc=True)
                                if kb == nkb - 1:
                                    if par == 0:
                                        num, den = slice(0, 64), slice(64, 128)
                                    else:
                                        num, den = slice(64, 128), slice(0, 64)
                                    tk = slice(g * 512, (g + 1) * 512)
                                    sy.op("act", lambda e, op_=op_: e.activation(out=rec[num, :], in_=op_[den, :], func=AF.Ln),
                                          reads=[opB], writes=[recB])
                                    sy.op("act", lambda e: e.activation(out=rec[num, :], in_=rec[num, :], func=AF.Exp, scale=-1.0),
                                          writes=[recB])
                                    o1, o1B = tp()
                                    sy.op("dve", lambda e, op_=op_, o1=o1: e.tensor_tensor(out=o1[num, :], in0=op_[num, :],
                                                                                         in1=rec[num, :], op=ALU.mult),
                                          reads=[opB, recB], writes=[o1B])
                                    sy.op("dve", lambda e, o1=o1: e.tensor_tensor(out=Zs[num, hp, tk], in0=o1[num, :],
                                                                                in1=Zs[num, hp, tk], op=ALU.mult),
                                          reads=[o1B], writes=[ZsB[hp][g]])
                            yield

                    for _ in gen_head(0):
                        pass
                    for h in range(16):
                        gens = [attn_head(h)]
                        if h + 1 < 16:
                            gens.append(gen_head(h + 1))
                        while gens:
                            for g_ in list(gens):
                                try:
                                    next(g_)
                                except StopIteration:
                                    gens.remove(g_)
                    sy.fence()

                dump("og", Zs[:, 0, 0:512], [ZsB[0][0]])
                with ExitStack() as p3:
                    big = p3.enter_context(nc.sbuf_tensor(f"o_big_{l}_{s}", [128, KC, 512], F32))
                    bigB = [Buf(f"o_big{c}") for c in range(KC)]
                    big2 = p3.enter_context(nc.sbuf_tensor(f"o_big2_{l}_{s}", [128, KC, 512], F32))
                    big2B = [Buf(f"o_big2{c}") for c in range(KC)]
                    sqt = p3.enter_context(nc.sbuf_tensor(f"o_sq3_{l}_{s}", [128, KC, 512], BF16))
                    sqB = Buf("o_sq3")
                    wl = [acquire(("owout", j, n)) for n in range(8)]
                    bigs = [(big, bigB), (big2, big2B)]
                    pend = None
                    for b in range(NB):
                        tk = slice(b * 512, (b + 1) * 512)
                        bg, bgB = bigs[b % 2]
                        for n in range(8):
                            wt, wB, _ = wl[n]
                            pt, pB = mm8(wt, wB, lambda kc: Zs[:, kc, tk], [ZsB[c][b] for c in range(KC)])
                            sy.op("act", lambda e: e.activation(out=bg[:, n, :], in_=pt[:], func=AF.Identity),
                                  reads=[pB], writes=[bgB[n]])
                            if pend is not None:
                                try:
                                    next(pend)
                                    next(pend)
                                except StopIteration:
                                    pend = None
                        if pend is not None:
                            for _ in pend:
                                pass
                        pend = postnorm_gen(l, s, b * 512, bg, bgB, sqt, sqB, tp)
                    for _ in pend:
                        pass
                    for _w in wl:
                        release(_w[2])
                    sy.fence()
                tp_stats[0] = None

        allxs = [xsB[c][b] for c in range(KC) for b in range(NB)]
        outB = Buf("outst")
        for s in range(NSEQ):
            for c in range(KC):
                sy.dma("sp", xs[:, c, :], x_d.ap()[s, :, c, :], writes=xsB[c])
            for l in LAYERS:
                if l % 2 == 0:
                    even_layer(l, s)
                else:
                    odd_layer(l, s)
            for c in range(KC):
                sy.dma("sp", out_d.ap()[s, :, c, :], xs[:, c, :], reads=xsB[c], writes=[outB])
        nc.sync.wait_ge(outB.dsem, outB.dcnt)
        if DEBUG and dbgB.dsem is not None:
            nc.sync.wait_ge(dbgB.dsem, dbgB.dcnt)
        if RECORD:
            return ws["order"]
        assert ws["acq"] == len(ws["order"]) == ws["issued"], (ws["acq"], len(ws["order"]), ws["issued"])
        build.nins = sy.nins
    return nc


_CACHE = {}


def kernel(**inp):
    inp = {k: np.asarray(v) for k, v in inp.items()}
    x = inp["x"].astype(np.float32, copy=False)
    B, S, Dm = x.shape
    nseq = B // NCORES
    wts = pack_weights(inp)
    pvv = pack_pv(inp)
    key = (S, nseq)
    if key not in _CACHE:
        _CACHE[key] = build(S=S, NSEQ=nseq)
    nc = _CACHE[key]
    in_maps = []
    for cid in range(NCORES):
        bs = slice(cid * nseq, (cid + 1) * nseq)
        xf = np.ascontiguousarray(x[bs].reshape(nseq, S, KC, 128).transpose(0, 3, 2, 1))
        cf = np.ascontiguousarray(inp["c"][bs].astype(np.float32).reshape(nseq, KC, 128).transpose(2, 1, 0))
        pos = np.ascontiguousarray(inp["positions"][bs].astype(np.int32))
        in_maps.append({"x": xf, "c": cf, "pos": pos, "wts": wts, "pv": pvv})
    res = run_bass_kernel_spmd(nc, in_maps, core_ids=list(range(NCORES)))
    outs = []
    for cid in range(NCORES):
        o = np.asarray(res.results[cid]["out"]).reshape(nseq, 128, KC, S)
        outs.append(o.transpose(0, 3, 2, 1).reshape(nseq, S, Dm))
    return np.ascontiguousarray(np.concatenate(outs, axis=0).astype(np.float32))
```
